# Optimizing a Trainium2 kernel written in Bass

```python
import jax, jax.numpy as jnp
from jax import lax
import numpy as np

D_MODEL = 1024
BATCH = 8
SEQ = 2048
DEPTH = 1
DEC_BATCH = 128
DEC_SEQ = 8
PAST_LEN = 16384
PAGE_SIZE = 128

N_META = 16
H_A = 4
DV_A = D_MODEL // H_A
DK_A = DV_A // 2
CHUNK_A = 64
N_B = 64
H_B = D_MODEL // N_B
D_B = H_B * N_B
LORA_W = 64
LORA_A = 64
LORA_G = 128
D_FF = 2816
CONV_W = 3
LN_EPS = 1e-5
GN_EPS_B = 64e-5
ALPHA = (2 * DEPTH) ** 0.25
BETA = (8 * DEPTH) ** -0.25
P_A = 2 * H_A * DK_A + 2 * H_A * DV_A + 2 * H_A
P_B = 3 * D_B + LORA_W + LORA_A + LORA_G
OFF_B = 2 * D_MODEL + P_A
P_TOTAL = OFF_B + P_B

kernel_name = "hybrid_mlstm_rwkv7_gated_merge_convffn_step"


def _split(a, sizes):
    return jnp.split(a, np.cumsum(sizes)[:-1].tolist(), axis=-1)


def _layer_norm(x, g, b):
    xf = x.astype(jnp.float32)
    mu = jnp.mean(xf, -1, keepdims=True)
    var = jnp.mean(jnp.square(xf - mu), -1, keepdims=True)
    return ((xf - mu) * lax.rsqrt(var + LN_EPS) * g + b).astype(x.dtype)


def _head_norm(h, eps):
    hf = h.astype(jnp.float32)
    mu = jnp.mean(hf, -1, keepdims=True)
    var = jnp.mean(jnp.square(hf - mu), -1, keepdims=True)
    return (hf - mu) * lax.rsqrt(var + eps)


def _mlstm_chunk(carry, inp):
    C, n, m = carry
    q, k, v, li, lf = inp
    L = q.shape[2]
    b = jnp.cumsum(lf, axis=-1)
    causal = jnp.tril(jnp.ones((L, L), dtype=bool))
    dlog = jnp.where(causal, b[..., :, None] - b[..., None, :] + li[..., None, :], -jnp.inf)
    inter = b + m[..., None]
    m_row = jnp.maximum(inter, jnp.max(dlog, axis=-1))
    s = jnp.einsum('bhtd,bhsd->bhts', q, k) * jnp.exp(dlog - m_row[..., None])
    w_inter = jnp.exp(inter - m_row)
    num = jnp.einsum('bhts,bhsv->bhtv', s, v) + w_inter[..., None] * jnp.einsum('bhtd,bhdv->bhtv', q, C)
    den = jnp.sum(s, -1) + w_inter * jnp.einsum('bhtd,bhd->bht', q, n)
    h = num / jnp.maximum(jnp.abs(den), jnp.exp(-m_row))[..., None]
    b_last = b[..., -1]
    g_log = b_last[..., None] - b + li
    m_new = jnp.maximum(b_last + m, jnp.max(g_log, axis=-1))
    decay = jnp.exp(b_last + m - m_new)
    wk = jnp.exp(g_log - m_new[..., None])
    C_new = decay[..., None, None] * C + jnp.einsum('bhs,bhsd,bhsv->bhdv', wk, k, v)
    n_new = decay[..., None] * n + jnp.einsum('bhs,bhsd->bhd', wk, k)
    return (C_new, n_new, m_new), h


def _mlstm_scan(state, q, k, v, li, lf, L):
    B, H, T = li.shape
    nc = T // L

    def to_chunks(a):
        return jnp.moveaxis(a.reshape(B, H, nc, L, *a.shape[3:]), 2, 0)

    state, h = lax.scan(_mlstm_chunk, state, tuple(to_chunks(a) for a in (q, k, v, li, lf)))
    return state, jnp.moveaxis(h, 0, 2).reshape(B, H, T, DV_A)


def _mlstm_mixer(qA, kA, vA, oA, iA, fA, b_if, norm_g, state, segments):
    B, T, _ = qA.shape
    f32 = jnp.float32

    def heads(a, d):
        return a.reshape(B, T, H_A, d).transpose(0, 2, 1, 3).astype(f32)

    q = heads(qA, DK_A)
    k = heads(kA, DK_A) * (DK_A ** -0.5)
    v = heads(vA, DV_A)
    gates = (jnp.concatenate([iA, fA], -1) + b_if).astype(f32).transpose(0, 2, 1)
    li = gates[:, :H_A]
    lf = jax.nn.log_sigmoid(gates[:, H_A:])
    hs = []
    start = 0
    for length, chunk in segments:
        sl = slice(start, start + length)
        state, h = _mlstm_scan(state, q[:, :, sl], k[:, :, sl], v[:, :, sl], li[:, :, sl], lf[:, :, sl], chunk)
        hs.append(h)
        start += length
    h = jnp.concatenate(hs, axis=2)
    h = _head_norm(h, LN_EPS).transpose(0, 2, 1, 3).reshape(B, T, H_A * DV_A)
    return h * norm_g * jax.nn.sigmoid(oA.astype(f32)), state


def _rwkv7_step(S, inp):
    r, w, k, v, kk, a = inp
    sa = jnp.einsum('bhvk,bhk->bhv', S, -kk)
    S = S * w[:, :, None, :] + sa[..., None] * (kk * a)[:, :, None, :] + v[..., None] * k[:, :, None, :]
    return S, jnp.einsum('bhvk,bhk->bhv', S, r)


def _rwkv7_mixer(x, pB, w_inB, shift0, S0, mu, w0, w2, a0, a2, g2, kk_scale, ka_scale, r_k, lnx_g, lnx_b):
    B, T, _ = pB.shape
    f32 = jnp.float32
    prev = (shift0.astype(x.dtype) @ w_inB)[:, None].astype(pB.dtype)
    pB_prev = jnp.concatenate([prev, pB[:, :-1]], axis=1)
    xs = (pB + (pB_prev - pB) * mu).astype(f32)
    r, k, v, xw, xa, xg = _split(xs, [D_B, D_B, D_B, LORA_W, LORA_A, LORA_G])
    w_log = -jax.nn.softplus(-(w0 + jnp.tanh(xw) @ w2)) - 0.5
    decay = jnp.exp(-jnp.exp(w_log))
    a = jax.nn.sigmoid(a0 + xa @ a2)
    g = jax.nn.sigmoid(xg) @ g2

    def heads(t):
        return t.reshape(B, T, H_B, N_B)

    kk = heads(k * kk_scale)
    kk = kk / jnp.maximum(jnp.sqrt(jnp.sum(jnp.square(kk), -1, keepdims=True)), 1e-12)
    k = k * (1.0 + (a - 1.0) * ka_scale)
    r_h, k_h, v_h, w_h, a_h = heads(r), heads(k), heads(v), heads(decay), heads(a)
    seq = tuple(jnp.moveaxis(t, 1, 0) for t in (r_h, w_h, k_h, v_h, kk, a_h))
    S, y = lax.scan(_rwkv7_step, S0.astype(f32), seq)
    y = jnp.moveaxis(y, 0, 1)
    y = _head_norm(y, GN_EPS_B).reshape(B, T, D_B) * lnx_g + lnx_b
    bonus = jnp.sum(r_h * k_h * r_k.reshape(H_B, N_B), -1, keepdims=True) * v_h
    y = (y + bonus.reshape(B, T, D_B)) * g
    return y, S, x[:, -1]


def _conv_ffn(x1, conv0, w_up, conv_w, conv_b, w_down):
    T = x1.shape[1]
    ag, av = jnp.split(x1 @ w_up, 2, axis=-1)
    a_pad = jnp.concatenate([conv0.astype(ag.dtype), ag], axis=1)
    conv = conv_b + sum(a_pad[:, j:j + T] * conv_w[j] for j in range(CONV_W))
    h = jax.nn.gelu(conv) * av
    return h @ w_down, a_pad[:, T:]


def _layer(x, state, lw, segments):
    C0, n0, m0, S0, shift0, conv0 = state
    f32 = jnp.float32
    proj = x @ lw['w_in']
    gA, gB, qA, kA, vA, oA, iA, fA, pB = _split(
        proj, [D_MODEL, D_MODEL, H_A * DK_A, H_A * DK_A, H_A * DV_A, H_A * DV_A, H_A, H_A, P_B])
    mstate = (C0.astype(f32), n0.astype(f32), m0.astype(f32))
    hA, (C1, n1, m1) = _mlstm_mixer(qA, kA, vA, oA, iA, fA, lw['b_if'], lw['mlstm_norm_g'], mstate, segments)
    hB, S1, shift1 = _rwkv7_mixer(x, pB, lw['w_in'][:, OFF_B:], shift0, S0, lw['rwkv_mu'], lw['rwkv_w0'],
                                  lw['rwkv_w2'], lw['rwkv_a0'], lw['rwkv_a2'], lw['rwkv_g2'], lw['rwkv_kk_scale'],
                                  lw['rwkv_ka_scale'], lw['rwkv_rk'], lw['rwkv_lnx_g'], lw['rwkv_lnx_b'])
    merged = (jax.nn.sigmoid(gA.astype(f32)) * hA + jax.nn.sigmoid(gB.astype(f32)) * hB).astype(x.dtype)
    x1 = _layer_norm(ALPHA * x + merged @ lw['w_out'], lw['ln1_g'], lw['ln1_b'])
    f, conv1 = _conv_ffn(x1, conv0, lw['ffn_w_up'], lw['ffn_conv_w'], lw['ffn_conv_b'], lw['ffn_w_down'])
    x2 = _layer_norm(ALPHA * x1 + f, lw['ln2_g'], lw['ln2_b'])
    return x2, (C1, n1, m1, S1, shift1, conv1)


def setup_inputs(seed: int = 0) -> dict:
    key = jax.random.key(seed)
    ks = jax.random.split(key, 40)
    nrm = jax.random.normal
    f32 = jnp.float32
    L = DEPTH
    b_if = jnp.concatenate([
        -2.0 + 0.5 * nrm(ks[0], (L, H_A), f32),
        jnp.linspace(3.0, 6.0, H_A, dtype=f32) + 0.1 * nrm(ks[1], (L, H_A), f32)], axis=-1)
    return {
        'x_prompt': nrm(ks[2], (BATCH, SEQ, D_MODEL), f32),
        'x_sample': nrm(ks[3], (DEC_BATCH, DEC_SEQ, D_MODEL), f32),
        'state_mlstm_C': 0.1 * nrm(ks[4], (L, DEC_BATCH, H_A, DK_A, DV_A), f32),
        'state_mlstm_n': 0.3 * nrm(ks[5], (L, DEC_BATCH, H_A, DK_A), f32),
        'state_mlstm_m': nrm(ks[6], (L, DEC_BATCH, H_A), f32),
        'state_rwkv_S': 0.1 * nrm(ks[7], (L, DEC_BATCH, H_B, N_B, N_B), f32),
        'state_rwkv_shift': nrm(ks[8], (L, DEC_BATCH, D_MODEL), f32),
        'state_ffn_conv': nrm(ks[9], (L, DEC_BATCH, CONV_W - 1, D_FF), f32),
        'meta_tokens': nrm(ks[10], (N_META, D_MODEL), f32),
        'ln_in_g': 1.0 + 0.05 * nrm(ks[11], (D_MODEL,), f32),
        'ln_in_b': 0.02 * nrm(ks[12], (D_MODEL,), f32),
        'w_in': nrm(ks[13], (L, D_MODEL, P_TOTAL), f32) * D_MODEL ** -0.5,
        'b_if': b_if,
        'mlstm_norm_g': 1.0 + 0.1 * nrm(ks[14], (L, H_A * DV_A), f32),
        'rwkv_mu': jax.random.uniform(ks[15], (L, P_B), f32),
        'rwkv_w0': jax.random.uniform(ks[16], (L, D_B), f32, -6.0, 1.0),
        'rwkv_w2': nrm(ks[17], (L, LORA_W, D_B), f32) * 0.5 * LORA_W ** -0.5,
        'rwkv_a0': 0.1 * nrm(ks[18], (L, D_B), f32),
        'rwkv_a2': nrm(ks[19], (L, LORA_A, D_B), f32) * 0.5 * LORA_A ** -0.5,
        'rwkv_g2': nrm(ks[20], (L, LORA_G, D_B), f32) * LORA_G ** -0.5,
        'rwkv_kk_scale': 0.85 + 0.05 * nrm(ks[21], (L, D_B), f32),
        'rwkv_ka_scale': 1.0 + 0.05 * nrm(ks[22], (L, D_B), f32),
        'rwkv_rk': 0.1 * nrm(ks[23], (L, D_B), f32),
        'rwkv_lnx_g': 1.0 + 0.1 * nrm(ks[24], (L, D_B), f32),
        'rwkv_lnx_b': 0.02 * nrm(ks[25], (L, D_B), f32),
        'w_out': nrm(ks[26], (L, D_MODEL, D_MODEL), f32) * D_MODEL ** -0.5 * BETA,
        'ln1_g': 1.0 + 0.05 * nrm(ks[27], (L, D_MODEL), f32),
        'ln1_b': 0.02 * nrm(ks[28], (L, D_MODEL), f32),
        'ffn_w_up': nrm(ks[29], (L, D_MODEL, 2 * D_FF), f32) * D_MODEL ** -0.5,
        'ffn_conv_w': nrm(ks[30], (L, CONV_W, D_FF), f32) * CONV_W ** -0.5,
        'ffn_conv_b': 0.02 * nrm(ks[31], (L, D_FF), f32),
        'ffn_w_down': nrm(ks[32], (L, D_FF, D_MODEL), f32) * D_FF ** -0.5 * BETA,
        'ln2_g': 1.0 + 0.05 * nrm(ks[33], (L, D_MODEL), f32),
        'ln2_b': 0.02 * nrm(ks[34], (L, D_MODEL), f32),
    }


def reference(x_prompt, x_sample, state_mlstm_C, state_mlstm_n, state_mlstm_m, state_rwkv_S, state_rwkv_shift,
              state_ffn_conv, meta_tokens, ln_in_g, ln_in_b, w_in, b_if, mlstm_norm_g, rwkv_mu, rwkv_w0, rwkv_w2,
              rwkv_a0, rwkv_a2, rwkv_g2, rwkv_kk_scale, rwkv_ka_scale, rwkv_rk, rwkv_lnx_g, rwkv_lnx_b, w_out,
              ln1_g, ln1_b, ffn_w_up, ffn_conv_w, ffn_conv_b, ffn_w_down, ln2_g, ln2_b):
    f32 = jnp.float32
    Bp, Tp, _ = x_prompt.shape
    Bs, Ts, _ = x_sample.shape
    meta = jnp.broadcast_to(meta_tokens.astype(x_prompt.dtype)[None], (Bp, N_META, D_MODEL))
    xp = _layer_norm(jnp.concatenate([meta, x_prompt], axis=1), ln_in_g, ln_in_b)
    xs = _layer_norm(x_sample, ln_in_g, ln_in_b)
    zero_state = (jnp.zeros((Bp, H_A, DK_A, DV_A), f32), jnp.zeros((Bp, H_A, DK_A), f32),
                  jnp.zeros((Bp, H_A), f32), jnp.zeros((Bp, H_B, N_B, N_B), f32),
                  jnp.zeros((Bp, D_MODEL), x_prompt.dtype), jnp.zeros((Bp, CONV_W - 1, D_FF), x_prompt.dtype))
    seg_prompt = [(N_META, N_META), (Tp, CHUNK_A)]
    seg_sample = [(Ts, Ts)]
    p_new, s_new = [], []
    for l in range(DEPTH):
        lw = {'w_in': w_in[l], 'b_if': b_if[l], 'mlstm_norm_g': mlstm_norm_g[l], 'rwkv_mu': rwkv_mu[l],
              'rwkv_w0': rwkv_w0[l], 'rwkv_w2': rwkv_w2[l], 'rwkv_a0': rwkv_a0[l], 'rwkv_a2': rwkv_a2[l],
              'rwkv_g2': rwkv_g2[l], 'rwkv_kk_scale': rwkv_kk_scale[l], 'rwkv_ka_scale': rwkv_ka_scale[l],
              'rwkv_rk': rwkv_rk[l], 'rwkv_lnx_g': rwkv_lnx_g[l], 'rwkv_lnx_b': rwkv_lnx_b[l], 'w_out': w_out[l],
              'ln1_g': ln1_g[l], 'ln1_b': ln1_b[l], 'ffn_w_up': ffn_w_up[l], 'ffn_conv_w': ffn_conv_w[l],
              'ffn_conv_b': ffn_conv_b[l], 'ffn_w_down': ffn_w_down[l], 'ln2_g': ln2_g[l], 'ln2_b': ln2_b[l]}
        xp, st_p = _layer(xp, zero_state, lw, seg_prompt)
        st_in = (state_mlstm_C[l], state_mlstm_n[l], state_mlstm_m[l], state_rwkv_S[l], state_rwkv_shift[l],
                 state_ffn_conv[l])
        xs, st_s = _layer(xs, st_in, lw, seg_sample)
        p_new.append(st_p)
        s_new.append(st_s)

    def stack(lst, i):
        return jnp.stack([st[i] for st in lst], axis=0)

    y_prompt = xp[:, N_META:]
    y_sample = xs
    return (y_prompt, y_sample,
            stack(p_new, 0), stack(p_new, 1), stack(p_new, 2), stack(p_new, 3), stack(p_new, 4), stack(p_new, 5),
            stack(s_new, 0), stack(s_new, 1), stack(s_new, 2), stack(s_new, 3), stack(s_new, 4), stack(s_new, 5))
```

```python
import contextlib
import os
import numpy as np
DBGSTEP = int(os.environ.get("DBGSTEP", "0"))
DBGSUB = int(os.environ.get("DBGSUB", "0"))
DBGCHUNK = int(os.environ.get("DBGCHUNK", "0"))
import concourse.bass as bass
import concourse.mybir as mybir
from concourse.bass_utils import run_bass_kernel_spmd

F32 = mybir.dt.float32
BF16 = mybir.dt.bfloat16
AF = mybir.ActivationFunctionType
ALU = mybir.AluOpType

NS_DMA = 6
ENGS = ("pe", "act", "dve", "pool", "sp")


class Op:
    __slots__ = ("eng", "fn", "deps", "dma", "signal", "val", "slot", "strict")


class Prog:
    def __init__(self):
        self.ops = {e: [] for e in ENGS}
        self.lastw = {}
        self.readers = {}
        self.pend = {e: [] for e in ENGS}
        self.dmaq = {e: [] for e in ENGS}

    def op(self, eng, fn, r=(), w=(), dma=False, strict=False):
        o = Op()
        o.strict = strict
        o.eng, o.fn, o.dma, o.signal, o.val, o.slot = eng, fn, dma, False, 0, 0
        deps = set(self.pend[eng])
        self.pend[eng] = []
        for k in r:
            lw = self.lastw.get(k)
            if lw is not None:
                deps.add(lw)
        for k in w:
            lw = self.lastw.get(k)
            if lw is not None:
                deps.add(lw)
            deps.update(self.readers.get(k, ()))
        if dma:
            q = self.dmaq[eng]
            n = len(q)
            o.slot = n % NS_DMA
            o.val = 16 * (n // NS_DMA + 1)
            if n >= NS_DMA:
                deps.add(q[n - NS_DMA])
            q.append(o)
        o.deps = deps
        for k in r:
            lst = self.readers.setdefault(k, [])
            if not dma:
                lst[:] = [x for x in lst if x.dma or x.eng != eng]
            lst.append(o)
        for k in w:
            self.lastw[k] = o
            self.readers[k] = []
        self.ops[eng].append(o)
        return o

    def barrier(self):
        lasts = []
        for e in ENGS:
            comp = [x for x in self.ops[e] if not x.dma]
            if comp:
                lasts.append(comp[-1])
            lasts.extend(self.dmaq[e][-NS_DMA:])
        for e in ENGS:
            self.pend[e].extend(lasts)
        self.lastw.clear()
        self.readers.clear()

    def emit(self, nc):
        for e in ENGS:
            for o in self.ops[e]:
                for d in o.deps:
                    if not d.dma and not (d.eng == "pe" and e == "pe" and not o.strict):
                        d.signal = True
        for e in ENGS:
            c = 0
            for o in self.ops[e]:
                if not o.dma and o.signal:
                    c += 1
                    o.val = c
        with contextlib.ExitStack() as st:
            csem = {e: st.enter_context(nc.semaphore("c_" + e)) for e in ENGS}
            dsem = {e: [st.enter_context(nc.semaphore("d_%s%d" % (e, i))) for i in range(NS_DMA)]
                    for e in ENGS if self.dmaq[e]}
            block = st.enter_context(nc.Block())

            def semof(o):
                return dsem[o.eng][o.slot] if o.dma else csem[o.eng]

            def run(e, h):
                waited = {}
                for o in self.ops[e]:
                    need = {}
                    for d in o.deps:
                        if d.eng == "pe" and e == "pe" and not d.dma and not o.strict:
                            continue
                        sm = semof(d)
                        if waited.get(sm, 0) < d.val and need.get(sm, 0) < d.val:
                            need[sm] = d.val
                    for sm, v in need.items():
                        h.wait_ge(sm, v)
                        waited[sm] = v
                    ins = o.fn(h)
                    if o.dma:
                        ins.then_inc(semof(o), 16)
                    elif o.signal:
                        ins.then_inc(csem[e], 1)
                for o in self.dmaq[e][-NS_DMA:]:
                    if waited.get(semof(o), 0) < o.val:
                        h.wait_ge(semof(o), o.val)
                        waited[semof(o)] = o.val

            @block.tensor
            def _(h):
                run("pe", h)

            @block.scalar
            def _(h):
                run("act", h)

            @block.vector
            def _(h):
                run("dve", h)

            @block.gpsimd
            def _(h):
                run("pool", h)

            @block.sync
            def _(h):
                run("sp", h)


D = 1024
DF = 2816
W = 274
EPS = 1e-5
GN_EPS = 64e-5
ALPHA = float(2.0 ** 0.25)
C0 = float(np.exp(-0.5))
NEG = -1.0e30
GELU_K = float(2.0 * np.sqrt(2.0 / np.pi))

MERGED = 0
QA, KA, VA, OA, GA = 8, 12, 16, 24, 32
GB, KKT, RB, KB, VB, BHT, BON = 8, 16, 24, 32, 40, 48, 56
AG, AV = 8, 30
NSLOT = 64

PC_MU, PC_W0, PC_A0, PC_KKS, PC_KAS, PC_RK, PC_LXG, PC_LXB, PC_MNG = 0, 26, 34, 42, 50, 58, 66, 74, 82
PC_LIG, PC_LIB, PC_L1G, PC_L1B = 90, 98, 106, 114
PC_CW0, PC_CW1, PC_CW2, PC_CB = 128, 150, 172, 194

BLOCKS = [("p", 0, 272, [(0, 16), (16, 64), (80, 64), (144, 64), (208, 64)], 0)]
for _i in range(1, 8):
    BLOCKS.append(("p", 272 + 256 * (_i - 1), 256, [(64 * c, 64) for c in range(4)], 1))
BLOCKS.append(("s", 0, 128, [(8 * c, 8) for c in range(16)], 2))


def build(blocks=BLOCKS, stop=None):
    nc = bass.Bass("TRN2", target_bir_lowering=False)

    def din(name, shape):
        return nc.dram_tensor(name, list(shape), F32, kind="ExternalInput").ap()

    def dout(name, shape):
        return nc.dram_tensor(name, list(shape), F32, kind="ExternalOutput").ap()

    xp = din("xp", [2048, D]); xs = din("xs", [128, D]); meta = din("meta", [16, D])
    sC = din("sC", [16, 4, 128, 256]); sn = din("sn", [16, 128, 4]); sm = din("sm", [4, 16])
    sH = din("sH", [16, 128, 8, 64]); ssh = din("ssh", [128, 8, 16]); scv = din("scv", [128, 22, 16, 2])
    prm0 = din("prm0", [122, 128]); prm1 = din("prm1", [88, 128])
    bif = din("bif", [8, 1]); ln2g = din("ln2g", [D]); ln2b = din("ln2b", [D])
    w_in = din("w_in", [D, 8456]); w2a2 = din("w2a2", [128, D]); g2 = din("g2", [128, D])
    w_out = din("w_out", [D, D]); w_up = din("w_up", [D, 2 * DF]); w_down = din("w_down", [DF, D])

    oy_p = dout("oy_p", [2048, D]); oy_s = dout("oy_s", [128, D])
    oCN_p = dout("oCN_p", [128, 4, 257]); om_p = dout("om_p", [4, 1]); oH_p = dout("oH_p", [128, 8, 64])
    osh_p = dout("osh_p", [128, 8, 1]); ocv_p = dout("ocv_p", [128, 22, 2])
    oCN_s = dout("oCN_s", [16, 128, 4, 257]); om_s = dout("om_s", [4, 16]); oH_s = dout("oH_s", [16, 128, 8, 64])
    osh_s = dout("osh_s", [128, 8, 16]); ocv_s = dout("ocv_s", [128, 22, 16, 2])

    wb_in = nc.dram_tensor("wb_in", [D, 8456], BF16).ap(); wb_out = nc.dram_tensor("wb_out", [D, D], BF16).ap()
    wb_up = nc.dram_tensor("wb_up", [D, 2 * DF], BF16).ap(); wb_down = nc.dram_tensor("wb_down", [DF, D], BF16).ap()
    win_v = wb_in.rearrange("(kc p) c -> p kc c", p=128)
    wup_v = wb_up.rearrange("(kc p) c -> p kc c", p=128)
    wout_v = wb_out.rearrange("(c p) d -> p c d", p=128)
    wdn_v = wb_down.rearrange("(c p) d -> p c d", p=128)

    P = Prog()
    st = contextlib.ExitStack()

    def sb(name, shape):
        return st.enter_context(nc.sbuf_tensor(name, list(shape), F32))

    ARENA = sb("ARENA", [128, NSLOT * W])
    XT = sb("XT", [128, 8, 272])
    TOK = [sb("TOK0", [128, D]), sb("TOK1", [128, D])]
    LNG = sb("LNG", [128, D]); LNB = sb("LNB", [128, D])
    WB = [st.enter_context(nc.sbuf_tensor("WB%d" % i, [128, 4096], BF16)) for i in range(2)]
    XTb = st.enter_context(nc.sbuf_tensor("XTb", [128, 8, 272], BF16))
    XTlo = st.enter_context(nc.sbuf_tensor("XTlo", [128, 8, 272], BF16))
    PC = sb("PC", [128, 216]); PCN = sb("PCN", [128, 16])
    W2A2 = sb("W2A2", [128, D]); G2 = sb("G2", [128, D])
    ID = sb("ID", [128, 128]); ONES = sb("ONES", [128, 128]); BLK = sb("BLK", [128, 128]); AID = sb("AID", [128, 128])
    MK1 = sb("MK1", [64, 2, 64]); MK1N = sb("MK1N", [64, 2, 64]); MLN = sb("MLN", [64, 64]); MNEG = sb("MNEG", [64, 64])
    SEL = sb("SEL", [4, 4, 128])
    RST01_ = sb("RST01", [128, W])
    RST01 = [RST01_, RST01_, RST01_]
    RSTN = sb("RSTN", [4, W])
    CN = [sb("CN0", [128, 4, 257]), sb("CN1", [128, 4, 257])]
    NBC = st.enter_context(nc.sbuf_tensor("NBC", [128, 4, 128], BF16))
    CNb = st.enter_context(nc.sbuf_tensor("CNb", [128, 4, 257], BF16))
    Hb = [st.enter_context(nc.sbuf_tensor("Hb0", [128, 8, 64], BF16)), st.enter_context(nc.sbuf_tensor("Hb1", [128, 8, 64], BF16))]
    IDb = st.enter_context(nc.sbuf_tensor("IDb", [128, 128], BF16)); ONESb = st.enter_context(nc.sbuf_tensor("ONESb", [128, 128], BF16))
    H = [sb("H0", [128, 8, 64]), sb("H1", [128, 8, 64])]
    CARRY = sb("CARRY", [128, 26]); AGC = sb("AGC", [128, 22, 2])
    CV0 = sb("CV0", [128, 22, 16, 2]);
    XWA = sb("XWA", [128, W]); XG = sb("XG", [128, W])
    GL = sb("GL", [128, 8, 16])
    BI = sb("BI", [4, 1]); NBF = sb("NBF", [4, 1]); MCOL = sb("MCOL", [4, 2]); M0ROW = sb("M0ROW", [4, 16]); MNEW = sb("MNEW", [4, 16])
    ST6 = sb("ST6", [128, 12]); MV = sb("MV", [128, 2]); SD = sb("SD", [128, 1]); RS = sb("RS", [128, 1])
    TN = ["LW", "AA", "KKS", "SQ", "NRM", "KKN", "T1", "BB", "CUM", "EGP", "EGN", "CX", "EGX"]
    T = {n: sb("T_" + n, [128, W]) for n in TN}
    PR0 = T["KKS"][:, 0:128]; PR1 = T["SQ"][:, 0:128]
    SHS = T["EGX"][:, 0:128].rearrange("p (a b) -> p a b", a=8)
    LI = sb("LI", [4, W]); LP = sb("LP", [4, W]); CUMP = sb("CUMP", [4, W]); DD = sb("DD", [4, W]); CM = sb("CM", [4, W])
    MX = sb("MX", [4, 64]); ROWS = sb("ROWS", [4, 3, 64]); TR4 = sb("TR4", [4, 64]); WKR = sb("WKR", [4, 64])
    COLS = sb("COLS", [64, 8])
    WST = [sb("WST%d" % i, [64, 64]) for i in range(4)]; SST = [st.enter_context(nc.sbuf_tensor("SST%d" % i, [64, 64], BF16)) for i in range(4)]
    QW = [st.enter_context(nc.sbuf_tensor("QW%d" % i, [128, 64], BF16)) for i in range(4)]; ADEN = [sb("ADEN%d" % i, [128, 64]) for i in range(4)]
    DDT = [sb("DDT%d" % i, [128, 64]) for i in range(4)]; KW = [st.enter_context(nc.sbuf_tensor("KW%d" % i, [64, 128], BF16)) for i in range(4)]
    DCOL = sb("DCOL", [128, 4])
    VTK = st.enter_context(nc.sbuf_tensor("VTK", [64, 512], BF16)); KHK = st.enter_context(nc.sbuf_tensor("KHK", [64, 512], BF16)); BHK = st.enter_context(nc.sbuf_tensor("BHK", [64, 512], BF16))
    S1raw = st.enter_context(nc.sbuf_tensor("S1raw", [64, 1028], BF16)); S2 = st.enter_context(nc.sbuf_tensor("S2", [64, 8, 2, 64], BF16))
    S1 = S1raw[:, 0:1024].rearrange("p (a b c) -> p a b c", a=8, b=2)
    VTK1 = S1raw[:, 0:1028].rearrange("p (h c) -> p h c", h=4)
    KTK = KHK
    PTN = [st.enter_context(nc.sbuf_tensor("PTN%d" % i, [64, 8, 64], BF16)) for i in range(2)]
    UP = [st.enter_context(nc.sbuf_tensor("UP%d" % i, [64, 8, 128], BF16)) for i in range(2)]
    TMPH = sb("TMPH", [128, 4, 64])
    ps = [st.enter_context(nc.psum_tensor("ps%d" % i, [128, 512], F32)) for i in range(8)]
    psi = [0]

    def PS():
        i = psi[0]
        psi[0] = (i + 1) % 8
        return ps[i], ("ps", i)

    def slot(i):
        return ARENA[:, i * W:(i + 1) * W]

    def slots(i, n):
        return ARENA[:, i * W:(i + n) * W].rearrange("p (c w) -> p c w", c=n)

    def AK(i):
        return ("A", i)

    BFA = ARENA[:, KKT * W:(KKT + 8) * W].bitcast(BF16)
    BFB = ARENA[:, BHT * W:(BHT + 8) * W].bitcast(BF16)

    def KKTb(j):
        return BFA[:, j * W:(j + 1) * W]

    def RTb(j):
        return BFA[:, (8 + j) * W:(9 + j) * W]

    def KHb(j):
        return BFB[:, j * W:(j + 1) * W]

    def BHb(j):
        return BFB[:, (8 + j) * W:(9 + j) * W]

    def Qb(h):
        return slot(QA + h).bitcast(BF16)

    def Kb(h):
        return slot(KA + h).bitcast(BF16)

    def MM(out, lhsT, rhs, start=True, stop=True, r=(), w=(), strict=False):
        P.op("pe", lambda h: h.matmul(out, lhsT=lhsT, rhs=rhs, start=start, stop=stop), r, w, strict=strict)

    def TR(out, in_, n, r=(), w=(), bf=False):
        idn = IDb[0:n, 0:n] if bf else ID[0:n, 0:n]
        P.op("pe", lambda h: h.transpose(out=out, in_=in_, identity=idn), list(r) + ["ID"], w)

    def ACT(out, in_, func, r=(), w=(), bias=0.0, scale=1.0):
        P.op("act", lambda h: h.activation(out=out, in_=in_, func=func, bias=bias, scale=scale), r, w)

    def TT(eng, out, in0, in1, op, r=(), w=()):
        P.op(eng, lambda h: h.tensor_tensor(out=out, in0=in0, in1=in1, op=op), r, w)

    def TS(eng, out, in0, s1, op0, r=(), w=(), s2=None, op1=None):
        if op1 is None and eng == "pool" and op0 == ALU.mult:
            s2, op1 = 1.0, ALU.mult
        if op1 is None:
            P.op(eng, lambda h: h.tensor_scalar(out=out, in0=in0, scalar1=s1, scalar2=None, op0=op0), r, w)
        else:
            P.op(eng, lambda h: h.tensor_scalar(out=out, in0=in0, scalar1=s1, scalar2=s2, op0=op0, op1=op1), r, w)

    def STT(out, in0, sc, in1, op0, op1, r=(), w=()):
        P.op("dve", lambda h: h.scalar_tensor_tensor(out=out, in0=in0, scalar=sc, in1=in1, op0=op0, op1=op1), r, w)

    def CP(eng, out, in_, r=(), w=()):
        if eng == "act":
            P.op("act", lambda h: h.copy(out=out, in_=in_), r, w)
        else:
            P.op(eng, lambda h: h.tensor_copy(out=out, in_=in_), r, w)

    def RECIP(out, in_, r=(), w=()):
        P.op("dve", lambda h: h.reciprocal(out=out, in_=in_), r, w)

    def MSET(eng, ap, v, w=()):
        P.op(eng, lambda h: h.memset(ap, v), (), w)

    def DMA(q, out, in_, r=(), w=(), slow=False):
        P.op(q, lambda h: h.dma_start(out=out, in_=in_, allow_slow_non_contiguous=slow), r, w, dma=True)

    def PCc(i):
        return PC[:, i:i + 1]

    DMA("sp", PR0[0:122, :], prm0, w=["PR0"])
    DMA("sp", PR1[0:88, :], prm1, w=["PR1"])
    DMA("sp", W2A2[:], w2a2, w=["W2A2"])
    DMA("sp", G2[:], g2, w=["G2"])
    DMA("sp", LNG[:], ln2g.partition_broadcast(128), w=["LNG"])
    DMA("sp", LNB[:], ln2b.partition_broadcast(128), w=["LNB"])
    DMA("sp", BI[:], bif[0:4, :], w=["BI"])
    DMA("sp", NBF[:], bif[4:8, :], w=["NBF"])
    DMA("sp", M0ROW[:], sm, w=["M0ROW"])
    DMA("sp", CV0[:], scv, w=[("CV0", i) for i in range(22)])
    MSET("pool", ONES[:], 1.0, w=["ONES"])
    ZER = T["LW"][:, 0:128]; NEG1 = T["AA"][:, 0:128]
    MSET("pool", ZER, 0.0, w=["ZER"])
    MSET("pool", NEG1, -1.0, w=["NEG1"])
    P.op("pool", lambda h: h.affine_select(out=ID[:], in_=ONES[:], pattern=[[1, 128]], compare_op=ALU.is_equal,
                                           fill=0.0, base=0, channel_multiplier=-1), ["ONES"], ["ID"])
    TS("pool", AID[:], ID[:], ALPHA, ALU.mult, r=["ID"], w=["AID"])
    CP("pool", IDb[:], ID[:], r=["ID"], w=["ID"])
    MSET("pool", ONESb[:], 1.0, w=["ONES"])
    MSET("pool", BLK[:], 0.0, w=["BLK"])
    MSET("pool", BLK[0:64, 0:64], 1.0, w=["BLK"])
    MSET("pool", BLK[64:128, 64:128], 1.0, w=["BLK"])
    P.op("pool", lambda h: h.affine_select(out=MK1[:, 0, :], in_=ONES[0:64, 0:64], pattern=[[1, 64]], compare_op=ALU.is_gt,
                                           fill=0.0, base=0, channel_multiplier=-1), ["ONES"], ["MK1"])
    P.op("pool", lambda h: h.affine_select(out=MK1[:, 1, :], in_=ONES[0:64, 0:64], pattern=[[1, 64]], compare_op=ALU.is_ge,
                                           fill=0.0, base=0, channel_multiplier=-1), ["ONES"], ["MK1"])
    TS("pool", MK1N[:], MK1[:], -1.0, ALU.mult, r=["MK1"], w=["MK1N"])
    P.op("pool", lambda h: h.affine_select(out=MLN[:], in_=NEG1[0:64, 0:64], pattern=[[-1, 64]], compare_op=ALU.is_gt,
                                           fill=0.0, base=0, channel_multiplier=1), ["NEG1"], ["MLN"])
    P.op("pool", lambda h: h.affine_select(out=MNEG[:], in_=ZER[0:64, 0:64], pattern=[[1, 64]], compare_op=ALU.is_ge,
                                           fill=NEG, base=0, channel_multiplier=-1), ["ZER"], ["MNEG"])
    CP("dve", SEL[:], ID[0:4, 0:4].unsqueeze(2).broadcast_to([4, 4, 128]), r=["ID"], w=["SEL"])
    TS("pool", NBF[:], NBF[:], -1.0, ALU.mult, r=["NBF"], w=["NBF"])
    pt, pk = PS()
    TR(pt[:, 0:122], PR0[0:122, :], 122, r=["PR0"], w=[pk])
    TR(pt[:, 128:216], PR1[0:88, :], 88, r=["PR1"], w=[pk])
    CP("dve", PC[:, 0:122], pt[:, 0:122], r=[pk], w=["PC"])
    CP("dve", PC[:, 128:216], pt[:, 128:216], r=[pk], w=["PC"])
    TS("dve", PCN[:], PC[:, PC_W0:PC_W0 + 16], -1.0, ALU.mult, r=["PC"], w=["PC"])
    MSET("pool", CN[0][:], 0.0, w=[("CN", 0, h) for h in range(4)])
    MSET("pool", NBC[:], 0.0, w=[("NBC", h) for h in range(4)])
    MSET("pool", H[0][:], 0.0, w=[("H", 0, 0), ("H", 0, 1)])
    MSET("pool", Hb[0][:], 0.0, w=[("Hb", 0, 0), ("Hb", 0, 1)])
    MSET("pool", CNb[:], 0.0, w=[("CNb", h) for h in range(4)])
    MSET("pool", MCOL[:], 0.0, w=["MC0", "MC1"])
    MSET("pool", CARRY[:], 0.0, w=[("CARRY", c) for c in range(26)])
    MSET("pool", AGC[:], 0.0, w=[("AGC", i) for i in range(22)])
    pcs = []
    for (wsrc, wdst, R, C) in ((w_in, wb_in, D, 8456), (w_out, wb_out, D, D), (w_up, wb_up, D, 2 * DF), (w_down, wb_down, DF, D)):
        for r0 in range(0, R, 128):
            for c0 in range(0, C, 4228):
                pcs.append((wsrc, wdst, r0, c0, min(4228, C - c0)))
    for pi_, (wsrc, wdst, r0, c0, n_) in enumerate(pcs):
        sl = pi_ % 2
        stg = ARENA[:, sl * 4228:sl * 4228 + n_]
        ob = ARENA[:, 8456 + sl * 2114:8456 + (sl + 1) * 2114].bitcast(BF16)[:, 0:n_]
        DMA("sp", stg, wsrc[r0:r0 + 128, c0:c0 + n_], w=[("STG", sl)])
        CP(("act", "dve", "pool")[pi_ % 3], ob, stg, r=[("STG", sl)], w=[("OB", sl)])
        DMA("pool", wdst[r0:r0 + 128, c0:c0 + n_], ob, r=[("OB", sl)])
    P.barrier()

    wbi = [0]

    def nextWB():
        i = wbi[0]
        wbi[0] = 1 - i
        return WB[i], ("WB", i)

    def ln_stats(Tt, n, tkey, eps=EPS):
        P.op("dve", lambda h: h.bn_stats(out=ST6[0:n, 0:6], in_=Tt[0:n, 0:512]), [tkey], ["ST6"])
        P.op("dve", lambda h: h.bn_stats(out=ST6[0:n, 6:12], in_=Tt[0:n, 512:1024]), [tkey], ["ST6"])
        P.op("dve", lambda h: h.bn_aggr(out=MV[0:n, :], in_=ST6[0:n, :]), ["ST6"], ["MV"])
        ACT(SD[0:n, :], MV[0:n, 1:2], AF.Ln, r=["MV"], w=["SD"], bias=eps)
        ACT(RS[0:n, :], SD[0:n, :], AF.Exp, r=["SD"], w=["RS"], scale=-0.5)
        TS("dve", Tt[0:n, :], Tt[0:n, :], MV[0:n, 0:1], ALU.subtract, r=[tkey, "MV", "RS"], w=[tkey], s2=RS[0:n, 0:1], op1=ALU.mult)

    def to_feature_major(Tt, n, tkey, col0, gcol, bcol, banks=None):
        for half in range(2):
            if banks is None:
                pt_, pk_ = PS()
            else:
                pt_, pk_ = ps[banks[half]], ("ps", banks[half])
            for i in range(4):
                kc = half * 4 + i
                TR(pt_[:, i * 128:i * 128 + n], Tt[0:n, kc * 128:(kc + 1) * 128], n, r=[tkey], w=[pk_])
            for i in range(4):
                kc = half * 4 + i
                if i % 2 == 0:
                    ACT(XT[:, kc, col0:col0 + n], pt_[:, i * 128:i * 128 + n], AF.Identity, r=[pk_, "PC"], w=["XT"],
                        bias=PCc(bcol + kc), scale=PCc(gcol + kc))
                else:
                    TS("dve", XT[:, kc, col0:col0 + n], pt_[:, i * 128:i * 128 + n], PCc(gcol + kc), ALU.mult,
                       r=[pk_, "PC"], w=["XT"], s2=PCc(bcol + kc), op1=ALU.add)

    def proj_chunks(wv, col0, nch, ncols, evac):
        i = 0
        while i < nch:
            nb = min(4, nch - i)
            wb, wk = nextWB()
            wbv = wb[:, :].rearrange("p (k c) -> p k c", k=8)
            DMA("sp", wbv[:, :, 0:nb * 128], wv[:, :, col0 + i * 128:col0 + (i + nb) * 128], w=[wk])
            for b in range(nb):
                pt_, pk_ = PS()
                for kc in range(8):
                    MM(pt_[:, 0:ncols], wbv[:, kc, b * 128:(b + 1) * 128], XTb[:, kc, 0:ncols], start=(kc == 0), stop=(kc == 7),
                       r=[wk, "XTb"], w=[pk_])
                evac(i + b, pt_, pk_)
            i += nb

    cur_ty = [-1]
    gchunk = [0]

    for (kind, tok0, NT, chunks, ty) in blocks:
        smp = (kind == "s")
        NTB = NT + (16 if smp else 0)
        if ty != cur_ty[0]:
            cur_ty[0] = ty
            MSET("pool", RST01_[:], 1.0, w=["RST"])
            if ty == 0:
                MSET("pool", RST01_[:, 0:1], 0.0, w=["RST"])
                MSET("pool", RST01_[:, 16:272:64], 0.0, w=["RST"])
            elif ty == 1:
                MSET("pool", RST01_[:, 0:256:64], 0.0, w=["RST"])
            else:
                MSET("pool", RST01_[:, 0:128:8], 0.0, w=["RST"])
        tiles = []
        if smp:
            tiles.append((0, 128, [(0, 128, xs)]))
        else:
            c = 0
            while c < NT:
                n = min(128, NT - c)
                srcs = []
                t_lo, t_hi = tok0 + c, tok0 + c + n
                if t_lo < 16:
                    srcs.append((0, 16 - t_lo, meta[t_lo:16, :]))
                    srcs.append((16 - t_lo, n - (16 - t_lo), xp[0:t_hi - 16, :]))
                else:
                    srcs.append((0, n, xp[t_lo - 16:t_hi - 16, :]))
                tiles.append((c, n, srcs))
                c += n

        for ti, (col0, n, srcs) in enumerate(tiles):
            Tt, tkey = TOK[ti % 2], ("TOK", ti % 2)
            for (r0, nr, ap) in srcs:
                DMA("sp", Tt[r0:r0 + nr, :], ap, w=[tkey])
            ln_stats(Tt, n, tkey)
            to_feature_major(Tt, n, tkey, col0, PC_LIG, PC_LIB)
        if smp:
            DMA("sp", XT[:, :, 128:144], ssh, w=["XT"])
        CP("pool", XTb[:, :, 0:NTB], XT[:, :, 0:NTB], r=["XT"], w=["XTb"])
        TT("dve", XTlo[:, :, 0:NT], XT[:, :, 0:NT], XTb[:, :, 0:NT], ALU.subtract, r=["XT", "XTb"], w=["XTlo"])
        if smp:
            pass
            CP("pool", SHS[:], XT[:, :, 7:128:8], r=["XT"], w=["EGX"])
            DMA("pool", osh_s, SHS[:], r=["EGX"])
        elif tok0 + NT == 2064:
            CP("pool", SHS[:, :, 0:1], XT[:, :, NT - 1:NT], r=["XT"], w=["EGX"])
            DMA("pool", osh_p, SHS[:, :, 0:1], r=["EGX"], slow=True)

        if stop == "0":
            P.barrier()
            continue
        def evA(dst0, kindA):
            def f(i, pt_, pk_):
                d = slot(dst0 + i)[:, 0:NT]
                if dst0 in (QA, KA):
                    d = slot(dst0 + i).bitcast(BF16)[:, 0:NT]
                if kindA == "copy":
                    if i % 2 == 0:
                        CP("act", d, pt_[:, 0:NT], r=[pk_], w=[AK(dst0 + i)])
                    else:
                        CP("dve", d, pt_[:, 0:NT], r=[pk_], w=[AK(dst0 + i)])
                elif kindA == "kscale":
                    ACT(d, pt_[:, 0:NT], AF.Identity, r=[pk_], w=[AK(dst0 + i)], scale=float(128 ** -0.5))
                else:
                    ACT(d, pt_[:, 0:NT], AF.Sigmoid, r=[pk_], w=[AK(dst0 + i)])
            return f

        proj_chunks(win_v, 2048, 4, NT, evA(QA, "copy"))
        proj_chunks(win_v, 2560, 4, NT, evA(KA, "kscale"))
        proj_chunks(win_v, 3072, 8, NT, evA(VA, "copy"))
        proj_chunks(win_v, 4096, 8, NT, evA(OA, "sig"))
        wb, wk = nextWB()
        wbv = wb[:, :].rearrange("p (k c) -> p k c", k=8)
        DMA("sp", wbv[:, :, 0:8], win_v[:, :, 5120:5128], w=[wk])
        pi_, ki_ = PS()
        pf_, kf_ = PS()
        for kc in range(8):
            MM(pi_[0:4, 0:NT], wbv[:, kc, 0:4], XTb[:, kc, 0:NT], start=(kc == 0), stop=(kc == 7), r=[wk, "XTb"], w=[ki_])
        for kc in range(8):
            MM(pf_[0:4, 0:NT], wbv[:, kc, 4:8], XTb[:, kc, 0:NT], start=(kc == 0), stop=(kc == 7), r=[wk, "XTb"], w=[kf_])
        ACT(LI[0:4, 0:NT], pi_[0:4, 0:NT], AF.Identity, r=[ki_, "BI"], w=["LI"], bias=BI[0:4, 0:1])
        ACT(LP[0:4, 0:NT], pf_[0:4, 0:NT], AF.Exp, r=[kf_, "NBF"], w=["LP"], bias=NBF[0:4, 0:1], scale=-1.0)
        ACT(LP[0:4, 0:NT], LP[0:4, 0:NT], AF.Ln, r=["LP"], w=["LP"], bias=1.0)
        P.op("dve", lambda h, NT=NT, ty=ty: h.tensor_tensor_scan(out=CUMP[0:4, 0:NT], data0=RST01[ty][0:4, 0:NT], data1=LP[0:4, 0:NT],
                                                  initial=0.0, op0=ALU.mult, op1=ALU.add), ["LP", "RST"], ["CUMP"])
        TT("dve", DD[0:4, 0:NT], LI[0:4, 0:NT], CUMP[0:4, 0:NT], ALU.add, r=["LI", "CUMP"], w=["DD"])
        TS("pool", RSTN[0:4, 0:NT], RST01[ty][0:4, 0:NT], 1.0, ALU.subtract, r=["RST"], w=["RSN"], s2=1.0e30, op1=ALU.mult)
        P.op("dve", lambda h, NT=NT, ty=ty: h.tensor_tensor_scan(out=CM[0:4, 0:NT], data0=RSTN[0:4, 0:NT], data1=DD[0:4, 0:NT],
                                                  initial=0.0, op0=ALU.add, op1=ALU.max), ["DD", "RSN"], ["CM"])
        proj_chunks(win_v, 0, 8, NT, evA(GA, "sig"))
        for c in range(8):
            TT("pool", slot(OA + c)[:, 0:NT], slot(OA + c)[:, 0:NT], slot(GA + c)[:, 0:NT], ALU.mult, r=[AK(OA + c), AK(GA + c)], w=[AK(OA + c)])

        MSET("pool", VTK1[:, :, 256:257], 1.0, w=["S1"])
        for ci, (c0, L) in enumerate(chunks):
            cs = slice(c0, c0 + L)
            if smp:
                cnb = ci % 2
                m_in, m_in_k = M0ROW[0:4, ci:ci + 1], "M0ROW"
                m_out, m_out_k = MNEW[0:4, ci:ci + 1], "MNEW"
                cnk = [("CN", cnb, h) for h in range(4)]
                DMA("sp", CN[cnb][:, :, 0:256], sC[ci].rearrange("h k v -> k h v"), w=cnk)
                DMA("sp", CN[cnb][:, :, 256:257], sn[ci].unsqueeze(2), w=cnk, slow=True)
                CP("pool", NBC[:], CN[cnb][:, :, 256:257].broadcast_to([128, 4, 128]), r=cnk, w=[("NBC", h) for h in range(4)])
                CP("act", CNb[:], CN[cnb][:], r=cnk, w=[("CNb", h) for h in range(4)])
            else:
                cnb = 0
                g = gchunk[0]
                gchunk[0] += 1
                m_in, m_in_k = MCOL[0:4, g % 2:g % 2 + 1], "MC%d" % (g % 2)
                m_out, m_out_k = MCOL[0:4, (g + 1) % 2:(g + 1) % 2 + 1], "MC%d" % ((g + 1) % 2)
            CNt = CN[cnb]
            TS("dve", MX[0:4, 0:L], CM[0:4, cs], m_in, ALU.max, r=["CM", m_in_k], w=["MX"])
            TS("dve", ROWS[0:4, 0, 0:L], MX[0:4, 0:L], -1.0, ALU.mult, r=["MX"], w=["ROWS"])
            ACT(ROWS[0:4, 1, 0:L], MX[0:4, 0:L], AF.Exp, r=["MX", m_in_k], w=["ROWS"], bias=m_in, scale=-1.0)
            TT("dve", TR4[0:4, 0:L], CUMP[0:4, cs], MX[0:4, 0:L], ALU.subtract, r=["CUMP", "MX"], w=["TR4"])
            ACT(ROWS[0:4, 2, 0:L], TR4[0:4, 0:L], AF.Exp, r=["TR4"], w=["ROWS"])
            ACT(WKR[0:4, 0:L], DD[0:4, cs], AF.Exp, r=["DD", "ROWS"], w=["WKR"], bias=ROWS[0:4, 0, L - 1:L])
            TT("dve", m_out, MX[0:4, L - 1:L], CUMP[0:4, c0 + L - 1:c0 + L], ALU.subtract, r=["MX", "CUMP"], w=[m_out_k])
            pt_, pk_ = PS()
            TR(pt_[0:L, 0:4], DD[0:4, cs], 4, r=["DD"], w=[pk_])
            TR(pt_[0:L, 4:8], WKR[0:4, 0:L], 4, r=["WKR"], w=[pk_])
            CP("act", COLS[0:L, 0:8], pt_[0:L, 0:8], r=[pk_], w=["COLS"])
            pk1, kk1 = PS()
            pk1b = pk1[:, :].bitcast(BF16)
            for h in range(4):
                TR(pk1b[0:L, h * 128:(h + 1) * 128], Kb(h)[:, cs], 128, r=[AK(KA + h)], w=[kk1], bf=True)
            CP("dve", KTK[0:L, :], pk1b[0:L, 0:512], r=[kk1], w=["KHK"])
            for half in range(2):
                pv, kv = PS()
                for i in range(4):
                    c = half * 4 + i
                    TR(pv[0:L, i * 128:(i + 1) * 128], slot(VA + c)[:, cs], 128, r=[AK(VA + c)], w=[kv])
                CP("act", VTK1[0:L, half * 2:half * 2 + 2, 0:256], pv[0:L, :].rearrange("p (a b) -> p a b", a=2), r=[kv], w=["S1"])
            for h in range(4):
                TS("pool", KW[h][0:L, :], KTK[0:L, h * 128:(h + 1) * 128], COLS[0:L, 4 + h:5 + h], ALU.mult, r=["KHK", "COLS"], w=[("KW", h)])
            PB = [(ps[2 * h], ("ps", 2 * h)) for h in range(4)]
            PN_ = [(ps[2 * h + 1], ("ps", 2 * h + 1)) for h in range(4)]
            for h in range(4):
                pb, kb = PB[h]
                MM(pb[0:L, 0:L], SEL[0:4, h, 0:L], ROWS[0:4, 0, 0:L], start=True, stop=False, r=["SEL", "ROWS"], w=[kb])
                MM(pb[0:L, 0:L], ID[0:L, 0:L], MNEG[0:L, 0:L], start=False, stop=True, r=["ID", "MNEG"], w=[kb])
                MM(pb[:, L:3 * L], SEL[0:4, h, :], ROWS[0:4, 1:3, 0:L], r=["SEL", "ROWS"], w=[kb])
                MM(pb[0:L, 256:256 + L], Kb(h)[:, cs], Qb(h)[:, cs], r=[AK(KA + h), AK(QA + h)], w=[kb])
            for h in range(4):
                pb, kb = PB[h]
                ACT(WST[h][0:L, 0:L], pb[0:L, 0:L], AF.Exp, r=[kb, "COLS"], w=[("WST", h)], bias=COLS[0:L, h:h + 1])
            for h in range(4):
                pb, kb = PB[h]
                TT("dve", SST[h][0:L, 0:L], WST[h][0:L, 0:L], pb[0:L, 256:256 + L], ALU.mult, r=[("WST", h), kb], w=[("SST", h)])
                TT("dve", QW[h][:, 0:L], Qb(h)[:, cs], pb[:, L:2 * L], ALU.mult, r=[AK(QA + h), kb], w=[("QW", h)])
            for h in range(4):
                pn_, kn = PN_[h]
                for vc in range(2):
                    MM(pn_[:, vc * L:(vc + 1) * L], VTK1[0:L, h, vc * 128:(vc + 1) * 128], SST[h][0:L, 0:L], start=True, stop=False,
                       r=["S1", ("SST", h)], w=[kn])
                    MM(pn_[:, vc * L:(vc + 1) * L], CNb[:, h, vc * 128:(vc + 1) * 128], QW[h][:, 0:L], start=False, stop=True,
                       r=[("CNb", h), ("QW", h)], w=[kn])
                MM(pn_[:, 2 * L:3 * L], ONESb[0:L, :], SST[h][0:L, 0:L], start=True, stop=False, r=["ONES", ("SST", h)], w=[kn])
                MM(pn_[:, 2 * L:3 * L], NBC[:, h, :], QW[h][:, 0:L], start=False, stop=True, r=[("NBC", h), ("QW", h)], w=[kn])
                MM(pn_[:, 192:449], KW[h][0:L, :], VTK1[0:L, h, :], r=[("KW", h), "S1"], w=[kn])
            for h in range(4):
                pb, kb = PB[h]
                pn_, kn = PN_[h]
                ACT(ADEN[h][:, 0:L], pn_[:, 2 * L:3 * L], AF.Abs, r=[kn], w=[("ADEN", h)])
                CP("act", DCOL[:, h:h + 1], pb[:, 2 * L - 1:2 * L], r=[kb], w=[("DCOL", h)])
            for h in range(4):
                pb, kb = PB[h]
                pn_, kn = PN_[h]
                TT("dve", DDT[h][:, 0:L], ADEN[h][:, 0:L], pb[:, 2 * L:3 * L], ALU.max, r=[("ADEN", h), kb], w=[("DDT", h)])
                RECIP(DDT[h][:, 0:L], DDT[h][:, 0:L], r=[("DDT", h)], w=[("DDT", h)])
                TT("dve", slots(VA + 2 * h, 2)[:, :, cs], pn_[:, 0:2 * L].rearrange("p (a b) -> p a b", a=2),
                   DDT[h][:, 0:L].unsqueeze(1).broadcast_to([128, 2, L]), ALU.mult,
                   r=[kn, ("DDT", h)], w=[AK(VA + 2 * h), AK(VA + 2 * h + 1)])
                STT(CNt[:, h, :], CNt[:, h, :], DCOL[:, h:h + 1], pn_[:, 192:449], ALU.mult, ALU.add,
                    r=[("CN", cnb, h), kn, ("DCOL", h)], w=[("CN", cnb, h)])
            for h in range(4):
                CP("pool", NBC[:, h, :], CNt[:, h, 256:257].broadcast_to([128, 128]), r=[("CN", cnb, h)], w=[("NBC", h)])
                CP("act", CNb[:, h, :], CNt[:, h, :], r=[("CN", cnb, h)], w=[("CNb", h)])
            if smp:
                DMA("pool", oCN_s[ci], CNt[:], r=[("CN", cnb, h) for h in range(4)])
        if smp:
            DMA("pool", om_s, MNEW[:], r=["MNEW"])
        elif tok0 + NT == 2064:
            DMA("pool", oCN_p, CN[0][:], r=[("CN", 0, h) for h in range(4)])
            gl = gchunk[0] % 2
            DMA("pool", om_p, MCOL[0:4, gl:gl + 1], r=["MC%d" % gl])

        TN3 = ["LW", "AA", "KKS", "SQ", "NRM", "KKN", "T1", "BB", "CUM", "EGP", "EGN", "CX"]
        hs = list(range(4))
        tq = {h: (TN3[3 * h], TN3[3 * h + 1], TN3[3 * h + 2]) for h in hs}
        pmk = {}
        for h in hs:
            c0_, c1_ = VA + 2 * h, VA + 2 * h + 1
            pm, km = PS()
            pmk[h] = (pm, km)
            MM(pm[:, 0:NT], ONES[:, :], slot(c0_)[:, 0:NT], start=True, stop=False, r=["ONES", AK(c0_)], w=[km])
            MM(pm[:, 0:NT], ONES[:, :], slot(c1_)[:, 0:NT], start=False, stop=True, r=["ONES", AK(c1_)], w=[km])
        for h in hs:
            pm, km = pmk[h]
            for c_ in (VA + 2 * h, VA + 2 * h + 1):
                STT(slot(c_)[:, 0:NT], pm[:, 0:NT], -1.0 / 256, slot(c_)[:, 0:NT], ALU.mult, ALU.add, r=[km, AK(c_)], w=[AK(c_)])
        for h in hs:
            c0_, c1_ = VA + 2 * h, VA + 2 * h + 1
            TT("pool", T[tq[h][0]][:, 0:NT], slot(c0_)[:, 0:NT], slot(c0_)[:, 0:NT], ALU.mult, r=[AK(c0_)], w=[tq[h][0]])
            TT("pool", T[tq[h][1]][:, 0:NT], slot(c1_)[:, 0:NT], slot(c1_)[:, 0:NT], ALU.mult, r=[AK(c1_)], w=[tq[h][1]])
        for h in hs:
            pv2, kv2 = PS()
            pmk[h] = (pv2, kv2)
            MM(pv2[:, 0:NT], ONES[:, :], T[tq[h][0]][:, 0:NT], start=True, stop=False, r=["ONES", tq[h][0]], w=[kv2])
            MM(pv2[:, 0:NT], ONES[:, :], T[tq[h][1]][:, 0:NT], start=False, stop=True, r=["ONES", tq[h][1]], w=[kv2])
        for h in hs:
            pv2, kv2 = pmk[h]
            ACT(T[tq[h][2]][:, 0:NT], pv2[:, 0:NT], AF.Ln, r=[kv2], w=[tq[h][2]], bias=EPS, scale=1.0 / 256)
        for h in hs:
            ACT(T[tq[h][2]][:, 0:NT], T[tq[h][2]][:, 0:NT], AF.Exp, r=[tq[h][2]], w=[tq[h][2]], scale=-0.5)
        for h in hs:
            for vc, c_ in enumerate((VA + 2 * h, VA + 2 * h + 1)):
                cc = 2 * h + vc
                STT(slot(c_)[:, 0:NT], slot(c_)[:, 0:NT], PCc(PC_MNG + cc), T[tq[h][2]][:, 0:NT], ALU.mult, ALU.mult, r=[AK(c_), tq[h][2], "PC"], w=[AK(c_)])
        for h in hs:
            for vc, c_ in enumerate((VA + 2 * h, VA + 2 * h + 1)):
                cc = 2 * h + vc
                TT("pool" if vc else "dve", slot(MERGED + cc)[:, 0:NT], slot(c_)[:, 0:NT], slot(OA + cc)[:, 0:NT], ALU.mult, r=[AK(c_), AK(OA + cc)], w=[AK(MERGED + cc)])
        P.barrier()
        if stop == "A":
            continue

        def evGB(i, pt_, pk_):
            ACT(slot(GB + i)[:, 0:NT], pt_[:, 0:NT], AF.Sigmoid, r=[pk_], w=[AK(GB + i)])

        proj_chunks(win_v, 1024, 8, NT, evGB)

        def evPB(cc, pt_, pk_):
            if cc < 8:
                dst, dk = slot(RB + cc), AK(RB + cc)
            elif cc < 16:
                dst, dk = slot(KB + cc - 8), AK(KB + cc - 8)
            elif cc < 24:
                dst, dk = slot(VB + cc - 16), AK(VB + cc - 16)
            elif cc == 24:
                dst, dk = XWA, "XWA"
            else:
                dst, dk = XG, "XG"
            if cc % 2 == 0:
                CP("act", dst[:, 0:NTB], pt_[:, 0:NTB], r=[pk_], w=[dk])
            else:
                CP("dve", dst[:, 0:NTB], pt_[:, 0:NTB], r=[pk_], w=[dk])
            ds, dsk = (T["CX"], "CX") if cc % 2 == 0 else (T["EGX"], "EGX")
            if smp:
                d3 = dst[:, 0:128].rearrange("p (s t) -> p s t", t=8)
                s3 = ds[:, 0:128].rearrange("p (s t) -> p s t", t=8)
                TT("pool", s3[:, :, 1:8], d3[:, :, 0:7], d3[:, :, 1:8], ALU.subtract, r=[dk], w=[dsk])
                TT("pool", s3[:, :, 0:1], dst[:, 128:144].unsqueeze(2), d3[:, :, 0:1], ALU.subtract, r=[dk], w=[dsk])
            else:
                TT("pool", ds[:, 1:NT], dst[:, 0:NT - 1], dst[:, 1:NT], ALU.subtract, r=[dk], w=[dsk])
                TT("pool", ds[:, 0:1], CARRY[:, cc:cc + 1], dst[:, 0:1], ALU.subtract, r=[dk, ("CARRY", cc)], w=[dsk])
                CP("pool", CARRY[:, cc:cc + 1], dst[:, NT - 1:NT], r=[dk], w=[("CARRY", cc)])
            STT(dst[:, 0:NT], ds[:, 0:NT], PCc(PC_MU + cc), dst[:, 0:NT], ALU.mult, ALU.add, r=[dsk, dk, "PC"], w=[dk])

        proj_chunks(win_v, 5128, 26, NTB, evPB)
        if stop == "B1":
            P.barrier()
            continue
        ACT(XWA[0:64, 0:NT], XWA[0:64, 0:NT], AF.Tanh, r=["XWA"], w=["XWA"])
        ACT(XG[:, 0:NT], XG[:, 0:NT], AF.Sigmoid, r=["XG"], w=["XG"])

        nchunks = len(chunks)
        last0, Lc = chunks[-1][0] + chunks[-1][1], chunks[-1][1]
        first_last = chunks[0][0] + chunks[0][1] - 1
        for j in range(8):
            Rj, Kj, Vj = slot(RB + j)[:, 0:NT], slot(KB + j)[:, 0:NT], slot(VB + j)[:, 0:NT]
            rk_, kk_, vk_ = AK(RB + j), AK(KB + j), AK(VB + j)
            cols = slice(j * 128, (j + 1) * 128)

            def t(n):
                return T[n][:, 0:NT]
            pw, kw = PS()
            MM(pw[:, 0:NT], W2A2[0:64, cols], XWA[0:64, 0:NT], r=["W2A2", "XWA"], w=[kw])
            pa, ka = PS()
            MM(pa[:, 0:NT], W2A2[64:128, cols], XWA[64:128, 0:NT], r=["W2A2", "XWA"], w=[ka])
            ACT(t("LW"), pw[:, 0:NT], AF.Exp, r=[kw, "PC"], w=["LW"], bias=PCN[:, j:j + 1], scale=-1.0)
            ACT(t("AA"), pa[:, 0:NT], AF.Exp, r=[ka, "PC"], w=["AA"], bias=PCN[:, 8 + j:9 + j], scale=-1.0)
            ACT(t("LW"), t("LW"), AF.Ln, r=["LW"], w=["LW"], bias=1.0)
            ACT(t("AA"), t("AA"), AF.Ln, r=["AA"], w=["AA"], bias=1.0)
            ACT(t("LW"), t("LW"), AF.Exp, r=["LW"], w=["LW"], scale=-1.0)
            ACT(t("AA"), t("AA"), AF.Exp, r=["AA"], w=["AA"], scale=-1.0)
            TS("pool", t("KKS"), Kj, PCc(PC_KKS + j), ALU.mult, r=[kk_, "PC"], w=["KKS"])
            TT("pool", t("SQ"), t("KKS"), t("KKS"), ALU.mult, r=["KKS"], w=["SQ"])
            pn2, kn2 = PS()
            MM(pn2[:, 0:NT], BLK[:, :], t("SQ"), r=["BLK", "SQ"], w=[kn2])
            P.op("dve", lambda h, NT=NT, ty=ty: h.tensor_tensor_scan(out=T["CUM"][:, 0:NT], data0=RST01[ty][:, 0:NT], data1=T["LW"][:, 0:NT],
                                                         initial=0.0, op0=ALU.mult, op1=ALU.add), ["LW", "RST"], ["CUM"])
            TS("dve", t("NRM"), pn2[:, 0:NT], 1e-24, ALU.max, r=[kn2], w=["NRM"])
            ACT(t("NRM"), t("NRM"), AF.Ln, r=["NRM"], w=["NRM"])
            ACT(t("NRM"), t("NRM"), AF.Exp, r=["NRM"], w=["NRM"], scale=-0.5)
            ACT(t("EGP"), t("CUM"), AF.Exp, r=["CUM"], w=["EGP"], scale=-C0)
            ACT(t("EGN"), t("CUM"), AF.Exp, r=["CUM"], w=["EGN"], scale=C0)
            TT("pool", t("CX"), t("CUM"), t("LW"), ALU.subtract, r=["CUM", "LW"], w=["CX"])
            ACT(t("EGX"), t("CX"), AF.Exp, r=["CX"], w=["EGX"], scale=-C0)
            TT("dve", t("KKN"), t("KKS"), t("NRM"), ALU.mult, r=["KKS", "NRM"], w=["KKN"])
            TS("dve", t("T1"), t("AA"), 1.0, ALU.subtract, r=["AA", "PC"], w=["T1"], s2=PCc(PC_KAS + j), op1=ALU.mult)
            STT(Kj, t("T1"), 1.0, Kj, ALU.add, ALU.mult, r=["T1", kk_], w=[kk_])
            TT("pool", t("BB"), t("KKN"), t("AA"), ALU.mult, r=["KKN", "AA"], w=["BB"])
            CP("pool", GL[:, j, 0:nchunks], T["EGP"][:, first_last:last0:Lc] if nchunks > 1 else T["EGP"][:, first_last:first_last + 1],
               r=["EGP"], w=[("GL", j)])
            bon = slot(BON + j)[:, 0:NT]
            STT(bon, Rj, PCc(PC_RK + j), Kj, ALU.mult, ALU.mult, r=[rk_, kk_, "PC"], w=[AK(BON + j)])
            pb2, kb2 = PS()
            MM(pb2[:, 0:NT], BLK[:, :], bon, r=["BLK", AK(BON + j)], w=[kb2])
            TT("dve", RTb(j)[:, 0:NT], Rj, t("EGP"), ALU.mult, r=[rk_, "EGP"], w=[("RTb", j)])
            TT("dve", KHb(j)[:, 0:NT], Kj, t("EGN"), ALU.mult, r=[kk_, "EGN"], w=[("KHb", j)])
            TT("pool", KKTb(j)[:, 0:NT], t("KKN"), t("EGX"), ALU.mult, r=["KKN", "EGX"], w=[AK(KKT + j)])
            TT("pool", BHb(j)[:, 0:NT], t("BB"), t("EGN"), ALU.mult, r=["BB", "EGN"], w=[AK(BHT + j)])
            TT("dve", bon, pb2[:, 0:NT], Vj, ALU.mult, r=[kb2, vk_], w=[AK(BON + j)])
        if stop == "B2":
            P.barrier()
            continue

        def QQ(j, rows, cs):
            return ARENA[:, KKT * W:(KKT + 16) * W].bitcast(BF16)[:, j * W:j * W + 16 * W].rearrange("p (two d) -> p two d", two=2)[rows, :, cs]

        for ci, (c0, L) in enumerate(chunks):
            cs = slice(c0, c0 + L)
            nl = {8: 3, 16: 4, 64: 6}[L]
            if DBGSTEP and ci < DBGCHUNK:
                continue
            hb = ci % 2 if smp else 0
            Ht = H[hb]
            if smp:
                DMA("sp", Ht[:], sH[ci], w=[("H", hb, 0), ("H", hb, 1)])
                CP("act", Hb[hb][:], Ht[:], r=[("H", hb, 0), ("H", hb, 1)], w=[("Hb", hb, 0), ("Hb", hb, 1)])
            for jh in range(2):
                hk = ("H", hb, jh)
                hbk = ("Hb", hb, jh)
                Hbt = Hb[hb]
                pA, kA = PS(); pB, kB = PS(); pC, kC = PS()
                pBb = pB[:, :].bitcast(BF16); pCb = pC[:, :].bitcast(BF16)
                for jj in range(4):
                    j = 4 * jh + jj
                    TR(pA[0:L, jj * 128:(jj + 1) * 128], slot(VB + j)[:, cs], 128, r=[AK(VB + j)], w=[kA])
                    TR(pBb[0:L, jj * 128:(jj + 1) * 128], KHb(j)[:, cs], 128, r=[("KHb", j)], w=[kB], bf=True)
                    TR(pCb[0:L, jj * 128:(jj + 1) * 128], BHb(j)[:, cs], 128, r=[AK(BHT + j)], w=[kC], bf=True)
                CP("act", VTK[0:L, :], pA[0:L, :], r=[kA], w=["VTK"])
                CP("dve", KHK[0:L, :], pBb[0:L, 0:512], r=[kB], w=["KHK"])
                ACT(BHK[0:L, :], pCb[0:L, 0:512], AF.Identity, r=[kC], w=["BHK"], scale=-1.0)

                if DBGSTEP == 1:
                    continue
                def hd(hq):
                    hp, jj = divmod(hq, 4)
                    return 4 * jh + jj, jj, hp, slice(64 * hp, 64 * hp + 64), slice(jj * 128 + hp * 64, jj * 128 + hp * 64 + 64)
                b1 = [PS(), PS()]
                for hq in range(8):
                    j, jj, hp, rows, tc = hd(hq)
                    MM(b1[hp][0][0:L, jj * 2 * L:(jj + 1) * 2 * L], KHb(j)[rows, cs], QQ(j, rows, cs),
                       r=[("KHb", j), AK(KKT + j), ("RTb", j)], w=[b1[hp][1]])
                if DBGSTEP == 2:
                    continue
                for hp in range(2):
                    mk = MK1[0:L, :, 0:L].unsqueeze(1).broadcast_to([L, 4, 2, L])
                    TT("dve", S1[0:L, 4 * hp:4 * hp + 4, :, 0:L],
                       b1[hp][0][0:L, 0:8 * L].rearrange("p (a b c) -> p a b c", a=4, b=2), mk, ALU.mult,
                       r=[b1[hp][1], "MK1"], w=["S1"])
                if DBGSTEP == 3:
                    continue
                b2 = [PS(), PS()]
                for hq in range(8):
                    j, jj, hp, rows, tc = hd(hq)
                    MM(b2[hp][0][0:L, jj * 2 * L:(jj + 1) * 2 * L], BHb(j)[rows, cs], QQ(j, rows, cs),
                       r=[AK(BHT + j), AK(KKT + j), ("RTb", j)], w=[b2[hp][1]])
                for hp in range(2):
                    mkn = MK1N[0:L, :, 0:L].unsqueeze(1).broadcast_to([L, 4, 2, L])
                    TT("dve", S2[0:L, 4 * hp:4 * hp + 4, :, 0:L],
                       b2[hp][0][0:L, 0:8 * L].rearrange("p (a b c) -> p a b c", a=4, b=2), mkn, ALU.mult,
                       r=[b2[hp][1], "MK1N"], w=["S2"])
                if DBGSTEP == 4:
                    continue
                p3 = [PS(), PS()]
                for hq in range(8):
                    j, jj, hp, rows, tc = hd(hq)
                    MM(p3[hp][0][0:L, jj * L:(jj + 1) * L], KKTb(j)[rows, cs], BHb(j)[rows, cs],
                       r=[AK(KKT + j), AK(BHT + j)], w=[p3[hp][1]])
                for hp in range(2):
                    TT("dve", UP[0][0:L, 4 * hp:4 * hp + 4, 64:64 + L], p3[hp][0][0:L, 0:4 * L].rearrange("p (a b) -> p a b", a=4),
                       MLN[0:L, 0:L].unsqueeze(1).broadcast_to([L, 4, L]), ALU.mult, r=[p3[hp][1], "MLN"], w=[("UP", 0)])
                if DBGSTEP == 5:
                    continue
                pU2 = [PS(), PS()]
                for hq in range(8):
                    j, jj, hp, rows, tc = hd(hq)
                    MM(pU2[hp][0][0:L, jj * 64:(jj + 1) * 64], KKTb(j)[rows, cs], Hbt[rows, j, :], start=True, stop=False,
                       r=[AK(KKT + j), hbk], w=[pU2[hp][1]], strict=(hp == 1))
                    MM(pU2[hp][0][0:L, jj * 64:(jj + 1) * 64], S1[0:L, hq, 0, 0:L], VTK[0:L, tc], start=False, stop=True,
                       r=["S1", "VTK"], w=[pU2[hp][1]], strict=(hp == 1))
                for hp in range(2):
                    CP("act", UP[0][0:L, 4 * hp:4 * hp + 4, 0:64], pU2[hp][0][0:L, 0:256].rearrange("p (a b) -> p a b", a=4),
                       r=[pU2[hp][1]], w=[("UP", 0)])
                if DBGSTEP == 6:
                    continue
                cur = 0
                PTt, PTk = S2, "S2"

                def PTv(hq):
                    return PTt[0:L, hq, 0, 0:L] if PTk == "S2" else PTt[0:L, hq, 0:L]
                for l in range(nl):
                    last = (l == nl - 1)
                    wN = 64 if last else 64 + L
                    pUP = [PS(), PS()]
                    for hq in range(8):
                        b_, o_ = divmod(hq, 4)
                        MM(pUP[b_][0][0:L, o_ * 128:o_ * 128 + wN], PTv(hq), UP[cur][0:L, hq, 0:wN], r=[PTk, ("UP", cur)], w=[pUP[b_][1]])
                    if not last:
                        pT, kT = PS()
                        for hq in range(8):
                            MM(pT[0:L, hq * L:(hq + 1) * L], UP[cur][0:L, hq, 64:64 + L], PTv(hq), r=[PTk, ("UP", cur)], w=[kT])
                    for b_ in range(2):
                        v_ = pUP[b_][0][0:L, 0:512].rearrange("p (a b) -> p a b", a=4)
                        TT("dve", UP[1 - cur][0:L, 4 * b_:4 * b_ + 4, 0:64], UP[cur][0:L, 4 * b_:4 * b_ + 4, 0:64], v_[:, :, 0:64], ALU.add,
                           r=[("UP", cur), pUP[b_][1]], w=[("UP", 1 - cur)])
                        if not last:
                            CP("act", UP[1 - cur][0:L, 4 * b_:4 * b_ + 4, 64:64 + L], v_[:, :, 64:64 + L], r=[pUP[b_][1]], w=[("UP", 1 - cur)])
                    if not last:
                        nPT = PTN[l % 2]
                        CP("dve" if l % 2 else "act", nPT[0:L, :, 0:L], pT[0:L, 0:8 * L].rearrange("p (a b) -> p a b", a=8), r=[kT], w=[("PTN", l % 2)])
                        PTt, PTk = nPT, ("PTN", l % 2)
                    cur = 1 - cur
                if DBGSTEP == 7:
                    continue
                pY2 = [PS(), PS()]
                for hq in range(8):
                    j, jj, hp, rows, tc = hd(hq)
                    o_ = pY2[hp][0][rows, jj * L:(jj + 1) * L]
                    MM(o_, Hbt[rows, j, :], RTb(j)[rows, cs], start=True, stop=False, r=[hbk, ("RTb", j)], w=[pY2[hp][1]], strict=(hp == 1))
                    MM(o_, VTK[0:L, tc], S1[0:L, hq, 1, 0:L], start=False, stop=False, r=["VTK", "S1"], w=[pY2[hp][1]], strict=(hp == 1))
                    MM(o_, UP[cur][0:L, hq, 0:64], S2[0:L, hq, 1, 0:L], start=False, stop=True, r=[("UP", cur), "S2"], w=[pY2[hp][1]], strict=(hp == 1))
                pH, kH = PS()
                for hq in range(8):
                    j, jj, hp, rows, tc = hd(hq)
                    o_ = pH[rows, jj * 64:(jj + 1) * 64]
                    MM(o_, KHK[0:L, tc], VTK[0:L, tc], start=True, stop=False, r=["KHK", "VTK"], w=[kH])
                    MM(o_, BHK[0:L, tc], UP[cur][0:L, hq, 0:64], start=False, stop=True, r=["BHK", ("UP", cur)], w=[kH])
                if DBGSTEP == 8:
                    continue
                for hp in range(2):
                    rows = slice(64 * hp, 64 * hp + 64)
                    CP("act", slots(RB + 4 * jh, 4)[rows, :, cs], pY2[hp][0][rows, 0:4 * L].rearrange("p (a b) -> p a b", a=4), r=[pY2[hp][1]],
                       w=[AK(RB + 4 * jh + q) for q in range(4)])
                TT("dve", TMPH[:], Ht[:, 4 * jh:4 * jh + 4, :], pH[:, 0:256].rearrange("p (a b) -> p a b", a=4), ALU.add, r=[hk, kH], w=["TMPH"])
                TT("pool", Ht[:, 4 * jh:4 * jh + 4, :], TMPH[:], GL[:, 4 * jh:4 * jh + 4, ci:ci + 1].broadcast_to([128, 4, 64]), ALU.mult,
                   r=["TMPH"] + [("GL", 4 * jh + q) for q in range(4)], w=[hk])
                CP("act", Hbt[:, 4 * jh:4 * jh + 4, :], Ht[:, 4 * jh:4 * jh + 4, :], r=[hk], w=[hbk])
            if smp:
                DMA("pool", oH_s[ci], Ht[:], r=[("H", hb, 0), ("H", hb, 1)])
            if DBGSTEP and ci >= DBGCHUNK:
                break
        if (not smp) and tok0 + NT == 2064:
            DMA("pool", oH_p, H[0][:], r=[("H", 0, 0), ("H", 0, 1)])

        if stop == "B3":
            P.barrier()
            continue
        TN3 = ["LW", "AA", "KKS", "SQ", "NRM", "KKN", "T1", "BB", "CUM", "EGP", "EGN", "CX"]
        for g0 in (0, 4):
            js = list(range(g0, g0 + 4))
            tq = {j: (TN3[3 * (j - g0)], TN3[3 * (j - g0) + 1], TN3[3 * (j - g0) + 2]) for j in js}
            pk_ = {}
            for j in js:
                pm, km = PS()
                pk_[j] = (pm, km)
                MM(pm[:, 0:NT], BLK[:, :], slot(RB + j)[:, 0:NT], r=["BLK", AK(RB + j)], w=[km])
            for j in js:
                pm, km = pk_[j]
                Yj, yk = slot(RB + j)[:, 0:NT], AK(RB + j)
                STT(Yj, pm[:, 0:NT], -1.0 / 64, Yj, ALU.mult, ALU.add, r=[km, yk], w=[yk])
            for j in js:
                Yj, yk = slot(RB + j)[:, 0:NT], AK(RB + j)
                TT("pool", T[tq[j][0]][:, 0:NT], Yj, Yj, ALU.mult, r=[yk], w=[tq[j][0]])
            for j in js:
                pv2, kv2 = PS()
                pk_[j] = (pv2, kv2)
                MM(pv2[:, 0:NT], BLK[:, :], T[tq[j][0]][:, 0:NT], r=["BLK", tq[j][0]], w=[kv2])
            for j in js:
                pv2, kv2 = pk_[j]
                ACT(T[tq[j][1]][:, 0:NT], pv2[:, 0:NT], AF.Ln, r=[kv2], w=[tq[j][1]], bias=GN_EPS, scale=1.0 / 64)
            for j in js:
                ACT(T[tq[j][1]][:, 0:NT], T[tq[j][1]][:, 0:NT], AF.Exp, r=[tq[j][1]], w=[tq[j][1]], scale=-0.5)
            for j in js:
                pg, kg = PS()
                pk_[j] = (pg, kg)
                MM(pg[:, 0:NT], G2[:, j * 128:(j + 1) * 128], XG[:, 0:NT], r=["G2", "XG"], w=[kg])
            for j in js:
                Yj, yk = slot(RB + j)[:, 0:NT], AK(RB + j)
                t1 = T[tq[j][2]][:, 0:NT]
                STT(t1, Yj, PCc(PC_LXG + j), T[tq[j][1]][:, 0:NT], ALU.mult, ALU.mult, r=[yk, tq[j][1], "PC"], w=[tq[j][2]])
                STT(t1, t1, PCc(PC_LXB + j), slot(BON + j)[:, 0:NT], ALU.add, ALU.add, r=[tq[j][2], AK(BON + j), "PC"], w=[tq[j][2]])
            for j in js:
                pg, kg = pk_[j]
                t1 = T[tq[j][2]][:, 0:NT]
                TT("dve", t1, t1, pg[:, 0:NT], ALU.mult, r=[tq[j][2], kg], w=[tq[j][2]])
            for j in js:
                t1 = T[tq[j][2]][:, 0:NT]
                TT("pool", t1, t1, slot(GB + j)[:, 0:NT], ALU.mult, r=[tq[j][2], AK(GB + j)], w=[tq[j][2]])
                TT("pool", slot(MERGED + j)[:, 0:NT], slot(MERGED + j)[:, 0:NT], t1, ALU.add, r=[tq[j][2], AK(MERGED + j)], w=[AK(MERGED + j)])
        P.barrier()
        if stop == "B":
            continue

        def big_out(wv, nrowch, lhs_of, resid):
            for cp_ in range((nrowch + 3) // 4):
                wb, wk = nextWB()
                wbv2 = wb[:, :].rearrange("p (c d) -> p c d", c=4)
                ncc = min(4, nrowch - 4 * cp_)
                DMA("sp", wbv2[:, 0:ncc, :], wv[:, 4 * cp_:4 * cp_ + ncc, :], w=[wk])
                for ci_ in range(ncc):
                    c = 4 * cp_ + ci_
                    for ti, (col0, n, _) in enumerate(tiles):
                        for half in range(2):
                            MM(ps[2 * ti + half][0:n, 0:512], lhs_of(c)[:, col0:col0 + n], wbv2[:, ci_, half * 512:(half + 1) * 512],
                               start=(c == 0), stop=False, r=[wk] + resid[1], w=[("ps", 2 * ti + half)])
            for ti, (col0, n, _) in enumerate(tiles):
                for c in range(8):
                    o_ = ps[2 * ti + c // 4][0:n, (c % 4) * 128:(c % 4 + 1) * 128]
                    MM(o_, XTb[:, c, col0:col0 + n], IDb[:, :], start=False, stop=False, r=["XTb", "ID"], w=[("ps", 2 * ti + c // 4)])
                    MM(o_, XTlo[:, c, col0:col0 + n], IDb[:, :], start=False, stop=(c % 4 == 3), r=["XTlo", "ID"], w=[("ps", 2 * ti + c // 4)])

        MRGb = ARENA[:, 52 * W:56 * W].bitcast(BF16).rearrange("p (c w) -> p c w", c=8)
        TS("pool", MRGb[:, :, 0:NT], slots(MERGED, 8)[:, :, 0:NT], 1.0 / ALPHA, ALU.mult, r=[AK(MERGED + c) for c in range(8)], w=["MRGb"])
        big_out(wout_v, 8, lambda c: MRGb[:, c, :], (None, ["MRGb"]))
        for ti, (col0, n, _) in enumerate(tiles):
            Tt, tkey = TOK[ti % 2], ("TOK", ti % 2)
            CP("act", Tt[0:n, 0:512], ps[2 * ti][0:n, :], r=[("ps", 2 * ti)], w=[tkey])
            CP("dve", Tt[0:n, 512:1024], ps[2 * ti + 1][0:n, :], r=[("ps", 2 * ti + 1)], w=[tkey])
            ln_stats(Tt, n, tkey, eps=EPS / (ALPHA * ALPHA))
            to_feature_major(Tt, n, tkey, col0, PC_L1G, PC_L1B, banks=(6, 7))
        CP("pool", XTb[:, :, 0:NT], XT[:, :, 0:NT], r=["XT"], w=["XTb"])
        TT("dve", XTlo[:, :, 0:NT], XT[:, :, 0:NT], XTb[:, :, 0:NT], ALU.subtract, r=["XT", "XTb"], w=["XTlo"])

        def evAG(i, pt_, pk_):
            if smp:
                d = slot(AG + i)[:, 0:160].rearrange("p (s t) -> p s t", t=10)[:, :, 2:10]
                CP("act", d, pt_[:, 0:128].rearrange("p (s t) -> p s t", t=8), r=[pk_], w=[AK(AG + i)])
            else:
                CP("act", slot(AG + i)[:, 2:2 + NT], pt_[:, 0:NT], r=[pk_], w=[AK(AG + i)])

        def evAV(i, pt_, pk_):
            CP("dve", slot(AV + i)[:, 0:NT], pt_[:, 0:NT], r=[pk_], w=[AK(AV + i)])

        proj_chunks(wup_v, 0, 22, NT, evAG)
        proj_chunks(wup_v, DF, 22, NT, evAV)
        _tn = ["LW", "AA", "KKS", "SQ", "NRM", "KKN", "T1", "BB", "CUM", "EGP", "EGN", "CX"]
        for g0 in range(0, 22, 6):
            idx = list(range(g0, min(g0 + 6, 22)))
            tk = {i: (_tn[2 * (i - g0)], _tn[2 * (i - g0) + 1]) for i in idx}
            for i in idx:
                ag, agk = slot(AG + i), AK(AG + i)
                if smp:
                    a3 = ag[:, 0:160].rearrange("p (s t) -> p s t", t=10)
                    CP("pool", a3[:, :, 0:2], CV0[:, i, :, :], r=[("CV0", i)], w=[agk])
                    CP("pool", CV0[:, i, :, :], a3[:, :, 8:10], r=[agk], w=[("CV0", i)])
                else:
                    CP("pool", ag[:, 0:2], AGC[:, i, :], r=[("AGC", i)], w=[agk])
                    CP("pool", AGC[:, i, :], ag[:, NT:NT + 2], r=[agk], w=[("AGC", i)])
            for i in idx:
                ag, agk = slot(AG + i), AK(AG + i)
                cvk, g1k = tk[i]
                cv = T[cvk]
                if smp:
                    a3 = ag[:, 0:160].rearrange("p (s t) -> p s t", t=10)
                    cv3 = cv[:, 0:128].rearrange("p (s t) -> p s t", t=8)
                    TS("dve", cv3, a3[:, :, 0:8], PCc(PC_CW0 + i), ALU.mult, r=[agk, "PC"], w=[cvk], s2=PCc(PC_CB + i), op1=ALU.add)
                    STT(cv3, a3[:, :, 1:9], PCc(PC_CW1 + i), cv3, ALU.mult, ALU.add, r=[agk, cvk, "PC"], w=[cvk])
                    STT(cv3, a3[:, :, 2:10], PCc(PC_CW2 + i), cv3, ALU.mult, ALU.add, r=[agk, cvk, "PC"], w=[cvk])
                else:
                    ACT(cv[:, 0:NT], ag[:, 0:NT], AF.Identity, r=[agk, "PC"], w=[cvk], bias=PCc(PC_CB + i), scale=PCc(PC_CW0 + i))
                    STT(cv[:, 0:NT], ag[:, 1:NT + 1], PCc(PC_CW1 + i), cv[:, 0:NT], ALU.mult, ALU.add, r=[agk, cvk, "PC"], w=[cvk])
                    STT(cv[:, 0:NT], ag[:, 2:NT + 2], PCc(PC_CW2 + i), cv[:, 0:NT], ALU.mult, ALU.add, r=[agk, cvk, "PC"], w=[cvk])
            for i in idx:
                cvk, g1k = tk[i]
                ACT(T[g1k][:, 0:NT], T[cvk][:, 0:NT], AF.Square, r=[cvk], w=[g1k])
            for i in idx:
                cvk, g1k = tk[i]
                TS("dve", T[g1k][:, 0:NT], T[g1k][:, 0:NT], 0.044715, ALU.mult, r=[g1k], w=[g1k], s2=1.0, op1=ALU.add)
            for i in idx:
                cvk, g1k = tk[i]
                TT("pool", T[g1k][:, 0:NT], T[g1k][:, 0:NT], T[cvk][:, 0:NT], ALU.mult, r=[g1k, cvk], w=[g1k])
            for i in idx:
                cvk, g1k = tk[i]
                ACT(T[g1k][:, 0:NT], T[g1k][:, 0:NT], AF.Sigmoid, r=[g1k], w=[g1k], scale=GELU_K)
            for i in idx:
                cvk, g1k = tk[i]
                TT("pool", T[g1k][:, 0:NT], T[g1k][:, 0:NT], T[cvk][:, 0:NT], ALU.mult, r=[g1k, cvk], w=[g1k])
            for i in idx:
                cvk, g1k = tk[i]
                STT(slot(AG + i).bitcast(BF16)[:, 0:NT], T[g1k][:, 0:NT], 1.0 / ALPHA, slot(AV + i)[:, 0:NT], ALU.mult, ALU.mult,
                    r=[g1k, AK(AV + i), cvk], w=[AK(AG + i)])
        if smp:
            DMA("pool", ocv_s, CV0[:], r=[("CV0", i) for i in range(22)])
        elif tok0 + NT == 2064:
            DMA("pool", ocv_p, AGC[:], r=[("AGC", i) for i in range(22)])

        big_out(wdn_v, 22, lambda c: slot(AG + c).bitcast(BF16), (None, [AK(AG + c) for c in range(22)]))
        for ti, (col0, n, _) in enumerate(tiles):
            Tt, tkey = TOK[ti % 2], ("TOK", ti % 2)
            CP("act", Tt[0:n, 0:512], ps[2 * ti][0:n, :], r=[("ps", 2 * ti)], w=[tkey])
            CP("dve", Tt[0:n, 512:1024], ps[2 * ti + 1][0:n, :], r=[("ps", 2 * ti + 1)], w=[tkey])
            ln_stats(Tt, n, tkey, eps=EPS / (ALPHA * ALPHA))
            TT("pool", Tt[0:n, :], Tt[0:n, :], LNG[0:n, :], ALU.mult, r=[tkey, "LNG"], w=[tkey])
            TT("dve", Tt[0:n, :], Tt[0:n, :], LNB[0:n, :], ALU.add, r=[tkey, "LNB"], w=[tkey])
            if smp:
                DMA("pool", oy_s, Tt[0:128, :], r=[tkey])
            else:
                t_lo = tok0 + col0
                if t_lo < 16:
                    DMA("pool", oy_p[0:n - (16 - t_lo), :], Tt[16 - t_lo:n, :], r=[tkey])
                else:
                    DMA("pool", oy_p[t_lo - 16:t_lo - 16 + n, :], Tt[0:n, :], r=[tkey])
        P.barrier()

    P.emit(nc)
    st.close()
    return nc


_NC_CACHE = {}


def _host_inputs(inp, b):
    f = lambda a: np.ascontiguousarray(a, dtype=np.float32)
    s0, s1 = 16 * b, 16 * b + 16
    prm0 = np.concatenate([inp["rwkv_mu"][0], inp["rwkv_w0"][0], inp["rwkv_a0"][0], inp["rwkv_kk_scale"][0], inp["rwkv_ka_scale"][0],
                           inp["rwkv_rk"][0], inp["rwkv_lnx_g"][0], inp["rwkv_lnx_b"][0], inp["mlstm_norm_g"][0],
                           inp["ln_in_g"], inp["ln_in_b"], inp["ln1_g"][0], inp["ln1_b"][0]]).reshape(122, 128)
    cw = inp["ffn_conv_w"][0]
    prm1 = np.concatenate([cw[0], cw[1], cw[2], inp["ffn_conv_b"][0]]).reshape(88, 128)
    sS = inp["state_rwkv_S"][0, s0:s1]
    sH = sS.reshape(16, 8, 2, 64, 64).transpose(0, 2, 4, 1, 3).reshape(16, 128, 8, 64)
    ssh = inp["state_rwkv_shift"][0, s0:s1].reshape(16, 8, 128).transpose(2, 1, 0)
    scv = inp["state_ffn_conv"][0, s0:s1].reshape(16, 2, 22, 128).transpose(3, 2, 0, 1)
    return {
        "xp": f(inp["x_prompt"][b]), "xs": f(inp["x_sample"][s0:s1].reshape(128, D)), "meta": f(inp["meta_tokens"]),
        "sC": f(inp["state_mlstm_C"][0, s0:s1]), "sn": f(inp["state_mlstm_n"][0, s0:s1].transpose(0, 2, 1)),
        "sm": f(inp["state_mlstm_m"][0, s0:s1].T), "sH": f(sH), "ssh": f(ssh), "scv": f(scv),
        "prm0": f(prm0), "prm1": f(prm1), "bif": f(inp["b_if"][0].reshape(8, 1)),
        "ln2g": f(inp["ln2_g"][0]), "ln2b": f(inp["ln2_b"][0]),
        "w_in": f(inp["w_in"][0]), "w2a2": f(np.concatenate([inp["rwkv_w2"][0], inp["rwkv_a2"][0]], 0)), "g2": f(inp["rwkv_g2"][0]),
        "w_out": f(inp["w_out"][0]), "w_up": f(inp["ffn_w_up"][0]), "w_down": f(inp["ffn_w_down"][0]),
    }


def kernel(**inputs):
    inp = {k: np.asarray(v) for k, v in inputs.items()}
    if "nc" not in _NC_CACHE:
        _NC_CACHE["nc"] = build()
    nc = _NC_CACHE["nc"]
    in_maps = [_host_inputs(inp, b) for b in range(8)]
    res = run_bass_kernel_spmd(nc, in_maps, core_ids=list(range(8))).results
    g = lambda k: [np.asarray(r[k], dtype=np.float32) for r in res]
    y_p = np.stack(g("oy_p"), 0)
    y_s = np.concatenate(g("oy_s"), 0).reshape(128, 8, D)
    cn_p = np.stack(g("oCN_p"), 0)
    pC = cn_p[..., 0:256].transpose(0, 2, 1, 3)[None]
    pn = cn_p[..., 256].transpose(0, 2, 1)[None]
    pm = np.stack(g("om_p"), 0)[:, :, 0][None]
    Hp = np.stack(g("oH_p"), 0)
    pS = Hp.reshape(8, 2, 64, 8, 64).transpose(0, 3, 1, 4, 2).reshape(8, 16, 64, 64)[None]
    psh = np.stack(g("osh_p"), 0)[..., 0].transpose(0, 2, 1).reshape(8, D)[None]
    pcv = np.stack(g("ocv_p"), 0).transpose(0, 3, 2, 1).reshape(8, 2, DF)[None]
    cn_s = np.concatenate(g("oCN_s"), 0)
    sC = cn_s[..., 0:256].transpose(0, 2, 1, 3)[None]
    sn = cn_s[..., 256].transpose(0, 2, 1)[None]
    sm = np.concatenate([a.T for a in g("om_s")], 0)[None]
    Hs = np.concatenate(g("oH_s"), 0)
    sS = Hs.reshape(128, 2, 64, 8, 64).transpose(0, 3, 1, 4, 2).reshape(128, 16, 64, 64)[None]
    ssh = np.concatenate([a.transpose(2, 1, 0).reshape(16, D) for a in g("osh_s")], 0)[None]
    scv = np.concatenate([a.transpose(2, 3, 1, 0).reshape(16, 2, DF) for a in g("ocv_s")], 0)[None]
    c = lambda a: np.ascontiguousarray(a, dtype=np.float32)
    return (c(y_p), c(y_s), c(pC), c(pn), c(pm), c(pS), c(psh), c(pcv), c(sC), c(sn), c(sm), c(sS), c(ssh), c(scv))
```

```python
import contextlib
import os
import numpy as np
DBGSTEP = int(os.environ.get("DBGSTEP", "0"))
DBGSUB = int(os.environ.get("DBGSUB", "0"))
DBGCHUNK = int(os.environ.get("DBGCHUNK", "0"))
import concourse.bass as bass
import concourse.mybir as mybir
from concourse.bass_utils import run_bass_kernel_spmd

F32 = mybir.dt.float32
BF16 = mybir.dt.bfloat16
AF = mybir.ActivationFunctionType
ALU = mybir.AluOpType

NS_DMA = 6
ENGS = ("pe", "act", "dve", "pool", "sp")


class Op:
    __slots__ = ("eng", "fn", "deps", "dma", "signal", "val", "slot", "strict")


class Prog:
    def __init__(self):
        self.ops = {e: [] for e in ENGS}
        self.lastw = {}
        self.readers = {}
        self.pend = {e: [] for e in ENGS}
        self.dmaq = {e: [] for e in ENGS}

    def op(self, eng, fn, r=(), w=(), dma=False, strict=False):
        o = Op()
        o.strict = strict
        o.eng, o.fn, o.dma, o.signal, o.val, o.slot = eng, fn, dma, False, 0, 0
        deps = set(self.pend[eng])
        self.pend[eng] = []
        for k in r:
            lw = self.lastw.get(k)
            if lw is not None:
                deps.add(lw)
        for k in w:
            lw = self.lastw.get(k)
            if lw is not None:
                deps.add(lw)
            deps.update(self.readers.get(k, ()))
        if dma:
            q = self.dmaq[eng]
            n = len(q)
            o.slot = n % NS_DMA
            o.val = 16 * (n // NS_DMA + 1)
            if n >= NS_DMA:
                deps.add(q[n - NS_DMA])
            q.append(o)
        o.deps = deps
        for k in r:
            lst = self.readers.setdefault(k, [])
            if not dma:
                lst[:] = [x for x in lst if x.dma or x.eng != eng]
            lst.append(o)
        for k in w:
            self.lastw[k] = o
            self.readers[k] = []
        self.ops[eng].append(o)
        return o

    def barrier(self):
        lasts = []
        for e in ENGS:
            comp = [x for x in self.ops[e] if not x.dma]
            if comp:
                lasts.append(comp[-1])
            lasts.extend(self.dmaq[e][-NS_DMA:])
        for e in ENGS:
            self.pend[e].extend(lasts)
        self.lastw.clear()
        self.readers.clear()

    def emit(self, nc):
        for e in ENGS:
            for o in self.ops[e]:
                for d in o.deps:
                    if not d.dma and not (d.eng == "pe" and e == "pe" and not o.strict):
                        d.signal = True
        for e in ENGS:
            c = 0
            for o in self.ops[e]:
                if not o.dma and o.signal:
                    c += 1
                    o.val = c
        with contextlib.ExitStack() as st:
            csem = {e: st.enter_context(nc.semaphore("c_" + e)) for e in ENGS}
            dsem = {e: [st.enter_context(nc.semaphore("d_%s%d" % (e, i))) for i in range(NS_DMA)]
                    for e in ENGS if self.dmaq[e]}
            block = st.enter_context(nc.Block())

            def semof(o):
                return dsem[o.eng][o.slot] if o.dma else csem[o.eng]

            def run(e, h):
                waited = {}
                for o in self.ops[e]:
                    need = {}
                    for d in o.deps:
                        if d.eng == "pe" and e == "pe" and not d.dma and not o.strict:
                            continue
                        sm = semof(d)
                        if waited.get(sm, 0) < d.val and need.get(sm, 0) < d.val:
                            need[sm] = d.val
                    for sm, v in need.items():
                        h.wait_ge(sm, v)
                        waited[sm] = v
                    ins = o.fn(h)
                    if o.dma:
                        ins.then_inc(semof(o), 16)
                    elif o.signal:
                        ins.then_inc(csem[e], 1)
                for o in self.dmaq[e][-NS_DMA:]:
                    if waited.get(semof(o), 0) < o.val:
                        h.wait_ge(semof(o), o.val)
                        waited[semof(o)] = o.val

            @block.tensor
            def _(h):
                run("pe", h)

            @block.scalar
            def _(h):
                run("act", h)

            @block.vector
            def _(h):
                run("dve", h)

            @block.gpsimd
            def _(h):
                run("pool", h)

            @block.sync
            def _(h):
                run("sp", h)


D = 1024
DF = 2816
W = 274
EPS = 1e-5
GN_EPS = 64e-5
ALPHA = float(2.0 ** 0.25)
C0 = float(np.exp(-0.5))
NEG = -1.0e30
GELU_K = float(2.0 * np.sqrt(2.0 / np.pi))

MERGED = 0
QA, KA, VA, OA, GA = 8, 12, 16, 24, 32
GB, KKT, RB, KB, VB, BHT, BON = 8, 16, 24, 32, 40, 48, 56
AG, AV = 8, 30
NSLOT = 64

PC_MU, PC_W0, PC_A0, PC_KKS, PC_KAS, PC_RK, PC_LXG, PC_LXB, PC_MNG = 0, 26, 34, 42, 50, 58, 66, 74, 82
PC_LIG, PC_LIB, PC_L1G, PC_L1B = 90, 98, 106, 114
PC_CW0, PC_CW1, PC_CW2, PC_CB = 128, 150, 172, 194

BLOCKS = [("p", 0, 272, [(0, 16), (16, 64), (80, 64), (144, 64), (208, 64)], 0)]
for _i in range(1, 8):
    BLOCKS.append(("p", 272 + 256 * (_i - 1), 256, [(64 * c, 64) for c in range(4)], 1))
BLOCKS.append(("s", 0, 128, [(8 * c, 8) for c in range(16)], 2))


def build(blocks=BLOCKS, stop=None):
    nc = bass.Bass("TRN2", target_bir_lowering=False)

    def din(name, shape):
        return nc.dram_tensor(name, list(shape), F32, kind="ExternalInput").ap()

    def dout(name, shape):
        return nc.dram_tensor(name, list(shape), F32, kind="ExternalOutput").ap()

    xp = din("xp", [2048, D]); xs = din("xs", [128, D]); meta = din("meta", [16, D])
    sC = din("sC", [16, 4, 128, 256]); sn = din("sn", [16, 128, 4]); sm = din("sm", [4, 16])
    sH = din("sH", [16, 128, 8, 64]); ssh = din("ssh", [128, 8, 16]); scv = din("scv", [128, 22, 16, 2])
    prm0 = din("prm0", [122, 128]); prm1 = din("prm1", [88, 128])
    bif = din("bif", [8, 1]); ln2g = din("ln2g", [D]); ln2b = din("ln2b", [D])
    w_in = din("w_in", [D, 8456]); w2a2 = din("w2a2", [128, D]); g2 = din("g2", [128, D])
    w_out = din("w_out", [D, D]); w_up = din("w_up", [D, 2 * DF]); w_down = din("w_down", [DF, D])

    oy_p = dout("oy_p", [2048, D]); oy_s = dout("oy_s", [128, D])
    oCN_p = dout("oCN_p", [128, 4, 257]); om_p = dout("om_p", [4, 1]); oH_p = dout("oH_p", [128, 8, 64])
    osh_p = dout("osh_p", [128, 8, 1]); ocv_p = dout("ocv_p", [128, 22, 2])
    oCN_s = dout("oCN_s", [16, 128, 4, 257]); om_s = dout("om_s", [4, 16]); oH_s = dout("oH_s", [16, 128, 8, 64])
    osh_s = dout("osh_s", [128, 8, 16]); ocv_s = dout("ocv_s", [128, 22, 16, 2])

    wb_in = nc.dram_tensor("wb_in", [D, 8456], BF16).ap(); wb_out = nc.dram_tensor("wb_out", [D, D], BF16).ap()
    wb_up = nc.dram_tensor("wb_up", [D, 2 * DF], BF16).ap(); wb_down = nc.dram_tensor("wb_down", [DF, D], BF16).ap()
    win_v = wb_in.rearrange("(kc p) c -> p kc c", p=128)
    wup_v = wb_up.rearrange("(kc p) c -> p kc c", p=128)
    wout_v = wb_out.rearrange("(c p) d -> p c d", p=128)
    wdn_v = wb_down.rearrange("(c p) d -> p c d", p=128)

    P = Prog()
    st = contextlib.ExitStack()

    def sb(name, shape):
        return st.enter_context(nc.sbuf_tensor(name, list(shape), F32))

    ARENA = sb("ARENA", [128, NSLOT * W])
    XT = sb("XT", [128, 8, 272])
    TOK = [sb("TOK0", [128, D]), sb("TOK1", [128, D])]
    LNG = sb("LNG", [128, D]); LNB = sb("LNB", [128, D])
    WB = [st.enter_context(nc.sbuf_tensor("WB%d" % i, [128, 4096], BF16)) for i in range(2)]
    XTb = st.enter_context(nc.sbuf_tensor("XTb", [128, 8, 272], BF16))
    XTlo = st.enter_context(nc.sbuf_tensor("XTlo", [128, 8, 272], BF16))
    PC = sb("PC", [128, 216]); PCN = sb("PCN", [128, 16])
    W2A2 = sb("W2A2", [128, D]); G2 = sb("G2", [128, D])
    ID = sb("ID", [128, 128]); ONES = sb("ONES", [128, 128]); BLK = sb("BLK", [128, 128]); AID = sb("AID", [128, 128])
    MK1 = sb("MK1", [64, 2, 64]); MK1N = sb("MK1N", [64, 2, 64]); MLN = sb("MLN", [64, 64]); MNEG = sb("MNEG", [64, 64])
    SEL = sb("SEL", [4, 4, 128])
    RST01_ = sb("RST01", [128, W])
    RST01 = [RST01_, RST01_, RST01_]
    RSTN = sb("RSTN", [4, W])
    CN = [sb("CN0", [128, 4, 257]), sb("CN1", [128, 4, 257])]
    NBC = st.enter_context(nc.sbuf_tensor("NBC", [128, 4, 128], BF16))
    CNb = st.enter_context(nc.sbuf_tensor("CNb", [128, 4, 257], BF16))
    Hb = [st.enter_context(nc.sbuf_tensor("Hb0", [128, 8, 64], BF16)), st.enter_context(nc.sbuf_tensor("Hb1", [128, 8, 64], BF16))]
    IDb = st.enter_context(nc.sbuf_tensor("IDb", [128, 128], BF16)); ONESb = st.enter_context(nc.sbuf_tensor("ONESb", [128, 128], BF16))
    H = [sb("H0", [128, 8, 64]), sb("H1", [128, 8, 64])]
    CARRY = sb("CARRY", [128, 26]); AGC = sb("AGC", [128, 22, 2])
    CV0 = sb("CV0", [128, 22, 16, 2]);
    XWA = sb("XWA", [128, W]); XG = sb("XG", [128, W])
    GL = sb("GL", [128, 8, 16])
    BI = sb("BI", [4, 1]); NBF = sb("NBF", [4, 1]); MCOL = sb("MCOL", [4, 2]); M0ROW = sb("M0ROW", [4, 16]); MNEW = sb("MNEW", [4, 16])
    ST6 = sb("ST6", [128, 12]); MV = sb("MV", [128, 2]); SD = sb("SD", [128, 1]); RS = sb("RS", [128, 1])
    TN = ["LW", "AA", "KKS", "SQ", "NRM", "KKN", "T1", "BB", "CUM", "EGP", "EGN", "CX", "EGX"]
    T = {n: sb("T_" + n, [128, W]) for n in TN}
    PR0 = T["KKS"][:, 0:128]; PR1 = T["SQ"][:, 0:128]
    SHS = T["EGX"][:, 0:128].rearrange("p (a b) -> p a b", a=8)
    LI = sb("LI", [4, W]); LP = sb("LP", [4, W]); CUMP = sb("CUMP", [4, W]); DD = sb("DD", [4, W]); CM = sb("CM", [4, W])
    MX = sb("MX", [4, 64]); ROWS = sb("ROWS", [4, 3, 64]); TR4 = sb("TR4", [4, 64]); WKR = sb("WKR", [4, 64])
    COLS = sb("COLS", [64, 8])
    WST = [sb("WST%d" % i, [64, 64]) for i in range(4)]; SST = [st.enter_context(nc.sbuf_tensor("SST%d" % i, [64, 64], BF16)) for i in range(4)]
    QW = [st.enter_context(nc.sbuf_tensor("QW%d" % i, [128, 64], BF16)) for i in range(4)]; ADEN = [sb("ADEN%d" % i, [128, 64]) for i in range(4)]
    DDT = [sb("DDT%d" % i, [128, 64]) for i in range(4)]; KW = [st.enter_context(nc.sbuf_tensor("KW%d" % i, [64, 128], BF16)) for i in range(4)]
    DCOL = sb("DCOL", [128, 4])
    VTK = st.enter_context(nc.sbuf_tensor("VTK", [64, 512], BF16)); KHK = st.enter_context(nc.sbuf_tensor("KHK", [64, 512], BF16)); BHK = st.enter_context(nc.sbuf_tensor("BHK", [64, 512], BF16))
    S1raw = st.enter_context(nc.sbuf_tensor("S1raw", [64, 1028], BF16)); S2 = st.enter_context(nc.sbuf_tensor("S2", [64, 8, 2, 64], BF16)); PA = st.enter_context(nc.sbuf_tensor("PA", [64, 8, 64], BF16))
    S1 = S1raw[:, 0:1024].rearrange("p (a b c) -> p a b c", a=8, b=2)
    VTK1 = S1raw[:, 0:1028].rearrange("p (h c) -> p h c", h=4)
    KTK = KHK
    PN = [st.enter_context(nc.sbuf_tensor("PN%d" % i, [64, 8, 64], BF16)) for i in range(2)]; PTN = [st.enter_context(nc.sbuf_tensor("PTN%d" % i, [64, 8, 64], BF16)) for i in range(2)]
    U = [st.enter_context(nc.sbuf_tensor("U0", [64, 512], BF16)), st.enter_context(nc.sbuf_tensor("U1", [64, 512], BF16))]
    TMPH = sb("TMPH", [128, 4, 64])
    ps = [st.enter_context(nc.psum_tensor("ps%d" % i, [128, 512], F32)) for i in range(8)]
    psi = [0]

    def PS():
        i = psi[0]
        psi[0] = (i + 1) % 8
        return ps[i], ("ps", i)

    def slot(i):
        return ARENA[:, i * W:(i + 1) * W]

    def slots(i, n):
        return ARENA[:, i * W:(i + n) * W].rearrange("p (c w) -> p c w", c=n)

    def AK(i):
        return ("A", i)

    BFA = ARENA[:, KKT * W:(KKT + 8) * W].bitcast(BF16)
    BFB = ARENA[:, BHT * W:(BHT + 8) * W].bitcast(BF16)

    def KKTb(j):
        return BFA[:, j * W:(j + 1) * W]

    def RTb(j):
        return BFA[:, (8 + j) * W:(9 + j) * W]

    def KHb(j):
        return BFB[:, j * W:(j + 1) * W]

    def BHb(j):
        return BFB[:, (8 + j) * W:(9 + j) * W]

    def Qb(h):
        return slot(QA + h).bitcast(BF16)

    def Kb(h):
        return slot(KA + h).bitcast(BF16)

    def MM(out, lhsT, rhs, start=True, stop=True, r=(), w=(), strict=False):
        P.op("pe", lambda h: h.matmul(out, lhsT=lhsT, rhs=rhs, start=start, stop=stop), r, w, strict=strict)

    def TR(out, in_, n, r=(), w=(), bf=False):
        idn = IDb[0:n, 0:n] if bf else ID[0:n, 0:n]
        P.op("pe", lambda h: h.transpose(out=out, in_=in_, identity=idn), list(r) + ["ID"], w)

    def ACT(out, in_, func, r=(), w=(), bias=0.0, scale=1.0):
        P.op("act", lambda h: h.activation(out=out, in_=in_, func=func, bias=bias, scale=scale), r, w)

    def TT(eng, out, in0, in1, op, r=(), w=()):
        P.op(eng, lambda h: h.tensor_tensor(out=out, in0=in0, in1=in1, op=op), r, w)

    def TS(eng, out, in0, s1, op0, r=(), w=(), s2=None, op1=None):
        if op1 is None and eng == "pool" and op0 == ALU.mult:
            s2, op1 = 1.0, ALU.mult
        if op1 is None:
            P.op(eng, lambda h: h.tensor_scalar(out=out, in0=in0, scalar1=s1, scalar2=None, op0=op0), r, w)
        else:
            P.op(eng, lambda h: h.tensor_scalar(out=out, in0=in0, scalar1=s1, scalar2=s2, op0=op0, op1=op1), r, w)

    def STT(out, in0, sc, in1, op0, op1, r=(), w=()):
        P.op("dve", lambda h: h.scalar_tensor_tensor(out=out, in0=in0, scalar=sc, in1=in1, op0=op0, op1=op1), r, w)

    def CP(eng, out, in_, r=(), w=()):
        if eng == "act":
            P.op("act", lambda h: h.copy(out=out, in_=in_), r, w)
        else:
            P.op(eng, lambda h: h.tensor_copy(out=out, in_=in_), r, w)

    def RECIP(out, in_, r=(), w=()):
        P.op("dve", lambda h: h.reciprocal(out=out, in_=in_), r, w)

    def MSET(eng, ap, v, w=()):
        P.op(eng, lambda h: h.memset(ap, v), (), w)

    def DMA(q, out, in_, r=(), w=(), slow=False):
        P.op(q, lambda h: h.dma_start(out=out, in_=in_, allow_slow_non_contiguous=slow), r, w, dma=True)

    def PCc(i):
        return PC[:, i:i + 1]

    DMA("sp", PR0[0:122, :], prm0, w=["PR0"])
    DMA("sp", PR1[0:88, :], prm1, w=["PR1"])
    DMA("sp", W2A2[:], w2a2, w=["W2A2"])
    DMA("sp", G2[:], g2, w=["G2"])
    DMA("sp", LNG[:], ln2g.partition_broadcast(128), w=["LNG"])
    DMA("sp", LNB[:], ln2b.partition_broadcast(128), w=["LNB"])
    DMA("sp", BI[:], bif[0:4, :], w=["BI"])
    DMA("sp", NBF[:], bif[4:8, :], w=["NBF"])
    DMA("sp", M0ROW[:], sm, w=["M0ROW"])
    DMA("sp", CV0[:], scv, w=[("CV0", i) for i in range(22)])
    MSET("pool", ONES[:], 1.0, w=["ONES"])
    ZER = T["LW"][:, 0:128]; NEG1 = T["AA"][:, 0:128]
    MSET("pool", ZER, 0.0, w=["ZER"])
    MSET("pool", NEG1, -1.0, w=["NEG1"])
    P.op("pool", lambda h: h.affine_select(out=ID[:], in_=ONES[:], pattern=[[1, 128]], compare_op=ALU.is_equal,
                                           fill=0.0, base=0, channel_multiplier=-1), ["ONES"], ["ID"])
    TS("pool", AID[:], ID[:], ALPHA, ALU.mult, r=["ID"], w=["AID"])
    CP("pool", IDb[:], ID[:], r=["ID"], w=["ID"])
    MSET("pool", ONESb[:], 1.0, w=["ONES"])
    MSET("pool", BLK[:], 0.0, w=["BLK"])
    MSET("pool", BLK[0:64, 0:64], 1.0, w=["BLK"])
    MSET("pool", BLK[64:128, 64:128], 1.0, w=["BLK"])
    P.op("pool", lambda h: h.affine_select(out=MK1[:, 0, :], in_=ONES[0:64, 0:64], pattern=[[1, 64]], compare_op=ALU.is_gt,
                                           fill=0.0, base=0, channel_multiplier=-1), ["ONES"], ["MK1"])
    P.op("pool", lambda h: h.affine_select(out=MK1[:, 1, :], in_=ONES[0:64, 0:64], pattern=[[1, 64]], compare_op=ALU.is_ge,
                                           fill=0.0, base=0, channel_multiplier=-1), ["ONES"], ["MK1"])
    TS("pool", MK1N[:], MK1[:], -1.0, ALU.mult, r=["MK1"], w=["MK1N"])
    P.op("pool", lambda h: h.affine_select(out=MLN[:], in_=NEG1[0:64, 0:64], pattern=[[-1, 64]], compare_op=ALU.is_gt,
                                           fill=0.0, base=0, channel_multiplier=1), ["NEG1"], ["MLN"])
    P.op("pool", lambda h: h.affine_select(out=MNEG[:], in_=ZER[0:64, 0:64], pattern=[[1, 64]], compare_op=ALU.is_ge,
                                           fill=NEG, base=0, channel_multiplier=-1), ["ZER"], ["MNEG"])
    CP("dve", SEL[:], ID[0:4, 0:4].unsqueeze(2).broadcast_to([4, 4, 128]), r=["ID"], w=["SEL"])
    TS("pool", NBF[:], NBF[:], -1.0, ALU.mult, r=["NBF"], w=["NBF"])
    pt, pk = PS()
    TR(pt[:, 0:122], PR0[0:122, :], 122, r=["PR0"], w=[pk])
    TR(pt[:, 128:216], PR1[0:88, :], 88, r=["PR1"], w=[pk])
    CP("dve", PC[:, 0:122], pt[:, 0:122], r=[pk], w=["PC"])
    CP("dve", PC[:, 128:216], pt[:, 128:216], r=[pk], w=["PC"])
    TS("dve", PCN[:], PC[:, PC_W0:PC_W0 + 16], -1.0, ALU.mult, r=["PC"], w=["PC"])
    MSET("pool", CN[0][:], 0.0, w=[("CN", 0, h) for h in range(4)])
    MSET("pool", NBC[:], 0.0, w=[("NBC", h) for h in range(4)])
    MSET("pool", H[0][:], 0.0, w=[("H", 0, 0), ("H", 0, 1)])
    MSET("pool", Hb[0][:], 0.0, w=[("Hb", 0, 0), ("Hb", 0, 1)])
    MSET("pool", CNb[:], 0.0, w=[("CNb", h) for h in range(4)])
    MSET("pool", MCOL[:], 0.0, w=["MC0", "MC1"])
    MSET("pool", CARRY[:], 0.0, w=[("CARRY", c) for c in range(26)])
    MSET("pool", AGC[:], 0.0, w=[("AGC", i) for i in range(22)])
    pcs = []
    for (wsrc, wdst, R, C) in ((w_in, wb_in, D, 8456), (w_out, wb_out, D, D), (w_up, wb_up, D, 2 * DF), (w_down, wb_down, DF, D)):
        for r0 in range(0, R, 128):
            for c0 in range(0, C, 4228):
                pcs.append((wsrc, wdst, r0, c0, min(4228, C - c0)))
    for pi_, (wsrc, wdst, r0, c0, n_) in enumerate(pcs):
        sl = pi_ % 2
        stg = ARENA[:, sl * 4228:sl * 4228 + n_]
        ob = ARENA[:, 8456 + sl * 2114:8456 + (sl + 1) * 2114].bitcast(BF16)[:, 0:n_]
        DMA("sp", stg, wsrc[r0:r0 + 128, c0:c0 + n_], w=[("STG", sl)])
        CP(("act", "dve", "pool")[pi_ % 3], ob, stg, r=[("STG", sl)], w=[("OB", sl)])
        DMA("pool", wdst[r0:r0 + 128, c0:c0 + n_], ob, r=[("OB", sl)])
    P.barrier()

    wbi = [0]

    def nextWB():
        i = wbi[0]
        wbi[0] = 1 - i
        return WB[i], ("WB", i)

    def ln_stats(Tt, n, tkey, eps=EPS):
        P.op("dve", lambda h: h.bn_stats(out=ST6[0:n, 0:6], in_=Tt[0:n, 0:512]), [tkey], ["ST6"])
        P.op("dve", lambda h: h.bn_stats(out=ST6[0:n, 6:12], in_=Tt[0:n, 512:1024]), [tkey], ["ST6"])
        P.op("dve", lambda h: h.bn_aggr(out=MV[0:n, :], in_=ST6[0:n, :]), ["ST6"], ["MV"])
        ACT(SD[0:n, :], MV[0:n, 1:2], AF.Ln, r=["MV"], w=["SD"], bias=eps)
        ACT(RS[0:n, :], SD[0:n, :], AF.Exp, r=["SD"], w=["RS"], scale=-0.5)
        TS("dve", Tt[0:n, :], Tt[0:n, :], MV[0:n, 0:1], ALU.subtract, r=[tkey, "MV", "RS"], w=[tkey], s2=RS[0:n, 0:1], op1=ALU.mult)

    def to_feature_major(Tt, n, tkey, col0, gcol, bcol, banks=None):
        for half in range(2):
            if banks is None:
                pt_, pk_ = PS()
            else:
                pt_, pk_ = ps[banks[half]], ("ps", banks[half])
            for i in range(4):
                kc = half * 4 + i
                TR(pt_[:, i * 128:i * 128 + n], Tt[0:n, kc * 128:(kc + 1) * 128], n, r=[tkey], w=[pk_])
            for i in range(4):
                kc = half * 4 + i
                if i % 2 == 0:
                    ACT(XT[:, kc, col0:col0 + n], pt_[:, i * 128:i * 128 + n], AF.Identity, r=[pk_, "PC"], w=["XT"],
                        bias=PCc(bcol + kc), scale=PCc(gcol + kc))
                else:
                    TS("dve", XT[:, kc, col0:col0 + n], pt_[:, i * 128:i * 128 + n], PCc(gcol + kc), ALU.mult,
                       r=[pk_, "PC"], w=["XT"], s2=PCc(bcol + kc), op1=ALU.add)

    def proj_chunks(wv, col0, nch, ncols, evac):
        i = 0
        while i < nch:
            nb = min(4, nch - i)
            wb, wk = nextWB()
            wbv = wb[:, :].rearrange("p (k c) -> p k c", k=8)
            DMA("sp", wbv[:, :, 0:nb * 128], wv[:, :, col0 + i * 128:col0 + (i + nb) * 128], w=[wk])
            for b in range(nb):
                pt_, pk_ = PS()
                for kc in range(8):
                    MM(pt_[:, 0:ncols], wbv[:, kc, b * 128:(b + 1) * 128], XTb[:, kc, 0:ncols], start=(kc == 0), stop=(kc == 7),
                       r=[wk, "XTb"], w=[pk_])
                evac(i + b, pt_, pk_)
            i += nb

    cur_ty = [-1]
    gchunk = [0]

    for (kind, tok0, NT, chunks, ty) in blocks:
        smp = (kind == "s")
        NTB = NT + (16 if smp else 0)
        if ty != cur_ty[0]:
            cur_ty[0] = ty
            MSET("pool", RST01_[:], 1.0, w=["RST"])
            if ty == 0:
                MSET("pool", RST01_[:, 0:1], 0.0, w=["RST"])
                MSET("pool", RST01_[:, 16:272:64], 0.0, w=["RST"])
            elif ty == 1:
                MSET("pool", RST01_[:, 0:256:64], 0.0, w=["RST"])
            else:
                MSET("pool", RST01_[:, 0:128:8], 0.0, w=["RST"])
        tiles = []
        if smp:
            tiles.append((0, 128, [(0, 128, xs)]))
        else:
            c = 0
            while c < NT:
                n = min(128, NT - c)
                srcs = []
                t_lo, t_hi = tok0 + c, tok0 + c + n
                if t_lo < 16:
                    srcs.append((0, 16 - t_lo, meta[t_lo:16, :]))
                    srcs.append((16 - t_lo, n - (16 - t_lo), xp[0:t_hi - 16, :]))
                else:
                    srcs.append((0, n, xp[t_lo - 16:t_hi - 16, :]))
                tiles.append((c, n, srcs))
                c += n

        for ti, (col0, n, srcs) in enumerate(tiles):
            Tt, tkey = TOK[ti % 2], ("TOK", ti % 2)
            for (r0, nr, ap) in srcs:
                DMA("sp", Tt[r0:r0 + nr, :], ap, w=[tkey])
            ln_stats(Tt, n, tkey)
            to_feature_major(Tt, n, tkey, col0, PC_LIG, PC_LIB)
        if smp:
            DMA("sp", XT[:, :, 128:144], ssh, w=["XT"])
        CP("pool", XTb[:, :, 0:NTB], XT[:, :, 0:NTB], r=["XT"], w=["XTb"])
        TT("dve", XTlo[:, :, 0:NT], XT[:, :, 0:NT], XTb[:, :, 0:NT], ALU.subtract, r=["XT", "XTb"], w=["XTlo"])
        if smp:
            pass
            CP("pool", SHS[:], XT[:, :, 7:128:8], r=["XT"], w=["EGX"])
            DMA("pool", osh_s, SHS[:], r=["EGX"])
        elif tok0 + NT == 2064:
            CP("pool", SHS[:, :, 0:1], XT[:, :, NT - 1:NT], r=["XT"], w=["EGX"])
            DMA("pool", osh_p, SHS[:, :, 0:1], r=["EGX"], slow=True)

        if stop == "0":
            P.barrier()
            continue
        def evA(dst0, kindA):
            def f(i, pt_, pk_):
                d = slot(dst0 + i)[:, 0:NT]
                if dst0 in (QA, KA):
                    d = slot(dst0 + i).bitcast(BF16)[:, 0:NT]
                if kindA == "copy":
                    if i % 2 == 0:
                        CP("act", d, pt_[:, 0:NT], r=[pk_], w=[AK(dst0 + i)])
                    else:
                        CP("dve", d, pt_[:, 0:NT], r=[pk_], w=[AK(dst0 + i)])
                elif kindA == "kscale":
                    ACT(d, pt_[:, 0:NT], AF.Identity, r=[pk_], w=[AK(dst0 + i)], scale=float(128 ** -0.5))
                else:
                    ACT(d, pt_[:, 0:NT], AF.Sigmoid, r=[pk_], w=[AK(dst0 + i)])
            return f

        proj_chunks(win_v, 2048, 4, NT, evA(QA, "copy"))
        proj_chunks(win_v, 2560, 4, NT, evA(KA, "kscale"))
        proj_chunks(win_v, 3072, 8, NT, evA(VA, "copy"))
        proj_chunks(win_v, 4096, 8, NT, evA(OA, "sig"))
        wb, wk = nextWB()
        wbv = wb[:, :].rearrange("p (k c) -> p k c", k=8)
        DMA("sp", wbv[:, :, 0:8], win_v[:, :, 5120:5128], w=[wk])
        pi_, ki_ = PS()
        pf_, kf_ = PS()
        for kc in range(8):
            MM(pi_[0:4, 0:NT], wbv[:, kc, 0:4], XTb[:, kc, 0:NT], start=(kc == 0), stop=(kc == 7), r=[wk, "XTb"], w=[ki_])
        for kc in range(8):
            MM(pf_[0:4, 0:NT], wbv[:, kc, 4:8], XTb[:, kc, 0:NT], start=(kc == 0), stop=(kc == 7), r=[wk, "XTb"], w=[kf_])
        ACT(LI[0:4, 0:NT], pi_[0:4, 0:NT], AF.Identity, r=[ki_, "BI"], w=["LI"], bias=BI[0:4, 0:1])
        ACT(LP[0:4, 0:NT], pf_[0:4, 0:NT], AF.Exp, r=[kf_, "NBF"], w=["LP"], bias=NBF[0:4, 0:1], scale=-1.0)
        ACT(LP[0:4, 0:NT], LP[0:4, 0:NT], AF.Ln, r=["LP"], w=["LP"], bias=1.0)
        P.op("dve", lambda h, NT=NT, ty=ty: h.tensor_tensor_scan(out=CUMP[0:4, 0:NT], data0=RST01[ty][0:4, 0:NT], data1=LP[0:4, 0:NT],
                                                  initial=0.0, op0=ALU.mult, op1=ALU.add), ["LP", "RST"], ["CUMP"])
        TT("dve", DD[0:4, 0:NT], LI[0:4, 0:NT], CUMP[0:4, 0:NT], ALU.add, r=["LI", "CUMP"], w=["DD"])
        TS("pool", RSTN[0:4, 0:NT], RST01[ty][0:4, 0:NT], 1.0, ALU.subtract, r=["RST"], w=["RSN"], s2=1.0e30, op1=ALU.mult)
        P.op("dve", lambda h, NT=NT, ty=ty: h.tensor_tensor_scan(out=CM[0:4, 0:NT], data0=RSTN[0:4, 0:NT], data1=DD[0:4, 0:NT],
                                                  initial=0.0, op0=ALU.add, op1=ALU.max), ["DD", "RSN"], ["CM"])
        proj_chunks(win_v, 0, 8, NT, evA(GA, "sig"))
        for c in range(8):
            TT("pool", slot(OA + c)[:, 0:NT], slot(OA + c)[:, 0:NT], slot(GA + c)[:, 0:NT], ALU.mult, r=[AK(OA + c), AK(GA + c)], w=[AK(OA + c)])

        MSET("pool", VTK1[:, :, 256:257], 1.0, w=["S1"])
        for ci, (c0, L) in enumerate(chunks):
            cs = slice(c0, c0 + L)
            if smp:
                cnb = ci % 2
                m_in, m_in_k = M0ROW[0:4, ci:ci + 1], "M0ROW"
                m_out, m_out_k = MNEW[0:4, ci:ci + 1], "MNEW"
                cnk = [("CN", cnb, h) for h in range(4)]
                DMA("sp", CN[cnb][:, :, 0:256], sC[ci].rearrange("h k v -> k h v"), w=cnk)
                DMA("sp", CN[cnb][:, :, 256:257], sn[ci].unsqueeze(2), w=cnk, slow=True)
                CP("pool", NBC[:], CN[cnb][:, :, 256:257].broadcast_to([128, 4, 128]), r=cnk, w=[("NBC", h) for h in range(4)])
                CP("act", CNb[:], CN[cnb][:], r=cnk, w=[("CNb", h) for h in range(4)])
            else:
                cnb = 0
                g = gchunk[0]
                gchunk[0] += 1
                m_in, m_in_k = MCOL[0:4, g % 2:g % 2 + 1], "MC%d" % (g % 2)
                m_out, m_out_k = MCOL[0:4, (g + 1) % 2:(g + 1) % 2 + 1], "MC%d" % ((g + 1) % 2)
            CNt = CN[cnb]
            TS("dve", MX[0:4, 0:L], CM[0:4, cs], m_in, ALU.max, r=["CM", m_in_k], w=["MX"])
            TS("dve", ROWS[0:4, 0, 0:L], MX[0:4, 0:L], -1.0, ALU.mult, r=["MX"], w=["ROWS"])
            ACT(ROWS[0:4, 1, 0:L], MX[0:4, 0:L], AF.Exp, r=["MX", m_in_k], w=["ROWS"], bias=m_in, scale=-1.0)
            TT("dve", TR4[0:4, 0:L], CUMP[0:4, cs], MX[0:4, 0:L], ALU.subtract, r=["CUMP", "MX"], w=["TR4"])
            ACT(ROWS[0:4, 2, 0:L], TR4[0:4, 0:L], AF.Exp, r=["TR4"], w=["ROWS"])
            ACT(WKR[0:4, 0:L], DD[0:4, cs], AF.Exp, r=["DD", "ROWS"], w=["WKR"], bias=ROWS[0:4, 0, L - 1:L])
            TT("dve", m_out, MX[0:4, L - 1:L], CUMP[0:4, c0 + L - 1:c0 + L], ALU.subtract, r=["MX", "CUMP"], w=[m_out_k])
            pt_, pk_ = PS()
            TR(pt_[0:L, 0:4], DD[0:4, cs], 4, r=["DD"], w=[pk_])
            TR(pt_[0:L, 4:8], WKR[0:4, 0:L], 4, r=["WKR"], w=[pk_])
            CP("act", COLS[0:L, 0:8], pt_[0:L, 0:8], r=[pk_], w=["COLS"])
            pk1, kk1 = PS()
            pk1b = pk1[:, :].bitcast(BF16)
            for h in range(4):
                TR(pk1b[0:L, h * 128:(h + 1) * 128], Kb(h)[:, cs], 128, r=[AK(KA + h)], w=[kk1], bf=True)
            CP("dve", KTK[0:L, :], pk1b[0:L, 0:512], r=[kk1], w=["KHK"])
            for half in range(2):
                pv, kv = PS()
                for i in range(4):
                    c = half * 4 + i
                    TR(pv[0:L, i * 128:(i + 1) * 128], slot(VA + c)[:, cs], 128, r=[AK(VA + c)], w=[kv])
                CP("act", VTK1[0:L, half * 2:half * 2 + 2, 0:256], pv[0:L, :].rearrange("p (a b) -> p a b", a=2), r=[kv], w=["S1"])
            for h in range(4):
                TS("pool", KW[h][0:L, :], KTK[0:L, h * 128:(h + 1) * 128], COLS[0:L, 4 + h:5 + h], ALU.mult, r=["KHK", "COLS"], w=[("KW", h)])
            PB = [(ps[2 * h], ("ps", 2 * h)) for h in range(4)]
            PN_ = [(ps[2 * h + 1], ("ps", 2 * h + 1)) for h in range(4)]
            for h in range(4):
                pb, kb = PB[h]
                MM(pb[0:L, 0:L], SEL[0:4, h, 0:L], ROWS[0:4, 0, 0:L], start=True, stop=False, r=["SEL", "ROWS"], w=[kb])
                MM(pb[0:L, 0:L], ID[0:L, 0:L], MNEG[0:L, 0:L], start=False, stop=True, r=["ID", "MNEG"], w=[kb])
                MM(pb[:, L:3 * L], SEL[0:4, h, :], ROWS[0:4, 1:3, 0:L], r=["SEL", "ROWS"], w=[kb])
                MM(pb[0:L, 256:256 + L], Kb(h)[:, cs], Qb(h)[:, cs], r=[AK(KA + h), AK(QA + h)], w=[kb])
            for h in range(4):
                pb, kb = PB[h]
                ACT(WST[h][0:L, 0:L], pb[0:L, 0:L], AF.Exp, r=[kb, "COLS"], w=[("WST", h)], bias=COLS[0:L, h:h + 1])
            for h in range(4):
                pb, kb = PB[h]
                TT("dve", SST[h][0:L, 0:L], WST[h][0:L, 0:L], pb[0:L, 256:256 + L], ALU.mult, r=[("WST", h), kb], w=[("SST", h)])
                TT("dve", QW[h][:, 0:L], Qb(h)[:, cs], pb[:, L:2 * L], ALU.mult, r=[AK(QA + h), kb], w=[("QW", h)])
            for h in range(4):
                pn_, kn = PN_[h]
                for vc in range(2):
                    MM(pn_[:, vc * L:(vc + 1) * L], VTK1[0:L, h, vc * 128:(vc + 1) * 128], SST[h][0:L, 0:L], start=True, stop=False,
                       r=["S1", ("SST", h)], w=[kn])
                    MM(pn_[:, vc * L:(vc + 1) * L], CNb[:, h, vc * 128:(vc + 1) * 128], QW[h][:, 0:L], start=False, stop=True,
                       r=[("CNb", h), ("QW", h)], w=[kn])
                MM(pn_[:, 2 * L:3 * L], ONESb[0:L, :], SST[h][0:L, 0:L], start=True, stop=False, r=["ONES", ("SST", h)], w=[kn])
                MM(pn_[:, 2 * L:3 * L], NBC[:, h, :], QW[h][:, 0:L], start=False, stop=True, r=[("NBC", h), ("QW", h)], w=[kn])
                MM(pn_[:, 192:449], KW[h][0:L, :], VTK1[0:L, h, :], r=[("KW", h), "S1"], w=[kn])
            for h in range(4):
                pb, kb = PB[h]
                pn_, kn = PN_[h]
                ACT(ADEN[h][:, 0:L], pn_[:, 2 * L:3 * L], AF.Abs, r=[kn], w=[("ADEN", h)])
                CP("act", DCOL[:, h:h + 1], pb[:, 2 * L - 1:2 * L], r=[kb], w=[("DCOL", h)])
            for h in range(4):
                pb, kb = PB[h]
                pn_, kn = PN_[h]
                TT("dve", DDT[h][:, 0:L], ADEN[h][:, 0:L], pb[:, 2 * L:3 * L], ALU.max, r=[("ADEN", h), kb], w=[("DDT", h)])
                RECIP(DDT[h][:, 0:L], DDT[h][:, 0:L], r=[("DDT", h)], w=[("DDT", h)])
                TT("dve", slots(VA + 2 * h, 2)[:, :, cs], pn_[:, 0:2 * L].rearrange("p (a b) -> p a b", a=2),
                   DDT[h][:, 0:L].unsqueeze(1).broadcast_to([128, 2, L]), ALU.mult,
                   r=[kn, ("DDT", h)], w=[AK(VA + 2 * h), AK(VA + 2 * h + 1)])
                STT(CNt[:, h, :], CNt[:, h, :], DCOL[:, h:h + 1], pn_[:, 192:449], ALU.mult, ALU.add,
                    r=[("CN", cnb, h), kn, ("DCOL", h)], w=[("CN", cnb, h)])
            for h in range(4):
                CP("pool", NBC[:, h, :], CNt[:, h, 256:257].broadcast_to([128, 128]), r=[("CN", cnb, h)], w=[("NBC", h)])
                CP("act", CNb[:, h, :], CNt[:, h, :], r=[("CN", cnb, h)], w=[("CNb", h)])
            if smp:
                DMA("pool", oCN_s[ci], CNt[:], r=[("CN", cnb, h) for h in range(4)])
        if smp:
            DMA("pool", om_s, MNEW[:], r=["MNEW"])
        elif tok0 + NT == 2064:
            DMA("pool", oCN_p, CN[0][:], r=[("CN", 0, h) for h in range(4)])
            gl = gchunk[0] % 2
            DMA("pool", om_p, MCOL[0:4, gl:gl + 1], r=["MC%d" % gl])

        TN3 = ["LW", "AA", "KKS", "SQ", "NRM", "KKN", "T1", "BB", "CUM", "EGP", "EGN", "CX"]
        hs = list(range(4))
        tq = {h: (TN3[3 * h], TN3[3 * h + 1], TN3[3 * h + 2]) for h in hs}
        pmk = {}
        for h in hs:
            c0_, c1_ = VA + 2 * h, VA + 2 * h + 1
            pm, km = PS()
            pmk[h] = (pm, km)
            MM(pm[:, 0:NT], ONES[:, :], slot(c0_)[:, 0:NT], start=True, stop=False, r=["ONES", AK(c0_)], w=[km])
            MM(pm[:, 0:NT], ONES[:, :], slot(c1_)[:, 0:NT], start=False, stop=True, r=["ONES", AK(c1_)], w=[km])
        for h in hs:
            pm, km = pmk[h]
            for c_ in (VA + 2 * h, VA + 2 * h + 1):
                STT(slot(c_)[:, 0:NT], pm[:, 0:NT], -1.0 / 256, slot(c_)[:, 0:NT], ALU.mult, ALU.add, r=[km, AK(c_)], w=[AK(c_)])
        for h in hs:
            c0_, c1_ = VA + 2 * h, VA + 2 * h + 1
            TT("pool", T[tq[h][0]][:, 0:NT], slot(c0_)[:, 0:NT], slot(c0_)[:, 0:NT], ALU.mult, r=[AK(c0_)], w=[tq[h][0]])
            TT("pool", T[tq[h][1]][:, 0:NT], slot(c1_)[:, 0:NT], slot(c1_)[:, 0:NT], ALU.mult, r=[AK(c1_)], w=[tq[h][1]])
        for h in hs:
            pv2, kv2 = PS()
            pmk[h] = (pv2, kv2)
            MM(pv2[:, 0:NT], ONES[:, :], T[tq[h][0]][:, 0:NT], start=True, stop=False, r=["ONES", tq[h][0]], w=[kv2])
            MM(pv2[:, 0:NT], ONES[:, :], T[tq[h][1]][:, 0:NT], start=False, stop=True, r=["ONES", tq[h][1]], w=[kv2])
        for h in hs:
            pv2, kv2 = pmk[h]
            ACT(T[tq[h][2]][:, 0:NT], pv2[:, 0:NT], AF.Ln, r=[kv2], w=[tq[h][2]], bias=EPS, scale=1.0 / 256)
        for h in hs:
            ACT(T[tq[h][2]][:, 0:NT], T[tq[h][2]][:, 0:NT], AF.Exp, r=[tq[h][2]], w=[tq[h][2]], scale=-0.5)
        for h in hs:
            for vc, c_ in enumerate((VA + 2 * h, VA + 2 * h + 1)):
                cc = 2 * h + vc
                STT(slot(c_)[:, 0:NT], slot(c_)[:, 0:NT], PCc(PC_MNG + cc), T[tq[h][2]][:, 0:NT], ALU.mult, ALU.mult, r=[AK(c_), tq[h][2], "PC"], w=[AK(c_)])
        for h in hs:
            for vc, c_ in enumerate((VA + 2 * h, VA + 2 * h + 1)):
                cc = 2 * h + vc
                TT("pool" if vc else "dve", slot(MERGED + cc)[:, 0:NT], slot(c_)[:, 0:NT], slot(OA + cc)[:, 0:NT], ALU.mult, r=[AK(c_), AK(OA + cc)], w=[AK(MERGED + cc)])
        P.barrier()
        if stop == "A":
            continue

        def evGB(i, pt_, pk_):
            ACT(slot(GB + i)[:, 0:NT], pt_[:, 0:NT], AF.Sigmoid, r=[pk_], w=[AK(GB + i)])

        proj_chunks(win_v, 1024, 8, NT, evGB)

        def evPB(cc, pt_, pk_):
            if cc < 8:
                dst, dk = slot(RB + cc), AK(RB + cc)
            elif cc < 16:
                dst, dk = slot(KB + cc - 8), AK(KB + cc - 8)
            elif cc < 24:
                dst, dk = slot(VB + cc - 16), AK(VB + cc - 16)
            elif cc == 24:
                dst, dk = XWA, "XWA"
            else:
                dst, dk = XG, "XG"
            if cc % 2 == 0:
                CP("act", dst[:, 0:NTB], pt_[:, 0:NTB], r=[pk_], w=[dk])
            else:
                CP("dve", dst[:, 0:NTB], pt_[:, 0:NTB], r=[pk_], w=[dk])
            ds, dsk = (T["CX"], "CX") if cc % 2 == 0 else (T["EGX"], "EGX")
            if smp:
                d3 = dst[:, 0:128].rearrange("p (s t) -> p s t", t=8)
                s3 = ds[:, 0:128].rearrange("p (s t) -> p s t", t=8)
                TT("pool", s3[:, :, 1:8], d3[:, :, 0:7], d3[:, :, 1:8], ALU.subtract, r=[dk], w=[dsk])
                TT("pool", s3[:, :, 0:1], dst[:, 128:144].unsqueeze(2), d3[:, :, 0:1], ALU.subtract, r=[dk], w=[dsk])
            else:
                TT("pool", ds[:, 1:NT], dst[:, 0:NT - 1], dst[:, 1:NT], ALU.subtract, r=[dk], w=[dsk])
                TT("pool", ds[:, 0:1], CARRY[:, cc:cc + 1], dst[:, 0:1], ALU.subtract, r=[dk, ("CARRY", cc)], w=[dsk])
                CP("pool", CARRY[:, cc:cc + 1], dst[:, NT - 1:NT], r=[dk], w=[("CARRY", cc)])
            STT(dst[:, 0:NT], ds[:, 0:NT], PCc(PC_MU + cc), dst[:, 0:NT], ALU.mult, ALU.add, r=[dsk, dk, "PC"], w=[dk])

        proj_chunks(win_v, 5128, 26, NTB, evPB)
        if stop == "B1":
            P.barrier()
            continue
        ACT(XWA[0:64, 0:NT], XWA[0:64, 0:NT], AF.Tanh, r=["XWA"], w=["XWA"])
        ACT(XG[:, 0:NT], XG[:, 0:NT], AF.Sigmoid, r=["XG"], w=["XG"])

        nchunks = len(chunks)
        last0, Lc = chunks[-1][0] + chunks[-1][1], chunks[-1][1]
        first_last = chunks[0][0] + chunks[0][1] - 1
        for j in range(8):
            Rj, Kj, Vj = slot(RB + j)[:, 0:NT], slot(KB + j)[:, 0:NT], slot(VB + j)[:, 0:NT]
            rk_, kk_, vk_ = AK(RB + j), AK(KB + j), AK(VB + j)
            cols = slice(j * 128, (j + 1) * 128)

            def t(n):
                return T[n][:, 0:NT]
            pw, kw = PS()
            MM(pw[:, 0:NT], W2A2[0:64, cols], XWA[0:64, 0:NT], r=["W2A2", "XWA"], w=[kw])
            pa, ka = PS()
            MM(pa[:, 0:NT], W2A2[64:128, cols], XWA[64:128, 0:NT], r=["W2A2", "XWA"], w=[ka])
            ACT(t("LW"), pw[:, 0:NT], AF.Exp, r=[kw, "PC"], w=["LW"], bias=PCN[:, j:j + 1], scale=-1.0)
            ACT(t("AA"), pa[:, 0:NT], AF.Exp, r=[ka, "PC"], w=["AA"], bias=PCN[:, 8 + j:9 + j], scale=-1.0)
            ACT(t("LW"), t("LW"), AF.Ln, r=["LW"], w=["LW"], bias=1.0)
            ACT(t("AA"), t("AA"), AF.Ln, r=["AA"], w=["AA"], bias=1.0)
            ACT(t("LW"), t("LW"), AF.Exp, r=["LW"], w=["LW"], scale=-1.0)
            ACT(t("AA"), t("AA"), AF.Exp, r=["AA"], w=["AA"], scale=-1.0)
            TS("pool", t("KKS"), Kj, PCc(PC_KKS + j), ALU.mult, r=[kk_, "PC"], w=["KKS"])
            TT("pool", t("SQ"), t("KKS"), t("KKS"), ALU.mult, r=["KKS"], w=["SQ"])
            pn2, kn2 = PS()
            MM(pn2[:, 0:NT], BLK[:, :], t("SQ"), r=["BLK", "SQ"], w=[kn2])
            P.op("dve", lambda h, NT=NT, ty=ty: h.tensor_tensor_scan(out=T["CUM"][:, 0:NT], data0=RST01[ty][:, 0:NT], data1=T["LW"][:, 0:NT],
                                                         initial=0.0, op0=ALU.mult, op1=ALU.add), ["LW", "RST"], ["CUM"])
            TS("dve", t("NRM"), pn2[:, 0:NT], 1e-24, ALU.max, r=[kn2], w=["NRM"])
            ACT(t("NRM"), t("NRM"), AF.Ln, r=["NRM"], w=["NRM"])
            ACT(t("NRM"), t("NRM"), AF.Exp, r=["NRM"], w=["NRM"], scale=-0.5)
            ACT(t("EGP"), t("CUM"), AF.Exp, r=["CUM"], w=["EGP"], scale=-C0)
            ACT(t("EGN"), t("CUM"), AF.Exp, r=["CUM"], w=["EGN"], scale=C0)
            TT("pool", t("CX"), t("CUM"), t("LW"), ALU.subtract, r=["CUM", "LW"], w=["CX"])
            ACT(t("EGX"), t("CX"), AF.Exp, r=["CX"], w=["EGX"], scale=-C0)
            TT("dve", t("KKN"), t("KKS"), t("NRM"), ALU.mult, r=["KKS", "NRM"], w=["KKN"])
            TS("dve", t("T1"), t("AA"), 1.0, ALU.subtract, r=["AA", "PC"], w=["T1"], s2=PCc(PC_KAS + j), op1=ALU.mult)
            STT(Kj, t("T1"), 1.0, Kj, ALU.add, ALU.mult, r=["T1", kk_], w=[kk_])
            TT("pool", t("BB"), t("KKN"), t("AA"), ALU.mult, r=["KKN", "AA"], w=["BB"])
            CP("pool", GL[:, j, 0:nchunks], T["EGP"][:, first_last:last0:Lc] if nchunks > 1 else T["EGP"][:, first_last:first_last + 1],
               r=["EGP"], w=[("GL", j)])
            bon = slot(BON + j)[:, 0:NT]
            STT(bon, Rj, PCc(PC_RK + j), Kj, ALU.mult, ALU.mult, r=[rk_, kk_, "PC"], w=[AK(BON + j)])
            pb2, kb2 = PS()
            MM(pb2[:, 0:NT], BLK[:, :], bon, r=["BLK", AK(BON + j)], w=[kb2])
            TT("dve", RTb(j)[:, 0:NT], Rj, t("EGP"), ALU.mult, r=[rk_, "EGP"], w=[("RTb", j)])
            TT("dve", KHb(j)[:, 0:NT], Kj, t("EGN"), ALU.mult, r=[kk_, "EGN"], w=[("KHb", j)])
            TT("pool", KKTb(j)[:, 0:NT], t("KKN"), t("EGX"), ALU.mult, r=["KKN", "EGX"], w=[AK(KKT + j)])
            TT("pool", BHb(j)[:, 0:NT], t("BB"), t("EGN"), ALU.mult, r=["BB", "EGN"], w=[AK(BHT + j)])
            TT("dve", bon, pb2[:, 0:NT], Vj, ALU.mult, r=[kb2, vk_], w=[AK(BON + j)])
        if stop == "B2":
            P.barrier()
            continue

        def QQ(j, rows, cs):
            return ARENA[:, KKT * W:(KKT + 16) * W].bitcast(BF16)[:, j * W:j * W + 16 * W].rearrange("p (two d) -> p two d", two=2)[rows, :, cs]

        for ci, (c0, L) in enumerate(chunks):
            cs = slice(c0, c0 + L)
            nl = {8: 3, 16: 4, 64: 6}[L]
            if DBGSTEP and ci < DBGCHUNK:
                continue
            hb = ci % 2 if smp else 0
            Ht = H[hb]
            if smp:
                DMA("sp", Ht[:], sH[ci], w=[("H", hb, 0), ("H", hb, 1)])
                CP("act", Hb[hb][:], Ht[:], r=[("H", hb, 0), ("H", hb, 1)], w=[("Hb", hb, 0), ("Hb", hb, 1)])
            for jh in range(2):
                hk = ("H", hb, jh)
                hbk = ("Hb", hb, jh)
                Hbt = Hb[hb]
                pA, kA = PS(); pB, kB = PS(); pC, kC = PS()
                pBb = pB[:, :].bitcast(BF16); pCb = pC[:, :].bitcast(BF16)
                for jj in range(4):
                    j = 4 * jh + jj
                    TR(pA[0:L, jj * 128:(jj + 1) * 128], slot(VB + j)[:, cs], 128, r=[AK(VB + j)], w=[kA])
                    TR(pBb[0:L, jj * 128:(jj + 1) * 128], KHb(j)[:, cs], 128, r=[("KHb", j)], w=[kB], bf=True)
                    TR(pCb[0:L, jj * 128:(jj + 1) * 128], BHb(j)[:, cs], 128, r=[AK(BHT + j)], w=[kC], bf=True)
                CP("act", VTK[0:L, :], pA[0:L, :], r=[kA], w=["VTK"])
                CP("dve", KHK[0:L, :], pBb[0:L, 0:512], r=[kB], w=["KHK"])
                ACT(BHK[0:L, :], pCb[0:L, 0:512], AF.Identity, r=[kC], w=["BHK"], scale=-1.0)

                if DBGSTEP == 1:
                    continue
                def hd(hq):
                    hp, jj = divmod(hq, 4)
                    return 4 * jh + jj, jj, hp, slice(64 * hp, 64 * hp + 64), slice(jj * 128 + hp * 64, jj * 128 + hp * 64 + 64)
                b1 = [PS(), PS()]
                for hq in range(8):
                    j, jj, hp, rows, tc = hd(hq)
                    MM(b1[hp][0][0:L, jj * 2 * L:(jj + 1) * 2 * L], KHb(j)[rows, cs], QQ(j, rows, cs),
                       r=[("KHb", j), AK(KKT + j), ("RTb", j)], w=[b1[hp][1]])
                if DBGSTEP == 2:
                    continue
                for hp in range(2):
                    mk = MK1[0:L, :, 0:L].unsqueeze(1).broadcast_to([L, 4, 2, L])
                    TT("dve", S1[0:L, 4 * hp:4 * hp + 4, :, 0:L],
                       b1[hp][0][0:L, 0:8 * L].rearrange("p (a b c) -> p a b c", a=4, b=2), mk, ALU.mult,
                       r=[b1[hp][1], "MK1"], w=["S1"])
                if DBGSTEP == 3:
                    continue
                b2 = [PS(), PS()]
                for hq in range(8):
                    j, jj, hp, rows, tc = hd(hq)
                    MM(b2[hp][0][0:L, jj * 2 * L:(jj + 1) * 2 * L], BHb(j)[rows, cs], QQ(j, rows, cs),
                       r=[AK(BHT + j), AK(KKT + j), ("RTb", j)], w=[b2[hp][1]])
                for hp in range(2):
                    mkn = MK1N[0:L, :, 0:L].unsqueeze(1).broadcast_to([L, 4, 2, L])
                    TT("dve", S2[0:L, 4 * hp:4 * hp + 4, :, 0:L],
                       b2[hp][0][0:L, 0:8 * L].rearrange("p (a b c) -> p a b c", a=4, b=2), mkn, ALU.mult,
                       r=[b2[hp][1], "MK1N"], w=["S2"])
                if DBGSTEP == 4:
                    continue
                p3 = [PS(), PS()]
                for hq in range(8):
                    j, jj, hp, rows, tc = hd(hq)
                    MM(p3[hp][0][0:L, jj * L:(jj + 1) * L], KKTb(j)[rows, cs], BHb(j)[rows, cs],
                       r=[AK(KKT + j), AK(BHT + j)], w=[p3[hp][1]])
                for hp in range(2):
                    TT("dve", PA[0:L, 4 * hp:4 * hp + 4, 0:L], p3[hp][0][0:L, 0:4 * L].rearrange("p (a b) -> p a b", a=4),
                       MLN[0:L, 0:L].unsqueeze(1).broadcast_to([L, 4, L]), ALU.mult, r=[p3[hp][1], "MLN"], w=["PA"])
                if DBGSTEP == 5:
                    continue
                pU2 = [PS(), PS()]
                for hq in range(8):
                    j, jj, hp, rows, tc = hd(hq)
                    MM(pU2[hp][0][0:L, jj * 64:(jj + 1) * 64], KKTb(j)[rows, cs], Hbt[rows, j, :], start=(jj == 0), stop=False,
                       r=[AK(KKT + j), hbk], w=[pU2[hp][1]])
                for hq in range(8):
                    j, jj, hp, rows, tc = hd(hq)
                    MM(pU2[hp][0][0:L, jj * 64:(jj + 1) * 64], S1[0:L, hq, 0, 0:L], VTK[0:L, tc], start=False, stop=(jj == 3),
                       r=["S1", "VTK"], w=[pU2[hp][1]], strict=(hp == 1 and jj == 0))
                CP("act", U[0][0:L, 0:256], pU2[0][0][0:L, 0:256], r=[pU2[0][1]], w=[("U", 0)])
                CP("act", U[0][0:L, 256:512], pU2[1][0][0:L, 0:256], r=[pU2[1][1]], w=[("U", 0)])
                if DBGSTEP == 6:
                    continue
                cur = 0
                Pt, Pk = PA, "PA"
                PTt, PTk = S2, "S2"

                def PTv(hq):
                    return PTt[0:L, hq, 0, 0:L] if PTk == "S2" else PTt[0:L, hq, 0:L]
                for l in range(nl):
                    pU, kU = PS()
                    for hq in range(8):
                        MM(pU[0:L, hq * 64:(hq + 1) * 64], PTv(hq), U[cur][0:L, hq * 64:(hq + 1) * 64], r=[PTk, ("U", cur)], w=[kU])
                    TT("dve", U[1 - cur][0:L, :], U[cur][0:L, :], pU[0:L, :], ALU.add, r=[("U", cur), kU], w=[("U", 1 - cur)])
                    cur = 1 - cur
                    if l < nl - 1:
                        pP, kP = PS(); pT, kT = PS()
                        for hq in range(8):
                            MM(pP[0:L, hq * L:(hq + 1) * L], PTv(hq), Pt[0:L, hq, 0:L], r=[PTk, Pk], w=[kP])
                        for hq in range(8):
                            MM(pT[0:L, hq * L:(hq + 1) * L], Pt[0:L, hq, 0:L], PTv(hq), r=[PTk, Pk], w=[kT])
                        nP, nPT = PN[l % 2], PTN[l % 2]
                        CP("act", nP[0:L, :, 0:L], pP[0:L, 0:8 * L].rearrange("p (a b) -> p a b", a=8), r=[kP], w=[("PN", l % 2)])
                        CP("dve", nPT[0:L, :, 0:L], pT[0:L, 0:8 * L].rearrange("p (a b) -> p a b", a=8), r=[kT], w=[("PTN", l % 2)])
                        Pt, Pk = nP, ("PN", l % 2)
                        PTt, PTk = nPT, ("PTN", l % 2)
                if DBGSTEP == 7:
                    continue
                pY2 = [PS(), PS()]
                for hq in range(8):
                    j, jj, hp, rows, tc = hd(hq)
                    o_ = pY2[hp][0][rows, jj * L:(jj + 1) * L]
                    MM(o_, Hbt[rows, j, :], RTb(j)[rows, cs], start=(jj == 0), stop=False, r=[hbk, ("RTb", j)], w=[pY2[hp][1]])
                for hq in range(8):
                    j, jj, hp, rows, tc = hd(hq)
                    o_ = pY2[hp][0][rows, jj * L:(jj + 1) * L]
                    MM(o_, VTK[0:L, tc], S1[0:L, hq, 1, 0:L], start=False, stop=False, r=["VTK", "S1"], w=[pY2[hp][1]], strict=(hp == 1 and jj == 0))
                    MM(o_, U[cur][0:L, hq * 64:(hq + 1) * 64], S2[0:L, hq, 1, 0:L], start=False, stop=(jj == 3), r=[("U", cur), "S2"], w=[pY2[hp][1]])
                pH, kH = PS()
                for hq in range(8):
                    j, jj, hp, rows, tc = hd(hq)
                    o_ = pH[rows, jj * 64:(jj + 1) * 64]
                    MM(o_, KHK[0:L, tc], VTK[0:L, tc], start=True, stop=False, r=["KHK", "VTK"], w=[kH])
                    MM(o_, BHK[0:L, tc], U[cur][0:L, hq * 64:(hq + 1) * 64], start=False, stop=True, r=["BHK", ("U", cur)], w=[kH])
                if DBGSTEP == 8:
                    continue
                for hp in range(2):
                    rows = slice(64 * hp, 64 * hp + 64)
                    CP("act", slots(RB + 4 * jh, 4)[rows, :, cs], pY2[hp][0][rows, 0:4 * L].rearrange("p (a b) -> p a b", a=4), r=[pY2[hp][1]],
                       w=[AK(RB + 4 * jh + q) for q in range(4)])
                TT("dve", TMPH[:], Ht[:, 4 * jh:4 * jh + 4, :], pH[:, 0:256].rearrange("p (a b) -> p a b", a=4), ALU.add, r=[hk, kH], w=["TMPH"])
                TT("pool", Ht[:, 4 * jh:4 * jh + 4, :], TMPH[:], GL[:, 4 * jh:4 * jh + 4, ci:ci + 1].broadcast_to([128, 4, 64]), ALU.mult,
                   r=["TMPH"] + [("GL", 4 * jh + q) for q in range(4)], w=[hk])
                CP("act", Hbt[:, 4 * jh:4 * jh + 4, :], Ht[:, 4 * jh:4 * jh + 4, :], r=[hk], w=[hbk])
            if smp:
                DMA("pool", oH_s[ci], Ht[:], r=[("H", hb, 0), ("H", hb, 1)])
            if DBGSTEP and ci >= DBGCHUNK:
                break
        if (not smp) and tok0 + NT == 2064:
            DMA("pool", oH_p, H[0][:], r=[("H", 0, 0), ("H", 0, 1)])

        if stop == "B3":
            P.barrier()
            continue
        TN3 = ["LW", "AA", "KKS", "SQ", "NRM", "KKN", "T1", "BB", "CUM", "EGP", "EGN", "CX"]
        for g0 in (0, 4):
            js = list(range(g0, g0 + 4))
            tq = {j: (TN3[3 * (j - g0)], TN3[3 * (j - g0) + 1], TN3[3 * (j - g0) + 2]) for j in js}
            pk_ = {}
            for j in js:
                pm, km = PS()
                pk_[j] = (pm, km)
                MM(pm[:, 0:NT], BLK[:, :], slot(RB + j)[:, 0:NT], r=["BLK", AK(RB + j)], w=[km])
            for j in js:
                pm, km = pk_[j]
                Yj, yk = slot(RB + j)[:, 0:NT], AK(RB + j)
                STT(Yj, pm[:, 0:NT], -1.0 / 64, Yj, ALU.mult, ALU.add, r=[km, yk], w=[yk])
            for j in js:
                Yj, yk = slot(RB + j)[:, 0:NT], AK(RB + j)
                TT("pool", T[tq[j][0]][:, 0:NT], Yj, Yj, ALU.mult, r=[yk], w=[tq[j][0]])
            for j in js:
                pv2, kv2 = PS()
                pk_[j] = (pv2, kv2)
                MM(pv2[:, 0:NT], BLK[:, :], T[tq[j][0]][:, 0:NT], r=["BLK", tq[j][0]], w=[kv2])
            for j in js:
                pv2, kv2 = pk_[j]
                ACT(T[tq[j][1]][:, 0:NT], pv2[:, 0:NT], AF.Ln, r=[kv2], w=[tq[j][1]], bias=GN_EPS, scale=1.0 / 64)
            for j in js:
                ACT(T[tq[j][1]][:, 0:NT], T[tq[j][1]][:, 0:NT], AF.Exp, r=[tq[j][1]], w=[tq[j][1]], scale=-0.5)
            for j in js:
                pg, kg = PS()
                pk_[j] = (pg, kg)
                MM(pg[:, 0:NT], G2[:, j * 128:(j + 1) * 128], XG[:, 0:NT], r=["G2", "XG"], w=[kg])
            for j in js:
                Yj, yk = slot(RB + j)[:, 0:NT], AK(RB + j)
                t1 = T[tq[j][2]][:, 0:NT]
                STT(t1, Yj, PCc(PC_LXG + j), T[tq[j][1]][:, 0:NT], ALU.mult, ALU.mult, r=[yk, tq[j][1], "PC"], w=[tq[j][2]])
                STT(t1, t1, PCc(PC_LXB + j), slot(BON + j)[:, 0:NT], ALU.add, ALU.add, r=[tq[j][2], AK(BON + j), "PC"], w=[tq[j][2]])
            for j in js:
                pg, kg = pk_[j]
                t1 = T[tq[j][2]][:, 0:NT]
                TT("dve", t1, t1, pg[:, 0:NT], ALU.mult, r=[tq[j][2], kg], w=[tq[j][2]])
            for j in js:
                t1 = T[tq[j][2]][:, 0:NT]
                TT("pool", t1, t1, slot(GB + j)[:, 0:NT], ALU.mult, r=[tq[j][2], AK(GB + j)], w=[tq[j][2]])
                TT("pool", slot(MERGED + j)[:, 0:NT], slot(MERGED + j)[:, 0:NT], t1, ALU.add, r=[tq[j][2], AK(MERGED + j)], w=[AK(MERGED + j)])
        P.barrier()
        if stop == "B":
            continue

        def big_out(wv, nrowch, lhs_of, resid):
            for cp_ in range((nrowch + 3) // 4):
                wb, wk = nextWB()
                wbv2 = wb[:, :].rearrange("p (c d) -> p c d", c=4)
                ncc = min(4, nrowch - 4 * cp_)
                DMA("sp", wbv2[:, 0:ncc, :], wv[:, 4 * cp_:4 * cp_ + ncc, :], w=[wk])
                for ci_ in range(ncc):
                    c = 4 * cp_ + ci_
                    for ti, (col0, n, _) in enumerate(tiles):
                        for half in range(2):
                            MM(ps[2 * ti + half][0:n, 0:512], lhs_of(c)[:, col0:col0 + n], wbv2[:, ci_, half * 512:(half + 1) * 512],
                               start=(c == 0), stop=False, r=[wk] + resid[1], w=[("ps", 2 * ti + half)])
            for ti, (col0, n, _) in enumerate(tiles):
                for c in range(8):
                    o_ = ps[2 * ti + c // 4][0:n, (c % 4) * 128:(c % 4 + 1) * 128]
                    MM(o_, XTb[:, c, col0:col0 + n], IDb[:, :], start=False, stop=False, r=["XTb", "ID"], w=[("ps", 2 * ti + c // 4)])
                    MM(o_, XTlo[:, c, col0:col0 + n], IDb[:, :], start=False, stop=(c % 4 == 3), r=["XTlo", "ID"], w=[("ps", 2 * ti + c // 4)])

        MRGb = ARENA[:, 52 * W:56 * W].bitcast(BF16).rearrange("p (c w) -> p c w", c=8)
        TS("pool", MRGb[:, :, 0:NT], slots(MERGED, 8)[:, :, 0:NT], 1.0 / ALPHA, ALU.mult, r=[AK(MERGED + c) for c in range(8)], w=["MRGb"])
        big_out(wout_v, 8, lambda c: MRGb[:, c, :], (None, ["MRGb"]))
        for ti, (col0, n, _) in enumerate(tiles):
            Tt, tkey = TOK[ti % 2], ("TOK", ti % 2)
            CP("act", Tt[0:n, 0:512], ps[2 * ti][0:n, :], r=[("ps", 2 * ti)], w=[tkey])
            CP("dve", Tt[0:n, 512:1024], ps[2 * ti + 1][0:n, :], r=[("ps", 2 * ti + 1)], w=[tkey])
            ln_stats(Tt, n, tkey, eps=EPS / (ALPHA * ALPHA))
            to_feature_major(Tt, n, tkey, col0, PC_L1G, PC_L1B, banks=(6, 7))
        CP("pool", XTb[:, :, 0:NT], XT[:, :, 0:NT], r=["XT"], w=["XTb"])
        TT("dve", XTlo[:, :, 0:NT], XT[:, :, 0:NT], XTb[:, :, 0:NT], ALU.subtract, r=["XT", "XTb"], w=["XTlo"])

        def evAG(i, pt_, pk_):
            if smp:
                d = slot(AG + i)[:, 0:160].rearrange("p (s t) -> p s t", t=10)[:, :, 2:10]
                CP("act", d, pt_[:, 0:128].rearrange("p (s t) -> p s t", t=8), r=[pk_], w=[AK(AG + i)])
            else:
                CP("act", slot(AG + i)[:, 2:2 + NT], pt_[:, 0:NT], r=[pk_], w=[AK(AG + i)])

        def evAV(i, pt_, pk_):
            CP("dve", slot(AV + i)[:, 0:NT], pt_[:, 0:NT], r=[pk_], w=[AK(AV + i)])

        proj_chunks(wup_v, 0, 22, NT, evAG)
        proj_chunks(wup_v, DF, 22, NT, evAV)
        _tn = ["LW", "AA", "KKS", "SQ", "NRM", "KKN", "T1", "BB", "CUM", "EGP", "EGN", "CX"]
        for g0 in range(0, 22, 6):
            idx = list(range(g0, min(g0 + 6, 22)))
            tk = {i: (_tn[2 * (i - g0)], _tn[2 * (i - g0) + 1]) for i in idx}
            for i in idx:
                ag, agk = slot(AG + i), AK(AG + i)
                if smp:
                    a3 = ag[:, 0:160].rearrange("p (s t) -> p s t", t=10)
                    CP("pool", a3[:, :, 0:2], CV0[:, i, :, :], r=[("CV0", i)], w=[agk])
                    CP("pool", CV0[:, i, :, :], a3[:, :, 8:10], r=[agk], w=[("CV0", i)])
                else:
                    CP("pool", ag[:, 0:2], AGC[:, i, :], r=[("AGC", i)], w=[agk])
                    CP("pool", AGC[:, i, :], ag[:, NT:NT + 2], r=[agk], w=[("AGC", i)])
            for i in idx:
                ag, agk = slot(AG + i), AK(AG + i)
                cvk, g1k = tk[i]
                cv = T[cvk]
                if smp:
                    a3 = ag[:, 0:160].rearrange("p (s t) -> p s t", t=10)
                    cv3 = cv[:, 0:128].rearrange("p (s t) -> p s t", t=8)
                    TS("dve", cv3, a3[:, :, 0:8], PCc(PC_CW0 + i), ALU.mult, r=[agk, "PC"], w=[cvk], s2=PCc(PC_CB + i), op1=ALU.add)
                    STT(cv3, a3[:, :, 1:9], PCc(PC_CW1 + i), cv3, ALU.mult, ALU.add, r=[agk, cvk, "PC"], w=[cvk])
                    STT(cv3, a3[:, :, 2:10], PCc(PC_CW2 + i), cv3, ALU.mult, ALU.add, r=[agk, cvk, "PC"], w=[cvk])
                else:
                    ACT(cv[:, 0:NT], ag[:, 0:NT], AF.Identity, r=[agk, "PC"], w=[cvk], bias=PCc(PC_CB + i), scale=PCc(PC_CW0 + i))
                    STT(cv[:, 0:NT], ag[:, 1:NT + 1], PCc(PC_CW1 + i), cv[:, 0:NT], ALU.mult, ALU.add, r=[agk, cvk, "PC"], w=[cvk])
                    STT(cv[:, 0:NT], ag[:, 2:NT + 2], PCc(PC_CW2 + i), cv[:, 0:NT], ALU.mult, ALU.add, r=[agk, cvk, "PC"], w=[cvk])
            for i in idx:
                cvk, g1k = tk[i]
                ACT(T[g1k][:, 0:NT], T[cvk][:, 0:NT], AF.Square, r=[cvk], w=[g1k])
            for i in idx:
                cvk, g1k = tk[i]
                TS("dve", T[g1k][:, 0:NT], T[g1k][:, 0:NT], 0.044715, ALU.mult, r=[g1k], w=[g1k], s2=1.0, op1=ALU.add)
            for i in idx:
                cvk, g1k = tk[i]
                TT("pool", T[g1k][:, 0:NT], T[g1k][:, 0:NT], T[cvk][:, 0:NT], ALU.mult, r=[g1k, cvk], w=[g1k])
            for i in idx:
                cvk, g1k = tk[i]
                ACT(T[g1k][:, 0:NT], T[g1k][:, 0:NT], AF.Sigmoid, r=[g1k], w=[g1k], scale=GELU_K)
            for i in idx:
                cvk, g1k = tk[i]
                TT("pool", T[g1k][:, 0:NT], T[g1k][:, 0:NT], T[cvk][:, 0:NT], ALU.mult, r=[g1k, cvk], w=[g1k])
            for i in idx:
                cvk, g1k = tk[i]
                STT(slot(AG + i).bitcast(BF16)[:, 0:NT], T[g1k][:, 0:NT], 1.0 / ALPHA, slot(AV + i)[:, 0:NT], ALU.mult, ALU.mult,
                    r=[g1k, AK(AV + i), cvk], w=[AK(AG + i)])
        if smp:
            DMA("pool", ocv_s, CV0[:], r=[("CV0", i) for i in range(22)])
        elif tok0 + NT == 2064:
            DMA("pool", ocv_p, AGC[:], r=[("AGC", i) for i in range(22)])

        big_out(wdn_v, 22, lambda c: slot(AG + c).bitcast(BF16), (None, [AK(AG + c) for c in range(22)]))
        for ti, (col0, n, _) in enumerate(tiles):
            Tt, tkey = TOK[ti % 2], ("TOK", ti % 2)
            CP("act", Tt[0:n, 0:512], ps[2 * ti][0:n, :], r=[("ps", 2 * ti)], w=[tkey])
            CP("dve", Tt[0:n, 512:1024], ps[2 * ti + 1][0:n, :], r=[("ps", 2 * ti + 1)], w=[tkey])
            ln_stats(Tt, n, tkey, eps=EPS / (ALPHA * ALPHA))
            TT("pool", Tt[0:n, :], Tt[0:n, :], LNG[0:n, :], ALU.mult, r=[tkey, "LNG"], w=[tkey])
            TT("dve", Tt[0:n, :], Tt[0:n, :], LNB[0:n, :], ALU.add, r=[tkey, "LNB"], w=[tkey])
            if smp:
                DMA("pool", oy_s, Tt[0:128, :], r=[tkey])
            else:
                t_lo = tok0 + col0
                if t_lo < 16:
                    DMA("pool", oy_p[0:n - (16 - t_lo), :], Tt[16 - t_lo:n, :], r=[tkey])
                else:
                    DMA("pool", oy_p[t_lo - 16:t_lo - 16 + n, :], Tt[0:n, :], r=[tkey])
        P.barrier()

    P.emit(nc)
    st.close()
    return nc


_NC_CACHE = {}


def _host_inputs(inp, b):
    f = lambda a: np.ascontiguousarray(a, dtype=np.float32)
    s0, s1 = 16 * b, 16 * b + 16
    prm0 = np.concatenate([inp["rwkv_mu"][0], inp["rwkv_w0"][0], inp["rwkv_a0"][0], inp["rwkv_kk_scale"][0], inp["rwkv_ka_scale"][0],
                           inp["rwkv_rk"][0], inp["rwkv_lnx_g"][0], inp["rwkv_lnx_b"][0], inp["mlstm_norm_g"][0],
                           inp["ln_in_g"], inp["ln_in_b"], inp["ln1_g"][0], inp["ln1_b"][0]]).reshape(122, 128)
    cw = inp["ffn_conv_w"][0]
    prm1 = np.concatenate([cw[0], cw[1], cw[2], inp["ffn_conv_b"][0]]).reshape(88, 128)
    sS = inp["state_rwkv_S"][0, s0:s1]
    sH = sS.reshape(16, 8, 2, 64, 64).transpose(0, 2, 4, 1, 3).reshape(16, 128, 8, 64)
    ssh = inp["state_rwkv_shift"][0, s0:s1].reshape(16, 8, 128).transpose(2, 1, 0)
    scv = inp["state_ffn_conv"][0, s0:s1].reshape(16, 2, 22, 128).transpose(3, 2, 0, 1)
    return {
        "xp": f(inp["x_prompt"][b]), "xs": f(inp["x_sample"][s0:s1].reshape(128, D)), "meta": f(inp["meta_tokens"]),
        "sC": f(inp["state_mlstm_C"][0, s0:s1]), "sn": f(inp["state_mlstm_n"][0, s0:s1].transpose(0, 2, 1)),
        "sm": f(inp["state_mlstm_m"][0, s0:s1].T), "sH": f(sH), "ssh": f(ssh), "scv": f(scv),
        "prm0": f(prm0), "prm1": f(prm1), "bif": f(inp["b_if"][0].reshape(8, 1)),
        "ln2g": f(inp["ln2_g"][0]), "ln2b": f(inp["ln2_b"][0]),
        "w_in": f(inp["w_in"][0]), "w2a2": f(np.concatenate([inp["rwkv_w2"][0], inp["rwkv_a2"][0]], 0)), "g2": f(inp["rwkv_g2"][0]),
        "w_out": f(inp["w_out"][0]), "w_up": f(inp["ffn_w_up"][0]), "w_down": f(inp["ffn_w_down"][0]),
    }


def kernel(**inputs):
    inp = {k: np.asarray(v) for k, v in inputs.items()}
    if "nc" not in _NC_CACHE:
        _NC_CACHE["nc"] = build()
    nc = _NC_CACHE["nc"]
    in_maps = [_host_inputs(inp, b) for b in range(8)]
    res = run_bass_kernel_spmd(nc, in_maps, core_ids=list(range(8))).results
    g = lambda k: [np.asarray(r[k], dtype=np.float32) for r in res]
    y_p = np.stack(g("oy_p"), 0)
    y_s = np.concatenate(g("oy_s"), 0).reshape(128, 8, D)
    cn_p = np.stack(g("oCN_p"), 0)
    pC = cn_p[..., 0:256].transpose(0, 2, 1, 3)[None]
    pn = cn_p[..., 256].transpose(0, 2, 1)[None]
    pm = np.stack(g("om_p"), 0)[:, :, 0][None]
    Hp = np.stack(g("oH_p"), 0)
    pS = Hp.reshape(8, 2, 64, 8, 64).transpose(0, 3, 1, 4, 2).reshape(8, 16, 64, 64)[None]
    psh = np.stack(g("osh_p"), 0)[..., 0].transpose(0, 2, 1).reshape(8, D)[None]
    pcv = np.stack(g("ocv_p"), 0).transpose(0, 3, 2, 1).reshape(8, 2, DF)[None]
    cn_s = np.concatenate(g("oCN_s"), 0)
    sC = cn_s[..., 0:256].transpose(0, 2, 1, 3)[None]
    sn = cn_s[..., 256].transpose(0, 2, 1)[None]
    sm = np.concatenate([a.T for a in g("om_s")], 0)[None]
    Hs = np.concatenate(g("oH_s"), 0)
    sS = Hs.reshape(128, 2, 64, 8, 64).transpose(0, 3, 1, 4, 2).reshape(128, 16, 64, 64)[None]
    ssh = np.concatenate([a.transpose(2, 1, 0).reshape(16, D) for a in g("osh_s")], 0)[None]
    scv = np.concatenate([a.transpose(2, 3, 1, 0).reshape(16, 2, DF) for a in g("ocv_s")], 0)[None]
    c = lambda a: np.ascontiguousarray(a, dtype=np.float32)
    return (c(y_p), c(y_s), c(pC), c(pn), c(pm), c(pS), c(psh), c(pcv), c(sC), c(sn), c(sm), c(sS), c(ssh), c(scv))
```

```python
import contextlib
import os
import numpy as np
DBGSTEP = int(os.environ.get("DBGSTEP", "0"))
DBGSUB = int(os.environ.get("DBGSUB", "0"))
DBGCHUNK = int(os.environ.get("DBGCHUNK", "0"))
import concourse.bass as bass
import concourse.mybir as mybir
from concourse.bass_utils import run_bass_kernel_spmd

F32 = mybir.dt.float32
BF16 = mybir.dt.bfloat16
AF = mybir.ActivationFunctionType
ALU = mybir.AluOpType

NS_DMA = 6
ENGS = ("pe", "act", "dve", "pool", "sp")


class Op:
    __slots__ = ("eng", "fn", "deps", "dma", "signal", "val", "slot", "strict")


class Prog:
    def __init__(self):
        self.ops = {e: [] for e in ENGS}
        self.lastw = {}
        self.readers = {}
        self.pend = {e: [] for e in ENGS}
        self.dmaq = {e: [] for e in ENGS}

    def op(self, eng, fn, r=(), w=(), dma=False, strict=False):
        o = Op()
        o.strict = strict
        o.eng, o.fn, o.dma, o.signal, o.val, o.slot = eng, fn, dma, False, 0, 0
        deps = set(self.pend[eng])
        self.pend[eng] = []
        for k in r:
            lw = self.lastw.get(k)
            if lw is not None:
                deps.add(lw)
        for k in w:
            lw = self.lastw.get(k)
            if lw is not None:
                deps.add(lw)
            deps.update(self.readers.get(k, ()))
        if dma:
            q = self.dmaq[eng]
            n = len(q)
            o.slot = n % NS_DMA
            o.val = 16 * (n // NS_DMA + 1)
            if n >= NS_DMA:
                deps.add(q[n - NS_DMA])
            q.append(o)
        o.deps = deps
        for k in r:
            lst = self.readers.setdefault(k, [])
            if not dma:
                lst[:] = [x for x in lst if x.dma or x.eng != eng]
            lst.append(o)
        for k in w:
            self.lastw[k] = o
            self.readers[k] = []
        self.ops[eng].append(o)
        return o

    def barrier(self):
        lasts = []
        for e in ENGS:
            comp = [x for x in self.ops[e] if not x.dma]
            if comp:
                lasts.append(comp[-1])
            lasts.extend(self.dmaq[e][-NS_DMA:])
        for e in ENGS:
            self.pend[e].extend(lasts)
        self.lastw.clear()
        self.readers.clear()

    def emit(self, nc):
        for e in ENGS:
            for o in self.ops[e]:
                for d in o.deps:
                    if not d.dma and not (d.eng == "pe" and e == "pe" and not o.strict):
                        d.signal = True
        for e in ENGS:
            c = 0
            for o in self.ops[e]:
                if not o.dma and o.signal:
                    c += 1
                    o.val = c
        with contextlib.ExitStack() as st:
            csem = {e: st.enter_context(nc.semaphore("c_" + e)) for e in ENGS}
            dsem = {e: [st.enter_context(nc.semaphore("d_%s%d" % (e, i))) for i in range(NS_DMA)]
                    for e in ENGS if self.dmaq[e]}
            block = st.enter_context(nc.Block())

            def semof(o):
                return dsem[o.eng][o.slot] if o.dma else csem[o.eng]

            def run(e, h):
                waited = {}
                for o in self.ops[e]:
                    need = {}
                    for d in o.deps:
                        if d.eng == "pe" and e == "pe" and not d.dma and not o.strict:
                            continue
                        sm = semof(d)
                        if waited.get(sm, 0) < d.val and need.get(sm, 0) < d.val:
                            need[sm] = d.val
                    for sm, v in need.items():
                        h.wait_ge(sm, v)
                        waited[sm] = v
                    ins = o.fn(h)
                    if o.dma:
                        ins.then_inc(semof(o), 16)
                    elif o.signal:
                        ins.then_inc(csem[e], 1)
                for o in self.dmaq[e][-NS_DMA:]:
                    if waited.get(semof(o), 0) < o.val:
                        h.wait_ge(semof(o), o.val)
                        waited[semof(o)] = o.val

            @block.tensor
            def _(h):
                run("pe", h)

            @block.scalar
            def _(h):
                run("act", h)

            @block.vector
            def _(h):
                run("dve", h)

            @block.gpsimd
            def _(h):
                run("pool", h)

            @block.sync
            def _(h):
                run("sp", h)


D = 1024
DF = 2816
W = 274
EPS = 1e-5
GN_EPS = 64e-5
ALPHA = float(2.0 ** 0.25)
C0 = float(np.exp(-0.5))
NEG = -1.0e30
GELU_K = float(2.0 * np.sqrt(2.0 / np.pi))

MERGED = 0
QA, KA, VA, OA, GA = 8, 12, 16, 24, 32
GB, KKT, RB, KB, VB, BHT, BON = 8, 16, 24, 32, 40, 48, 56
AG, AV = 8, 30
NSLOT = 64

PC_MU, PC_W0, PC_A0, PC_KKS, PC_KAS, PC_RK, PC_LXG, PC_LXB, PC_MNG = 0, 26, 34, 42, 50, 58, 66, 74, 82
PC_LIG, PC_LIB, PC_L1G, PC_L1B = 90, 98, 106, 114
PC_CW0, PC_CW1, PC_CW2, PC_CB = 128, 150, 172, 194

BLOCKS = [("p", 0, 272, [(0, 16), (16, 64), (80, 64), (144, 64), (208, 64)], 0)]
for _i in range(1, 8):
    BLOCKS.append(("p", 272 + 256 * (_i - 1), 256, [(64 * c, 64) for c in range(4)], 1))
BLOCKS.append(("s", 0, 128, [(8 * c, 8) for c in range(16)], 2))


def build(blocks=BLOCKS, stop=None):
    nc = bass.Bass("TRN2", target_bir_lowering=False)

    def din(name, shape):
        return nc.dram_tensor(name, list(shape), F32, kind="ExternalInput").ap()

    def dout(name, shape):
        return nc.dram_tensor(name, list(shape), F32, kind="ExternalOutput").ap()

    xp = din("xp", [2048, D]); xs = din("xs", [128, D]); meta = din("meta", [16, D])
    sC = din("sC", [16, 4, 128, 256]); sn = din("sn", [16, 128, 4]); sm = din("sm", [4, 16])
    sH = din("sH", [16, 128, 8, 64]); ssh = din("ssh", [128, 8, 16]); scv = din("scv", [128, 22, 16, 2])
    prm0 = din("prm0", [122, 128]); prm1 = din("prm1", [88, 128])
    bif = din("bif", [8, 1]); ln2g = din("ln2g", [D]); ln2b = din("ln2b", [D])
    w_in = din("w_in", [D, 8456]); w2a2 = din("w2a2", [128, D]); g2 = din("g2", [128, D])
    w_out = din("w_out", [D, D]); w_up = din("w_up", [D, 2 * DF]); w_down = din("w_down", [DF, D])

    oy_p = dout("oy_p", [2048, D]); oy_s = dout("oy_s", [128, D])
    oCN_p = dout("oCN_p", [128, 4, 257]); om_p = dout("om_p", [4, 1]); oH_p = dout("oH_p", [128, 8, 64])
    osh_p = dout("osh_p", [128, 8, 1]); ocv_p = dout("ocv_p", [128, 22, 2])
    oCN_s = dout("oCN_s", [16, 128, 4, 257]); om_s = dout("om_s", [4, 16]); oH_s = dout("oH_s", [16, 128, 8, 64])
    osh_s = dout("osh_s", [128, 8, 16]); ocv_s = dout("ocv_s", [128, 22, 16, 2])

    wb_in = nc.dram_tensor("wb_in", [D, 8456], BF16).ap(); wb_out = nc.dram_tensor("wb_out", [D, D], BF16).ap()
    wb_up = nc.dram_tensor("wb_up", [D, 2 * DF], BF16).ap(); wb_down = nc.dram_tensor("wb_down", [DF, D], BF16).ap()
    win_v = wb_in.rearrange("(kc p) c -> p kc c", p=128)
    wup_v = wb_up.rearrange("(kc p) c -> p kc c", p=128)
    wout_v = wb_out.rearrange("(c p) d -> p c d", p=128)
    wdn_v = wb_down.rearrange("(c p) d -> p c d", p=128)

    P = Prog()
    st = contextlib.ExitStack()

    def sb(name, shape):
        return st.enter_context(nc.sbuf_tensor(name, list(shape), F32))

    ARENA = sb("ARENA", [128, NSLOT * W])
    XT = sb("XT", [128, 8, 272])
    TOK = [sb("TOK0", [128, D]), sb("TOK1", [128, D])]
    LNG = sb("LNG", [128, D]); LNB = sb("LNB", [128, D])
    WB = [st.enter_context(nc.sbuf_tensor("WB%d" % i, [128, 4096], BF16)) for i in range(2)]
    XTb = st.enter_context(nc.sbuf_tensor("XTb", [128, 8, 272], BF16))
    XTlo = st.enter_context(nc.sbuf_tensor("XTlo", [128, 8, 272], BF16))
    PC = sb("PC", [128, 216]); PCN = sb("PCN", [128, 16])
    W2A2 = sb("W2A2", [128, D]); G2 = sb("G2", [128, D])
    ID = sb("ID", [128, 128]); ONES = sb("ONES", [128, 128]); BLK = sb("BLK", [128, 128]); AID = sb("AID", [128, 128])
    MK1 = sb("MK1", [64, 2, 64]); MK1N = sb("MK1N", [64, 2, 64]); MLN = sb("MLN", [64, 64]); MNEG = sb("MNEG", [64, 64])
    SEL = sb("SEL", [4, 4, 128])
    RST01_ = sb("RST01", [128, W])
    RST01 = [RST01_, RST01_, RST01_]
    RSTN = sb("RSTN", [4, W])
    CN = [sb("CN0", [128, 4, 257]), sb("CN1", [128, 4, 257])]
    NBC = st.enter_context(nc.sbuf_tensor("NBC", [128, 4, 128], BF16))
    CNb = st.enter_context(nc.sbuf_tensor("CNb", [128, 4, 257], BF16))
    Hb = [st.enter_context(nc.sbuf_tensor("Hb0", [128, 8, 64], BF16)), st.enter_context(nc.sbuf_tensor("Hb1", [128, 8, 64], BF16))]
    IDb = st.enter_context(nc.sbuf_tensor("IDb", [128, 128], BF16)); ONESb = st.enter_context(nc.sbuf_tensor("ONESb", [128, 128], BF16))
    H = [sb("H0", [128, 8, 64]), sb("H1", [128, 8, 64])]
    CARRY = sb("CARRY", [128, 26]); AGC = sb("AGC", [128, 22, 2])
    CV0 = sb("CV0", [128, 22, 16, 2]);
    XWA = sb("XWA", [128, W]); XG = sb("XG", [128, W])
    GL = sb("GL", [128, 8, 16])
    BI = sb("BI", [4, 1]); NBF = sb("NBF", [4, 1]); MCOL = sb("MCOL", [4, 2]); M0ROW = sb("M0ROW", [4, 16]); MNEW = sb("MNEW", [4, 16])
    ST6 = sb("ST6", [128, 12]); MV = sb("MV", [128, 2]); SD = sb("SD", [128, 1]); RS = sb("RS", [128, 1])
    TN = ["LW", "AA", "KKS", "SQ", "NRM", "KKN", "T1", "BB", "CUM", "EGP", "EGN", "CX", "EGX"]
    T = {n: sb("T_" + n, [128, W]) for n in TN}
    PR0 = T["KKS"][:, 0:128]; PR1 = T["SQ"][:, 0:128]
    SHS = T["EGX"][:, 0:128].rearrange("p (a b) -> p a b", a=8)
    LI = sb("LI", [4, W]); LP = sb("LP", [4, W]); CUMP = sb("CUMP", [4, W]); DD = sb("DD", [4, W]); CM = sb("CM", [4, W])
    MX = sb("MX", [4, 64]); ROWS = sb("ROWS", [4, 3, 64]); TR4 = sb("TR4", [4, 64]); WKR = sb("WKR", [4, 64])
    COLS = sb("COLS", [64, 8])
    WST = [sb("WST%d" % i, [64, 64]) for i in range(4)]; SST = [st.enter_context(nc.sbuf_tensor("SST%d" % i, [64, 64], BF16)) for i in range(4)]
    QW = [st.enter_context(nc.sbuf_tensor("QW%d" % i, [128, 64], BF16)) for i in range(4)]; ADEN = [sb("ADEN%d" % i, [128, 64]) for i in range(4)]
    DDT = [sb("DDT%d" % i, [128, 64]) for i in range(4)]; KW = [st.enter_context(nc.sbuf_tensor("KW%d" % i, [64, 128], BF16)) for i in range(4)]
    DCOL = sb("DCOL", [128, 4])
    VTK = st.enter_context(nc.sbuf_tensor("VTK", [64, 512], BF16)); KHK = st.enter_context(nc.sbuf_tensor("KHK", [64, 512], BF16)); BHK = st.enter_context(nc.sbuf_tensor("BHK", [64, 512], BF16))
    S1raw = st.enter_context(nc.sbuf_tensor("S1raw", [64, 1028], BF16)); S2 = st.enter_context(nc.sbuf_tensor("S2", [64, 8, 2, 64], BF16)); PA = st.enter_context(nc.sbuf_tensor("PA", [64, 8, 64], BF16))
    S1 = S1raw[:, 0:1024].rearrange("p (a b c) -> p a b c", a=8, b=2)
    VTK1 = S1raw[:, 0:1028].rearrange("p (h c) -> p h c", h=4)
    KTK = KHK
    PN = [st.enter_context(nc.sbuf_tensor("PN%d" % i, [64, 8, 64], BF16)) for i in range(2)]; PTN = [st.enter_context(nc.sbuf_tensor("PTN%d" % i, [64, 8, 64], BF16)) for i in range(2)]
    U = [st.enter_context(nc.sbuf_tensor("U0", [64, 512], BF16)), st.enter_context(nc.sbuf_tensor("U1", [64, 512], BF16))]
    TMPH = sb("TMPH", [128, 4, 64])

    def sbs(name, shape):
        return st.enter_context(nc.sbuf_tensor(name, shape, BF16))
    setB = (sbs("VTKs", [16, 512]), sbs("KHKs", [16, 512]), sbs("BHKs", [16, 512]), sbs("S1s", [16, 8, 2, 16]), sbs("S2s", [16, 8, 2, 16]),
            sbs("PAs", [16, 8, 16]), [sbs("PNs%d" % i, [16, 8, 16]) for i in range(2)], [sbs("PTNs%d" % i, [16, 8, 16]) for i in range(2)],
            [sbs("Us%d" % i, [16, 512]) for i in range(2)], sb("TMPHs", [128, 4, 64]), "B")
    ps = [st.enter_context(nc.psum_tensor("ps%d" % i, [128, 512], F32)) for i in range(8)]
    psi = [0]

    def PS():
        i = psi[0]
        psi[0] = (i + 1) % 8
        return ps[i], ("ps", i)

    def slot(i):
        return ARENA[:, i * W:(i + 1) * W]

    def slots(i, n):
        return ARENA[:, i * W:(i + n) * W].rearrange("p (c w) -> p c w", c=n)

    def AK(i):
        return ("A", i)

    BFA = ARENA[:, KKT * W:(KKT + 8) * W].bitcast(BF16)
    BFB = ARENA[:, BHT * W:(BHT + 8) * W].bitcast(BF16)

    def KKTb(j):
        return BFA[:, j * W:(j + 1) * W]

    def RTb(j):
        return BFA[:, (8 + j) * W:(9 + j) * W]

    def KHb(j):
        return BFB[:, j * W:(j + 1) * W]

    def BHb(j):
        return BFB[:, (8 + j) * W:(9 + j) * W]

    def Qb(h):
        return slot(QA + h).bitcast(BF16)

    def Kb(h):
        return slot(KA + h).bitcast(BF16)

    def MM(out, lhsT, rhs, start=True, stop=True, r=(), w=(), strict=False):
        P.op("pe", lambda h: h.matmul(out, lhsT=lhsT, rhs=rhs, start=start, stop=stop), r, w, strict=strict)

    def TR(out, in_, n, r=(), w=(), bf=False):
        idn = IDb[0:n, 0:n] if bf else ID[0:n, 0:n]
        P.op("pe", lambda h: h.transpose(out=out, in_=in_, identity=idn), list(r) + ["ID"], w)

    def ACT(out, in_, func, r=(), w=(), bias=0.0, scale=1.0):
        P.op("act", lambda h: h.activation(out=out, in_=in_, func=func, bias=bias, scale=scale), r, w)

    def TT(eng, out, in0, in1, op, r=(), w=()):
        P.op(eng, lambda h: h.tensor_tensor(out=out, in0=in0, in1=in1, op=op), r, w)

    def TS(eng, out, in0, s1, op0, r=(), w=(), s2=None, op1=None):
        if op1 is None and eng == "pool" and op0 == ALU.mult:
            s2, op1 = 1.0, ALU.mult
        if op1 is None:
            P.op(eng, lambda h: h.tensor_scalar(out=out, in0=in0, scalar1=s1, scalar2=None, op0=op0), r, w)
        else:
            P.op(eng, lambda h: h.tensor_scalar(out=out, in0=in0, scalar1=s1, scalar2=s2, op0=op0, op1=op1), r, w)

    def STT(out, in0, sc, in1, op0, op1, r=(), w=()):
        P.op("dve", lambda h: h.scalar_tensor_tensor(out=out, in0=in0, scalar=sc, in1=in1, op0=op0, op1=op1), r, w)

    def CP(eng, out, in_, r=(), w=()):
        if eng == "act":
            P.op("act", lambda h: h.copy(out=out, in_=in_), r, w)
        else:
            P.op(eng, lambda h: h.tensor_copy(out=out, in_=in_), r, w)

    def RECIP(out, in_, r=(), w=()):
        P.op("dve", lambda h: h.reciprocal(out=out, in_=in_), r, w)

    def MSET(eng, ap, v, w=()):
        P.op(eng, lambda h: h.memset(ap, v), (), w)

    def DMA(q, out, in_, r=(), w=(), slow=False):
        P.op(q, lambda h: h.dma_start(out=out, in_=in_, allow_slow_non_contiguous=slow), r, w, dma=True)

    def PCc(i):
        return PC[:, i:i + 1]

    DMA("sp", PR0[0:122, :], prm0, w=["PR0"])
    DMA("sp", PR1[0:88, :], prm1, w=["PR1"])
    DMA("sp", W2A2[:], w2a2, w=["W2A2"])
    DMA("sp", G2[:], g2, w=["G2"])
    DMA("sp", LNG[:], ln2g.partition_broadcast(128), w=["LNG"])
    DMA("sp", LNB[:], ln2b.partition_broadcast(128), w=["LNB"])
    DMA("sp", BI[:], bif[0:4, :], w=["BI"])
    DMA("sp", NBF[:], bif[4:8, :], w=["NBF"])
    DMA("sp", M0ROW[:], sm, w=["M0ROW"])
    DMA("sp", CV0[:], scv, w=[("CV0", i) for i in range(22)])
    MSET("pool", ONES[:], 1.0, w=["ONES"])
    ZER = T["LW"][:, 0:128]; NEG1 = T["AA"][:, 0:128]
    MSET("pool", ZER, 0.0, w=["ZER"])
    MSET("pool", NEG1, -1.0, w=["NEG1"])
    P.op("pool", lambda h: h.affine_select(out=ID[:], in_=ONES[:], pattern=[[1, 128]], compare_op=ALU.is_equal,
                                           fill=0.0, base=0, channel_multiplier=-1), ["ONES"], ["ID"])
    TS("pool", AID[:], ID[:], ALPHA, ALU.mult, r=["ID"], w=["AID"])
    CP("pool", IDb[:], ID[:], r=["ID"], w=["ID"])
    MSET("pool", ONESb[:], 1.0, w=["ONES"])
    MSET("pool", BLK[:], 0.0, w=["BLK"])
    MSET("pool", BLK[0:64, 0:64], 1.0, w=["BLK"])
    MSET("pool", BLK[64:128, 64:128], 1.0, w=["BLK"])
    P.op("pool", lambda h: h.affine_select(out=MK1[:, 0, :], in_=ONES[0:64, 0:64], pattern=[[1, 64]], compare_op=ALU.is_gt,
                                           fill=0.0, base=0, channel_multiplier=-1), ["ONES"], ["MK1"])
    P.op("pool", lambda h: h.affine_select(out=MK1[:, 1, :], in_=ONES[0:64, 0:64], pattern=[[1, 64]], compare_op=ALU.is_ge,
                                           fill=0.0, base=0, channel_multiplier=-1), ["ONES"], ["MK1"])
    TS("pool", MK1N[:], MK1[:], -1.0, ALU.mult, r=["MK1"], w=["MK1N"])
    P.op("pool", lambda h: h.affine_select(out=MLN[:], in_=NEG1[0:64, 0:64], pattern=[[-1, 64]], compare_op=ALU.is_gt,
                                           fill=0.0, base=0, channel_multiplier=1), ["NEG1"], ["MLN"])
    P.op("pool", lambda h: h.affine_select(out=MNEG[:], in_=ZER[0:64, 0:64], pattern=[[1, 64]], compare_op=ALU.is_ge,
                                           fill=NEG, base=0, channel_multiplier=-1), ["ZER"], ["MNEG"])
    CP("dve", SEL[:], ID[0:4, 0:4].unsqueeze(2).broadcast_to([4, 4, 128]), r=["ID"], w=["SEL"])
    TS("pool", NBF[:], NBF[:], -1.0, ALU.mult, r=["NBF"], w=["NBF"])
    pt, pk = PS()
    TR(pt[:, 0:122], PR0[0:122, :], 122, r=["PR0"], w=[pk])
    TR(pt[:, 128:216], PR1[0:88, :], 88, r=["PR1"], w=[pk])
    CP("dve", PC[:, 0:122], pt[:, 0:122], r=[pk], w=["PC"])
    CP("dve", PC[:, 128:216], pt[:, 128:216], r=[pk], w=["PC"])
    TS("dve", PCN[:], PC[:, PC_W0:PC_W0 + 16], -1.0, ALU.mult, r=["PC"], w=["PC"])
    MSET("pool", CN[0][:], 0.0, w=[("CN", 0, h) for h in range(4)])
    MSET("pool", NBC[:], 0.0, w=[("NBC", h) for h in range(4)])
    MSET("pool", H[0][:], 0.0, w=[("H", 0, 0), ("H", 0, 1)])
    MSET("pool", Hb[0][:], 0.0, w=[("Hb", 0, 0), ("Hb", 0, 1)])
    MSET("pool", CNb[:], 0.0, w=[("CNb", h) for h in range(4)])
    MSET("pool", MCOL[:], 0.0, w=["MC0", "MC1"])
    MSET("pool", CARRY[:], 0.0, w=[("CARRY", c) for c in range(26)])
    MSET("pool", AGC[:], 0.0, w=[("AGC", i) for i in range(22)])
    pcs = []
    for (wsrc, wdst, R, C) in ((w_in, wb_in, D, 8456), (w_out, wb_out, D, D), (w_up, wb_up, D, 2 * DF), (w_down, wb_down, DF, D)):
        for r0 in range(0, R, 128):
            for c0 in range(0, C, 4228):
                pcs.append((wsrc, wdst, r0, c0, min(4228, C - c0)))
    for pi_, (wsrc, wdst, r0, c0, n_) in enumerate(pcs):
        sl = pi_ % 2
        stg = ARENA[:, sl * 4228:sl * 4228 + n_]
        ob = ARENA[:, 8456 + sl * 2114:8456 + (sl + 1) * 2114].bitcast(BF16)[:, 0:n_]
        DMA("sp", stg, wsrc[r0:r0 + 128, c0:c0 + n_], w=[("STG", sl)])
        CP(("act", "dve", "pool")[pi_ % 3], ob, stg, r=[("STG", sl)], w=[("OB", sl)])
        DMA("pool", wdst[r0:r0 + 128, c0:c0 + n_], ob, r=[("OB", sl)])
    P.barrier()

    wbi = [0]

    def nextWB():
        i = wbi[0]
        wbi[0] = 1 - i
        return WB[i], ("WB", i)

    def ln_stats(Tt, n, tkey, eps=EPS):
        P.op("dve", lambda h: h.bn_stats(out=ST6[0:n, 0:6], in_=Tt[0:n, 0:512]), [tkey], ["ST6"])
        P.op("dve", lambda h: h.bn_stats(out=ST6[0:n, 6:12], in_=Tt[0:n, 512:1024]), [tkey], ["ST6"])
        P.op("dve", lambda h: h.bn_aggr(out=MV[0:n, :], in_=ST6[0:n, :]), ["ST6"], ["MV"])
        ACT(SD[0:n, :], MV[0:n, 1:2], AF.Ln, r=["MV"], w=["SD"], bias=eps)
        ACT(RS[0:n, :], SD[0:n, :], AF.Exp, r=["SD"], w=["RS"], scale=-0.5)
        TS("dve", Tt[0:n, :], Tt[0:n, :], MV[0:n, 0:1], ALU.subtract, r=[tkey, "MV", "RS"], w=[tkey], s2=RS[0:n, 0:1], op1=ALU.mult)

    def to_feature_major(Tt, n, tkey, col0, gcol, bcol, banks=None):
        for half in range(2):
            if banks is None:
                pt_, pk_ = PS()
            else:
                pt_, pk_ = ps[banks[half]], ("ps", banks[half])
            for i in range(4):
                kc = half * 4 + i
                TR(pt_[:, i * 128:i * 128 + n], Tt[0:n, kc * 128:(kc + 1) * 128], n, r=[tkey], w=[pk_])
            for i in range(4):
                kc = half * 4 + i
                if i % 2 == 0:
                    ACT(XT[:, kc, col0:col0 + n], pt_[:, i * 128:i * 128 + n], AF.Identity, r=[pk_, "PC"], w=["XT"],
                        bias=PCc(bcol + kc), scale=PCc(gcol + kc))
                else:
                    TS("dve", XT[:, kc, col0:col0 + n], pt_[:, i * 128:i * 128 + n], PCc(gcol + kc), ALU.mult,
                       r=[pk_, "PC"], w=["XT"], s2=PCc(bcol + kc), op1=ALU.add)

    def proj_chunks(wv, col0, nch, ncols, evac):
        i = 0
        while i < nch:
            nb = min(4, nch - i)
            wb, wk = nextWB()
            wbv = wb[:, :].rearrange("p (k c) -> p k c", k=8)
            DMA("sp", wbv[:, :, 0:nb * 128], wv[:, :, col0 + i * 128:col0 + (i + nb) * 128], w=[wk])
            for b in range(nb):
                pt_, pk_ = PS()
                for kc in range(8):
                    MM(pt_[:, 0:ncols], wbv[:, kc, b * 128:(b + 1) * 128], XTb[:, kc, 0:ncols], start=(kc == 0), stop=(kc == 7),
                       r=[wk, "XTb"], w=[pk_])
                evac(i + b, pt_, pk_)
            i += nb

    cur_ty = [-1]
    gchunk = [0]

    for (kind, tok0, NT, chunks, ty) in blocks:
        smp = (kind == "s")
        NTB = NT + (16 if smp else 0)
        if ty != cur_ty[0]:
            cur_ty[0] = ty
            MSET("pool", RST01_[:], 1.0, w=["RST"])
            if ty == 0:
                MSET("pool", RST01_[:, 0:1], 0.0, w=["RST"])
                MSET("pool", RST01_[:, 16:272:64], 0.0, w=["RST"])
            elif ty == 1:
                MSET("pool", RST01_[:, 0:256:64], 0.0, w=["RST"])
            else:
                MSET("pool", RST01_[:, 0:128:8], 0.0, w=["RST"])
        tiles = []
        if smp:
            tiles.append((0, 128, [(0, 128, xs)]))
        else:
            c = 0
            while c < NT:
                n = min(128, NT - c)
                srcs = []
                t_lo, t_hi = tok0 + c, tok0 + c + n
                if t_lo < 16:
                    srcs.append((0, 16 - t_lo, meta[t_lo:16, :]))
                    srcs.append((16 - t_lo, n - (16 - t_lo), xp[0:t_hi - 16, :]))
                else:
                    srcs.append((0, n, xp[t_lo - 16:t_hi - 16, :]))
                tiles.append((c, n, srcs))
                c += n

        for ti, (col0, n, srcs) in enumerate(tiles):
            Tt, tkey = TOK[ti % 2], ("TOK", ti % 2)
            for (r0, nr, ap) in srcs:
                DMA("sp", Tt[r0:r0 + nr, :], ap, w=[tkey])
            ln_stats(Tt, n, tkey)
            to_feature_major(Tt, n, tkey, col0, PC_LIG, PC_LIB)
        if smp:
            DMA("sp", XT[:, :, 128:144], ssh, w=["XT"])
        CP("pool", XTb[:, :, 0:NTB], XT[:, :, 0:NTB], r=["XT"], w=["XTb"])
        TT("dve", XTlo[:, :, 0:NT], XT[:, :, 0:NT], XTb[:, :, 0:NT], ALU.subtract, r=["XT", "XTb"], w=["XTlo"])
        if smp:
            pass
            CP("pool", SHS[:], XT[:, :, 7:128:8], r=["XT"], w=["EGX"])
            DMA("pool", osh_s, SHS[:], r=["EGX"])
        elif tok0 + NT == 2064:
            CP("pool", SHS[:, :, 0:1], XT[:, :, NT - 1:NT], r=["XT"], w=["EGX"])
            DMA("pool", osh_p, SHS[:, :, 0:1], r=["EGX"], slow=True)

        if stop == "0":
            P.barrier()
            continue
        def evA(dst0, kindA):
            def f(i, pt_, pk_):
                d = slot(dst0 + i)[:, 0:NT]
                if dst0 in (QA, KA):
                    d = slot(dst0 + i).bitcast(BF16)[:, 0:NT]
                if kindA == "copy":
                    if i % 2 == 0:
                        CP("act", d, pt_[:, 0:NT], r=[pk_], w=[AK(dst0 + i)])
                    else:
                        CP("dve", d, pt_[:, 0:NT], r=[pk_], w=[AK(dst0 + i)])
                elif kindA == "kscale":
                    ACT(d, pt_[:, 0:NT], AF.Identity, r=[pk_], w=[AK(dst0 + i)], scale=float(128 ** -0.5))
                else:
                    ACT(d, pt_[:, 0:NT], AF.Sigmoid, r=[pk_], w=[AK(dst0 + i)])
            return f

        proj_chunks(win_v, 2048, 4, NT, evA(QA, "copy"))
        proj_chunks(win_v, 2560, 4, NT, evA(KA, "kscale"))
        proj_chunks(win_v, 3072, 8, NT, evA(VA, "copy"))
        proj_chunks(win_v, 4096, 8, NT, evA(OA, "sig"))
        wb, wk = nextWB()
        wbv = wb[:, :].rearrange("p (k c) -> p k c", k=8)
        DMA("sp", wbv[:, :, 0:8], win_v[:, :, 5120:5128], w=[wk])
        pi_, ki_ = PS()
        pf_, kf_ = PS()
        for kc in range(8):
            MM(pi_[0:4, 0:NT], wbv[:, kc, 0:4], XTb[:, kc, 0:NT], start=(kc == 0), stop=(kc == 7), r=[wk, "XTb"], w=[ki_])
        for kc in range(8):
            MM(pf_[0:4, 0:NT], wbv[:, kc, 4:8], XTb[:, kc, 0:NT], start=(kc == 0), stop=(kc == 7), r=[wk, "XTb"], w=[kf_])
        ACT(LI[0:4, 0:NT], pi_[0:4, 0:NT], AF.Identity, r=[ki_, "BI"], w=["LI"], bias=BI[0:4, 0:1])
        ACT(LP[0:4, 0:NT], pf_[0:4, 0:NT], AF.Exp, r=[kf_, "NBF"], w=["LP"], bias=NBF[0:4, 0:1], scale=-1.0)
        ACT(LP[0:4, 0:NT], LP[0:4, 0:NT], AF.Ln, r=["LP"], w=["LP"], bias=1.0)
        P.op("dve", lambda h, NT=NT, ty=ty: h.tensor_tensor_scan(out=CUMP[0:4, 0:NT], data0=RST01[ty][0:4, 0:NT], data1=LP[0:4, 0:NT],
                                                  initial=0.0, op0=ALU.mult, op1=ALU.add), ["LP", "RST"], ["CUMP"])
        TT("dve", DD[0:4, 0:NT], LI[0:4, 0:NT], CUMP[0:4, 0:NT], ALU.add, r=["LI", "CUMP"], w=["DD"])
        TS("pool", RSTN[0:4, 0:NT], RST01[ty][0:4, 0:NT], 1.0, ALU.subtract, r=["RST"], w=["RSN"], s2=1.0e30, op1=ALU.mult)
        P.op("dve", lambda h, NT=NT, ty=ty: h.tensor_tensor_scan(out=CM[0:4, 0:NT], data0=RSTN[0:4, 0:NT], data1=DD[0:4, 0:NT],
                                                  initial=0.0, op0=ALU.add, op1=ALU.max), ["DD", "RSN"], ["CM"])
        proj_chunks(win_v, 0, 8, NT, evA(GA, "sig"))
        for c in range(8):
            TT("pool", slot(OA + c)[:, 0:NT], slot(OA + c)[:, 0:NT], slot(GA + c)[:, 0:NT], ALU.mult, r=[AK(OA + c), AK(GA + c)], w=[AK(OA + c)])

        MSET("pool", VTK1[:, :, 256:257], 1.0, w=["S1"])
        for ci, (c0, L) in enumerate(chunks):
            cs = slice(c0, c0 + L)
            if smp:
                cnb = ci % 2
                m_in, m_in_k = M0ROW[0:4, ci:ci + 1], "M0ROW"
                m_out, m_out_k = MNEW[0:4, ci:ci + 1], "MNEW"
                cnk = [("CN", cnb, h) for h in range(4)]
                DMA("sp", CN[cnb][:, :, 0:256], sC[ci].rearrange("h k v -> k h v"), w=cnk)
                DMA("sp", CN[cnb][:, :, 256:257], sn[ci].unsqueeze(2), w=cnk, slow=True)
                CP("pool", NBC[:], CN[cnb][:, :, 256:257].broadcast_to([128, 4, 128]), r=cnk, w=[("NBC", h) for h in range(4)])
                CP("act", CNb[:], CN[cnb][:], r=cnk, w=[("CNb", h) for h in range(4)])
            else:
                cnb = 0
                g = gchunk[0]
                gchunk[0] += 1
                m_in, m_in_k = MCOL[0:4, g % 2:g % 2 + 1], "MC%d" % (g % 2)
                m_out, m_out_k = MCOL[0:4, (g + 1) % 2:(g + 1) % 2 + 1], "MC%d" % ((g + 1) % 2)
            CNt = CN[cnb]
            TS("dve", MX[0:4, 0:L], CM[0:4, cs], m_in, ALU.max, r=["CM", m_in_k], w=["MX"])
            TS("dve", ROWS[0:4, 0, 0:L], MX[0:4, 0:L], -1.0, ALU.mult, r=["MX"], w=["ROWS"])
            ACT(ROWS[0:4, 1, 0:L], MX[0:4, 0:L], AF.Exp, r=["MX", m_in_k], w=["ROWS"], bias=m_in, scale=-1.0)
            TT("dve", TR4[0:4, 0:L], CUMP[0:4, cs], MX[0:4, 0:L], ALU.subtract, r=["CUMP", "MX"], w=["TR4"])
            ACT(ROWS[0:4, 2, 0:L], TR4[0:4, 0:L], AF.Exp, r=["TR4"], w=["ROWS"])
            ACT(WKR[0:4, 0:L], DD[0:4, cs], AF.Exp, r=["DD", "ROWS"], w=["WKR"], bias=ROWS[0:4, 0, L - 1:L])
            TT("dve", m_out, MX[0:4, L - 1:L], CUMP[0:4, c0 + L - 1:c0 + L], ALU.subtract, r=["MX", "CUMP"], w=[m_out_k])
            pt_, pk_ = PS()
            TR(pt_[0:L, 0:4], DD[0:4, cs], 4, r=["DD"], w=[pk_])
            TR(pt_[0:L, 4:8], WKR[0:4, 0:L], 4, r=["WKR"], w=[pk_])
            CP("act", COLS[0:L, 0:8], pt_[0:L, 0:8], r=[pk_], w=["COLS"])
            pk1, kk1 = PS()
            pk1b = pk1[:, :].bitcast(BF16)
            for h in range(4):
                TR(pk1b[0:L, h * 128:(h + 1) * 128], Kb(h)[:, cs], 128, r=[AK(KA + h)], w=[kk1], bf=True)
            CP("dve", KTK[0:L, :], pk1b[0:L, 0:512], r=[kk1], w=["KHK"])
            for half in range(2):
                pv, kv = PS()
                for i in range(4):
                    c = half * 4 + i
                    TR(pv[0:L, i * 128:(i + 1) * 128], slot(VA + c)[:, cs], 128, r=[AK(VA + c)], w=[kv])
                CP("act", VTK1[0:L, half * 2:half * 2 + 2, 0:256], pv[0:L, :].rearrange("p (a b) -> p a b", a=2), r=[kv], w=["S1"])
            for h in range(4):
                TS("pool", KW[h][0:L, :], KTK[0:L, h * 128:(h + 1) * 128], COLS[0:L, 4 + h:5 + h], ALU.mult, r=["KHK", "COLS"], w=[("KW", h)])
            PB = [(ps[2 * h], ("ps", 2 * h)) for h in range(4)]
            PN_ = [(ps[2 * h + 1], ("ps", 2 * h + 1)) for h in range(4)]
            for h in range(4):
                pb, kb = PB[h]
                MM(pb[0:L, 0:L], SEL[0:4, h, 0:L], ROWS[0:4, 0, 0:L], start=True, stop=False, r=["SEL", "ROWS"], w=[kb])
                MM(pb[0:L, 0:L], ID[0:L, 0:L], MNEG[0:L, 0:L], start=False, stop=True, r=["ID", "MNEG"], w=[kb])
                MM(pb[:, L:3 * L], SEL[0:4, h, :], ROWS[0:4, 1:3, 0:L], r=["SEL", "ROWS"], w=[kb])
                MM(pb[0:L, 256:256 + L], Kb(h)[:, cs], Qb(h)[:, cs], r=[AK(KA + h), AK(QA + h)], w=[kb])
            for h in range(4):
                pb, kb = PB[h]
                ACT(WST[h][0:L, 0:L], pb[0:L, 0:L], AF.Exp, r=[kb, "COLS"], w=[("WST", h)], bias=COLS[0:L, h:h + 1])
            for h in range(4):
                pb, kb = PB[h]
                TT("dve", SST[h][0:L, 0:L], WST[h][0:L, 0:L], pb[0:L, 256:256 + L], ALU.mult, r=[("WST", h), kb], w=[("SST", h)])
                TT("dve", QW[h][:, 0:L], Qb(h)[:, cs], pb[:, L:2 * L], ALU.mult, r=[AK(QA + h), kb], w=[("QW", h)])
            for h in range(4):
                pn_, kn = PN_[h]
                for vc in range(2):
                    MM(pn_[:, vc * L:(vc + 1) * L], VTK1[0:L, h, vc * 128:(vc + 1) * 128], SST[h][0:L, 0:L], start=True, stop=False,
                       r=["S1", ("SST", h)], w=[kn])
                    MM(pn_[:, vc * L:(vc + 1) * L], CNb[:, h, vc * 128:(vc + 1) * 128], QW[h][:, 0:L], start=False, stop=True,
                       r=[("CNb", h), ("QW", h)], w=[kn])
                MM(pn_[:, 2 * L:3 * L], ONESb[0:L, :], SST[h][0:L, 0:L], start=True, stop=False, r=["ONES", ("SST", h)], w=[kn])
                MM(pn_[:, 2 * L:3 * L], NBC[:, h, :], QW[h][:, 0:L], start=False, stop=True, r=[("NBC", h), ("QW", h)], w=[kn])
                MM(pn_[:, 192:449], KW[h][0:L, :], VTK1[0:L, h, :], r=[("KW", h), "S1"], w=[kn])
            for h in range(4):
                pb, kb = PB[h]
                pn_, kn = PN_[h]
                ACT(ADEN[h][:, 0:L], pn_[:, 2 * L:3 * L], AF.Abs, r=[kn], w=[("ADEN", h)])
                CP("act", DCOL[:, h:h + 1], pb[:, 2 * L - 1:2 * L], r=[kb], w=[("DCOL", h)])
            for h in range(4):
                pb, kb = PB[h]
                pn_, kn = PN_[h]
                TT("dve", DDT[h][:, 0:L], ADEN[h][:, 0:L], pb[:, 2 * L:3 * L], ALU.max, r=[("ADEN", h), kb], w=[("DDT", h)])
                RECIP(DDT[h][:, 0:L], DDT[h][:, 0:L], r=[("DDT", h)], w=[("DDT", h)])
                TT("dve", slots(VA + 2 * h, 2)[:, :, cs], pn_[:, 0:2 * L].rearrange("p (a b) -> p a b", a=2),
                   DDT[h][:, 0:L].unsqueeze(1).broadcast_to([128, 2, L]), ALU.mult,
                   r=[kn, ("DDT", h)], w=[AK(VA + 2 * h), AK(VA + 2 * h + 1)])
                STT(CNt[:, h, :], CNt[:, h, :], DCOL[:, h:h + 1], pn_[:, 192:449], ALU.mult, ALU.add,
                    r=[("CN", cnb, h), kn, ("DCOL", h)], w=[("CN", cnb, h)])
            for h in range(4):
                CP("pool", NBC[:, h, :], CNt[:, h, 256:257].broadcast_to([128, 128]), r=[("CN", cnb, h)], w=[("NBC", h)])
                CP("act", CNb[:, h, :], CNt[:, h, :], r=[("CN", cnb, h)], w=[("CNb", h)])
            if smp:
                DMA("pool", oCN_s[ci], CNt[:], r=[("CN", cnb, h) for h in range(4)])
        if smp:
            DMA("pool", om_s, MNEW[:], r=["MNEW"])
        elif tok0 + NT == 2064:
            DMA("pool", oCN_p, CN[0][:], r=[("CN", 0, h) for h in range(4)])
            gl = gchunk[0] % 2
            DMA("pool", om_p, MCOL[0:4, gl:gl + 1], r=["MC%d" % gl])

        TN3 = ["LW", "AA", "KKS", "SQ", "NRM", "KKN", "T1", "BB", "CUM", "EGP", "EGN", "CX"]
        hs = list(range(4))
        tq = {h: (TN3[3 * h], TN3[3 * h + 1], TN3[3 * h + 2]) for h in hs}
        pmk = {}
        for h in hs:
            c0_, c1_ = VA + 2 * h, VA + 2 * h + 1
            pm, km = PS()
            pmk[h] = (pm, km)
            MM(pm[:, 0:NT], ONES[:, :], slot(c0_)[:, 0:NT], start=True, stop=False, r=["ONES", AK(c0_)], w=[km])
            MM(pm[:, 0:NT], ONES[:, :], slot(c1_)[:, 0:NT], start=False, stop=True, r=["ONES", AK(c1_)], w=[km])
        for h in hs:
            pm, km = pmk[h]
            for c_ in (VA + 2 * h, VA + 2 * h + 1):
                STT(slot(c_)[:, 0:NT], pm[:, 0:NT], -1.0 / 256, slot(c_)[:, 0:NT], ALU.mult, ALU.add, r=[km, AK(c_)], w=[AK(c_)])
        for h in hs:
            c0_, c1_ = VA + 2 * h, VA + 2 * h + 1
            TT("pool", T[tq[h][0]][:, 0:NT], slot(c0_)[:, 0:NT], slot(c0_)[:, 0:NT], ALU.mult, r=[AK(c0_)], w=[tq[h][0]])
            TT("pool", T[tq[h][1]][:, 0:NT], slot(c1_)[:, 0:NT], slot(c1_)[:, 0:NT], ALU.mult, r=[AK(c1_)], w=[tq[h][1]])
        for h in hs:
            pv2, kv2 = PS()
            pmk[h] = (pv2, kv2)
            MM(pv2[:, 0:NT], ONES[:, :], T[tq[h][0]][:, 0:NT], start=True, stop=False, r=["ONES", tq[h][0]], w=[kv2])
            MM(pv2[:, 0:NT], ONES[:, :], T[tq[h][1]][:, 0:NT], start=False, stop=True, r=["ONES", tq[h][1]], w=[kv2])
        for h in hs:
            pv2, kv2 = pmk[h]
            ACT(T[tq[h][2]][:, 0:NT], pv2[:, 0:NT], AF.Ln, r=[kv2], w=[tq[h][2]], bias=EPS, scale=1.0 / 256)
        for h in hs:
            ACT(T[tq[h][2]][:, 0:NT], T[tq[h][2]][:, 0:NT], AF.Exp, r=[tq[h][2]], w=[tq[h][2]], scale=-0.5)
        for h in hs:
            for vc, c_ in enumerate((VA + 2 * h, VA + 2 * h + 1)):
                cc = 2 * h + vc
                STT(slot(c_)[:, 0:NT], slot(c_)[:, 0:NT], PCc(PC_MNG + cc), T[tq[h][2]][:, 0:NT], ALU.mult, ALU.mult, r=[AK(c_), tq[h][2], "PC"], w=[AK(c_)])
        for h in hs:
            for vc, c_ in enumerate((VA + 2 * h, VA + 2 * h + 1)):
                cc = 2 * h + vc
                TT("pool" if vc else "dve", slot(MERGED + cc)[:, 0:NT], slot(c_)[:, 0:NT], slot(OA + cc)[:, 0:NT], ALU.mult, r=[AK(c_), AK(OA + cc)], w=[AK(MERGED + cc)])
        P.barrier()
        if stop == "A":
            continue

        def evGB(i, pt_, pk_):
            ACT(slot(GB + i)[:, 0:NT], pt_[:, 0:NT], AF.Sigmoid, r=[pk_], w=[AK(GB + i)])

        proj_chunks(win_v, 1024, 8, NT, evGB)

        def evPB(cc, pt_, pk_):
            if cc < 8:
                dst, dk = slot(RB + cc), AK(RB + cc)
            elif cc < 16:
                dst, dk = slot(KB + cc - 8), AK(KB + cc - 8)
            elif cc < 24:
                dst, dk = slot(VB + cc - 16), AK(VB + cc - 16)
            elif cc == 24:
                dst, dk = XWA, "XWA"
            else:
                dst, dk = XG, "XG"
            if cc % 2 == 0:
                CP("act", dst[:, 0:NTB], pt_[:, 0:NTB], r=[pk_], w=[dk])
            else:
                CP("dve", dst[:, 0:NTB], pt_[:, 0:NTB], r=[pk_], w=[dk])
            ds, dsk = (T["CX"], "CX") if cc % 2 == 0 else (T["EGX"], "EGX")
            if smp:
                d3 = dst[:, 0:128].rearrange("p (s t) -> p s t", t=8)
                s3 = ds[:, 0:128].rearrange("p (s t) -> p s t", t=8)
                TT("pool", s3[:, :, 1:8], d3[:, :, 0:7], d3[:, :, 1:8], ALU.subtract, r=[dk], w=[dsk])
                TT("pool", s3[:, :, 0:1], dst[:, 128:144].unsqueeze(2), d3[:, :, 0:1], ALU.subtract, r=[dk], w=[dsk])
            else:
                TT("pool", ds[:, 1:NT], dst[:, 0:NT - 1], dst[:, 1:NT], ALU.subtract, r=[dk], w=[dsk])
                TT("pool", ds[:, 0:1], CARRY[:, cc:cc + 1], dst[:, 0:1], ALU.subtract, r=[dk, ("CARRY", cc)], w=[dsk])
                CP("pool", CARRY[:, cc:cc + 1], dst[:, NT - 1:NT], r=[dk], w=[("CARRY", cc)])
            STT(dst[:, 0:NT], ds[:, 0:NT], PCc(PC_MU + cc), dst[:, 0:NT], ALU.mult, ALU.add, r=[dsk, dk, "PC"], w=[dk])

        proj_chunks(win_v, 5128, 26, NTB, evPB)
        if stop == "B1":
            P.barrier()
            continue
        ACT(XWA[0:64, 0:NT], XWA[0:64, 0:NT], AF.Tanh, r=["XWA"], w=["XWA"])
        ACT(XG[:, 0:NT], XG[:, 0:NT], AF.Sigmoid, r=["XG"], w=["XG"])

        nchunks = len(chunks)
        last0, Lc = chunks[-1][0] + chunks[-1][1], chunks[-1][1]
        first_last = chunks[0][0] + chunks[0][1] - 1
        for j in range(8):
            Rj, Kj, Vj = slot(RB + j)[:, 0:NT], slot(KB + j)[:, 0:NT], slot(VB + j)[:, 0:NT]
            rk_, kk_, vk_ = AK(RB + j), AK(KB + j), AK(VB + j)
            cols = slice(j * 128, (j + 1) * 128)

            def t(n):
                return T[n][:, 0:NT]
            pw, kw = PS()
            MM(pw[:, 0:NT], W2A2[0:64, cols], XWA[0:64, 0:NT], r=["W2A2", "XWA"], w=[kw])
            pa, ka = PS()
            MM(pa[:, 0:NT], W2A2[64:128, cols], XWA[64:128, 0:NT], r=["W2A2", "XWA"], w=[ka])
            ACT(t("LW"), pw[:, 0:NT], AF.Exp, r=[kw, "PC"], w=["LW"], bias=PCN[:, j:j + 1], scale=-1.0)
            ACT(t("AA"), pa[:, 0:NT], AF.Exp, r=[ka, "PC"], w=["AA"], bias=PCN[:, 8 + j:9 + j], scale=-1.0)
            ACT(t("LW"), t("LW"), AF.Ln, r=["LW"], w=["LW"], bias=1.0)
            ACT(t("AA"), t("AA"), AF.Ln, r=["AA"], w=["AA"], bias=1.0)
            ACT(t("LW"), t("LW"), AF.Exp, r=["LW"], w=["LW"], scale=-1.0)
            ACT(t("AA"), t("AA"), AF.Exp, r=["AA"], w=["AA"], scale=-1.0)
            TS("pool", t("KKS"), Kj, PCc(PC_KKS + j), ALU.mult, r=[kk_, "PC"], w=["KKS"])
            TT("pool", t("SQ"), t("KKS"), t("KKS"), ALU.mult, r=["KKS"], w=["SQ"])
            pn2, kn2 = PS()
            MM(pn2[:, 0:NT], BLK[:, :], t("SQ"), r=["BLK", "SQ"], w=[kn2])
            P.op("dve", lambda h, NT=NT, ty=ty: h.tensor_tensor_scan(out=T["CUM"][:, 0:NT], data0=RST01[ty][:, 0:NT], data1=T["LW"][:, 0:NT],
                                                         initial=0.0, op0=ALU.mult, op1=ALU.add), ["LW", "RST"], ["CUM"])
            TS("dve", t("NRM"), pn2[:, 0:NT], 1e-24, ALU.max, r=[kn2], w=["NRM"])
            ACT(t("NRM"), t("NRM"), AF.Ln, r=["NRM"], w=["NRM"])
            ACT(t("NRM"), t("NRM"), AF.Exp, r=["NRM"], w=["NRM"], scale=-0.5)
            ACT(t("EGP"), t("CUM"), AF.Exp, r=["CUM"], w=["EGP"], scale=-C0)
            ACT(t("EGN"), t("CUM"), AF.Exp, r=["CUM"], w=["EGN"], scale=C0)
            TT("pool", t("CX"), t("CUM"), t("LW"), ALU.subtract, r=["CUM", "LW"], w=["CX"])
            ACT(t("EGX"), t("CX"), AF.Exp, r=["CX"], w=["EGX"], scale=-C0)
            TT("dve", t("KKN"), t("KKS"), t("NRM"), ALU.mult, r=["KKS", "NRM"], w=["KKN"])
            TS("dve", t("T1"), t("AA"), 1.0, ALU.subtract, r=["AA", "PC"], w=["T1"], s2=PCc(PC_KAS + j), op1=ALU.mult)
            STT(Kj, t("T1"), 1.0, Kj, ALU.add, ALU.mult, r=["T1", kk_], w=[kk_])
            TT("pool", t("BB"), t("KKN"), t("AA"), ALU.mult, r=["KKN", "AA"], w=["BB"])
            CP("pool", GL[:, j, 0:nchunks], T["EGP"][:, first_last:last0:Lc] if nchunks > 1 else T["EGP"][:, first_last:first_last + 1],
               r=["EGP"], w=[("GL", j)])
            bon = slot(BON + j)[:, 0:NT]
            STT(bon, Rj, PCc(PC_RK + j), Kj, ALU.mult, ALU.mult, r=[rk_, kk_, "PC"], w=[AK(BON + j)])
            pb2, kb2 = PS()
            MM(pb2[:, 0:NT], BLK[:, :], bon, r=["BLK", AK(BON + j)], w=[kb2])
            TT("dve", RTb(j)[:, 0:NT], Rj, t("EGP"), ALU.mult, r=[rk_, "EGP"], w=[("RTb", j)])
            TT("dve", KHb(j)[:, 0:NT], Kj, t("EGN"), ALU.mult, r=[kk_, "EGN"], w=[("KHb", j)])
            TT("pool", KKTb(j)[:, 0:NT], t("KKN"), t("EGX"), ALU.mult, r=["KKN", "EGX"], w=[AK(KKT + j)])
            TT("pool", BHb(j)[:, 0:NT], t("BB"), t("EGN"), ALU.mult, r=["BB", "EGN"], w=[AK(BHT + j)])
            TT("dve", bon, pb2[:, 0:NT], Vj, ALU.mult, r=[kb2, vk_], w=[AK(BON + j)])
        if stop == "B2":
            P.barrier()
            continue

        def QQ(j, rows, cs):
            return ARENA[:, KKT * W:(KKT + 16) * W].bitcast(BF16)[:, j * W:j * W + 16 * W].rearrange("p (two d) -> p two d", two=2)[rows, :, cs]

        for ci, (c0, L) in enumerate(chunks):
            cs = slice(c0, c0 + L)
            nl = {8: 3, 16: 4, 64: 6}[L]
            if DBGSTEP and ci < DBGCHUNK:
                continue
            hb = ci % 2 if smp else 0
            Ht = H[hb]
            if smp:
                DMA("sp", Ht[:], sH[ci], w=[("H", hb, 0), ("H", hb, 1)])
                CP("act", Hb[hb][:], Ht[:], r=[("H", hb, 0), ("H", hb, 1)], w=[("Hb", hb, 0), ("Hb", hb, 1)])
            def half_steps(jh, TSet):
                VTK, KHK, BHK, S1, S2, PA, PN, PTN, U, TMPH, kp = TSet
                hk = ("H", hb, jh)
                hbk = ("Hb", hb, jh)
                Hbt = Hb[hb]
                pA, kA = PS(); pB, kB = PS(); pC, kC = PS()
                pBb = pB[:, :].bitcast(BF16); pCb = pC[:, :].bitcast(BF16)
                for jj in range(4):
                    j = 4 * jh + jj
                    TR(pA[0:L, jj * 128:(jj + 1) * 128], slot(VB + j)[:, cs], 128, r=[AK(VB + j)], w=[kA])
                    TR(pBb[0:L, jj * 128:(jj + 1) * 128], KHb(j)[:, cs], 128, r=[("KHb", j)], w=[kB], bf=True)
                    TR(pCb[0:L, jj * 128:(jj + 1) * 128], BHb(j)[:, cs], 128, r=[AK(BHT + j)], w=[kC], bf=True)
                CP("act", VTK[0:L, :], pA[0:L, :], r=[kA], w=[(kp, "VTK")])
                CP("dve", KHK[0:L, :], pBb[0:L, 0:512], r=[kB], w=[(kp, "KHK")])
                ACT(BHK[0:L, :], pCb[0:L, 0:512], AF.Identity, r=[kC], w=[(kp, "BHK")], scale=-1.0)

                yield
                def hd(hq):
                    hp, jj = divmod(hq, 4)
                    return 4 * jh + jj, jj, hp, slice(64 * hp, 64 * hp + 64), slice(jj * 128 + hp * 64, jj * 128 + hp * 64 + 64)
                b1 = [PS(), PS()]
                for hq in range(8):
                    j, jj, hp, rows, tc = hd(hq)
                    MM(b1[hp][0][0:L, jj * 2 * L:(jj + 1) * 2 * L], KHb(j)[rows, cs], QQ(j, rows, cs),
                       r=[("KHb", j), AK(KKT + j), ("RTb", j)], w=[b1[hp][1]])
                yield
                for hp in range(2):
                    mk = MK1[0:L, :, 0:L].unsqueeze(1).broadcast_to([L, 4, 2, L])
                    TT("dve", S1[0:L, 4 * hp:4 * hp + 4, :, 0:L],
                       b1[hp][0][0:L, 0:8 * L].rearrange("p (a b c) -> p a b c", a=4, b=2), mk, ALU.mult,
                       r=[b1[hp][1], "MK1"], w=[(kp, "S1")])
                yield
                b2 = [PS(), PS()]
                for hq in range(8):
                    j, jj, hp, rows, tc = hd(hq)
                    MM(b2[hp][0][0:L, jj * 2 * L:(jj + 1) * 2 * L], BHb(j)[rows, cs], QQ(j, rows, cs),
                       r=[AK(BHT + j), AK(KKT + j), ("RTb", j)], w=[b2[hp][1]])
                for hp in range(2):
                    mkn = MK1N[0:L, :, 0:L].unsqueeze(1).broadcast_to([L, 4, 2, L])
                    TT("dve", S2[0:L, 4 * hp:4 * hp + 4, :, 0:L],
                       b2[hp][0][0:L, 0:8 * L].rearrange("p (a b c) -> p a b c", a=4, b=2), mkn, ALU.mult,
                       r=[b2[hp][1], "MK1N"], w=[(kp, "S2")])
                yield
                p3 = [PS(), PS()]
                for hq in range(8):
                    j, jj, hp, rows, tc = hd(hq)
                    MM(p3[hp][0][0:L, jj * L:(jj + 1) * L], KKTb(j)[rows, cs], BHb(j)[rows, cs],
                       r=[AK(KKT + j), AK(BHT + j)], w=[p3[hp][1]])
                for hp in range(2):
                    TT("dve", PA[0:L, 4 * hp:4 * hp + 4, 0:L], p3[hp][0][0:L, 0:4 * L].rearrange("p (a b) -> p a b", a=4),
                       MLN[0:L, 0:L].unsqueeze(1).broadcast_to([L, 4, L]), ALU.mult, r=[p3[hp][1], "MLN"], w=[(kp, "PA")])
                yield
                pU2 = [PS(), PS()]
                for hq in range(8):
                    j, jj, hp, rows, tc = hd(hq)
                    MM(pU2[hp][0][0:L, jj * 64:(jj + 1) * 64], KKTb(j)[rows, cs], Hbt[rows, j, :], start=(jj == 0), stop=False,
                       r=[AK(KKT + j), hbk], w=[pU2[hp][1]])
                for hq in range(8):
                    j, jj, hp, rows, tc = hd(hq)
                    MM(pU2[hp][0][0:L, jj * 64:(jj + 1) * 64], S1[0:L, hq, 0, 0:L], VTK[0:L, tc], start=False, stop=(jj == 3),
                       r=[(kp, "S1"), (kp, "VTK")], w=[pU2[hp][1]], strict=(hp == 1 and jj == 0))
                CP("act", U[0][0:L, 0:256], pU2[0][0][0:L, 0:256], r=[pU2[0][1]], w=[(kp, "U", 0)])
                CP("act", U[0][0:L, 256:512], pU2[1][0][0:L, 0:256], r=[pU2[1][1]], w=[(kp, "U", 0)])
                yield
                cur = 0
                Pt, Pk = PA, (kp, "PA")
                PTt, PTk = S2, (kp, "S2")

                def PTv(hq):
                    return PTt[0:L, hq, 0, 0:L] if PTk == (kp, "S2") else PTt[0:L, hq, 0:L]
                for l in range(nl):
                    pU, kU = PS()
                    for hq in range(8):
                        MM(pU[0:L, hq * 64:(hq + 1) * 64], PTv(hq), U[cur][0:L, hq * 64:(hq + 1) * 64], r=[PTk, (kp, "U", cur)], w=[kU])
                    TT("dve", U[1 - cur][0:L, :], U[cur][0:L, :], pU[0:L, :], ALU.add, r=[(kp, "U", cur), kU], w=[(kp, "U", 1 - cur)])
                    cur = 1 - cur
                    if l < nl - 1:
                        pP, kP = PS(); pT, kT = PS()
                        for hq in range(8):
                            MM(pP[0:L, hq * L:(hq + 1) * L], PTv(hq), Pt[0:L, hq, 0:L], r=[PTk, Pk], w=[kP])
                        for hq in range(8):
                            MM(pT[0:L, hq * L:(hq + 1) * L], Pt[0:L, hq, 0:L], PTv(hq), r=[PTk, Pk], w=[kT])
                        nP, nPT = PN[l % 2], PTN[l % 2]
                        CP("act", nP[0:L, :, 0:L], pP[0:L, 0:8 * L].rearrange("p (a b) -> p a b", a=8), r=[kP], w=[(kp, "PN", l % 2)])
                        CP("dve", nPT[0:L, :, 0:L], pT[0:L, 0:8 * L].rearrange("p (a b) -> p a b", a=8), r=[kT], w=[(kp, "PTN", l % 2)])
                        Pt, Pk = nP, (kp, "PN", l % 2)
                        PTt, PTk = nPT, (kp, "PTN", l % 2)
                    yield
                yield
                pY2 = [PS(), PS()]
                for hq in range(8):
                    j, jj, hp, rows, tc = hd(hq)
                    o_ = pY2[hp][0][rows, jj * L:(jj + 1) * L]
                    MM(o_, Hbt[rows, j, :], RTb(j)[rows, cs], start=(jj == 0), stop=False, r=[hbk, ("RTb", j)], w=[pY2[hp][1]])
                for hq in range(8):
                    j, jj, hp, rows, tc = hd(hq)
                    o_ = pY2[hp][0][rows, jj * L:(jj + 1) * L]
                    MM(o_, VTK[0:L, tc], S1[0:L, hq, 1, 0:L], start=False, stop=False, r=[(kp, "VTK"), (kp, "S1")], w=[pY2[hp][1]], strict=(hp == 1 and jj == 0))
                    MM(o_, U[cur][0:L, hq * 64:(hq + 1) * 64], S2[0:L, hq, 1, 0:L], start=False, stop=(jj == 3), r=[(kp, "U", cur), (kp, "S2")], w=[pY2[hp][1]])
                pH, kH = PS()
                for hq in range(8):
                    j, jj, hp, rows, tc = hd(hq)
                    o_ = pH[rows, jj * 64:(jj + 1) * 64]
                    MM(o_, KHK[0:L, tc], VTK[0:L, tc], start=True, stop=False, r=[(kp, "KHK"), (kp, "VTK")], w=[kH])
                    MM(o_, BHK[0:L, tc], U[cur][0:L, hq * 64:(hq + 1) * 64], start=False, stop=True, r=[(kp, "BHK"), (kp, "U", cur)], w=[kH])
                yield
                for hp in range(2):
                    rows = slice(64 * hp, 64 * hp + 64)
                    CP("act", slots(RB + 4 * jh, 4)[rows, :, cs], pY2[hp][0][rows, 0:4 * L].rearrange("p (a b) -> p a b", a=4), r=[pY2[hp][1]],
                       w=[AK(RB + 4 * jh + q) for q in range(4)])
                TT("dve", TMPH[:], Ht[:, 4 * jh:4 * jh + 4, :], pH[:, 0:256].rearrange("p (a b) -> p a b", a=4), ALU.add, r=[hk, kH], w=[(kp, "TMPH")])
                TT("pool", Ht[:, 4 * jh:4 * jh + 4, :], TMPH[:], GL[:, 4 * jh:4 * jh + 4, ci:ci + 1].broadcast_to([128, 4, 64]), ALU.mult,
                   r=[(kp, "TMPH")] + [("GL", 4 * jh + q) for q in range(4)], w=[hk])
                CP("act", Hbt[:, 4 * jh:4 * jh + 4, :], Ht[:, 4 * jh:4 * jh + 4, :], r=[hk], w=[hbk])

            setA = (VTK, KHK, BHK, S1, S2, PA, PN, PTN, U, TMPH, "A")
            if smp:
                gens = [half_steps(0, setA), half_steps(1, setB)]
                while gens:
                    for g_ in list(gens):
                        try:
                            next(g_)
                        except StopIteration:
                            gens.remove(g_)
            else:
                for jh in range(2):
                    for _ in half_steps(jh, setA):
                        pass
            if smp:
                DMA("pool", oH_s[ci], Ht[:], r=[("H", hb, 0), ("H", hb, 1)])
            if DBGSTEP and ci >= DBGCHUNK:
                break
        if (not smp) and tok0 + NT == 2064:
            DMA("pool", oH_p, H[0][:], r=[("H", 0, 0), ("H", 0, 1)])

        if stop == "B3":
            P.barrier()
            continue
        TN3 = ["LW", "AA", "KKS", "SQ", "NRM", "KKN", "T1", "BB", "CUM", "EGP", "EGN", "CX"]
        for g0 in (0, 4):
            js = list(range(g0, g0 + 4))
            tq = {j: (TN3[3 * (j - g0)], TN3[3 * (j - g0) + 1], TN3[3 * (j - g0) + 2]) for j in js}
            pk_ = {}
            for j in js:
                pm, km = PS()
                pk_[j] = (pm, km)
                MM(pm[:, 0:NT], BLK[:, :], slot(RB + j)[:, 0:NT], r=["BLK", AK(RB + j)], w=[km])
            for j in js:
                pm, km = pk_[j]
                Yj, yk = slot(RB + j)[:, 0:NT], AK(RB + j)
                STT(Yj, pm[:, 0:NT], -1.0 / 64, Yj, ALU.mult, ALU.add, r=[km, yk], w=[yk])
            for j in js:
                Yj, yk = slot(RB + j)[:, 0:NT], AK(RB + j)
                TT("pool", T[tq[j][0]][:, 0:NT], Yj, Yj, ALU.mult, r=[yk], w=[tq[j][0]])
            for j in js:
                pv2, kv2 = PS()
                pk_[j] = (pv2, kv2)
                MM(pv2[:, 0:NT], BLK[:, :], T[tq[j][0]][:, 0:NT], r=["BLK", tq[j][0]], w=[kv2])
            for j in js:
                pv2, kv2 = pk_[j]
                ACT(T[tq[j][1]][:, 0:NT], pv2[:, 0:NT], AF.Ln, r=[kv2], w=[tq[j][1]], bias=GN_EPS, scale=1.0 / 64)
            for j in js:
                ACT(T[tq[j][1]][:, 0:NT], T[tq[j][1]][:, 0:NT], AF.Exp, r=[tq[j][1]], w=[tq[j][1]], scale=-0.5)
            for j in js:
                pg, kg = PS()
                pk_[j] = (pg, kg)
                MM(pg[:, 0:NT], G2[:, j * 128:(j + 1) * 128], XG[:, 0:NT], r=["G2", "XG"], w=[kg])
            for j in js:
                Yj, yk = slot(RB + j)[:, 0:NT], AK(RB + j)
                t1 = T[tq[j][2]][:, 0:NT]
                STT(t1, Yj, PCc(PC_LXG + j), T[tq[j][1]][:, 0:NT], ALU.mult, ALU.mult, r=[yk, tq[j][1], "PC"], w=[tq[j][2]])
                STT(t1, t1, PCc(PC_LXB + j), slot(BON + j)[:, 0:NT], ALU.add, ALU.add, r=[tq[j][2], AK(BON + j), "PC"], w=[tq[j][2]])
            for j in js:
                pg, kg = pk_[j]
                t1 = T[tq[j][2]][:, 0:NT]
                TT("dve", t1, t1, pg[:, 0:NT], ALU.mult, r=[tq[j][2], kg], w=[tq[j][2]])
            for j in js:
                t1 = T[tq[j][2]][:, 0:NT]
                TT("pool", t1, t1, slot(GB + j)[:, 0:NT], ALU.mult, r=[tq[j][2], AK(GB + j)], w=[tq[j][2]])
                TT("pool", slot(MERGED + j)[:, 0:NT], slot(MERGED + j)[:, 0:NT], t1, ALU.add, r=[tq[j][2], AK(MERGED + j)], w=[AK(MERGED + j)])
        P.barrier()
        if stop == "B":
            continue

        def big_out(wv, nrowch, lhs_of, resid):
            for cp_ in range((nrowch + 3) // 4):
                wb, wk = nextWB()
                wbv2 = wb[:, :].rearrange("p (c d) -> p c d", c=4)
                ncc = min(4, nrowch - 4 * cp_)
                DMA("sp", wbv2[:, 0:ncc, :], wv[:, 4 * cp_:4 * cp_ + ncc, :], w=[wk])
                for ci_ in range(ncc):
                    c = 4 * cp_ + ci_
                    for ti, (col0, n, _) in enumerate(tiles):
                        for half in range(2):
                            MM(ps[2 * ti + half][0:n, 0:512], lhs_of(c)[:, col0:col0 + n], wbv2[:, ci_, half * 512:(half + 1) * 512],
                               start=(c == 0), stop=False, r=[wk] + resid[1], w=[("ps", 2 * ti + half)])
            for ti, (col0, n, _) in enumerate(tiles):
                for c in range(8):
                    o_ = ps[2 * ti + c // 4][0:n, (c % 4) * 128:(c % 4 + 1) * 128]
                    MM(o_, XTb[:, c, col0:col0 + n], IDb[:, :], start=False, stop=False, r=["XTb", "ID"], w=[("ps", 2 * ti + c // 4)])
                    MM(o_, XTlo[:, c, col0:col0 + n], IDb[:, :], start=False, stop=(c % 4 == 3), r=["XTlo", "ID"], w=[("ps", 2 * ti + c // 4)])

        MRGb = ARENA[:, 52 * W:56 * W].bitcast(BF16).rearrange("p (c w) -> p c w", c=8)
        TS("pool", MRGb[:, :, 0:NT], slots(MERGED, 8)[:, :, 0:NT], 1.0 / ALPHA, ALU.mult, r=[AK(MERGED + c) for c in range(8)], w=["MRGb"])
        big_out(wout_v, 8, lambda c: MRGb[:, c, :], (None, ["MRGb"]))
        for ti, (col0, n, _) in enumerate(tiles):
            Tt, tkey = TOK[ti % 2], ("TOK", ti % 2)
            CP("act", Tt[0:n, 0:512], ps[2 * ti][0:n, :], r=[("ps", 2 * ti)], w=[tkey])
            CP("dve", Tt[0:n, 512:1024], ps[2 * ti + 1][0:n, :], r=[("ps", 2 * ti + 1)], w=[tkey])
            ln_stats(Tt, n, tkey, eps=EPS / (ALPHA * ALPHA))
            to_feature_major(Tt, n, tkey, col0, PC_L1G, PC_L1B, banks=(6, 7))
        CP("pool", XTb[:, :, 0:NT], XT[:, :, 0:NT], r=["XT"], w=["XTb"])
        TT("dve", XTlo[:, :, 0:NT], XT[:, :, 0:NT], XTb[:, :, 0:NT], ALU.subtract, r=["XT", "XTb"], w=["XTlo"])

        def evAG(i, pt_, pk_):
            if smp:
                d = slot(AG + i)[:, 0:160].rearrange("p (s t) -> p s t", t=10)[:, :, 2:10]
                CP("act", d, pt_[:, 0:128].rearrange("p (s t) -> p s t", t=8), r=[pk_], w=[AK(AG + i)])
            else:
                CP("act", slot(AG + i)[:, 2:2 + NT], pt_[:, 0:NT], r=[pk_], w=[AK(AG + i)])

        def evAV(i, pt_, pk_):
            CP("dve", slot(AV + i)[:, 0:NT], pt_[:, 0:NT], r=[pk_], w=[AK(AV + i)])

        proj_chunks(wup_v, 0, 22, NT, evAG)
        proj_chunks(wup_v, DF, 22, NT, evAV)
        _tn = ["LW", "AA", "KKS", "SQ", "NRM", "KKN", "T1", "BB", "CUM", "EGP", "EGN", "CX"]
        for g0 in range(0, 22, 6):
            idx = list(range(g0, min(g0 + 6, 22)))
            tk = {i: (_tn[2 * (i - g0)], _tn[2 * (i - g0) + 1]) for i in idx}
            for i in idx:
                ag, agk = slot(AG + i), AK(AG + i)
                if smp:
                    a3 = ag[:, 0:160].rearrange("p (s t) -> p s t", t=10)
                    CP("pool", a3[:, :, 0:2], CV0[:, i, :, :], r=[("CV0", i)], w=[agk])
                    CP("pool", CV0[:, i, :, :], a3[:, :, 8:10], r=[agk], w=[("CV0", i)])
                else:
                    CP("pool", ag[:, 0:2], AGC[:, i, :], r=[("AGC", i)], w=[agk])
                    CP("pool", AGC[:, i, :], ag[:, NT:NT + 2], r=[agk], w=[("AGC", i)])
            for i in idx:
                ag, agk = slot(AG + i), AK(AG + i)
                cvk, g1k = tk[i]
                cv = T[cvk]
                if smp:
                    a3 = ag[:, 0:160].rearrange("p (s t) -> p s t", t=10)
                    cv3 = cv[:, 0:128].rearrange("p (s t) -> p s t", t=8)
                    TS("dve", cv3, a3[:, :, 0:8], PCc(PC_CW0 + i), ALU.mult, r=[agk, "PC"], w=[cvk], s2=PCc(PC_CB + i), op1=ALU.add)
                    STT(cv3, a3[:, :, 1:9], PCc(PC_CW1 + i), cv3, ALU.mult, ALU.add, r=[agk, cvk, "PC"], w=[cvk])
                    STT(cv3, a3[:, :, 2:10], PCc(PC_CW2 + i), cv3, ALU.mult, ALU.add, r=[agk, cvk, "PC"], w=[cvk])
                else:
                    ACT(cv[:, 0:NT], ag[:, 0:NT], AF.Identity, r=[agk, "PC"], w=[cvk], bias=PCc(PC_CB + i), scale=PCc(PC_CW0 + i))
                    STT(cv[:, 0:NT], ag[:, 1:NT + 1], PCc(PC_CW1 + i), cv[:, 0:NT], ALU.mult, ALU.add, r=[agk, cvk, "PC"], w=[cvk])
                    STT(cv[:, 0:NT], ag[:, 2:NT + 2], PCc(PC_CW2 + i), cv[:, 0:NT], ALU.mult, ALU.add, r=[agk, cvk, "PC"], w=[cvk])
            for i in idx:
                cvk, g1k = tk[i]
                ACT(T[g1k][:, 0:NT], T[cvk][:, 0:NT], AF.Square, r=[cvk], w=[g1k])
            for i in idx:
                cvk, g1k = tk[i]
                TS("dve", T[g1k][:, 0:NT], T[g1k][:, 0:NT], 0.044715, ALU.mult, r=[g1k], w=[g1k], s2=1.0, op1=ALU.add)
            for i in idx:
                cvk, g1k = tk[i]
                TT("pool", T[g1k][:, 0:NT], T[g1k][:, 0:NT], T[cvk][:, 0:NT], ALU.mult, r=[g1k, cvk], w=[g1k])
            for i in idx:
                cvk, g1k = tk[i]
                ACT(T[g1k][:, 0:NT], T[g1k][:, 0:NT], AF.Sigmoid, r=[g1k], w=[g1k], scale=GELU_K)
            for i in idx:
                cvk, g1k = tk[i]
                TT("pool", T[g1k][:, 0:NT], T[g1k][:, 0:NT], T[cvk][:, 0:NT], ALU.mult, r=[g1k, cvk], w=[g1k])
            for i in idx:
                cvk, g1k = tk[i]
                STT(slot(AG + i).bitcast(BF16)[:, 0:NT], T[g1k][:, 0:NT], 1.0 / ALPHA, slot(AV + i)[:, 0:NT], ALU.mult, ALU.mult,
                    r=[g1k, AK(AV + i), cvk], w=[AK(AG + i)])
        if smp:
            DMA("pool", ocv_s, CV0[:], r=[("CV0", i) for i in range(22)])
        elif tok0 + NT == 2064:
            DMA("pool", ocv_p, AGC[:], r=[("AGC", i) for i in range(22)])

        big_out(wdn_v, 22, lambda c: slot(AG + c).bitcast(BF16), (None, [AK(AG + c) for c in range(22)]))
        for ti, (col0, n, _) in enumerate(tiles):
            Tt, tkey = TOK[ti % 2], ("TOK", ti % 2)
            CP("act", Tt[0:n, 0:512], ps[2 * ti][0:n, :], r=[("ps", 2 * ti)], w=[tkey])
            CP("dve", Tt[0:n, 512:1024], ps[2 * ti + 1][0:n, :], r=[("ps", 2 * ti + 1)], w=[tkey])
            ln_stats(Tt, n, tkey, eps=EPS / (ALPHA * ALPHA))
            TT("pool", Tt[0:n, :], Tt[0:n, :], LNG[0:n, :], ALU.mult, r=[tkey, "LNG"], w=[tkey])
            TT("dve", Tt[0:n, :], Tt[0:n, :], LNB[0:n, :], ALU.add, r=[tkey, "LNB"], w=[tkey])
            if smp:
                DMA("pool", oy_s, Tt[0:128, :], r=[tkey])
            else:
                t_lo = tok0 + col0
                if t_lo < 16:
                    DMA("pool", oy_p[0:n - (16 - t_lo), :], Tt[16 - t_lo:n, :], r=[tkey])
                else:
                    DMA("pool", oy_p[t_lo - 16:t_lo - 16 + n, :], Tt[0:n, :], r=[tkey])
        P.barrier()

    P.emit(nc)
    st.close()
    return nc


_NC_CACHE = {}


def _host_inputs(inp, b):
    f = lambda a: np.ascontiguousarray(a, dtype=np.float32)
    s0, s1 = 16 * b, 16 * b + 16
    prm0 = np.concatenate([inp["rwkv_mu"][0], inp["rwkv_w0"][0], inp["rwkv_a0"][0], inp["rwkv_kk_scale"][0], inp["rwkv_ka_scale"][0],
                           inp["rwkv_rk"][0], inp["rwkv_lnx_g"][0], inp["rwkv_lnx_b"][0], inp["mlstm_norm_g"][0],
                           inp["ln_in_g"], inp["ln_in_b"], inp["ln1_g"][0], inp["ln1_b"][0]]).reshape(122, 128)
    cw = inp["ffn_conv_w"][0]
    prm1 = np.concatenate([cw[0], cw[1], cw[2], inp["ffn_conv_b"][0]]).reshape(88, 128)
    sS = inp["state_rwkv_S"][0, s0:s1]
    sH = sS.reshape(16, 8, 2, 64, 64).transpose(0, 2, 4, 1, 3).reshape(16, 128, 8, 64)
    ssh = inp["state_rwkv_shift"][0, s0:s1].reshape(16, 8, 128).transpose(2, 1, 0)
    scv = inp["state_ffn_conv"][0, s0:s1].reshape(16, 2, 22, 128).transpose(3, 2, 0, 1)
    return {
        "xp": f(inp["x_prompt"][b]), "xs": f(inp["x_sample"][s0:s1].reshape(128, D)), "meta": f(inp["meta_tokens"]),
        "sC": f(inp["state_mlstm_C"][0, s0:s1]), "sn": f(inp["state_mlstm_n"][0, s0:s1].transpose(0, 2, 1)),
        "sm": f(inp["state_mlstm_m"][0, s0:s1].T), "sH": f(sH), "ssh": f(ssh), "scv": f(scv),
        "prm0": f(prm0), "prm1": f(prm1), "bif": f(inp["b_if"][0].reshape(8, 1)),
        "ln2g": f(inp["ln2_g"][0]), "ln2b": f(inp["ln2_b"][0]),
        "w_in": f(inp["w_in"][0]), "w2a2": f(np.concatenate([inp["rwkv_w2"][0], inp["rwkv_a2"][0]], 0)), "g2": f(inp["rwkv_g2"][0]),
        "w_out": f(inp["w_out"][0]), "w_up": f(inp["ffn_w_up"][0]), "w_down": f(inp["ffn_w_down"][0]),
    }


def kernel(**inputs):
    inp = {k: np.asarray(v) for k, v in inputs.items()}
    if "nc" not in _NC_CACHE:
        _NC_CACHE["nc"] = build()
    nc = _NC_CACHE["nc"]
    in_maps = [_host_inputs(inp, b) for b in range(8)]
    res = run_bass_kernel_spmd(nc, in_maps, core_ids=list(range(8))).results
    g = lambda k: [np.asarray(r[k], dtype=np.float32) for r in res]
    y_p = np.stack(g("oy_p"), 0)
    y_s = np.concatenate(g("oy_s"), 0).reshape(128, 8, D)
    cn_p = np.stack(g("oCN_p"), 0)
    pC = cn_p[..., 0:256].transpose(0, 2, 1, 3)[None]
    pn = cn_p[..., 256].transpose(0, 2, 1)[None]
    pm = np.stack(g("om_p"), 0)[:, :, 0][None]
    Hp = np.stack(g("oH_p"), 0)
    pS = Hp.reshape(8, 2, 64, 8, 64).transpose(0, 3, 1, 4, 2).reshape(8, 16, 64, 64)[None]
    psh = np.stack(g("osh_p"), 0)[..., 0].transpose(0, 2, 1).reshape(8, D)[None]
    pcv = np.stack(g("ocv_p"), 0).transpose(0, 3, 2, 1).reshape(8, 2, DF)[None]
    cn_s = np.concatenate(g("oCN_s"), 0)
    sC = cn_s[..., 0:256].transpose(0, 2, 1, 3)[None]
    sn = cn_s[..., 256].transpose(0, 2, 1)[None]
    sm = np.concatenate([a.T for a in g("om_s")], 0)[None]
    Hs = np.concatenate(g("oH_s"), 0)
    sS = Hs.reshape(128, 2, 64, 8, 64).transpose(0, 3, 1, 4, 2).reshape(128, 16, 64, 64)[None]
    ssh = np.concatenate([a.transpose(2, 1, 0).reshape(16, D) for a in g("osh_s")], 0)[None]
    scv = np.concatenate([a.transpose(2, 3, 1, 0).reshape(16, 2, DF) for a in g("ocv_s")], 0)[None]
    c = lambda a: np.ascontiguousarray(a, dtype=np.float32)
    return (c(y_p), c(y_s), c(pC), c(pn), c(pm), c(pS), c(psh), c(pcv), c(sC), c(sn), c(sm), c(sS), c(ssh), c(scv))
```

```python
import contextlib
import os
import numpy as np
DBGSTEP = int(os.environ.get("DBGSTEP", "0"))
DBGSUB = int(os.environ.get("DBGSUB", "0"))
DBGCHUNK = int(os.environ.get("DBGCHUNK", "0"))
import concourse.bass as bass
import concourse.mybir as mybir
from concourse.bass_utils import run_bass_kernel_spmd

F32 = mybir.dt.float32
BF16 = mybir.dt.bfloat16
AF = mybir.ActivationFunctionType
ALU = mybir.AluOpType

NS_DMA = 6
ENGS = ("pe", "act", "dve", "pool", "sp")


class Op:
    __slots__ = ("eng", "fn", "deps", "dma", "signal", "val", "slot", "strict")


class Prog:
    def __init__(self):
        self.ops = {e: [] for e in ENGS}
        self.lastw = {}
        self.readers = {}
        self.pend = {e: [] for e in ENGS}
        self.dmaq = {e: [] for e in ENGS}

    def op(self, eng, fn, r=(), w=(), dma=False, strict=False):
        o = Op()
        o.strict = strict
        o.eng, o.fn, o.dma, o.signal, o.val, o.slot = eng, fn, dma, False, 0, 0
        deps = set(self.pend[eng])
        self.pend[eng] = []
        for k in r:
            lw = self.lastw.get(k)
            if lw is not None:
                deps.add(lw)
        for k in w:
            lw = self.lastw.get(k)
            if lw is not None:
                deps.add(lw)
            deps.update(self.readers.get(k, ()))
        if dma:
            q = self.dmaq[eng]
            n = len(q)
            o.slot = n % NS_DMA
            o.val = 16 * (n // NS_DMA + 1)
            if n >= NS_DMA:
                deps.add(q[n - NS_DMA])
            q.append(o)
        o.deps = deps
        for k in r:
            lst = self.readers.setdefault(k, [])
            if not dma:
                lst[:] = [x for x in lst if x.dma or x.eng != eng]
            lst.append(o)
        for k in w:
            self.lastw[k] = o
            self.readers[k] = []
        self.ops[eng].append(o)
        return o

    def barrier(self):
        lasts = []
        for e in ENGS:
            comp = [x for x in self.ops[e] if not x.dma]
            if comp:
                lasts.append(comp[-1])
            lasts.extend(self.dmaq[e][-NS_DMA:])
        for e in ENGS:
            self.pend[e].extend(lasts)
        self.lastw.clear()
        self.readers.clear()

    def emit(self, nc):
        for e in ENGS:
            for o in self.ops[e]:
                for d in o.deps:
                    if not d.dma and not (d.eng == "pe" and e == "pe" and not o.strict):
                        d.signal = True
        for e in ENGS:
            c = 0
            for o in self.ops[e]:
                if not o.dma and o.signal:
                    c += 1
                    o.val = c
        with contextlib.ExitStack() as st:
            csem = {e: st.enter_context(nc.semaphore("c_" + e)) for e in ENGS}
            dsem = {e: [st.enter_context(nc.semaphore("d_%s%d" % (e, i))) for i in range(NS_DMA)]
                    for e in ENGS if self.dmaq[e]}
            block = st.enter_context(nc.Block())

            def semof(o):
                return dsem[o.eng][o.slot] if o.dma else csem[o.eng]

            def run(e, h):
                waited = {}
                for o in self.ops[e]:
                    need = {}
                    for d in o.deps:
                        if d.eng == "pe" and e == "pe" and not d.dma and not o.strict:
                            continue
                        sm = semof(d)
                        if waited.get(sm, 0) < d.val and need.get(sm, 0) < d.val:
                            need[sm] = d.val
                    for sm, v in need.items():
                        h.wait_ge(sm, v)
                        waited[sm] = v
                    ins = o.fn(h)
                    if o.dma:
                        ins.then_inc(semof(o), 16)
                    elif o.signal:
                        ins.then_inc(csem[e], 1)
                for o in self.dmaq[e][-NS_DMA:]:
                    if waited.get(semof(o), 0) < o.val:
                        h.wait_ge(semof(o), o.val)
                        waited[semof(o)] = o.val

            @block.tensor
            def _(h):
                run("pe", h)

            @block.scalar
            def _(h):
                run("act", h)

            @block.vector
            def _(h):
                run("dve", h)

            @block.gpsimd
            def _(h):
                run("pool", h)

            @block.sync
            def _(h):
                run("sp", h)


D = 1024
DF = 2816
W = 274
EPS = 1e-5
GN_EPS = 64e-5
ALPHA = float(2.0 ** 0.25)
C0 = float(np.exp(-0.5))
NEG = -1.0e30
GELU_K = float(2.0 * np.sqrt(2.0 / np.pi))

MERGED = 0
QA, KA, VA, OA, GA = 8, 12, 16, 24, 32
GB, KKT, RB, KB, VB, BHT, BON = 8, 16, 24, 32, 40, 48, 56
AG, AV = 8, 30
NSLOT = 64

PC_MU, PC_W0, PC_A0, PC_KKS, PC_KAS, PC_RK, PC_LXG, PC_LXB, PC_MNG = 0, 26, 34, 42, 50, 58, 66, 74, 82
PC_LIG, PC_LIB, PC_L1G, PC_L1B = 90, 98, 106, 114
PC_CW0, PC_CW1, PC_CW2, PC_CB = 128, 150, 172, 194

BLOCKS = [("p", 0, 272, [(0, 16), (16, 64), (80, 64), (144, 64), (208, 64)], 0)]
for _i in range(1, 8):
    BLOCKS.append(("p", 272 + 256 * (_i - 1), 256, [(64 * c, 64) for c in range(4)], 1))
BLOCKS.append(("s", 0, 128, [(8 * c, 8) for c in range(16)], 2))


def build(blocks=BLOCKS, stop=None):
    nc = bass.Bass("TRN2", target_bir_lowering=False)

    def din(name, shape):
        return nc.dram_tensor(name, list(shape), F32, kind="ExternalInput").ap()

    def dout(name, shape):
        return nc.dram_tensor(name, list(shape), F32, kind="ExternalOutput").ap()

    xp = din("xp", [2048, D]); xs = din("xs", [128, D]); meta = din("meta", [16, D])
    sC = din("sC", [16, 4, 128, 256]); sn = din("sn", [16, 128, 4]); sm = din("sm", [4, 16])
    sH = din("sH", [16, 128, 8, 64]); ssh = din("ssh", [128, 8, 16]); scv = din("scv", [128, 22, 16, 2])
    prm0 = din("prm0", [122, 128]); prm1 = din("prm1", [88, 128])
    bif = din("bif", [8, 1]); ln2g = din("ln2g", [D]); ln2b = din("ln2b", [D])
    w_in = din("w_in", [D, 8456]); w2a2 = din("w2a2", [128, D]); g2 = din("g2", [128, D])
    w_out = din("w_out", [D, D]); w_up = din("w_up", [D, 2 * DF]); w_down = din("w_down", [DF, D])

    oy_p = dout("oy_p", [2048, D]); oy_s = dout("oy_s", [128, D])
    oCN_p = dout("oCN_p", [128, 4, 257]); om_p = dout("om_p", [4, 1]); oH_p = dout("oH_p", [128, 8, 64])
    osh_p = dout("osh_p", [128, 8, 1]); ocv_p = dout("ocv_p", [128, 22, 2])
    oCN_s = dout("oCN_s", [16, 128, 4, 257]); om_s = dout("om_s", [4, 16]); oH_s = dout("oH_s", [16, 128, 8, 64])
    osh_s = dout("osh_s", [128, 8, 16]); ocv_s = dout("ocv_s", [128, 22, 16, 2])

    wb_in = nc.dram_tensor("wb_in", [D, 8456], BF16).ap(); wb_out = nc.dram_tensor("wb_out", [D, D], BF16).ap()
    wb_up = nc.dram_tensor("wb_up", [D, 2 * DF], BF16).ap(); wb_down = nc.dram_tensor("wb_down", [DF, D], BF16).ap()
    win_v = wb_in.rearrange("(kc p) c -> p kc c", p=128)
    wup_v = wb_up.rearrange("(kc p) c -> p kc c", p=128)
    wout_v = wb_out.rearrange("(c p) d -> p c d", p=128)
    wdn_v = wb_down.rearrange("(c p) d -> p c d", p=128)

    P = Prog()
    st = contextlib.ExitStack()

    def sb(name, shape):
        return st.enter_context(nc.sbuf_tensor(name, list(shape), F32))

    ARENA = sb("ARENA", [128, NSLOT * W])
    XT = sb("XT", [128, 8, 272])
    TOK = [sb("TOK0", [128, D]), sb("TOK1", [128, D])]
    LNG = sb("LNG", [128, D]); LNB = sb("LNB", [128, D])
    WB = [st.enter_context(nc.sbuf_tensor("WB%d" % i, [128, 4096], BF16)) for i in range(2)]
    XTb = st.enter_context(nc.sbuf_tensor("XTb", [128, 8, 272], BF16))
    XTlo = st.enter_context(nc.sbuf_tensor("XTlo", [128, 8, 272], BF16))
    PC = sb("PC", [128, 216]); PCN = sb("PCN", [128, 16])
    W2A2 = sb("W2A2", [128, D]); G2 = sb("G2", [128, D])
    ID = sb("ID", [128, 128]); ONES = sb("ONES", [128, 128]); BLK = sb("BLK", [128, 128]); AID = sb("AID", [128, 128])
    MK1 = sb("MK1", [64, 2, 64]); MK1N = sb("MK1N", [64, 2, 64]); MLN = sb("MLN", [64, 64]); MNEG = sb("MNEG", [64, 64])
    SEL = sb("SEL", [4, 4, 128])
    RST01_ = sb("RST01", [128, W])
    RST01 = [RST01_, RST01_, RST01_]
    RSTN = sb("RSTN", [4, W])
    CN = [sb("CN0", [128, 4, 257]), sb("CN1", [128, 4, 257])]
    NBC = st.enter_context(nc.sbuf_tensor("NBC", [128, 4, 128], BF16))
    CNb = st.enter_context(nc.sbuf_tensor("CNb", [128, 4, 257], BF16))
    Hb = [st.enter_context(nc.sbuf_tensor("Hb0", [128, 8, 64], BF16)), st.enter_context(nc.sbuf_tensor("Hb1", [128, 8, 64], BF16))]
    IDb = st.enter_context(nc.sbuf_tensor("IDb", [128, 128], BF16)); ONESb = st.enter_context(nc.sbuf_tensor("ONESb", [128, 128], BF16))
    H = [sb("H0", [128, 8, 64]), sb("H1", [128, 8, 64])]
    CARRY = sb("CARRY", [128, 26]); AGC = sb("AGC", [128, 22, 2])
    CV0 = sb("CV0", [128, 22, 16, 2]);
    XWA = sb("XWA", [128, W]); XG = sb("XG", [128, W])
    GL = sb("GL", [128, 8, 16])
    BI = sb("BI", [4, 1]); NBF = sb("NBF", [4, 1]); MCOL = sb("MCOL", [4, 2]); M0ROW = sb("M0ROW", [4, 16]); MNEW = sb("MNEW", [4, 16])
    ST6 = sb("ST6", [128, 12]); MV = sb("MV", [128, 2]); SD = sb("SD", [128, 1]); RS = sb("RS", [128, 1])
    TN = ["LW", "AA", "KKS", "SQ", "NRM", "KKN", "T1", "BB", "CUM", "EGP", "EGN", "CX", "EGX"]
    T = {n: sb("T_" + n, [128, W]) for n in TN}
    PR0 = T["KKS"][:, 0:128]; PR1 = T["SQ"][:, 0:128]
    SHS = T["EGX"][:, 0:128].rearrange("p (a b) -> p a b", a=8)
    LI = sb("LI", [4, W]); LP = sb("LP", [4, W]); CUMP = sb("CUMP", [4, W]); DD = sb("DD", [4, W]); CM = sb("CM", [4, W])
    MX = sb("MX", [4, 64]); ROWS = sb("ROWS", [4, 3, 64]); TR4 = sb("TR4", [4, 64]); WKR = sb("WKR", [4, 64])
    COLS = sb("COLS", [64, 8])
    WST = [sb("WST%d" % i, [64, 64]) for i in range(4)]; SST = [st.enter_context(nc.sbuf_tensor("SST%d" % i, [64, 64], BF16)) for i in range(4)]
    QW = [st.enter_context(nc.sbuf_tensor("QW%d" % i, [128, 64], BF16)) for i in range(4)]; ADEN = [sb("ADEN%d" % i, [128, 64]) for i in range(4)]
    DDT = [sb("DDT%d" % i, [128, 64]) for i in range(4)]; KW = [st.enter_context(nc.sbuf_tensor("KW%d" % i, [64, 128], BF16)) for i in range(4)]
    DCOL = sb("DCOL", [128, 4])
    VTK = st.enter_context(nc.sbuf_tensor("VTK", [64, 512], BF16)); KHK = st.enter_context(nc.sbuf_tensor("KHK", [64, 512], BF16)); BHK = st.enter_context(nc.sbuf_tensor("BHK", [64, 512], BF16))
    S1raw = st.enter_context(nc.sbuf_tensor("S1raw", [64, 1028], BF16)); S2 = st.enter_context(nc.sbuf_tensor("S2", [64, 8, 2, 64], BF16)); PA = st.enter_context(nc.sbuf_tensor("PA", [64, 8, 64], BF16))
    S1 = S1raw[:, 0:1024].rearrange("p (a b c) -> p a b c", a=8, b=2)
    VTK1 = S1raw[:, 0:1028].rearrange("p (h c) -> p h c", h=4)
    KTK = KHK
    PN = [st.enter_context(nc.sbuf_tensor("PN%d" % i, [64, 8, 64], BF16)) for i in range(2)]; PTN = [st.enter_context(nc.sbuf_tensor("PTN%d" % i, [64, 8, 64], BF16)) for i in range(2)]
    U = [st.enter_context(nc.sbuf_tensor("U0", [64, 512], BF16)), st.enter_context(nc.sbuf_tensor("U1", [64, 512], BF16))]
    TMPH = sb("TMPH", [128, 4, 64])

    def sbs(name, shape):
        return st.enter_context(nc.sbuf_tensor(name, shape, BF16))
    setB = (sbs("VTKs", [16, 512]), sbs("KHKs", [16, 512]), sbs("BHKs", [16, 512]), sbs("S1s", [16, 8, 2, 16]), sbs("S2s", [16, 8, 2, 16]),
            sbs("PAs", [16, 8, 16]), [sbs("PNs%d" % i, [16, 8, 16]) for i in range(2)], [sbs("PTNs%d" % i, [16, 8, 16]) for i in range(2)],
            [sbs("Us%d" % i, [16, 512]) for i in range(2)], sb("TMPHs", [128, 4, 64]), "B")
    ps = [st.enter_context(nc.psum_tensor("ps%d" % i, [128, 512], F32)) for i in range(8)]
    psi = [0]

    def PS():
        i = psi[0]
        psi[0] = (i + 1) % 8
        return ps[i], ("ps", i)

    def slot(i):
        return ARENA[:, i * W:(i + 1) * W]

    def slots(i, n):
        return ARENA[:, i * W:(i + n) * W].rearrange("p (c w) -> p c w", c=n)

    def AK(i):
        return ("A", i)

    BFA = ARENA[:, KKT * W:(KKT + 8) * W].bitcast(BF16)
    BFB = ARENA[:, BHT * W:(BHT + 8) * W].bitcast(BF16)

    def KKTb(j):
        return BFA[:, j * W:(j + 1) * W]

    def RTb(j):
        return BFA[:, (8 + j) * W:(9 + j) * W]

    def KHb(j):
        return BFB[:, j * W:(j + 1) * W]

    def BHb(j):
        return BFB[:, (8 + j) * W:(9 + j) * W]

    def Qb(h):
        return slot(QA + h).bitcast(BF16)

    def Kb(h):
        return slot(KA + h).bitcast(BF16)

    def MM(out, lhsT, rhs, start=True, stop=True, r=(), w=(), strict=False):
        P.op("pe", lambda h: h.matmul(out, lhsT=lhsT, rhs=rhs, start=start, stop=stop), r, w, strict=strict)

    def TR(out, in_, n, r=(), w=(), bf=False):
        idn = IDb[0:n, 0:n] if bf else ID[0:n, 0:n]
        P.op("pe", lambda h: h.transpose(out=out, in_=in_, identity=idn), list(r) + ["ID"], w)

    def ACT(out, in_, func, r=(), w=(), bias=0.0, scale=1.0):
        P.op("act", lambda h: h.activation(out=out, in_=in_, func=func, bias=bias, scale=scale), r, w)

    def TT(eng, out, in0, in1, op, r=(), w=()):
        P.op(eng, lambda h: h.tensor_tensor(out=out, in0=in0, in1=in1, op=op), r, w)

    def TS(eng, out, in0, s1, op0, r=(), w=(), s2=None, op1=None):
        if op1 is None and eng == "pool" and op0 == ALU.mult:
            s2, op1 = 1.0, ALU.mult
        if op1 is None:
            P.op(eng, lambda h: h.tensor_scalar(out=out, in0=in0, scalar1=s1, scalar2=None, op0=op0), r, w)
        else:
            P.op(eng, lambda h: h.tensor_scalar(out=out, in0=in0, scalar1=s1, scalar2=s2, op0=op0, op1=op1), r, w)

    def STT(out, in0, sc, in1, op0, op1, r=(), w=()):
        P.op("dve", lambda h: h.scalar_tensor_tensor(out=out, in0=in0, scalar=sc, in1=in1, op0=op0, op1=op1), r, w)

    def CP(eng, out, in_, r=(), w=()):
        if eng == "act":
            P.op("act", lambda h: h.copy(out=out, in_=in_), r, w)
        else:
            P.op(eng, lambda h: h.tensor_copy(out=out, in_=in_), r, w)

    def RECIP(out, in_, r=(), w=()):
        P.op("dve", lambda h: h.reciprocal(out=out, in_=in_), r, w)

    def MSET(eng, ap, v, w=()):
        P.op(eng, lambda h: h.memset(ap, v), (), w)

    def DMA(q, out, in_, r=(), w=(), slow=False):
        P.op(q, lambda h: h.dma_start(out=out, in_=in_, allow_slow_non_contiguous=slow), r, w, dma=True)

    def PCc(i):
        return PC[:, i:i + 1]

    DMA("sp", PR0[0:122, :], prm0, w=["PR0"])
    DMA("sp", PR1[0:88, :], prm1, w=["PR1"])
    DMA("sp", W2A2[:], w2a2, w=["W2A2"])
    DMA("sp", G2[:], g2, w=["G2"])
    DMA("sp", LNG[:], ln2g.partition_broadcast(128), w=["LNG"])
    DMA("sp", LNB[:], ln2b.partition_broadcast(128), w=["LNB"])
    DMA("sp", BI[:], bif[0:4, :], w=["BI"])
    DMA("sp", NBF[:], bif[4:8, :], w=["NBF"])
    DMA("sp", M0ROW[:], sm, w=["M0ROW"])
    DMA("sp", CV0[:], scv, w=[("CV0", i) for i in range(22)])
    MSET("pool", ONES[:], 1.0, w=["ONES"])
    ZER = T["LW"][:, 0:128]; NEG1 = T["AA"][:, 0:128]
    MSET("pool", ZER, 0.0, w=["ZER"])
    MSET("pool", NEG1, -1.0, w=["NEG1"])
    P.op("pool", lambda h: h.affine_select(out=ID[:], in_=ONES[:], pattern=[[1, 128]], compare_op=ALU.is_equal,
                                           fill=0.0, base=0, channel_multiplier=-1), ["ONES"], ["ID"])
    TS("pool", AID[:], ID[:], ALPHA, ALU.mult, r=["ID"], w=["AID"])
    CP("pool", IDb[:], ID[:], r=["ID"], w=["ID"])
    MSET("pool", ONESb[:], 1.0, w=["ONES"])
    MSET("pool", BLK[:], 0.0, w=["BLK"])
    MSET("pool", BLK[0:64, 0:64], 1.0, w=["BLK"])
    MSET("pool", BLK[64:128, 64:128], 1.0, w=["BLK"])
    P.op("pool", lambda h: h.affine_select(out=MK1[:, 0, :], in_=ONES[0:64, 0:64], pattern=[[1, 64]], compare_op=ALU.is_gt,
                                           fill=0.0, base=0, channel_multiplier=-1), ["ONES"], ["MK1"])
    P.op("pool", lambda h: h.affine_select(out=MK1[:, 1, :], in_=ONES[0:64, 0:64], pattern=[[1, 64]], compare_op=ALU.is_ge,
                                           fill=0.0, base=0, channel_multiplier=-1), ["ONES"], ["MK1"])
    TS("pool", MK1N[:], MK1[:], -1.0, ALU.mult, r=["MK1"], w=["MK1N"])
    P.op("pool", lambda h: h.affine_select(out=MLN[:], in_=NEG1[0:64, 0:64], pattern=[[-1, 64]], compare_op=ALU.is_gt,
                                           fill=0.0, base=0, channel_multiplier=1), ["NEG1"], ["MLN"])
    P.op("pool", lambda h: h.affine_select(out=MNEG[:], in_=ZER[0:64, 0:64], pattern=[[1, 64]], compare_op=ALU.is_ge,
                                           fill=NEG, base=0, channel_multiplier=-1), ["ZER"], ["MNEG"])
    CP("dve", SEL[:], ID[0:4, 0:4].unsqueeze(2).broadcast_to([4, 4, 128]), r=["ID"], w=["SEL"])
    TS("pool", NBF[:], NBF[:], -1.0, ALU.mult, r=["NBF"], w=["NBF"])
    pt, pk = PS()
    TR(pt[:, 0:122], PR0[0:122, :], 122, r=["PR0"], w=[pk])
    TR(pt[:, 128:216], PR1[0:88, :], 88, r=["PR1"], w=[pk])
    CP("dve", PC[:, 0:122], pt[:, 0:122], r=[pk], w=["PC"])
    CP("dve", PC[:, 128:216], pt[:, 128:216], r=[pk], w=["PC"])
    TS("dve", PCN[:], PC[:, PC_W0:PC_W0 + 16], -1.0, ALU.mult, r=["PC"], w=["PC"])
    MSET("pool", CN[0][:], 0.0, w=[("CN", 0, h) for h in range(4)])
    MSET("pool", NBC[:], 0.0, w=[("NBC", h) for h in range(4)])
    MSET("pool", H[0][:], 0.0, w=[("H", 0, 0), ("H", 0, 1)])
    MSET("pool", Hb[0][:], 0.0, w=[("Hb", 0, 0), ("Hb", 0, 1)])
    MSET("pool", CNb[:], 0.0, w=[("CNb", h) for h in range(4)])
    MSET("pool", MCOL[:], 0.0, w=["MC0", "MC1"])
    MSET("pool", CARRY[:], 0.0, w=[("CARRY", c) for c in range(26)])
    MSET("pool", AGC[:], 0.0, w=[("AGC", i) for i in range(22)])
    pcs = []
    for (wsrc, wdst, R, C) in ((w_in, wb_in, D, 8456), (w_out, wb_out, D, D), (w_up, wb_up, D, 2 * DF), (w_down, wb_down, DF, D)):
        for r0 in range(0, R, 128):
            for c0 in range(0, C, 4228):
                pcs.append((wsrc, wdst, r0, c0, min(4228, C - c0)))
    for pi_, (wsrc, wdst, r0, c0, n_) in enumerate(pcs):
        sl = pi_ % 2
        stg = ARENA[:, sl * 4228:sl * 4228 + n_]
        ob = ARENA[:, 8456 + sl * 2114:8456 + (sl + 1) * 2114].bitcast(BF16)[:, 0:n_]
        DMA("sp", stg, wsrc[r0:r0 + 128, c0:c0 + n_], w=[("STG", sl)])
        CP(("act", "dve", "pool")[pi_ % 3], ob, stg, r=[("STG", sl)], w=[("OB", sl)])
        DMA("pool", wdst[r0:r0 + 128, c0:c0 + n_], ob, r=[("OB", sl)])
    P.barrier()

    wbi = [0]

    def nextWB():
        i = wbi[0]
        wbi[0] = 1 - i
        return WB[i], ("WB", i)

    def ln_stats(Tt, n, tkey, eps=EPS):
        P.op("dve", lambda h: h.bn_stats(out=ST6[0:n, 0:6], in_=Tt[0:n, 0:512]), [tkey], ["ST6"])
        P.op("dve", lambda h: h.bn_stats(out=ST6[0:n, 6:12], in_=Tt[0:n, 512:1024]), [tkey], ["ST6"])
        P.op("dve", lambda h: h.bn_aggr(out=MV[0:n, :], in_=ST6[0:n, :]), ["ST6"], ["MV"])
        ACT(SD[0:n, :], MV[0:n, 1:2], AF.Ln, r=["MV"], w=["SD"], bias=eps)
        ACT(RS[0:n, :], SD[0:n, :], AF.Exp, r=["SD"], w=["RS"], scale=-0.5)
        TS("dve", Tt[0:n, :], Tt[0:n, :], MV[0:n, 0:1], ALU.subtract, r=[tkey, "MV", "RS"], w=[tkey], s2=RS[0:n, 0:1], op1=ALU.mult)

    def to_feature_major(Tt, n, tkey, col0, gcol, bcol, banks=None):
        for half in range(2):
            if banks is None:
                pt_, pk_ = PS()
            else:
                pt_, pk_ = ps[banks[half]], ("ps", banks[half])
            for i in range(4):
                kc = half * 4 + i
                TR(pt_[:, i * 128:i * 128 + n], Tt[0:n, kc * 128:(kc + 1) * 128], n, r=[tkey], w=[pk_])
            for i in range(4):
                kc = half * 4 + i
                if i % 2 == 0:
                    ACT(XT[:, kc, col0:col0 + n], pt_[:, i * 128:i * 128 + n], AF.Identity, r=[pk_, "PC"], w=["XT"],
                        bias=PCc(bcol + kc), scale=PCc(gcol + kc))
                else:
                    TS("dve", XT[:, kc, col0:col0 + n], pt_[:, i * 128:i * 128 + n], PCc(gcol + kc), ALU.mult,
                       r=[pk_, "PC"], w=["XT"], s2=PCc(bcol + kc), op1=ALU.add)

    def proj_chunks(wv, col0, nch, ncols, evac):
        i = 0
        while i < nch:
            nb = min(4, nch - i)
            wb, wk = nextWB()
            wbv = wb[:, :].rearrange("p (k c) -> p k c", k=8)
            DMA("sp", wbv[:, :, 0:nb * 128], wv[:, :, col0 + i * 128:col0 + (i + nb) * 128], w=[wk])
            for b in range(nb):
                pt_, pk_ = PS()
                for kc in range(8):
                    MM(pt_[:, 0:ncols], wbv[:, kc, b * 128:(b + 1) * 128], XTb[:, kc, 0:ncols], start=(kc == 0), stop=(kc == 7),
                       r=[wk, "XTb"], w=[pk_])
                evac(i + b, pt_, pk_)
            i += nb

    cur_ty = [-1]
    gchunk = [0]

    for (kind, tok0, NT, chunks, ty) in blocks:
        smp = (kind == "s")
        NTB = NT + (16 if smp else 0)
        if ty != cur_ty[0]:
            cur_ty[0] = ty
            MSET("pool", RST01_[:], 1.0, w=["RST"])
            if ty == 0:
                MSET("pool", RST01_[:, 0:1], 0.0, w=["RST"])
                MSET("pool", RST01_[:, 16:272:64], 0.0, w=["RST"])
            elif ty == 1:
                MSET("pool", RST01_[:, 0:256:64], 0.0, w=["RST"])
            else:
                MSET("pool", RST01_[:, 0:128:8], 0.0, w=["RST"])
        tiles = []
        if smp:
            tiles.append((0, 128, [(0, 128, xs)]))
        else:
            c = 0
            while c < NT:
                n = min(128, NT - c)
                srcs = []
                t_lo, t_hi = tok0 + c, tok0 + c + n
                if t_lo < 16:
                    srcs.append((0, 16 - t_lo, meta[t_lo:16, :]))
                    srcs.append((16 - t_lo, n - (16 - t_lo), xp[0:t_hi - 16, :]))
                else:
                    srcs.append((0, n, xp[t_lo - 16:t_hi - 16, :]))
                tiles.append((c, n, srcs))
                c += n

        for ti, (col0, n, srcs) in enumerate(tiles):
            Tt, tkey = TOK[ti % 2], ("TOK", ti % 2)
            for (r0, nr, ap) in srcs:
                DMA("sp", Tt[r0:r0 + nr, :], ap, w=[tkey])
            ln_stats(Tt, n, tkey)
            to_feature_major(Tt, n, tkey, col0, PC_LIG, PC_LIB)
        if smp:
            DMA("sp", XT[:, :, 128:144], ssh, w=["XT"])
        CP("pool", XTb[:, :, 0:NTB], XT[:, :, 0:NTB], r=["XT"], w=["XTb"])
        TT("dve", XTlo[:, :, 0:NT], XT[:, :, 0:NT], XTb[:, :, 0:NT], ALU.subtract, r=["XT", "XTb"], w=["XTlo"])
        if smp:
            pass
            CP("pool", SHS[:], XT[:, :, 7:128:8], r=["XT"], w=["EGX"])
            DMA("pool", osh_s, SHS[:], r=["EGX"])
        elif tok0 + NT == 2064:
            CP("pool", SHS[:, :, 0:1], XT[:, :, NT - 1:NT], r=["XT"], w=["EGX"])
            DMA("pool", osh_p, SHS[:, :, 0:1], r=["EGX"], slow=True)

        if stop == "0":
            P.barrier()
            continue
        def evA(dst0, kindA):
            def f(i, pt_, pk_):
                d = slot(dst0 + i)[:, 0:NT]
                if dst0 in (QA, KA):
                    d = slot(dst0 + i).bitcast(BF16)[:, 0:NT]
                if kindA == "copy":
                    if i % 2 == 0:
                        CP("act", d, pt_[:, 0:NT], r=[pk_], w=[AK(dst0 + i)])
                    else:
                        CP("dve", d, pt_[:, 0:NT], r=[pk_], w=[AK(dst0 + i)])
                elif kindA == "kscale":
                    ACT(d, pt_[:, 0:NT], AF.Identity, r=[pk_], w=[AK(dst0 + i)], scale=float(128 ** -0.5))
                else:
                    ACT(d, pt_[:, 0:NT], AF.Sigmoid, r=[pk_], w=[AK(dst0 + i)])
            return f

        proj_chunks(win_v, 2048, 4, NT, evA(QA, "copy"))
        proj_chunks(win_v, 2560, 4, NT, evA(KA, "kscale"))
        proj_chunks(win_v, 3072, 8, NT, evA(VA, "copy"))
        proj_chunks(win_v, 4096, 8, NT, evA(OA, "sig"))
        wb, wk = nextWB()
        wbv = wb[:, :].rearrange("p (k c) -> p k c", k=8)
        DMA("sp", wbv[:, :, 0:8], win_v[:, :, 5120:5128], w=[wk])
        pi_, ki_ = PS()
        pf_, kf_ = PS()
        for kc in range(8):
            MM(pi_[0:4, 0:NT], wbv[:, kc, 0:4], XTb[:, kc, 0:NT], start=(kc == 0), stop=(kc == 7), r=[wk, "XTb"], w=[ki_])
        for kc in range(8):
            MM(pf_[0:4, 0:NT], wbv[:, kc, 4:8], XTb[:, kc, 0:NT], start=(kc == 0), stop=(kc == 7), r=[wk, "XTb"], w=[kf_])
        ACT(LI[0:4, 0:NT], pi_[0:4, 0:NT], AF.Identity, r=[ki_, "BI"], w=["LI"], bias=BI[0:4, 0:1])
        ACT(LP[0:4, 0:NT], pf_[0:4, 0:NT], AF.Exp, r=[kf_, "NBF"], w=["LP"], bias=NBF[0:4, 0:1], scale=-1.0)
        ACT(LP[0:4, 0:NT], LP[0:4, 0:NT], AF.Ln, r=["LP"], w=["LP"], bias=1.0)
        P.op("dve", lambda h, NT=NT, ty=ty: h.tensor_tensor_scan(out=CUMP[0:4, 0:NT], data0=RST01[ty][0:4, 0:NT], data1=LP[0:4, 0:NT],
                                                  initial=0.0, op0=ALU.mult, op1=ALU.add), ["LP", "RST"], ["CUMP"])
        TT("dve", DD[0:4, 0:NT], LI[0:4, 0:NT], CUMP[0:4, 0:NT], ALU.add, r=["LI", "CUMP"], w=["DD"])
        TS("pool", RSTN[0:4, 0:NT], RST01[ty][0:4, 0:NT], 1.0, ALU.subtract, r=["RST"], w=["RSN"], s2=1.0e30, op1=ALU.mult)
        P.op("dve", lambda h, NT=NT, ty=ty: h.tensor_tensor_scan(out=CM[0:4, 0:NT], data0=RSTN[0:4, 0:NT], data1=DD[0:4, 0:NT],
                                                  initial=0.0, op0=ALU.add, op1=ALU.max), ["DD", "RSN"], ["CM"])
        proj_chunks(win_v, 0, 8, NT, evA(GA, "sig"))
        for c in range(8):
            TT("pool", slot(OA + c)[:, 0:NT], slot(OA + c)[:, 0:NT], slot(GA + c)[:, 0:NT], ALU.mult, r=[AK(OA + c), AK(GA + c)], w=[AK(OA + c)])

        MSET("pool", VTK1[:, :, 256:257], 1.0, w=["S1"])
        for ci, (c0, L) in enumerate(chunks):
            cs = slice(c0, c0 + L)
            if smp:
                cnb = ci % 2
                m_in, m_in_k = M0ROW[0:4, ci:ci + 1], "M0ROW"
                m_out, m_out_k = MNEW[0:4, ci:ci + 1], "MNEW"
                cnk = [("CN", cnb, h) for h in range(4)]
                DMA("sp", CN[cnb][:, :, 0:256], sC[ci].rearrange("h k v -> k h v"), w=cnk)
                DMA("sp", CN[cnb][:, :, 256:257], sn[ci].unsqueeze(2), w=cnk, slow=True)
                CP("pool", NBC[:], CN[cnb][:, :, 256:257].broadcast_to([128, 4, 128]), r=cnk, w=[("NBC", h) for h in range(4)])
                CP("act", CNb[:], CN[cnb][:], r=cnk, w=[("CNb", h) for h in range(4)])
            else:
                cnb = 0
                g = gchunk[0]
                gchunk[0] += 1
                m_in, m_in_k = MCOL[0:4, g % 2:g % 2 + 1], "MC%d" % (g % 2)
                m_out, m_out_k = MCOL[0:4, (g + 1) % 2:(g + 1) % 2 + 1], "MC%d" % ((g + 1) % 2)
            CNt = CN[cnb]
            TS("dve", MX[0:4, 0:L], CM[0:4, cs], m_in, ALU.max, r=["CM", m_in_k], w=["MX"])
            TS("dve", ROWS[0:4, 0, 0:L], MX[0:4, 0:L], -1.0, ALU.mult, r=["MX"], w=["ROWS"])
            ACT(ROWS[0:4, 1, 0:L], MX[0:4, 0:L], AF.Exp, r=["MX", m_in_k], w=["ROWS"], bias=m_in, scale=-1.0)
            TT("dve", TR4[0:4, 0:L], CUMP[0:4, cs], MX[0:4, 0:L], ALU.subtract, r=["CUMP", "MX"], w=["TR4"])
            ACT(ROWS[0:4, 2, 0:L], TR4[0:4, 0:L], AF.Exp, r=["TR4"], w=["ROWS"])
            ACT(WKR[0:4, 0:L], DD[0:4, cs], AF.Exp, r=["DD", "ROWS"], w=["WKR"], bias=ROWS[0:4, 0, L - 1:L])
            TT("dve", m_out, MX[0:4, L - 1:L], CUMP[0:4, c0 + L - 1:c0 + L], ALU.subtract, r=["MX", "CUMP"], w=[m_out_k])
            pt_, pk_ = PS()
            TR(pt_[0:L, 0:4], DD[0:4, cs], 4, r=["DD"], w=[pk_])
            TR(pt_[0:L, 4:8], WKR[0:4, 0:L], 4, r=["WKR"], w=[pk_])
            CP("act", COLS[0:L, 0:8], pt_[0:L, 0:8], r=[pk_], w=["COLS"])
            pk1, kk1 = PS()
            pk1b = pk1[:, :].bitcast(BF16)
            for h in range(4):
                TR(pk1b[0:L, h * 128:(h + 1) * 128], Kb(h)[:, cs], 128, r=[AK(KA + h)], w=[kk1], bf=True)
            CP("dve", KTK[0:L, :], pk1b[0:L, 0:512], r=[kk1], w=["KHK"])
            for half in range(2):
                pv, kv = PS()
                for i in range(4):
                    c = half * 4 + i
                    TR(pv[0:L, i * 128:(i + 1) * 128], slot(VA + c)[:, cs], 128, r=[AK(VA + c)], w=[kv])
                CP("act", VTK1[0:L, half * 2:half * 2 + 2, 0:256], pv[0:L, :].rearrange("p (a b) -> p a b", a=2), r=[kv], w=["S1"])
            for h in range(4):
                TS("pool", KW[h][0:L, :], KTK[0:L, h * 128:(h + 1) * 128], COLS[0:L, 4 + h:5 + h], ALU.mult, r=["KHK", "COLS"], w=[("KW", h)])
            PB = [(ps[2 * h], ("ps", 2 * h)) for h in range(4)]
            PN_ = [(ps[2 * h + 1], ("ps", 2 * h + 1)) for h in range(4)]
            for h in range(4):
                pb, kb = PB[h]
                MM(pb[0:L, 0:L], SEL[0:4, h, 0:L], ROWS[0:4, 0, 0:L], start=True, stop=False, r=["SEL", "ROWS"], w=[kb])
                MM(pb[0:L, 0:L], ID[0:L, 0:L], MNEG[0:L, 0:L], start=False, stop=True, r=["ID", "MNEG"], w=[kb])
                MM(pb[:, L:3 * L], SEL[0:4, h, :], ROWS[0:4, 1:3, 0:L], r=["SEL", "ROWS"], w=[kb])
                MM(pb[0:L, 256:256 + L], Kb(h)[:, cs], Qb(h)[:, cs], r=[AK(KA + h), AK(QA + h)], w=[kb])
            for h in range(4):
                pb, kb = PB[h]
                ACT(WST[h][0:L, 0:L], pb[0:L, 0:L], AF.Exp, r=[kb, "COLS"], w=[("WST", h)], bias=COLS[0:L, h:h + 1])
            for h in range(4):
                pb, kb = PB[h]
                TT("dve", SST[h][0:L, 0:L], WST[h][0:L, 0:L], pb[0:L, 256:256 + L], ALU.mult, r=[("WST", h), kb], w=[("SST", h)])
                TT("dve", QW[h][:, 0:L], Qb(h)[:, cs], pb[:, L:2 * L], ALU.mult, r=[AK(QA + h), kb], w=[("QW", h)])
            for h in range(4):
                pn_, kn = PN_[h]
                for vc in range(2):
                    MM(pn_[:, vc * L:(vc + 1) * L], VTK1[0:L, h, vc * 128:(vc + 1) * 128], SST[h][0:L, 0:L], start=True, stop=False,
                       r=["S1", ("SST", h)], w=[kn])
                    MM(pn_[:, vc * L:(vc + 1) * L], CNb[:, h, vc * 128:(vc + 1) * 128], QW[h][:, 0:L], start=False, stop=True,
                       r=[("CNb", h), ("QW", h)], w=[kn])
                MM(pn_[:, 2 * L:3 * L], ONESb[0:L, :], SST[h][0:L, 0:L], start=True, stop=False, r=["ONES", ("SST", h)], w=[kn])
                MM(pn_[:, 2 * L:3 * L], NBC[:, h, :], QW[h][:, 0:L], start=False, stop=True, r=[("NBC", h), ("QW", h)], w=[kn])
                MM(pn_[:, 192:449], KW[h][0:L, :], VTK1[0:L, h, :], r=[("KW", h), "S1"], w=[kn])
            for h in range(4):
                pb, kb = PB[h]
                pn_, kn = PN_[h]
                ACT(ADEN[h][:, 0:L], pn_[:, 2 * L:3 * L], AF.Abs, r=[kn], w=[("ADEN", h)])
                CP("act", DCOL[:, h:h + 1], pb[:, 2 * L - 1:2 * L], r=[kb], w=[("DCOL", h)])
            for h in range(4):
                pb, kb = PB[h]
                pn_, kn = PN_[h]
                TT("dve", DDT[h][:, 0:L], ADEN[h][:, 0:L], pb[:, 2 * L:3 * L], ALU.max, r=[("ADEN", h), kb], w=[("DDT", h)])
                RECIP(DDT[h][:, 0:L], DDT[h][:, 0:L], r=[("DDT", h)], w=[("DDT", h)])
                TT("dve", slots(VA + 2 * h, 2)[:, :, cs], pn_[:, 0:2 * L].rearrange("p (a b) -> p a b", a=2),
                   DDT[h][:, 0:L].unsqueeze(1).broadcast_to([128, 2, L]), ALU.mult,
                   r=[kn, ("DDT", h)], w=[AK(VA + 2 * h), AK(VA + 2 * h + 1)])
                STT(CNt[:, h, :], CNt[:, h, :], DCOL[:, h:h + 1], pn_[:, 192:449], ALU.mult, ALU.add,
                    r=[("CN", cnb, h), kn, ("DCOL", h)], w=[("CN", cnb, h)])
            for h in range(4):
                CP("pool", NBC[:, h, :], CNt[:, h, 256:257].broadcast_to([128, 128]), r=[("CN", cnb, h)], w=[("NBC", h)])
                CP("act", CNb[:, h, :], CNt[:, h, :], r=[("CN", cnb, h)], w=[("CNb", h)])
            if smp:
                DMA("pool", oCN_s[ci], CNt[:], r=[("CN", cnb, h) for h in range(4)])
        if smp:
            DMA("pool", om_s, MNEW[:], r=["MNEW"])
        elif tok0 + NT == 2064:
            DMA("pool", oCN_p, CN[0][:], r=[("CN", 0, h) for h in range(4)])
            gl = gchunk[0] % 2
            DMA("pool", om_p, MCOL[0:4, gl:gl + 1], r=["MC%d" % gl])

        TN3 = ["LW", "AA", "KKS", "SQ", "NRM", "KKN", "T1", "BB", "CUM", "EGP", "EGN", "CX"]
        hs = list(range(4))
        tq = {h: (TN3[3 * h], TN3[3 * h + 1], TN3[3 * h + 2]) for h in hs}
        pmk = {}
        for h in hs:
            c0_, c1_ = VA + 2 * h, VA + 2 * h + 1
            pm, km = PS()
            pmk[h] = (pm, km)
            MM(pm[:, 0:NT], ONES[:, :], slot(c0_)[:, 0:NT], start=True, stop=False, r=["ONES", AK(c0_)], w=[km])
            MM(pm[:, 0:NT], ONES[:, :], slot(c1_)[:, 0:NT], start=False, stop=True, r=["ONES", AK(c1_)], w=[km])
        for h in hs:
            pm, km = pmk[h]
            for c_ in (VA + 2 * h, VA + 2 * h + 1):
                STT(slot(c_)[:, 0:NT], pm[:, 0:NT], -1.0 / 256, slot(c_)[:, 0:NT], ALU.mult, ALU.add, r=[km, AK(c_)], w=[AK(c_)])
        for h in hs:
            c0_, c1_ = VA + 2 * h, VA + 2 * h + 1
            TT("pool", T[tq[h][0]][:, 0:NT], slot(c0_)[:, 0:NT], slot(c0_)[:, 0:NT], ALU.mult, r=[AK(c0_)], w=[tq[h][0]])
            TT("pool", T[tq[h][1]][:, 0:NT], slot(c1_)[:, 0:NT], slot(c1_)[:, 0:NT], ALU.mult, r=[AK(c1_)], w=[tq[h][1]])
        for h in hs:
            pv2, kv2 = PS()
            pmk[h] = (pv2, kv2)
            MM(pv2[:, 0:NT], ONES[:, :], T[tq[h][0]][:, 0:NT], start=True, stop=False, r=["ONES", tq[h][0]], w=[kv2])
            MM(pv2[:, 0:NT], ONES[:, :], T[tq[h][1]][:, 0:NT], start=False, stop=True, r=["ONES", tq[h][1]], w=[kv2])
        for h in hs:
            pv2, kv2 = pmk[h]
            ACT(T[tq[h][2]][:, 0:NT], pv2[:, 0:NT], AF.Ln, r=[kv2], w=[tq[h][2]], bias=EPS, scale=1.0 / 256)
        for h in hs:
            ACT(T[tq[h][2]][:, 0:NT], T[tq[h][2]][:, 0:NT], AF.Exp, r=[tq[h][2]], w=[tq[h][2]], scale=-0.5)
        for h in hs:
            for vc, c_ in enumerate((VA + 2 * h, VA + 2 * h + 1)):
                cc = 2 * h + vc
                STT(slot(c_)[:, 0:NT], slot(c_)[:, 0:NT], PCc(PC_MNG + cc), T[tq[h][2]][:, 0:NT], ALU.mult, ALU.mult, r=[AK(c_), tq[h][2], "PC"], w=[AK(c_)])
        for h in hs:
            for vc, c_ in enumerate((VA + 2 * h, VA + 2 * h + 1)):
                cc = 2 * h + vc
                TT("pool" if vc else "dve", slot(MERGED + cc)[:, 0:NT], slot(c_)[:, 0:NT], slot(OA + cc)[:, 0:NT], ALU.mult, r=[AK(c_), AK(OA + cc)], w=[AK(MERGED + cc)])
        P.barrier()
        if stop == "A":
            continue

        def evGB(i, pt_, pk_):
            ACT(slot(GB + i)[:, 0:NT], pt_[:, 0:NT], AF.Sigmoid, r=[pk_], w=[AK(GB + i)])

        proj_chunks(win_v, 1024, 8, NT, evGB)

        def evPB(cc, pt_, pk_):
            if cc < 8:
                dst, dk = slot(RB + cc), AK(RB + cc)
            elif cc < 16:
                dst, dk = slot(KB + cc - 8), AK(KB + cc - 8)
            elif cc < 24:
                dst, dk = slot(VB + cc - 16), AK(VB + cc - 16)
            elif cc == 24:
                dst, dk = XWA, "XWA"
            else:
                dst, dk = XG, "XG"
            if cc % 2 == 0:
                CP("act", dst[:, 0:NTB], pt_[:, 0:NTB], r=[pk_], w=[dk])
            else:
                CP("dve", dst[:, 0:NTB], pt_[:, 0:NTB], r=[pk_], w=[dk])
            ds, dsk = (T["CX"], "CX") if cc % 2 == 0 else (T["EGX"], "EGX")
            if smp:
                d3 = dst[:, 0:128].rearrange("p (s t) -> p s t", t=8)
                s3 = ds[:, 0:128].rearrange("p (s t) -> p s t", t=8)
                TT("pool", s3[:, :, 1:8], d3[:, :, 0:7], d3[:, :, 1:8], ALU.subtract, r=[dk], w=[dsk])
                TT("pool", s3[:, :, 0:1], dst[:, 128:144].unsqueeze(2), d3[:, :, 0:1], ALU.subtract, r=[dk], w=[dsk])
            else:
                TT("pool", ds[:, 1:NT], dst[:, 0:NT - 1], dst[:, 1:NT], ALU.subtract, r=[dk], w=[dsk])
                TT("pool", ds[:, 0:1], CARRY[:, cc:cc + 1], dst[:, 0:1], ALU.subtract, r=[dk, ("CARRY", cc)], w=[dsk])
                CP("pool", CARRY[:, cc:cc + 1], dst[:, NT - 1:NT], r=[dk], w=[("CARRY", cc)])
            STT(dst[:, 0:NT], ds[:, 0:NT], PCc(PC_MU + cc), dst[:, 0:NT], ALU.mult, ALU.add, r=[dsk, dk, "PC"], w=[dk])

        proj_chunks(win_v, 5128, 26, NTB, evPB)
        if stop == "B1":
            P.barrier()
            continue
        ACT(XWA[0:64, 0:NT], XWA[0:64, 0:NT], AF.Tanh, r=["XWA"], w=["XWA"])
        ACT(XG[:, 0:NT], XG[:, 0:NT], AF.Sigmoid, r=["XG"], w=["XG"])

        nchunks = len(chunks)
        last0, Lc = chunks[-1][0] + chunks[-1][1], chunks[-1][1]
        first_last = chunks[0][0] + chunks[0][1] - 1
        for j in range(8):
            Rj, Kj, Vj = slot(RB + j)[:, 0:NT], slot(KB + j)[:, 0:NT], slot(VB + j)[:, 0:NT]
            rk_, kk_, vk_ = AK(RB + j), AK(KB + j), AK(VB + j)
            cols = slice(j * 128, (j + 1) * 128)

            def t(n):
                return T[n][:, 0:NT]
            pw, kw = PS()
            MM(pw[:, 0:NT], W2A2[0:64, cols], XWA[0:64, 0:NT], r=["W2A2", "XWA"], w=[kw])
            pa, ka = PS()
            MM(pa[:, 0:NT], W2A2[64:128, cols], XWA[64:128, 0:NT], r=["W2A2", "XWA"], w=[ka])
            ACT(t("LW"), pw[:, 0:NT], AF.Exp, r=[kw, "PC"], w=["LW"], bias=PCN[:, j:j + 1], scale=-1.0)
            ACT(t("AA"), pa[:, 0:NT], AF.Exp, r=[ka, "PC"], w=["AA"], bias=PCN[:, 8 + j:9 + j], scale=-1.0)
            ACT(t("LW"), t("LW"), AF.Ln, r=["LW"], w=["LW"], bias=1.0)
            ACT(t("AA"), t("AA"), AF.Ln, r=["AA"], w=["AA"], bias=1.0)
            ACT(t("LW"), t("LW"), AF.Exp, r=["LW"], w=["LW"], scale=-1.0)
            ACT(t("AA"), t("AA"), AF.Exp, r=["AA"], w=["AA"], scale=-1.0)
            TS("pool", t("KKS"), Kj, PCc(PC_KKS + j), ALU.mult, r=[kk_, "PC"], w=["KKS"])
            TT("pool", t("SQ"), t("KKS"), t("KKS"), ALU.mult, r=["KKS"], w=["SQ"])
            pn2, kn2 = PS()
            MM(pn2[:, 0:NT], BLK[:, :], t("SQ"), r=["BLK", "SQ"], w=[kn2])
            P.op("dve", lambda h, NT=NT, ty=ty: h.tensor_tensor_scan(out=T["CUM"][:, 0:NT], data0=RST01[ty][:, 0:NT], data1=T["LW"][:, 0:NT],
                                                         initial=0.0, op0=ALU.mult, op1=ALU.add), ["LW", "RST"], ["CUM"])
            TS("dve", t("NRM"), pn2[:, 0:NT], 1e-24, ALU.max, r=[kn2], w=["NRM"])
            ACT(t("NRM"), t("NRM"), AF.Ln, r=["NRM"], w=["NRM"])
            ACT(t("NRM"), t("NRM"), AF.Exp, r=["NRM"], w=["NRM"], scale=-0.5)
            ACT(t("EGP"), t("CUM"), AF.Exp, r=["CUM"], w=["EGP"], scale=-C0)
            ACT(t("EGN"), t("CUM"), AF.Exp, r=["CUM"], w=["EGN"], scale=C0)
            TT("pool", t("CX"), t("CUM"), t("LW"), ALU.subtract, r=["CUM", "LW"], w=["CX"])
            ACT(t("EGX"), t("CX"), AF.Exp, r=["CX"], w=["EGX"], scale=-C0)
            TT("dve", t("KKN"), t("KKS"), t("NRM"), ALU.mult, r=["KKS", "NRM"], w=["KKN"])
            TS("dve", t("T1"), t("AA"), 1.0, ALU.subtract, r=["AA", "PC"], w=["T1"], s2=PCc(PC_KAS + j), op1=ALU.mult)
            STT(Kj, t("T1"), 1.0, Kj, ALU.add, ALU.mult, r=["T1", kk_], w=[kk_])
            TT("pool", t("BB"), t("KKN"), t("AA"), ALU.mult, r=["KKN", "AA"], w=["BB"])
            CP("pool", GL[:, j, 0:nchunks], T["EGP"][:, first_last:last0:Lc] if nchunks > 1 else T["EGP"][:, first_last:first_last + 1],
               r=["EGP"], w=[("GL", j)])
            bon = slot(BON + j)[:, 0:NT]
            STT(bon, Rj, PCc(PC_RK + j), Kj, ALU.mult, ALU.mult, r=[rk_, kk_, "PC"], w=[AK(BON + j)])
            pb2, kb2 = PS()
            MM(pb2[:, 0:NT], BLK[:, :], bon, r=["BLK", AK(BON + j)], w=[kb2])
            TT("dve", RTb(j)[:, 0:NT], Rj, t("EGP"), ALU.mult, r=[rk_, "EGP"], w=[("RTb", j)])
            TT("dve", KHb(j)[:, 0:NT], Kj, t("EGN"), ALU.mult, r=[kk_, "EGN"], w=[("KHb", j)])
            TT("pool", KKTb(j)[:, 0:NT], t("KKN"), t("EGX"), ALU.mult, r=["KKN", "EGX"], w=[AK(KKT + j)])
            TT("pool", BHb(j)[:, 0:NT], t("BB"), t("EGN"), ALU.mult, r=["BB", "EGN"], w=[AK(BHT + j)])
            TT("dve", bon, pb2[:, 0:NT], Vj, ALU.mult, r=[kb2, vk_], w=[AK(BON + j)])
        if stop == "B2":
            P.barrier()
            continue

        def QQ(j, rows, cs):
            return ARENA[:, KKT * W:(KKT + 16) * W].bitcast(BF16)[:, j * W:j * W + 16 * W].rearrange("p (two d) -> p two d", two=2)[rows, :, cs]

        for ci, (c0, L) in enumerate(chunks):
            cs = slice(c0, c0 + L)
            nl = {8: 3, 16: 4, 64: 6}[L]
            if DBGSTEP and ci < DBGCHUNK:
                continue
            hb = ci % 2 if smp else 0
            Ht = H[hb]
            if smp:
                DMA("sp", Ht[:], sH[ci], w=[("H", hb, 0), ("H", hb, 1)])
                CP("act", Hb[hb][:], Ht[:], r=[("H", hb, 0), ("H", hb, 1)], w=[("Hb", hb, 0), ("Hb", hb, 1)])
            def half_steps(jh, TSet):
                VTK, KHK, BHK, S1, S2, PA, PN, PTN, U, TMPH, kp = TSet
                hk = ("H", hb, jh)
                hbk = ("Hb", hb, jh)
                Hbt = Hb[hb]
                pA, kA = PS(); pB, kB = PS(); pC, kC = PS()
                pBb = pB[:, :].bitcast(BF16); pCb = pC[:, :].bitcast(BF16)
                for jj in range(4):
                    j = 4 * jh + jj
                    TR(pA[0:L, jj * 128:(jj + 1) * 128], slot(VB + j)[:, cs], 128, r=[AK(VB + j)], w=[kA])
                    TR(pBb[0:L, jj * 128:(jj + 1) * 128], KHb(j)[:, cs], 128, r=[("KHb", j)], w=[kB], bf=True)
                    TR(pCb[0:L, jj * 128:(jj + 1) * 128], BHb(j)[:, cs], 128, r=[AK(BHT + j)], w=[kC], bf=True)
                CP("act", VTK[0:L, :], pA[0:L, :], r=[kA], w=[(kp, "VTK")])
                CP("dve", KHK[0:L, :], pBb[0:L, 0:512], r=[kB], w=[(kp, "KHK")])
                ACT(BHK[0:L, :], pCb[0:L, 0:512], AF.Identity, r=[kC], w=[(kp, "BHK")], scale=-1.0)

                yield
                def hd(hq):
                    hp, jj = divmod(hq, 4)
                    return 4 * jh + jj, jj, hp, slice(64 * hp, 64 * hp + 64), slice(jj * 128 + hp * 64, jj * 128 + hp * 64 + 64)
                b1 = [PS(), PS()]
                for hq in range(8):
                    j, jj, hp, rows, tc = hd(hq)
                    MM(b1[hp][0][0:L, jj * 2 * L:(jj + 1) * 2 * L], KHb(j)[rows, cs], QQ(j, rows, cs),
                       r=[("KHb", j), AK(KKT + j), ("RTb", j)], w=[b1[hp][1]])
                yield
                for hp in range(2):
                    mk = MK1[0:L, :, 0:L].unsqueeze(1).broadcast_to([L, 4, 2, L])
                    TT("dve", S1[0:L, 4 * hp:4 * hp + 4, :, 0:L],
                       b1[hp][0][0:L, 0:8 * L].rearrange("p (a b c) -> p a b c", a=4, b=2), mk, ALU.mult,
                       r=[b1[hp][1], "MK1"], w=[(kp, "S1")])
                yield
                b2 = [PS(), PS()]
                for hq in range(8):
                    j, jj, hp, rows, tc = hd(hq)
                    MM(b2[hp][0][0:L, jj * 2 * L:(jj + 1) * 2 * L], BHb(j)[rows, cs], QQ(j, rows, cs),
                       r=[AK(BHT + j), AK(KKT + j), ("RTb", j)], w=[b2[hp][1]])
                for hp in range(2):
                    mkn = MK1N[0:L, :, 0:L].unsqueeze(1).broadcast_to([L, 4, 2, L])
                    TT("dve", S2[0:L, 4 * hp:4 * hp + 4, :, 0:L],
                       b2[hp][0][0:L, 0:8 * L].rearrange("p (a b c) -> p a b c", a=4, b=2), mkn, ALU.mult,
                       r=[b2[hp][1], "MK1N"], w=[(kp, "S2")])
                yield
                p3 = [PS(), PS()]
                for hq in range(8):
                    j, jj, hp, rows, tc = hd(hq)
                    MM(p3[hp][0][0:L, jj * L:(jj + 1) * L], KKTb(j)[rows, cs], BHb(j)[rows, cs],
                       r=[AK(KKT + j), AK(BHT + j)], w=[p3[hp][1]])
                for hp in range(2):
                    TT("dve", PA[0:L, 4 * hp:4 * hp + 4, 0:L], p3[hp][0][0:L, 0:4 * L].rearrange("p (a b) -> p a b", a=4),
                       MLN[0:L, 0:L].unsqueeze(1).broadcast_to([L, 4, L]), ALU.mult, r=[p3[hp][1], "MLN"], w=[(kp, "PA")])
                yield
                pU2 = [PS(), PS()]
                for hq in range(8):
                    j, jj, hp, rows, tc = hd(hq)
                    MM(pU2[hp][0][0:L, jj * 64:(jj + 1) * 64], KKTb(j)[rows, cs], Hbt[rows, j, :], start=(jj == 0), stop=False,
                       r=[AK(KKT + j), hbk], w=[pU2[hp][1]])
                for hq in range(8):
                    j, jj, hp, rows, tc = hd(hq)
                    MM(pU2[hp][0][0:L, jj * 64:(jj + 1) * 64], S1[0:L, hq, 0, 0:L], VTK[0:L, tc], start=False, stop=(jj == 3),
                       r=[(kp, "S1"), (kp, "VTK")], w=[pU2[hp][1]], strict=(hp == 1 and jj == 0))
                CP("act", U[0][0:L, 0:256], pU2[0][0][0:L, 0:256], r=[pU2[0][1]], w=[(kp, "U", 0)])
                CP("act", U[0][0:L, 256:512], pU2[1][0][0:L, 0:256], r=[pU2[1][1]], w=[(kp, "U", 0)])
                yield
                cur = 0
                Pt, Pk = PA, (kp, "PA")
                PTt, PTk = S2, (kp, "S2")

                def PTv(hq):
                    return PTt[0:L, hq, 0, 0:L] if PTk == (kp, "S2") else PTt[0:L, hq, 0:L]
                for l in range(nl):
                    pU, kU = PS()
                    for hq in range(8):
                        MM(pU[0:L, hq * 64:(hq + 1) * 64], PTv(hq), U[cur][0:L, hq * 64:(hq + 1) * 64], r=[PTk, (kp, "U", cur)], w=[kU])
                    TT("dve", U[1 - cur][0:L, :], U[cur][0:L, :], pU[0:L, :], ALU.add, r=[(kp, "U", cur), kU], w=[(kp, "U", 1 - cur)])
                    cur = 1 - cur
                    if l < nl - 1:
                        need_p = (l < nl - 2)
                        pT, kT = PS()
                        for hq in range(8):
                            MM(pT[0:L, hq * L:(hq + 1) * L], Pt[0:L, hq, 0:L], PTv(hq), r=[PTk, Pk], w=[kT])
                        nP, nPT = PN[l % 2], PTN[l % 2]
                        if need_p:
                            pP, kP = PS()
                            for hq in range(8):
                                MM(pP[0:L, hq * L:(hq + 1) * L], PTv(hq), Pt[0:L, hq, 0:L], r=[PTk, Pk], w=[kP])
                            CP("act", nP[0:L, :, 0:L], pP[0:L, 0:8 * L].rearrange("p (a b) -> p a b", a=8), r=[kP], w=[(kp, "PN", l % 2)])
                        CP("dve", nPT[0:L, :, 0:L], pT[0:L, 0:8 * L].rearrange("p (a b) -> p a b", a=8), r=[kT], w=[(kp, "PTN", l % 2)])
                        Pt, Pk = nP, (kp, "PN", l % 2)
                        PTt, PTk = nPT, (kp, "PTN", l % 2)
                    yield
                yield
                pY2 = [PS(), PS()]
                for hq in range(8):
                    j, jj, hp, rows, tc = hd(hq)
                    o_ = pY2[hp][0][rows, jj * L:(jj + 1) * L]
                    MM(o_, Hbt[rows, j, :], RTb(j)[rows, cs], start=(jj == 0), stop=False, r=[hbk, ("RTb", j)], w=[pY2[hp][1]])
                for hq in range(8):
                    j, jj, hp, rows, tc = hd(hq)
                    o_ = pY2[hp][0][rows, jj * L:(jj + 1) * L]
                    MM(o_, VTK[0:L, tc], S1[0:L, hq, 1, 0:L], start=False, stop=False, r=[(kp, "VTK"), (kp, "S1")], w=[pY2[hp][1]], strict=(hp == 1 and jj == 0))
                    MM(o_, U[cur][0:L, hq * 64:(hq + 1) * 64], S2[0:L, hq, 1, 0:L], start=False, stop=(jj == 3), r=[(kp, "U", cur), (kp, "S2")], w=[pY2[hp][1]])
                pH, kH = PS()
                for hq in range(8):
                    j, jj, hp, rows, tc = hd(hq)
                    o_ = pH[rows, jj * 64:(jj + 1) * 64]
                    MM(o_, KHK[0:L, tc], VTK[0:L, tc], start=True, stop=False, r=[(kp, "KHK"), (kp, "VTK")], w=[kH])
                    MM(o_, BHK[0:L, tc], U[cur][0:L, hq * 64:(hq + 1) * 64], start=False, stop=True, r=[(kp, "BHK"), (kp, "U", cur)], w=[kH])
                yield
                for hp in range(2):
                    rows = slice(64 * hp, 64 * hp + 64)
                    CP("act", slots(RB + 4 * jh, 4)[rows, :, cs], pY2[hp][0][rows, 0:4 * L].rearrange("p (a b) -> p a b", a=4), r=[pY2[hp][1]],
                       w=[AK(RB + 4 * jh + q) for q in range(4)])
                TT("dve", TMPH[:], Ht[:, 4 * jh:4 * jh + 4, :], pH[:, 0:256].rearrange("p (a b) -> p a b", a=4), ALU.add, r=[hk, kH], w=[(kp, "TMPH")])
                TT("pool", Ht[:, 4 * jh:4 * jh + 4, :], TMPH[:], GL[:, 4 * jh:4 * jh + 4, ci:ci + 1].broadcast_to([128, 4, 64]), ALU.mult,
                   r=[(kp, "TMPH")] + [("GL", 4 * jh + q) for q in range(4)], w=[hk])
                CP("act", Hbt[:, 4 * jh:4 * jh + 4, :], Ht[:, 4 * jh:4 * jh + 4, :], r=[hk], w=[hbk])

            setA = (VTK, KHK, BHK, S1, S2, PA, PN, PTN, U, TMPH, "A")
            if smp:
                gens = [half_steps(0, setA), half_steps(1, setB)]
                while gens:
                    for g_ in list(gens):
                        try:
                            next(g_)
                        except StopIteration:
                            gens.remove(g_)
            else:
                for jh in range(2):
                    for _ in half_steps(jh, setA):
                        pass
            if smp:
                DMA("pool", oH_s[ci], Ht[:], r=[("H", hb, 0), ("H", hb, 1)])
            if DBGSTEP and ci >= DBGCHUNK:
                break
        if (not smp) and tok0 + NT == 2064:
            DMA("pool", oH_p, H[0][:], r=[("H", 0, 0), ("H", 0, 1)])

        if stop == "B3":
            P.barrier()
            continue
        TN3 = ["LW", "AA", "KKS", "SQ", "NRM", "KKN", "T1", "BB", "CUM", "EGP", "EGN", "CX"]
        for g0 in (0, 4):
            js = list(range(g0, g0 + 4))
            tq = {j: (TN3[3 * (j - g0)], TN3[3 * (j - g0) + 1], TN3[3 * (j - g0) + 2]) for j in js}
            pk_ = {}
            for j in js:
                pm, km = PS()
                pk_[j] = (pm, km)
                MM(pm[:, 0:NT], BLK[:, :], slot(RB + j)[:, 0:NT], r=["BLK", AK(RB + j)], w=[km])
            for j in js:
                pm, km = pk_[j]
                Yj, yk = slot(RB + j)[:, 0:NT], AK(RB + j)
                STT(Yj, pm[:, 0:NT], -1.0 / 64, Yj, ALU.mult, ALU.add, r=[km, yk], w=[yk])
            for j in js:
                Yj, yk = slot(RB + j)[:, 0:NT], AK(RB + j)
                TT("pool", T[tq[j][0]][:, 0:NT], Yj, Yj, ALU.mult, r=[yk], w=[tq[j][0]])
            for j in js:
                pv2, kv2 = PS()
                pk_[j] = (pv2, kv2)
                MM(pv2[:, 0:NT], BLK[:, :], T[tq[j][0]][:, 0:NT], r=["BLK", tq[j][0]], w=[kv2])
            for j in js:
                pv2, kv2 = pk_[j]
                ACT(T[tq[j][1]][:, 0:NT], pv2[:, 0:NT], AF.Ln, r=[kv2], w=[tq[j][1]], bias=GN_EPS, scale=1.0 / 64)
            for j in js:
                ACT(T[tq[j][1]][:, 0:NT], T[tq[j][1]][:, 0:NT], AF.Exp, r=[tq[j][1]], w=[tq[j][1]], scale=-0.5)
            for j in js:
                pg, kg = PS()
                pk_[j] = (pg, kg)
                MM(pg[:, 0:NT], G2[:, j * 128:(j + 1) * 128], XG[:, 0:NT], r=["G2", "XG"], w=[kg])
            for j in js:
                Yj, yk = slot(RB + j)[:, 0:NT], AK(RB + j)
                t1 = T[tq[j][2]][:, 0:NT]
                STT(t1, Yj, PCc(PC_LXG + j), T[tq[j][1]][:, 0:NT], ALU.mult, ALU.mult, r=[yk, tq[j][1], "PC"], w=[tq[j][2]])
                STT(t1, t1, PCc(PC_LXB + j), slot(BON + j)[:, 0:NT], ALU.add, ALU.add, r=[tq[j][2], AK(BON + j), "PC"], w=[tq[j][2]])
            for j in js:
                pg, kg = pk_[j]
                t1 = T[tq[j][2]][:, 0:NT]
                TT("dve", t1, t1, pg[:, 0:NT], ALU.mult, r=[tq[j][2], kg], w=[tq[j][2]])
            for j in js:
                t1 = T[tq[j][2]][:, 0:NT]
                TT("pool", t1, t1, slot(GB + j)[:, 0:NT], ALU.mult, r=[tq[j][2], AK(GB + j)], w=[tq[j][2]])
                TT("pool", slot(MERGED + j)[:, 0:NT], slot(MERGED + j)[:, 0:NT], t1, ALU.add, r=[tq[j][2], AK(MERGED + j)], w=[AK(MERGED + j)])
        P.barrier()
        if stop == "B":
            continue

        def big_out(wv, nrowch, lhs_of, resid):
            for cp_ in range((nrowch + 3) // 4):
                wb, wk = nextWB()
                wbv2 = wb[:, :].rearrange("p (c d) -> p c d", c=4)
                ncc = min(4, nrowch - 4 * cp_)
                DMA("sp", wbv2[:, 0:ncc, :], wv[:, 4 * cp_:4 * cp_ + ncc, :], w=[wk])
                for ci_ in range(ncc):
                    c = 4 * cp_ + ci_
                    for ti, (col0, n, _) in enumerate(tiles):
                        for half in range(2):
                            MM(ps[2 * ti + half][0:n, 0:512], lhs_of(c)[:, col0:col0 + n], wbv2[:, ci_, half * 512:(half + 1) * 512],
                               start=(c == 0), stop=False, r=[wk] + resid[1], w=[("ps", 2 * ti + half)])
            for ti, (col0, n, _) in enumerate(tiles):
                for c in range(8):
                    o_ = ps[2 * ti + c // 4][0:n, (c % 4) * 128:(c % 4 + 1) * 128]
                    MM(o_, XTb[:, c, col0:col0 + n], IDb[:, :], start=False, stop=False, r=["XTb", "ID"], w=[("ps", 2 * ti + c // 4)])
                    MM(o_, XTlo[:, c, col0:col0 + n], IDb[:, :], start=False, stop=(c % 4 == 3), r=["XTlo", "ID"], w=[("ps", 2 * ti + c // 4)])

        MRGb = ARENA[:, 52 * W:56 * W].bitcast(BF16).rearrange("p (c w) -> p c w", c=8)
        TS("pool", MRGb[:, :, 0:NT], slots(MERGED, 8)[:, :, 0:NT], 1.0 / ALPHA, ALU.mult, r=[AK(MERGED + c) for c in range(8)], w=["MRGb"])
        big_out(wout_v, 8, lambda c: MRGb[:, c, :], (None, ["MRGb"]))
        for ti, (col0, n, _) in enumerate(tiles):
            Tt, tkey = TOK[ti % 2], ("TOK", ti % 2)
            CP("act", Tt[0:n, 0:512], ps[2 * ti][0:n, :], r=[("ps", 2 * ti)], w=[tkey])
            CP("dve", Tt[0:n, 512:1024], ps[2 * ti + 1][0:n, :], r=[("ps", 2 * ti + 1)], w=[tkey])
            ln_stats(Tt, n, tkey, eps=EPS / (ALPHA * ALPHA))
            to_feature_major(Tt, n, tkey, col0, PC_L1G, PC_L1B, banks=(6, 7))
        CP("pool", XTb[:, :, 0:NT], XT[:, :, 0:NT], r=["XT"], w=["XTb"])
        TT("dve", XTlo[:, :, 0:NT], XT[:, :, 0:NT], XTb[:, :, 0:NT], ALU.subtract, r=["XT", "XTb"], w=["XTlo"])

        def evAG(i, pt_, pk_):
            if smp:
                d = slot(AG + i)[:, 0:160].rearrange("p (s t) -> p s t", t=10)[:, :, 2:10]
                CP("act", d, pt_[:, 0:128].rearrange("p (s t) -> p s t", t=8), r=[pk_], w=[AK(AG + i)])
            else:
                CP("act", slot(AG + i)[:, 2:2 + NT], pt_[:, 0:NT], r=[pk_], w=[AK(AG + i)])

        def evAV(i, pt_, pk_):
            CP("dve", slot(AV + i)[:, 0:NT], pt_[:, 0:NT], r=[pk_], w=[AK(AV + i)])

        proj_chunks(wup_v, 0, 22, NT, evAG)
        proj_chunks(wup_v, DF, 22, NT, evAV)
        _tn = ["LW", "AA", "KKS", "SQ", "NRM", "KKN", "T1", "BB", "CUM", "EGP", "EGN", "CX"]
        for g0 in range(0, 22, 6):
            idx = list(range(g0, min(g0 + 6, 22)))
            tk = {i: (_tn[2 * (i - g0)], _tn[2 * (i - g0) + 1]) for i in idx}
            for i in idx:
                ag, agk = slot(AG + i), AK(AG + i)
                if smp:
                    a3 = ag[:, 0:160].rearrange("p (s t) -> p s t", t=10)
                    CP("pool", a3[:, :, 0:2], CV0[:, i, :, :], r=[("CV0", i)], w=[agk])
                    CP("pool", CV0[:, i, :, :], a3[:, :, 8:10], r=[agk], w=[("CV0", i)])
                else:
                    CP("pool", ag[:, 0:2], AGC[:, i, :], r=[("AGC", i)], w=[agk])
                    CP("pool", AGC[:, i, :], ag[:, NT:NT + 2], r=[agk], w=[("AGC", i)])
            for i in idx:
                ag, agk = slot(AG + i), AK(AG + i)
                cvk, g1k = tk[i]
                cv = T[cvk]
                if smp:
                    a3 = ag[:, 0:160].rearrange("p (s t) -> p s t", t=10)
                    cv3 = cv[:, 0:128].rearrange("p (s t) -> p s t", t=8)
                    TS("dve", cv3, a3[:, :, 0:8], PCc(PC_CW0 + i), ALU.mult, r=[agk, "PC"], w=[cvk], s2=PCc(PC_CB + i), op1=ALU.add)
                    STT(cv3, a3[:, :, 1:9], PCc(PC_CW1 + i), cv3, ALU.mult, ALU.add, r=[agk, cvk, "PC"], w=[cvk])
                    STT(cv3, a3[:, :, 2:10], PCc(PC_CW2 + i), cv3, ALU.mult, ALU.add, r=[agk, cvk, "PC"], w=[cvk])
                else:
                    ACT(cv[:, 0:NT], ag[:, 0:NT], AF.Identity, r=[agk, "PC"], w=[cvk], bias=PCc(PC_CB + i), scale=PCc(PC_CW0 + i))
                    STT(cv[:, 0:NT], ag[:, 1:NT + 1], PCc(PC_CW1 + i), cv[:, 0:NT], ALU.mult, ALU.add, r=[agk, cvk, "PC"], w=[cvk])
                    STT(cv[:, 0:NT], ag[:, 2:NT + 2], PCc(PC_CW2 + i), cv[:, 0:NT], ALU.mult, ALU.add, r=[agk, cvk, "PC"], w=[cvk])
            for i in idx:
                cvk, g1k = tk[i]
                ACT(T[g1k][:, 0:NT], T[cvk][:, 0:NT], AF.Square, r=[cvk], w=[g1k])
            for i in idx:
                cvk, g1k = tk[i]
                TS("dve", T[g1k][:, 0:NT], T[g1k][:, 0:NT], 0.044715, ALU.mult, r=[g1k], w=[g1k], s2=1.0, op1=ALU.add)
            for i in idx:
                cvk, g1k = tk[i]
                TT("pool", T[g1k][:, 0:NT], T[g1k][:, 0:NT], T[cvk][:, 0:NT], ALU.mult, r=[g1k, cvk], w=[g1k])
            for i in idx:
                cvk, g1k = tk[i]
                ACT(T[g1k][:, 0:NT], T[g1k][:, 0:NT], AF.Sigmoid, r=[g1k], w=[g1k], scale=GELU_K)
            for i in idx:
                cvk, g1k = tk[i]
                TT("pool", T[g1k][:, 0:NT], T[g1k][:, 0:NT], T[cvk][:, 0:NT], ALU.mult, r=[g1k, cvk], w=[g1k])
            for i in idx:
                cvk, g1k = tk[i]
                STT(slot(AG + i).bitcast(BF16)[:, 0:NT], T[g1k][:, 0:NT], 1.0 / ALPHA, slot(AV + i)[:, 0:NT], ALU.mult, ALU.mult,
                    r=[g1k, AK(AV + i), cvk], w=[AK(AG + i)])
        if smp:
            DMA("pool", ocv_s, CV0[:], r=[("CV0", i) for i in range(22)])
        elif tok0 + NT == 2064:
            DMA("pool", ocv_p, AGC[:], r=[("AGC", i) for i in range(22)])

        big_out(wdn_v, 22, lambda c: slot(AG + c).bitcast(BF16), (None, [AK(AG + c) for c in range(22)]))
        for ti, (col0, n, _) in enumerate(tiles):
            Tt, tkey = TOK[ti % 2], ("TOK", ti % 2)
            CP("act", Tt[0:n, 0:512], ps[2 * ti][0:n, :], r=[("ps", 2 * ti)], w=[tkey])
            CP("dve", Tt[0:n, 512:1024], ps[2 * ti + 1][0:n, :], r=[("ps", 2 * ti + 1)], w=[tkey])
            ln_stats(Tt, n, tkey, eps=EPS / (ALPHA * ALPHA))
            TT("pool", Tt[0:n, :], Tt[0:n, :], LNG[0:n, :], ALU.mult, r=[tkey, "LNG"], w=[tkey])
            TT("dve", Tt[0:n, :], Tt[0:n, :], LNB[0:n, :], ALU.add, r=[tkey, "LNB"], w=[tkey])
            if smp:
                DMA("pool", oy_s, Tt[0:128, :], r=[tkey])
            else:
                t_lo = tok0 + col0
                if t_lo < 16:
                    DMA("pool", oy_p[0:n - (16 - t_lo), :], Tt[16 - t_lo:n, :], r=[tkey])
                else:
                    DMA("pool", oy_p[t_lo - 16:t_lo - 16 + n, :], Tt[0:n, :], r=[tkey])
        P.barrier()

    P.emit(nc)
    st.close()
    return nc


_NC_CACHE = {}


def _host_inputs(inp, b):
    f = lambda a: np.ascontiguousarray(a, dtype=np.float32)
    s0, s1 = 16 * b, 16 * b + 16
    prm0 = np.concatenate([inp["rwkv_mu"][0], inp["rwkv_w0"][0], inp["rwkv_a0"][0], inp["rwkv_kk_scale"][0], inp["rwkv_ka_scale"][0],
                           inp["rwkv_rk"][0], inp["rwkv_lnx_g"][0], inp["rwkv_lnx_b"][0], inp["mlstm_norm_g"][0],
                           inp["ln_in_g"], inp["ln_in_b"], inp["ln1_g"][0], inp["ln1_b"][0]]).reshape(122, 128)
    cw = inp["ffn_conv_w"][0]
    prm1 = np.concatenate([cw[0], cw[1], cw[2], inp["ffn_conv_b"][0]]).reshape(88, 128)
    sS = inp["state_rwkv_S"][0, s0:s1]
    sH = sS.reshape(16, 8, 2, 64, 64).transpose(0, 2, 4, 1, 3).reshape(16, 128, 8, 64)
    ssh = inp["state_rwkv_shift"][0, s0:s1].reshape(16, 8, 128).transpose(2, 1, 0)
    scv = inp["state_ffn_conv"][0, s0:s1].reshape(16, 2, 22, 128).transpose(3, 2, 0, 1)
    return {
        "xp": f(inp["x_prompt"][b]), "xs": f(inp["x_sample"][s0:s1].reshape(128, D)), "meta": f(inp["meta_tokens"]),
        "sC": f(inp["state_mlstm_C"][0, s0:s1]), "sn": f(inp["state_mlstm_n"][0, s0:s1].transpose(0, 2, 1)),
        "sm": f(inp["state_mlstm_m"][0, s0:s1].T), "sH": f(sH), "ssh": f(ssh), "scv": f(scv),
        "prm0": f(prm0), "prm1": f(prm1), "bif": f(inp["b_if"][0].reshape(8, 1)),
        "ln2g": f(inp["ln2_g"][0]), "ln2b": f(inp["ln2_b"][0]),
        "w_in": f(inp["w_in"][0]), "w2a2": f(np.concatenate([inp["rwkv_w2"][0], inp["rwkv_a2"][0]], 0)), "g2": f(inp["rwkv_g2"][0]),
        "w_out": f(inp["w_out"][0]), "w_up": f(inp["ffn_w_up"][0]), "w_down": f(inp["ffn_w_down"][0]),
    }


def kernel(**inputs):
    inp = {k: np.asarray(v) for k, v in inputs.items()}
    if "nc" not in _NC_CACHE:
        _NC_CACHE["nc"] = build()
    nc = _NC_CACHE["nc"]
    in_maps = [_host_inputs(inp, b) for b in range(8)]
    res = run_bass_kernel_spmd(nc, in_maps, core_ids=list(range(8))).results
    g = lambda k: [np.asarray(r[k], dtype=np.float32) for r in res]
    y_p = np.stack(g("oy_p"), 0)
    y_s = np.concatenate(g("oy_s"), 0).reshape(128, 8, D)
    cn_p = np.stack(g("oCN_p"), 0)
    pC = cn_p[..., 0:256].transpose(0, 2, 1, 3)[None]
    pn = cn_p[..., 256].transpose(0, 2, 1)[None]
    pm = np.stack(g("om_p"), 0)[:, :, 0][None]
    Hp = np.stack(g("oH_p"), 0)
    pS = Hp.reshape(8, 2, 64, 8, 64).transpose(0, 3, 1, 4, 2).reshape(8, 16, 64, 64)[None]
    psh = np.stack(g("osh_p"), 0)[..., 0].transpose(0, 2, 1).reshape(8, D)[None]
    pcv = np.stack(g("ocv_p"), 0).transpose(0, 3, 2, 1).reshape(8, 2, DF)[None]
    cn_s = np.concatenate(g("oCN_s"), 0)
    sC = cn_s[..., 0:256].transpose(0, 2, 1, 3)[None]
    sn = cn_s[..., 256].transpose(0, 2, 1)[None]
    sm = np.concatenate([a.T for a in g("om_s")], 0)[None]
    Hs = np.concatenate(g("oH_s"), 0)
    sS = Hs.reshape(128, 2, 64, 8, 64).transpose(0, 3, 1, 4, 2).reshape(128, 16, 64, 64)[None]
    ssh = np.concatenate([a.transpose(2, 1, 0).reshape(16, D) for a in g("osh_s")], 0)[None]
    scv = np.concatenate([a.transpose(2, 3, 1, 0).reshape(16, 2, DF) for a in g("ocv_s")], 0)[None]
    c = lambda a: np.ascontiguousarray(a, dtype=np.float32)
    return (c(y_p), c(y_s), c(pC), c(pn), c(pm), c(pS), c(psh), c(pcv), c(sC), c(sn), c(sm), c(sS), c(ssh), c(scv))
```

```python
import contextlib
import os
import numpy as np
DBGSTEP = int(os.environ.get("DBGSTEP", "0"))
DBGSUB = int(os.environ.get("DBGSUB", "0"))
DBGCHUNK = int(os.environ.get("DBGCHUNK", "0"))
import concourse.bass as bass
import concourse.mybir as mybir
from concourse.bass_utils import run_bass_kernel_spmd

F32 = mybir.dt.float32
BF16 = mybir.dt.bfloat16
AF = mybir.ActivationFunctionType
ALU = mybir.AluOpType

NS_DMA = 6
ENGS = ("pe", "act", "dve", "pool", "sp")


class Op:
    __slots__ = ("eng", "fn", "deps", "dma", "signal", "val", "slot", "strict")


class Prog:
    def __init__(self):
        self.ops = {e: [] for e in ENGS}
        self.lastw = {}
        self.readers = {}
        self.pend = {e: [] for e in ENGS}
        self.dmaq = {e: [] for e in ENGS}

    def op(self, eng, fn, r=(), w=(), dma=False, strict=False):
        o = Op()
        o.strict = strict
        o.eng, o.fn, o.dma, o.signal, o.val, o.slot = eng, fn, dma, False, 0, 0
        deps = set(self.pend[eng])
        self.pend[eng] = []
        for k in r:
            lw = self.lastw.get(k)
            if lw is not None:
                deps.add(lw)
        for k in w:
            lw = self.lastw.get(k)
            if lw is not None:
                deps.add(lw)
            deps.update(self.readers.get(k, ()))
        if dma:
            q = self.dmaq[eng]
            n = len(q)
            o.slot = n % NS_DMA
            o.val = 16 * (n // NS_DMA + 1)
            if n >= NS_DMA:
                deps.add(q[n - NS_DMA])
            q.append(o)
        o.deps = deps
        for k in r:
            lst = self.readers.setdefault(k, [])
            if not dma:
                lst[:] = [x for x in lst if x.dma or x.eng != eng]
            lst.append(o)
        for k in w:
            self.lastw[k] = o
            self.readers[k] = []
        self.ops[eng].append(o)
        return o

    def barrier(self):
        lasts = []
        for e in ENGS:
            comp = [x for x in self.ops[e] if not x.dma]
            if comp:
                lasts.append(comp[-1])
            lasts.extend(self.dmaq[e][-NS_DMA:])
        for e in ENGS:
            self.pend[e].extend(lasts)
        self.lastw.clear()
        self.readers.clear()

    def emit(self, nc):
        for e in ENGS:
            for o in self.ops[e]:
                for d in o.deps:
                    if not d.dma and not (d.eng == "pe" and e == "pe" and not o.strict):
                        d.signal = True
        for e in ENGS:
            c = 0
            for o in self.ops[e]:
                if not o.dma and o.signal:
                    c += 1
                    o.val = c
        with contextlib.ExitStack() as st:
            csem = {e: st.enter_context(nc.semaphore("c_" + e)) for e in ENGS}
            dsem = {e: [st.enter_context(nc.semaphore("d_%s%d" % (e, i))) for i in range(NS_DMA)]
                    for e in ENGS if self.dmaq[e]}
            block = st.enter_context(nc.Block())

            def semof(o):
                return dsem[o.eng][o.slot] if o.dma else csem[o.eng]

            def run(e, h):
                waited = {}
                for o in self.ops[e]:
                    need = {}
                    for d in o.deps:
                        if d.eng == "pe" and e == "pe" and not d.dma and not o.strict:
                            continue
                        sm = semof(d)
                        if waited.get(sm, 0) < d.val and need.get(sm, 0) < d.val:
                            need[sm] = d.val
                    for sm, v in need.items():
                        h.wait_ge(sm, v)
                        waited[sm] = v
                    ins = o.fn(h)
                    if o.dma:
                        ins.then_inc(semof(o), 16)
                    elif o.signal:
                        ins.then_inc(csem[e], 1)
                for o in self.dmaq[e][-NS_DMA:]:
                    if waited.get(semof(o), 0) < o.val:
                        h.wait_ge(semof(o), o.val)
                        waited[semof(o)] = o.val

            @block.tensor
            def _(h):
                run("pe", h)

            @block.scalar
            def _(h):
                run("act", h)

            @block.vector
            def _(h):
                run("dve", h)

            @block.gpsimd
            def _(h):
                run("pool", h)

            @block.sync
            def _(h):
                run("sp", h)


D = 1024
DF = 2816
W = 274
EPS = 1e-5
GN_EPS = 64e-5
ALPHA = float(2.0 ** 0.25)
C0 = float(np.exp(-0.5))
NEG = -1.0e30
GELU_K = float(2.0 * np.sqrt(2.0 / np.pi))

MERGED = 0
QA, KA, VA, OA, GA = 8, 12, 16, 24, 32
GB, KKT, RB, KB, VB, BHT, BON = 8, 16, 24, 32, 40, 48, 56
AG, AV = 8, 30
NSLOT = 64

PC_MU, PC_W0, PC_A0, PC_KKS, PC_KAS, PC_RK, PC_LXG, PC_LXB, PC_MNG = 0, 26, 34, 42, 50, 58, 66, 74, 82
PC_LIG, PC_LIB, PC_L1G, PC_L1B = 90, 98, 106, 114
PC_CW0, PC_CW1, PC_CW2, PC_CB = 128, 150, 172, 194

BLOCKS = [("p", 0, 272, [(0, 16), (16, 64), (80, 64), (144, 64), (208, 64)], 0)]
for _i in range(1, 8):
    BLOCKS.append(("p", 272 + 256 * (_i - 1), 256, [(64 * c, 64) for c in range(4)], 1))
BLOCKS.append(("s", 0, 128, [(8 * c, 8) for c in range(16)], 2))


def build(blocks=BLOCKS, stop=None):
    nc = bass.Bass("TRN2", target_bir_lowering=False)

    def din(name, shape):
        return nc.dram_tensor(name, list(shape), F32, kind="ExternalInput").ap()

    def dout(name, shape):
        return nc.dram_tensor(name, list(shape), F32, kind="ExternalOutput").ap()

    xp = din("xp", [2048, D]); xs = din("xs", [128, D]); meta = din("meta", [16, D])
    sC = din("sC", [16, 4, 128, 256]); sn = din("sn", [16, 128, 4]); sm = din("sm", [4, 16])
    sH = din("sH", [16, 128, 8, 64]); ssh = din("ssh", [128, 8, 16]); scv = din("scv", [128, 22, 16, 2])
    prm0 = din("prm0", [122, 128]); prm1 = din("prm1", [88, 128])
    bif = din("bif", [8, 1]); ln2g = din("ln2g", [D]); ln2b = din("ln2b", [D])
    w_in = din("w_in", [D, 8456]); w2a2 = din("w2a2", [128, D]); g2 = din("g2", [128, D])
    w_out = din("w_out", [D, D]); w_up = din("w_up", [D, 2 * DF]); w_down = din("w_down", [DF, D])

    oy_p = dout("oy_p", [2048, D]); oy_s = dout("oy_s", [128, D])
    oCN_p = dout("oCN_p", [128, 4, 257]); om_p = dout("om_p", [4, 1]); oH_p = dout("oH_p", [128, 8, 64])
    osh_p = dout("osh_p", [128, 8, 1]); ocv_p = dout("ocv_p", [128, 22, 2])
    oCN_s = dout("oCN_s", [16, 128, 4, 257]); om_s = dout("om_s", [4, 16]); oH_s = dout("oH_s", [16, 128, 8, 64])
    osh_s = dout("osh_s", [128, 8, 16]); ocv_s = dout("ocv_s", [128, 22, 16, 2])

    wb_in = nc.dram_tensor("wb_in", [D, 8456], BF16).ap(); wb_out = nc.dram_tensor("wb_out", [D, D], BF16).ap()
    wb_up = nc.dram_tensor("wb_up", [D, 2 * DF], BF16).ap(); wb_down = nc.dram_tensor("wb_down", [DF, D], BF16).ap()
    win_v = wb_in.rearrange("(kc p) c -> p kc c", p=128)
    wup_v = wb_up.rearrange("(kc p) c -> p kc c", p=128)
    wout_v = wb_out.rearrange("(c p) d -> p c d", p=128)
    wdn_v = wb_down.rearrange("(c p) d -> p c d", p=128)

    P = Prog()
    st = contextlib.ExitStack()

    def sb(name, shape):
        return st.enter_context(nc.sbuf_tensor(name, list(shape), F32))

    ARENA = sb("ARENA", [128, NSLOT * W])
    XT = sb("XT", [128, 8, 272])
    TOK = [sb("TOK0", [128, D]), sb("TOK1", [128, D])]
    LNG = sb("LNG", [128, D]); LNB = sb("LNB", [128, D])
    WB = [st.enter_context(nc.sbuf_tensor("WB%d" % i, [128, 4096], BF16)) for i in range(2)]
    XTb = st.enter_context(nc.sbuf_tensor("XTb", [128, 8, 272], BF16))
    XTlo = st.enter_context(nc.sbuf_tensor("XTlo", [128, 8, 272], BF16))
    PC = sb("PC", [128, 216]); PCN = sb("PCN", [128, 16])
    W2A2 = sb("W2A2", [128, D]); G2 = sb("G2", [128, D])
    ID = sb("ID", [128, 128]); ONES = sb("ONES", [128, 128]); BLK = sb("BLK", [128, 128]); AID = sb("AID", [128, 128])
    MK1 = sb("MK1", [64, 2, 64]); MK1N = sb("MK1N", [64, 2, 64]); MLN = sb("MLN", [64, 64]); MNEG = sb("MNEG", [64, 64])
    SEL = sb("SEL", [4, 4, 128])
    RST01_ = sb("RST01", [128, W])
    RST01 = [RST01_, RST01_, RST01_]
    RSTN = sb("RSTN", [4, W])
    CN = [sb("CN0", [128, 4, 257]), sb("CN1", [128, 4, 257])]
    NBC = st.enter_context(nc.sbuf_tensor("NBC", [128, 4, 128], BF16))
    CNb = st.enter_context(nc.sbuf_tensor("CNb", [128, 4, 257], BF16))
    Hb = [st.enter_context(nc.sbuf_tensor("Hb0", [128, 8, 64], BF16)), st.enter_context(nc.sbuf_tensor("Hb1", [128, 8, 64], BF16))]
    IDb = st.enter_context(nc.sbuf_tensor("IDb", [128, 128], BF16)); ONESb = st.enter_context(nc.sbuf_tensor("ONESb", [128, 128], BF16))
    H = [sb("H0", [128, 8, 64]), sb("H1", [128, 8, 64])]
    CARRY = sb("CARRY", [128, 26]); AGC = sb("AGC", [128, 22, 2])
    CV0 = sb("CV0", [128, 22, 16, 2]);
    XWA = sb("XWA", [128, W]); XG = sb("XG", [128, W])
    GL = sb("GL", [128, 8, 16])
    BI = sb("BI", [4, 1]); NBF = sb("NBF", [4, 1]); MCOL = sb("MCOL", [4, 2]); M0ROW = sb("M0ROW", [4, 16]); MNEW = sb("MNEW", [4, 16])
    ST6 = sb("ST6", [128, 12]); MV = sb("MV", [128, 2]); SD = sb("SD", [128, 1]); RS = sb("RS", [128, 1])
    TN = ["LW", "AA", "KKS", "SQ", "NRM", "KKN", "T1", "BB", "CUM", "EGP", "EGN", "CX", "EGX"]
    T = {n: sb("T_" + n, [128, W]) for n in TN}
    PR0 = T["KKS"][:, 0:128]; PR1 = T["SQ"][:, 0:128]
    SHS = T["EGX"][:, 0:128].rearrange("p (a b) -> p a b", a=8)
    LI = sb("LI", [4, W]); LP = sb("LP", [4, W]); CUMP = sb("CUMP", [4, W]); DD = sb("DD", [4, W]); CM = sb("CM", [4, W])
    MX = sb("MX", [4, 64]); ROWS = sb("ROWS", [4, 3, 64]); TR4 = sb("TR4", [4, 64]); WKR = sb("WKR", [4, 64])
    COLS = sb("COLS", [64, 8])
    WST = [sb("WST%d" % i, [64, 64]) for i in range(4)]; SST = [st.enter_context(nc.sbuf_tensor("SST%d" % i, [64, 64], BF16)) for i in range(4)]
    QW = [st.enter_context(nc.sbuf_tensor("QW%d" % i, [128, 64], BF16)) for i in range(4)]; ADEN = [sb("ADEN%d" % i, [128, 64]) for i in range(4)]
    DDT = [sb("DDT%d" % i, [128, 64]) for i in range(4)]; KW = [st.enter_context(nc.sbuf_tensor("KW%d" % i, [64, 128], BF16)) for i in range(4)]
    DCOL = sb("DCOL", [128, 4])
    VTK = st.enter_context(nc.sbuf_tensor("VTK", [64, 512], BF16)); KHK = st.enter_context(nc.sbuf_tensor("KHK", [64, 512], BF16)); BHK = st.enter_context(nc.sbuf_tensor("BHK", [64, 512], BF16))
    S1raw = st.enter_context(nc.sbuf_tensor("S1raw", [64, 1028], BF16)); S2 = st.enter_context(nc.sbuf_tensor("S2", [64, 8, 2, 64], BF16)); PA = st.enter_context(nc.sbuf_tensor("PA", [64, 8, 64], BF16))
    S1 = S1raw[:, 0:1024].rearrange("p (a b c) -> p a b c", a=8, b=2)
    VTK1 = S1raw[:, 0:1028].rearrange("p (h c) -> p h c", h=4)
    KTK = KHK
    PN = [st.enter_context(nc.sbuf_tensor("PN%d" % i, [64, 8, 64], BF16)) for i in range(2)]; PTN = [st.enter_context(nc.sbuf_tensor("PTN%d" % i, [64, 8, 64], BF16)) for i in range(2)]
    U = [st.enter_context(nc.sbuf_tensor("U0", [64, 512], BF16)), st.enter_context(nc.sbuf_tensor("U1", [64, 512], BF16))]
    TMPH = sb("TMPH", [128, 4, 64])

    def sbs(name, shape):
        return st.enter_context(nc.sbuf_tensor(name, shape, BF16))
    setB = (sbs("VTKs", [16, 512]), sbs("KHKs", [16, 512]), sbs("BHKs", [16, 512]), sbs("S1s", [16, 8, 2, 16]), sbs("S2s", [16, 8, 2, 16]),
            sbs("PAs", [16, 8, 16]), [sbs("PNs%d" % i, [16, 8, 16]) for i in range(2)], [sbs("PTNs%d" % i, [16, 8, 16]) for i in range(2)],
            [sbs("Us%d" % i, [16, 512]) for i in range(2)], sb("TMPHs", [128, 4, 64]), "B")
    ps = [st.enter_context(nc.psum_tensor("ps%d" % i, [128, 512], F32)) for i in range(8)]
    psi = [0]

    def PS():
        i = psi[0]
        psi[0] = (i + 1) % 8
        return ps[i], ("ps", i)

    def slot(i):
        return ARENA[:, i * W:(i + 1) * W]

    def slots(i, n):
        return ARENA[:, i * W:(i + n) * W].rearrange("p (c w) -> p c w", c=n)

    def AK(i):
        return ("A", i)

    BFA = ARENA[:, KKT * W:(KKT + 8) * W].bitcast(BF16)
    BFB = ARENA[:, BHT * W:(BHT + 8) * W].bitcast(BF16)

    def KKTb(j):
        return BFA[:, j * W:(j + 1) * W]

    def RTb(j):
        return BFA[:, (8 + j) * W:(9 + j) * W]

    def KHb(j):
        return BFB[:, j * W:(j + 1) * W]

    def BHb(j):
        return BFB[:, (8 + j) * W:(9 + j) * W]

    def Qb(h):
        return slot(QA + h).bitcast(BF16)

    def Kb(h):
        return slot(KA + h).bitcast(BF16)

    def MM(out, lhsT, rhs, start=True, stop=True, r=(), w=(), strict=False):
        P.op("pe", lambda h: h.matmul(out, lhsT=lhsT, rhs=rhs, start=start, stop=stop), r, w, strict=strict)

    def TR(out, in_, n, r=(), w=(), bf=False):
        idn = IDb[0:n, 0:n] if bf else ID[0:n, 0:n]
        P.op("pe", lambda h: h.transpose(out=out, in_=in_, identity=idn), list(r) + ["ID"], w)

    def ACT(out, in_, func, r=(), w=(), bias=0.0, scale=1.0):
        P.op("act", lambda h: h.activation(out=out, in_=in_, func=func, bias=bias, scale=scale), r, w)

    def TT(eng, out, in0, in1, op, r=(), w=()):
        P.op(eng, lambda h: h.tensor_tensor(out=out, in0=in0, in1=in1, op=op), r, w)

    def TS(eng, out, in0, s1, op0, r=(), w=(), s2=None, op1=None):
        if op1 is None and eng == "pool" and op0 == ALU.mult:
            s2, op1 = 1.0, ALU.mult
        if op1 is None:
            P.op(eng, lambda h: h.tensor_scalar(out=out, in0=in0, scalar1=s1, scalar2=None, op0=op0), r, w)
        else:
            P.op(eng, lambda h: h.tensor_scalar(out=out, in0=in0, scalar1=s1, scalar2=s2, op0=op0, op1=op1), r, w)

    def STT(out, in0, sc, in1, op0, op1, r=(), w=()):
        P.op("dve", lambda h: h.scalar_tensor_tensor(out=out, in0=in0, scalar=sc, in1=in1, op0=op0, op1=op1), r, w)

    def CP(eng, out, in_, r=(), w=()):
        if eng == "act":
            P.op("act", lambda h: h.copy(out=out, in_=in_), r, w)
        else:
            P.op(eng, lambda h: h.tensor_copy(out=out, in_=in_), r, w)

    def RECIP(out, in_, r=(), w=()):
        P.op("dve", lambda h: h.reciprocal(out=out, in_=in_), r, w)

    def MSET(eng, ap, v, w=()):
        P.op(eng, lambda h: h.memset(ap, v), (), w)

    def DMA(q, out, in_, r=(), w=(), slow=False):
        P.op(q, lambda h: h.dma_start(out=out, in_=in_, allow_slow_non_contiguous=slow), r, w, dma=True)

    def PCc(i):
        return PC[:, i:i + 1]

    DMA("sp", PR0[0:122, :], prm0, w=["PR0"])
    DMA("sp", PR1[0:88, :], prm1, w=["PR1"])
    DMA("sp", W2A2[:], w2a2, w=["W2A2"])
    DMA("sp", G2[:], g2, w=["G2"])
    DMA("sp", LNG[:], ln2g.partition_broadcast(128), w=["LNG"])
    DMA("sp", LNB[:], ln2b.partition_broadcast(128), w=["LNB"])
    DMA("sp", BI[:], bif[0:4, :], w=["BI"])
    DMA("sp", NBF[:], bif[4:8, :], w=["NBF"])
    DMA("sp", M0ROW[:], sm, w=["M0ROW"])
    DMA("sp", CV0[:], scv, w=[("CV0", i) for i in range(22)])
    MSET("pool", ONES[:], 1.0, w=["ONES"])
    ZER = T["LW"][:, 0:128]; NEG1 = T["AA"][:, 0:128]
    MSET("pool", ZER, 0.0, w=["ZER"])
    MSET("pool", NEG1, -1.0, w=["NEG1"])
    P.op("pool", lambda h: h.affine_select(out=ID[:], in_=ONES[:], pattern=[[1, 128]], compare_op=ALU.is_equal,
                                           fill=0.0, base=0, channel_multiplier=-1), ["ONES"], ["ID"])
    TS("pool", AID[:], ID[:], ALPHA, ALU.mult, r=["ID"], w=["AID"])
    CP("pool", IDb[:], ID[:], r=["ID"], w=["ID"])
    MSET("pool", ONESb[:], 1.0, w=["ONES"])
    MSET("pool", BLK[:], 0.0, w=["BLK"])
    MSET("pool", BLK[0:64, 0:64], 1.0, w=["BLK"])
    MSET("pool", BLK[64:128, 64:128], 1.0, w=["BLK"])
    P.op("pool", lambda h: h.affine_select(out=MK1[:, 0, :], in_=ONES[0:64, 0:64], pattern=[[1, 64]], compare_op=ALU.is_gt,
                                           fill=0.0, base=0, channel_multiplier=-1), ["ONES"], ["MK1"])
    P.op("pool", lambda h: h.affine_select(out=MK1[:, 1, :], in_=ONES[0:64, 0:64], pattern=[[1, 64]], compare_op=ALU.is_ge,
                                           fill=0.0, base=0, channel_multiplier=-1), ["ONES"], ["MK1"])
    TS("pool", MK1N[:], MK1[:], -1.0, ALU.mult, r=["MK1"], w=["MK1N"])
    P.op("pool", lambda h: h.affine_select(out=MLN[:], in_=NEG1[0:64, 0:64], pattern=[[-1, 64]], compare_op=ALU.is_gt,
                                           fill=0.0, base=0, channel_multiplier=1), ["NEG1"], ["MLN"])
    P.op("pool", lambda h: h.affine_select(out=MNEG[:], in_=ZER[0:64, 0:64], pattern=[[1, 64]], compare_op=ALU.is_ge,
                                           fill=NEG, base=0, channel_multiplier=-1), ["ZER"], ["MNEG"])
    CP("dve", SEL[:], ID[0:4, 0:4].unsqueeze(2).broadcast_to([4, 4, 128]), r=["ID"], w=["SEL"])
    TS("pool", NBF[:], NBF[:], -1.0, ALU.mult, r=["NBF"], w=["NBF"])
    pt, pk = PS()
    TR(pt[:, 0:122], PR0[0:122, :], 122, r=["PR0"], w=[pk])
    TR(pt[:, 128:216], PR1[0:88, :], 88, r=["PR1"], w=[pk])
    CP("dve", PC[:, 0:122], pt[:, 0:122], r=[pk], w=["PC"])
    CP("dve", PC[:, 128:216], pt[:, 128:216], r=[pk], w=["PC"])
    TS("dve", PCN[:], PC[:, PC_W0:PC_W0 + 16], -1.0, ALU.mult, r=["PC"], w=["PC"])
    MSET("pool", CN[0][:], 0.0, w=[("CN", 0, h) for h in range(4)])
    MSET("pool", NBC[:], 0.0, w=[("NBC", h) for h in range(4)])
    MSET("pool", H[0][:], 0.0, w=[("H", 0, 0), ("H", 0, 1)])
    MSET("pool", Hb[0][:], 0.0, w=[("Hb", 0, 0), ("Hb", 0, 1)])
    MSET("pool", CNb[:], 0.0, w=[("CNb", h) for h in range(4)])
    MSET("pool", MCOL[:], 0.0, w=["MC0", "MC1"])
    MSET("pool", CARRY[:], 0.0, w=[("CARRY", c) for c in range(26)])
    MSET("pool", AGC[:], 0.0, w=[("AGC", i) for i in range(22)])
    pcs = []
    for (wsrc, wdst, R, C) in ((w_in, wb_in, D, 8456), (w_out, wb_out, D, D), (w_up, wb_up, D, 2 * DF), (w_down, wb_down, DF, D)):
        for r0 in range(0, R, 128):
            for c0 in range(0, C, 4228):
                pcs.append((wsrc, wdst, r0, c0, min(4228, C - c0)))
    for pi_, (wsrc, wdst, r0, c0, n_) in enumerate(pcs):
        sl = pi_ % 2
        stg = ARENA[:, sl * 4228:sl * 4228 + n_]
        ob = ARENA[:, 8456 + sl * 2114:8456 + (sl + 1) * 2114].bitcast(BF16)[:, 0:n_]
        DMA("sp", stg, wsrc[r0:r0 + 128, c0:c0 + n_], w=[("STG", sl)])
        CP(("act", "dve", "pool")[pi_ % 3], ob, stg, r=[("STG", sl)], w=[("OB", sl)])
        DMA("pool", wdst[r0:r0 + 128, c0:c0 + n_], ob, r=[("OB", sl)])
    P.barrier()

    wbi = [0]

    def nextWB():
        i = wbi[0]
        wbi[0] = 1 - i
        return WB[i], ("WB", i)

    def ln_stats(Tt, n, tkey, eps=EPS):
        P.op("dve", lambda h: h.bn_stats(out=ST6[0:n, 0:6], in_=Tt[0:n, 0:512]), [tkey], ["ST6"])
        P.op("dve", lambda h: h.bn_stats(out=ST6[0:n, 6:12], in_=Tt[0:n, 512:1024]), [tkey], ["ST6"])
        P.op("dve", lambda h: h.bn_aggr(out=MV[0:n, :], in_=ST6[0:n, :]), ["ST6"], ["MV"])
        ACT(SD[0:n, :], MV[0:n, 1:2], AF.Ln, r=["MV"], w=["SD"], bias=eps)
        ACT(RS[0:n, :], SD[0:n, :], AF.Exp, r=["SD"], w=["RS"], scale=-0.5)
        TS("dve", Tt[0:n, :], Tt[0:n, :], MV[0:n, 0:1], ALU.subtract, r=[tkey, "MV", "RS"], w=[tkey], s2=RS[0:n, 0:1], op1=ALU.mult)

    def to_feature_major(Tt, n, tkey, col0, gcol, bcol, banks=None):
        for half in range(2):
            if banks is None:
                pt_, pk_ = PS()
            else:
                pt_, pk_ = ps[banks[half]], ("ps", banks[half])
            for i in range(4):
                kc = half * 4 + i
                TR(pt_[:, i * 128:i * 128 + n], Tt[0:n, kc * 128:(kc + 1) * 128], n, r=[tkey], w=[pk_])
            for i in range(4):
                kc = half * 4 + i
                if i % 2 == 0:
                    ACT(XT[:, kc, col0:col0 + n], pt_[:, i * 128:i * 128 + n], AF.Identity, r=[pk_, "PC"], w=["XT"],
                        bias=PCc(bcol + kc), scale=PCc(gcol + kc))
                else:
                    TS("dve", XT[:, kc, col0:col0 + n], pt_[:, i * 128:i * 128 + n], PCc(gcol + kc), ALU.mult,
                       r=[pk_, "PC"], w=["XT"], s2=PCc(bcol + kc), op1=ALU.add)

    def proj_chunks(wv, col0, nch, ncols, evac):
        i = 0
        while i < nch:
            nb = min(4, nch - i)
            wb, wk = nextWB()
            wbv = wb[:, :].rearrange("p (k c) -> p k c", k=8)
            DMA("sp", wbv[:, :, 0:nb * 128], wv[:, :, col0 + i * 128:col0 + (i + nb) * 128], w=[wk])
            for b in range(nb):
                pt_, pk_ = PS()
                for kc in range(8):
                    MM(pt_[:, 0:ncols], wbv[:, kc, b * 128:(b + 1) * 128], XTb[:, kc, 0:ncols], start=(kc == 0), stop=(kc == 7),
                       r=[wk, "XTb"], w=[pk_])
                evac(i + b, pt_, pk_)
            i += nb

    cur_ty = [-1]
    gchunk = [0]

    for (kind, tok0, NT, chunks, ty) in blocks:
        smp = (kind == "s")
        NTB = NT + (16 if smp else 0)
        if ty != cur_ty[0]:
            cur_ty[0] = ty
            MSET("pool", RST01_[:], 1.0, w=["RST"])
            if ty == 0:
                MSET("pool", RST01_[:, 0:1], 0.0, w=["RST"])
                MSET("pool", RST01_[:, 16:272:64], 0.0, w=["RST"])
            elif ty == 1:
                MSET("pool", RST01_[:, 0:256:64], 0.0, w=["RST"])
            else:
                MSET("pool", RST01_[:, 0:128:8], 0.0, w=["RST"])
        tiles = []
        if smp:
            tiles.append((0, 128, [(0, 128, xs)]))
        else:
            c = 0
            while c < NT:
                n = min(128, NT - c)
                srcs = []
                t_lo, t_hi = tok0 + c, tok0 + c + n
                if t_lo < 16:
                    srcs.append((0, 16 - t_lo, meta[t_lo:16, :]))
                    srcs.append((16 - t_lo, n - (16 - t_lo), xp[0:t_hi - 16, :]))
                else:
                    srcs.append((0, n, xp[t_lo - 16:t_hi - 16, :]))
                tiles.append((c, n, srcs))
                c += n

        for ti, (col0, n, srcs) in enumerate(tiles):
            Tt, tkey = TOK[ti % 2], ("TOK", ti % 2)
            for (r0, nr, ap) in srcs:
                DMA("sp", Tt[r0:r0 + nr, :], ap, w=[tkey])
            ln_stats(Tt, n, tkey)
            to_feature_major(Tt, n, tkey, col0, PC_LIG, PC_LIB)
        if smp:
            DMA("sp", XT[:, :, 128:144], ssh, w=["XT"])
        CP("pool", XTb[:, :, 0:NTB], XT[:, :, 0:NTB], r=["XT"], w=["XTb"])
        TT("dve", XTlo[:, :, 0:NT], XT[:, :, 0:NT], XTb[:, :, 0:NT], ALU.subtract, r=["XT", "XTb"], w=["XTlo"])
        if smp:
            pass
            CP("pool", SHS[:], XT[:, :, 7:128:8], r=["XT"], w=["EGX"])
            DMA("pool", osh_s, SHS[:], r=["EGX"])
        elif tok0 + NT == 2064:
            CP("pool", SHS[:, :, 0:1], XT[:, :, NT - 1:NT], r=["XT"], w=["EGX"])
            DMA("pool", osh_p, SHS[:, :, 0:1], r=["EGX"], slow=True)

        if stop == "0":
            P.barrier()
            continue
        def evA(dst0, kindA):
            def f(i, pt_, pk_):
                d = slot(dst0 + i)[:, 0:NT]
                if dst0 in (QA, KA):
                    d = slot(dst0 + i).bitcast(BF16)[:, 0:NT]
                if kindA == "copy":
                    if i % 2 == 0:
                        CP("act", d, pt_[:, 0:NT], r=[pk_], w=[AK(dst0 + i)])
                    else:
                        CP("dve", d, pt_[:, 0:NT], r=[pk_], w=[AK(dst0 + i)])
                elif kindA == "kscale":
                    ACT(d, pt_[:, 0:NT], AF.Identity, r=[pk_], w=[AK(dst0 + i)], scale=float(128 ** -0.5))
                else:
                    ACT(d, pt_[:, 0:NT], AF.Sigmoid, r=[pk_], w=[AK(dst0 + i)])
            return f

        proj_chunks(win_v, 2048, 4, NT, evA(QA, "copy"))
        proj_chunks(win_v, 2560, 4, NT, evA(KA, "kscale"))
        proj_chunks(win_v, 3072, 8, NT, evA(VA, "copy"))
        proj_chunks(win_v, 4096, 8, NT, evA(OA, "sig"))
        wb, wk = nextWB()
        wbv = wb[:, :].rearrange("p (k c) -> p k c", k=8)
        DMA("sp", wbv[:, :, 0:8], win_v[:, :, 5120:5128], w=[wk])
        pi_, ki_ = PS()
        pf_, kf_ = PS()
        for kc in range(8):
            MM(pi_[0:4, 0:NT], wbv[:, kc, 0:4], XTb[:, kc, 0:NT], start=(kc == 0), stop=(kc == 7), r=[wk, "XTb"], w=[ki_])
        for kc in range(8):
            MM(pf_[0:4, 0:NT], wbv[:, kc, 4:8], XTb[:, kc, 0:NT], start=(kc == 0), stop=(kc == 7), r=[wk, "XTb"], w=[kf_])
        ACT(LI[0:4, 0:NT], pi_[0:4, 0:NT], AF.Identity, r=[ki_, "BI"], w=["LI"], bias=BI[0:4, 0:1])
        ACT(LP[0:4, 0:NT], pf_[0:4, 0:NT], AF.Exp, r=[kf_, "NBF"], w=["LP"], bias=NBF[0:4, 0:1], scale=-1.0)
        ACT(LP[0:4, 0:NT], LP[0:4, 0:NT], AF.Ln, r=["LP"], w=["LP"], bias=1.0)
        P.op("dve", lambda h, NT=NT, ty=ty: h.tensor_tensor_scan(out=CUMP[0:4, 0:NT], data0=RST01[ty][0:4, 0:NT], data1=LP[0:4, 0:NT],
                                                  initial=0.0, op0=ALU.mult, op1=ALU.add), ["LP", "RST"], ["CUMP"])
        TT("dve", DD[0:4, 0:NT], LI[0:4, 0:NT], CUMP[0:4, 0:NT], ALU.add, r=["LI", "CUMP"], w=["DD"])
        TS("pool", RSTN[0:4, 0:NT], RST01[ty][0:4, 0:NT], 1.0, ALU.subtract, r=["RST"], w=["RSN"], s2=1.0e30, op1=ALU.mult)
        P.op("dve", lambda h, NT=NT, ty=ty: h.tensor_tensor_scan(out=CM[0:4, 0:NT], data0=RSTN[0:4, 0:NT], data1=DD[0:4, 0:NT],
                                                  initial=0.0, op0=ALU.add, op1=ALU.max), ["DD", "RSN"], ["CM"])
        proj_chunks(win_v, 0, 8, NT, evA(GA, "sig"))
        for c in range(8):
            TT("pool", slot(OA + c)[:, 0:NT], slot(OA + c)[:, 0:NT], slot(GA + c)[:, 0:NT], ALU.mult, r=[AK(OA + c), AK(GA + c)], w=[AK(OA + c)])

        MSET("pool", VTK1[:, :, 256:257], 1.0, w=["S1"])
        for ci, (c0, L) in enumerate(chunks):
            cs = slice(c0, c0 + L)
            if smp:
                cnb = ci % 2
                m_in, m_in_k = M0ROW[0:4, ci:ci + 1], "M0ROW"
                m_out, m_out_k = MNEW[0:4, ci:ci + 1], "MNEW"
                cnk = [("CN", cnb, h) for h in range(4)]
                DMA("sp", CN[cnb][:, :, 0:256], sC[ci].rearrange("h k v -> k h v"), w=cnk)
                DMA("sp", CN[cnb][:, :, 256:257], sn[ci].unsqueeze(2), w=cnk, slow=True)
                CP("pool", NBC[:], CN[cnb][:, :, 256:257].broadcast_to([128, 4, 128]), r=cnk, w=[("NBC", h) for h in range(4)])
                CP("act", CNb[:], CN[cnb][:], r=cnk, w=[("CNb", h) for h in range(4)])
            else:
                cnb = 0
                g = gchunk[0]
                gchunk[0] += 1
                m_in, m_in_k = MCOL[0:4, g % 2:g % 2 + 1], "MC%d" % (g % 2)
                m_out, m_out_k = MCOL[0:4, (g + 1) % 2:(g + 1) % 2 + 1], "MC%d" % ((g + 1) % 2)
            CNt = CN[cnb]
            TS("dve", MX[0:4, 0:L], CM[0:4, cs], m_in, ALU.max, r=["CM", m_in_k], w=["MX"])
            TS("dve", ROWS[0:4, 0, 0:L], MX[0:4, 0:L], -1.0, ALU.mult, r=["MX"], w=["ROWS"])
            ACT(ROWS[0:4, 1, 0:L], MX[0:4, 0:L], AF.Exp, r=["MX", m_in_k], w=["ROWS"], bias=m_in, scale=-1.0)
            TT("dve", TR4[0:4, 0:L], CUMP[0:4, cs], MX[0:4, 0:L], ALU.subtract, r=["CUMP", "MX"], w=["TR4"])
            ACT(ROWS[0:4, 2, 0:L], TR4[0:4, 0:L], AF.Exp, r=["TR4"], w=["ROWS"])
            ACT(WKR[0:4, 0:L], DD[0:4, cs], AF.Exp, r=["DD", "ROWS"], w=["WKR"], bias=ROWS[0:4, 0, L - 1:L])
            TT("dve", m_out, MX[0:4, L - 1:L], CUMP[0:4, c0 + L - 1:c0 + L], ALU.subtract, r=["MX", "CUMP"], w=[m_out_k])
            pt_, pk_ = PS()
            TR(pt_[0:L, 0:4], DD[0:4, cs], 4, r=["DD"], w=[pk_])
            TR(pt_[0:L, 4:8], WKR[0:4, 0:L], 4, r=["WKR"], w=[pk_])
            CP("act", COLS[0:L, 0:8], pt_[0:L, 0:8], r=[pk_], w=["COLS"])
            pk1, kk1 = PS()
            pk1b = pk1[:, :].bitcast(BF16)
            for h in range(4):
                TR(pk1b[0:L, h * 128:(h + 1) * 128], Kb(h)[:, cs], 128, r=[AK(KA + h)], w=[kk1], bf=True)
            CP("dve", KTK[0:L, :], pk1b[0:L, 0:512], r=[kk1], w=["KHK"])
            for half in range(2):
                pv, kv = PS()
                for i in range(4):
                    c = half * 4 + i
                    TR(pv[0:L, i * 128:(i + 1) * 128], slot(VA + c)[:, cs], 128, r=[AK(VA + c)], w=[kv])
                CP("act", VTK1[0:L, half * 2:half * 2 + 2, 0:256], pv[0:L, :].rearrange("p (a b) -> p a b", a=2), r=[kv], w=["S1"])
            for h in range(4):
                TS("pool", KW[h][0:L, :], KTK[0:L, h * 128:(h + 1) * 128], COLS[0:L, 4 + h:5 + h], ALU.mult, r=["KHK", "COLS"], w=[("KW", h)])
            PB = [(ps[2 * h], ("ps", 2 * h)) for h in range(4)]
            PN_ = [(ps[2 * h + 1], ("ps", 2 * h + 1)) for h in range(4)]
            for h in range(4):
                pb, kb = PB[h]
                MM(pb[0:L, 0:L], SEL[0:4, h, 0:L], ROWS[0:4, 0, 0:L], start=True, stop=False, r=["SEL", "ROWS"], w=[kb])
                MM(pb[0:L, 0:L], ID[0:L, 0:L], MNEG[0:L, 0:L], start=False, stop=True, r=["ID", "MNEG"], w=[kb])
                MM(pb[:, L:3 * L], SEL[0:4, h, :], ROWS[0:4, 1:3, 0:L], r=["SEL", "ROWS"], w=[kb])
                MM(pb[0:L, 256:256 + L], Kb(h)[:, cs], Qb(h)[:, cs], r=[AK(KA + h), AK(QA + h)], w=[kb])
            for h in range(4):
                pb, kb = PB[h]
                ACT(WST[h][0:L, 0:L], pb[0:L, 0:L], AF.Exp, r=[kb, "COLS"], w=[("WST", h)], bias=COLS[0:L, h:h + 1])
            for h in range(4):
                pb, kb = PB[h]
                TT("dve", SST[h][0:L, 0:L], WST[h][0:L, 0:L], pb[0:L, 256:256 + L], ALU.mult, r=[("WST", h), kb], w=[("SST", h)])
                TT("dve", QW[h][:, 0:L], Qb(h)[:, cs], pb[:, L:2 * L], ALU.mult, r=[AK(QA + h), kb], w=[("QW", h)])
            for h in range(4):
                pn_, kn = PN_[h]
                for vc in range(2):
                    MM(pn_[:, vc * L:(vc + 1) * L], VTK1[0:L, h, vc * 128:(vc + 1) * 128], SST[h][0:L, 0:L], start=True, stop=False,
                       r=["S1", ("SST", h)], w=[kn])
                    MM(pn_[:, vc * L:(vc + 1) * L], CNb[:, h, vc * 128:(vc + 1) * 128], QW[h][:, 0:L], start=False, stop=True,
                       r=[("CNb", h), ("QW", h)], w=[kn])
                MM(pn_[:, 2 * L:3 * L], ONESb[0:L, :], SST[h][0:L, 0:L], start=True, stop=False, r=["ONES", ("SST", h)], w=[kn])
                MM(pn_[:, 2 * L:3 * L], NBC[:, h, :], QW[h][:, 0:L], start=False, stop=True, r=[("NBC", h), ("QW", h)], w=[kn])
                MM(pn_[:, 192:449], KW[h][0:L, :], VTK1[0:L, h, :], r=[("KW", h), "S1"], w=[kn])
            for h in range(4):
                pb, kb = PB[h]
                pn_, kn = PN_[h]
                ACT(ADEN[h][:, 0:L], pn_[:, 2 * L:3 * L], AF.Abs, r=[kn], w=[("ADEN", h)])
                CP("act", DCOL[:, h:h + 1], pb[:, 2 * L - 1:2 * L], r=[kb], w=[("DCOL", h)])
            for h in range(4):
                pb, kb = PB[h]
                pn_, kn = PN_[h]
                TT("dve", DDT[h][:, 0:L], ADEN[h][:, 0:L], pb[:, 2 * L:3 * L], ALU.max, r=[("ADEN", h), kb], w=[("DDT", h)])
                RECIP(DDT[h][:, 0:L], DDT[h][:, 0:L], r=[("DDT", h)], w=[("DDT", h)])
                TT("dve", slots(VA + 2 * h, 2)[:, :, cs], pn_[:, 0:2 * L].rearrange("p (a b) -> p a b", a=2),
                   DDT[h][:, 0:L].unsqueeze(1).broadcast_to([128, 2, L]), ALU.mult,
                   r=[kn, ("DDT", h)], w=[AK(VA + 2 * h), AK(VA + 2 * h + 1)])
                STT(CNt[:, h, :], CNt[:, h, :], DCOL[:, h:h + 1], pn_[:, 192:449], ALU.mult, ALU.add,
                    r=[("CN", cnb, h), kn, ("DCOL", h)], w=[("CN", cnb, h)])
            for h in range(4):
                CP("pool", NBC[:, h, :], CNt[:, h, 256:257].broadcast_to([128, 128]), r=[("CN", cnb, h)], w=[("NBC", h)])
                CP("act", CNb[:, h, :], CNt[:, h, :], r=[("CN", cnb, h)], w=[("CNb", h)])
            if smp:
                DMA("pool", oCN_s[ci], CNt[:], r=[("CN", cnb, h) for h in range(4)])
        if smp:
            DMA("pool", om_s, MNEW[:], r=["MNEW"])
        elif tok0 + NT == 2064:
            DMA("pool", oCN_p, CN[0][:], r=[("CN", 0, h) for h in range(4)])
            gl = gchunk[0] % 2
            DMA("pool", om_p, MCOL[0:4, gl:gl + 1], r=["MC%d" % gl])

        TN3 = ["LW", "AA", "KKS", "SQ", "NRM", "KKN", "T1", "BB", "CUM", "EGP", "EGN", "CX"]
        hs = list(range(4))
        tq = {h: (TN3[3 * h], TN3[3 * h + 1], TN3[3 * h + 2]) for h in hs}
        pmk = {}
        for h in hs:
            c0_, c1_ = VA + 2 * h, VA + 2 * h + 1
            pm, km = PS()
            pmk[h] = (pm, km)
            MM(pm[:, 0:NT], ONES[:, :], slot(c0_)[:, 0:NT], start=True, stop=False, r=["ONES", AK(c0_)], w=[km])
            MM(pm[:, 0:NT], ONES[:, :], slot(c1_)[:, 0:NT], start=False, stop=True, r=["ONES", AK(c1_)], w=[km])
        for h in hs:
            pm, km = pmk[h]
            for c_ in (VA + 2 * h, VA + 2 * h + 1):
                STT(slot(c_)[:, 0:NT], pm[:, 0:NT], -1.0 / 256, slot(c_)[:, 0:NT], ALU.mult, ALU.add, r=[km, AK(c_)], w=[AK(c_)])
        for h in hs:
            c0_, c1_ = VA + 2 * h, VA + 2 * h + 1
            TT("pool", T[tq[h][0]][:, 0:NT], slot(c0_)[:, 0:NT], slot(c0_)[:, 0:NT], ALU.mult, r=[AK(c0_)], w=[tq[h][0]])
            TT("pool", T[tq[h][1]][:, 0:NT], slot(c1_)[:, 0:NT], slot(c1_)[:, 0:NT], ALU.mult, r=[AK(c1_)], w=[tq[h][1]])
        for h in hs:
            pv2, kv2 = PS()
            pmk[h] = (pv2, kv2)
            MM(pv2[:, 0:NT], ONES[:, :], T[tq[h][0]][:, 0:NT], start=True, stop=False, r=["ONES", tq[h][0]], w=[kv2])
            MM(pv2[:, 0:NT], ONES[:, :], T[tq[h][1]][:, 0:NT], start=False, stop=True, r=["ONES", tq[h][1]], w=[kv2])
        for h in hs:
            pv2, kv2 = pmk[h]
            ACT(T[tq[h][2]][:, 0:NT], pv2[:, 0:NT], AF.Ln, r=[kv2], w=[tq[h][2]], bias=EPS, scale=1.0 / 256)
        for h in hs:
            ACT(T[tq[h][2]][:, 0:NT], T[tq[h][2]][:, 0:NT], AF.Exp, r=[tq[h][2]], w=[tq[h][2]], scale=-0.5)
        for h in hs:
            for vc, c_ in enumerate((VA + 2 * h, VA + 2 * h + 1)):
                cc = 2 * h + vc
                STT(slot(c_)[:, 0:NT], slot(c_)[:, 0:NT], PCc(PC_MNG + cc), T[tq[h][2]][:, 0:NT], ALU.mult, ALU.mult, r=[AK(c_), tq[h][2], "PC"], w=[AK(c_)])
        for h in hs:
            for vc, c_ in enumerate((VA + 2 * h, VA + 2 * h + 1)):
                cc = 2 * h + vc
                TT("pool" if vc else "dve", slot(MERGED + cc)[:, 0:NT], slot(c_)[:, 0:NT], slot(OA + cc)[:, 0:NT], ALU.mult, r=[AK(c_), AK(OA + cc)], w=[AK(MERGED + cc)])
        P.barrier()
        if stop == "A":
            continue

        def evGB(i, pt_, pk_):
            ACT(slot(GB + i)[:, 0:NT], pt_[:, 0:NT], AF.Sigmoid, r=[pk_], w=[AK(GB + i)])

        proj_chunks(win_v, 1024, 8, NT, evGB)

        def evPB(cc, pt_, pk_):
            if cc < 8:
                dst, dk = slot(RB + cc), AK(RB + cc)
            elif cc < 16:
                dst, dk = slot(KB + cc - 8), AK(KB + cc - 8)
            elif cc < 24:
                dst, dk = slot(VB + cc - 16), AK(VB + cc - 16)
            elif cc == 24:
                dst, dk = XWA, "XWA"
            else:
                dst, dk = XG, "XG"
            if cc % 2 == 0:
                CP("act", dst[:, 0:NTB], pt_[:, 0:NTB], r=[pk_], w=[dk])
            else:
                CP("dve", dst[:, 0:NTB], pt_[:, 0:NTB], r=[pk_], w=[dk])
            ds, dsk = (T["CX"], "CX") if cc % 2 == 0 else (T["EGX"], "EGX")
            if smp:
                d3 = dst[:, 0:128].rearrange("p (s t) -> p s t", t=8)
                s3 = ds[:, 0:128].rearrange("p (s t) -> p s t", t=8)
                TT("pool", s3[:, :, 1:8], d3[:, :, 0:7], d3[:, :, 1:8], ALU.subtract, r=[dk], w=[dsk])
                TT("pool", s3[:, :, 0:1], dst[:, 128:144].unsqueeze(2), d3[:, :, 0:1], ALU.subtract, r=[dk], w=[dsk])
            else:
                TT("pool", ds[:, 1:NT], dst[:, 0:NT - 1], dst[:, 1:NT], ALU.subtract, r=[dk], w=[dsk])
                TT("pool", ds[:, 0:1], CARRY[:, cc:cc + 1], dst[:, 0:1], ALU.subtract, r=[dk, ("CARRY", cc)], w=[dsk])
                CP("pool", CARRY[:, cc:cc + 1], dst[:, NT - 1:NT], r=[dk], w=[("CARRY", cc)])
            STT(dst[:, 0:NT], ds[:, 0:NT], PCc(PC_MU + cc), dst[:, 0:NT], ALU.mult, ALU.add, r=[dsk, dk, "PC"], w=[dk])

        proj_chunks(win_v, 5128, 26, NTB, evPB)
        if stop == "B1":
            P.barrier()
            continue
        ACT(XWA[0:64, 0:NT], XWA[0:64, 0:NT], AF.Tanh, r=["XWA"], w=["XWA"])
        ACT(XG[:, 0:NT], XG[:, 0:NT], AF.Sigmoid, r=["XG"], w=["XG"])

        nchunks = len(chunks)
        last0, Lc = chunks[-1][0] + chunks[-1][1], chunks[-1][1]
        first_last = chunks[0][0] + chunks[0][1] - 1
        for j in range(8):
            Rj, Kj, Vj = slot(RB + j)[:, 0:NT], slot(KB + j)[:, 0:NT], slot(VB + j)[:, 0:NT]
            rk_, kk_, vk_ = AK(RB + j), AK(KB + j), AK(VB + j)
            cols = slice(j * 128, (j + 1) * 128)

            def t(n):
                return T[n][:, 0:NT]
            pw, kw = PS()
            MM(pw[:, 0:NT], W2A2[0:64, cols], XWA[0:64, 0:NT], r=["W2A2", "XWA"], w=[kw])
            pa, ka = PS()
            MM(pa[:, 0:NT], W2A2[64:128, cols], XWA[64:128, 0:NT], r=["W2A2", "XWA"], w=[ka])
            ACT(t("LW"), pw[:, 0:NT], AF.Exp, r=[kw, "PC"], w=["LW"], bias=PCN[:, j:j + 1], scale=-1.0)
            ACT(t("AA"), pa[:, 0:NT], AF.Exp, r=[ka, "PC"], w=["AA"], bias=PCN[:, 8 + j:9 + j], scale=-1.0)
            ACT(t("LW"), t("LW"), AF.Ln, r=["LW"], w=["LW"], bias=1.0)
            ACT(t("AA"), t("AA"), AF.Ln, r=["AA"], w=["AA"], bias=1.0)
            ACT(t("LW"), t("LW"), AF.Exp, r=["LW"], w=["LW"], scale=-1.0)
            ACT(t("AA"), t("AA"), AF.Exp, r=["AA"], w=["AA"], scale=-1.0)
            TS("pool", t("KKS"), Kj, PCc(PC_KKS + j), ALU.mult, r=[kk_, "PC"], w=["KKS"])
            TT("pool", t("SQ"), t("KKS"), t("KKS"), ALU.mult, r=["KKS"], w=["SQ"])
            pn2, kn2 = PS()
            MM(pn2[:, 0:NT], BLK[:, :], t("SQ"), r=["BLK", "SQ"], w=[kn2])
            P.op("dve", lambda h, NT=NT, ty=ty: h.tensor_tensor_scan(out=T["CUM"][:, 0:NT], data0=RST01[ty][:, 0:NT], data1=T["LW"][:, 0:NT],
                                                         initial=0.0, op0=ALU.mult, op1=ALU.add), ["LW", "RST"], ["CUM"])
            TS("dve", t("NRM"), pn2[:, 0:NT], 1e-24, ALU.max, r=[kn2], w=["NRM"])
            ACT(t("NRM"), t("NRM"), AF.Ln, r=["NRM"], w=["NRM"])
            ACT(t("NRM"), t("NRM"), AF.Exp, r=["NRM"], w=["NRM"], scale=-0.5)
            ACT(t("EGP"), t("CUM"), AF.Exp, r=["CUM"], w=["EGP"], scale=-C0)
            ACT(t("EGN"), t("CUM"), AF.Exp, r=["CUM"], w=["EGN"], scale=C0)
            TT("pool", t("CX"), t("CUM"), t("LW"), ALU.subtract, r=["CUM", "LW"], w=["CX"])
            ACT(t("EGX"), t("CX"), AF.Exp, r=["CX"], w=["EGX"], scale=-C0)
            TT("dve", t("KKN"), t("KKS"), t("NRM"), ALU.mult, r=["KKS", "NRM"], w=["KKN"])
            TS("dve", t("T1"), t("AA"), 1.0, ALU.subtract, r=["AA", "PC"], w=["T1"], s2=PCc(PC_KAS + j), op1=ALU.mult)
            STT(Kj, t("T1"), 1.0, Kj, ALU.add, ALU.mult, r=["T1", kk_], w=[kk_])
            TT("pool", t("BB"), t("KKN"), t("AA"), ALU.mult, r=["KKN", "AA"], w=["BB"])
            CP("pool", GL[:, j, 0:nchunks], T["EGP"][:, first_last:last0:Lc] if nchunks > 1 else T["EGP"][:, first_last:first_last + 1],
               r=["EGP"], w=[("GL", j)])
            bon = slot(BON + j)[:, 0:NT]
            STT(bon, Rj, PCc(PC_RK + j), Kj, ALU.mult, ALU.mult, r=[rk_, kk_, "PC"], w=[AK(BON + j)])
            pb2, kb2 = PS()
            MM(pb2[:, 0:NT], BLK[:, :], bon, r=["BLK", AK(BON + j)], w=[kb2])
            TT("dve", RTb(j)[:, 0:NT], Rj, t("EGP"), ALU.mult, r=[rk_, "EGP"], w=[("RTb", j)])
            TT("dve", KHb(j)[:, 0:NT], Kj, t("EGN"), ALU.mult, r=[kk_, "EGN"], w=[("KHb", j)])
            TT("pool", KKTb(j)[:, 0:NT], t("KKN"), t("EGX"), ALU.mult, r=["KKN", "EGX"], w=[AK(KKT + j)])
            TT("pool", BHb(j)[:, 0:NT], t("BB"), t("EGN"), ALU.mult, r=["BB", "EGN"], w=[AK(BHT + j)])
            TT("dve", bon, pb2[:, 0:NT], Vj, ALU.mult, r=[kb2, vk_], w=[AK(BON + j)])
        if stop == "B2":
            P.barrier()
            continue

        def QQ(j, rows, cs):
            return ARENA[:, KKT * W:(KKT + 16) * W].bitcast(BF16)[:, j * W:j * W + 16 * W].rearrange("p (two d) -> p two d", two=2)[rows, :, cs]

        for ci, (c0, L) in enumerate(chunks):
            cs = slice(c0, c0 + L)
            nl = {8: 3, 16: 4, 64: 6}[L]
            if DBGSTEP and ci < DBGCHUNK:
                continue
            hb = ci % 2 if smp else 0
            Ht = H[hb]
            if smp:
                DMA("sp", Ht[:], sH[ci], w=[("H", hb, 0), ("H", hb, 1)])
                CP("act", Hb[hb][:], Ht[:], r=[("H", hb, 0), ("H", hb, 1)], w=[("Hb", hb, 0), ("Hb", hb, 1)])
            def half_steps(jh, TSet):
                VTK, KHK, BHK, S1, S2, PA, PN, PTN, U, TMPH, kp = TSet
                hk = ("H", hb, jh)
                hbk = ("Hb", hb, jh)
                Hbt = Hb[hb]
                pA, kA = PS(); pB, kB = PS(); pC, kC = PS()
                pBb = pB[:, :].bitcast(BF16); pCb = pC[:, :].bitcast(BF16)
                for jj in range(4):
                    j = 4 * jh + jj
                    TR(pA[0:L, jj * 128:(jj + 1) * 128], slot(VB + j)[:, cs], 128, r=[AK(VB + j)], w=[kA])
                    TR(pBb[0:L, jj * 128:(jj + 1) * 128], KHb(j)[:, cs], 128, r=[("KHb", j)], w=[kB], bf=True)
                    TR(pCb[0:L, jj * 128:(jj + 1) * 128], BHb(j)[:, cs], 128, r=[AK(BHT + j)], w=[kC], bf=True)
                CP("act", VTK[0:L, :], pA[0:L, :], r=[kA], w=[(kp, "VTK")])
                CP("dve", KHK[0:L, :], pBb[0:L, 0:512], r=[kB], w=[(kp, "KHK")])
                ACT(BHK[0:L, :], pCb[0:L, 0:512], AF.Identity, r=[kC], w=[(kp, "BHK")], scale=-1.0)

                yield
                def hd(hq):
                    hp, jj = divmod(hq, 4)
                    return 4 * jh + jj, jj, hp, slice(64 * hp, 64 * hp + 64), slice(jj * 128 + hp * 64, jj * 128 + hp * 64 + 64)
                b1 = [PS(), PS()]
                for hq in range(8):
                    j, jj, hp, rows, tc = hd(hq)
                    MM(b1[hp][0][0:L, jj * 2 * L:(jj + 1) * 2 * L], KHb(j)[rows, cs], QQ(j, rows, cs),
                       r=[("KHb", j), AK(KKT + j), ("RTb", j)], w=[b1[hp][1]])
                yield
                for hp in range(2):
                    mk = MK1[0:L, :, 0:L].unsqueeze(1).broadcast_to([L, 4, 2, L])
                    TT("dve", S1[0:L, 4 * hp:4 * hp + 4, :, 0:L],
                       b1[hp][0][0:L, 0:8 * L].rearrange("p (a b c) -> p a b c", a=4, b=2), mk, ALU.mult,
                       r=[b1[hp][1], "MK1"], w=[(kp, "S1")])
                yield
                b2 = [PS(), PS()]
                for hq in range(8):
                    j, jj, hp, rows, tc = hd(hq)
                    MM(b2[hp][0][0:L, jj * 2 * L:(jj + 1) * 2 * L], BHb(j)[rows, cs], QQ(j, rows, cs),
                       r=[AK(BHT + j), AK(KKT + j), ("RTb", j)], w=[b2[hp][1]])
                for hp in range(2):
                    mkn = MK1N[0:L, :, 0:L].unsqueeze(1).broadcast_to([L, 4, 2, L])
                    TT("dve", S2[0:L, 4 * hp:4 * hp + 4, :, 0:L],
                       b2[hp][0][0:L, 0:8 * L].rearrange("p (a b c) -> p a b c", a=4, b=2), mkn, ALU.mult,
                       r=[b2[hp][1], "MK1N"], w=[(kp, "S2")])
                yield
                p3 = [PS(), PS()]
                for hq in range(8):
                    j, jj, hp, rows, tc = hd(hq)
                    MM(p3[hp][0][0:L, jj * L:(jj + 1) * L], KKTb(j)[rows, cs], BHb(j)[rows, cs],
                       r=[AK(KKT + j), AK(BHT + j)], w=[p3[hp][1]])
                for hp in range(2):
                    TT("dve", PA[0:L, 4 * hp:4 * hp + 4, 0:L], p3[hp][0][0:L, 0:4 * L].rearrange("p (a b) -> p a b", a=4),
                       MLN[0:L, 0:L].unsqueeze(1).broadcast_to([L, 4, L]), ALU.mult, r=[p3[hp][1], "MLN"], w=[(kp, "PA")])
                yield
                pU2 = [PS(), PS()]
                for hq in range(8):
                    j, jj, hp, rows, tc = hd(hq)
                    MM(pU2[hp][0][0:L, jj * 64:(jj + 1) * 64], KKTb(j)[rows, cs], Hbt[rows, j, :], start=(jj == 0), stop=False,
                       r=[AK(KKT + j), hbk], w=[pU2[hp][1]])
                for hq in range(8):
                    j, jj, hp, rows, tc = hd(hq)
                    MM(pU2[hp][0][0:L, jj * 64:(jj + 1) * 64], S1[0:L, hq, 0, 0:L], VTK[0:L, tc], start=False, stop=(jj == 3),
                       r=[(kp, "S1"), (kp, "VTK")], w=[pU2[hp][1]], strict=(hp == 1 and jj == 0))
                CP("act", U[0][0:L, 0:256], pU2[0][0][0:L, 0:256], r=[pU2[0][1]], w=[(kp, "U", 0)])
                CP("act", U[0][0:L, 256:512], pU2[1][0][0:L, 0:256], r=[pU2[1][1]], w=[(kp, "U", 0)])
                yield
                cur = 0
                Pt, Pk = PA, (kp, "PA")
                PTt, PTk = S2, (kp, "S2")

                def PTv(hq):
                    return PTt[0:L, hq, 0, 0:L] if PTk == (kp, "S2") else PTt[0:L, hq, 0:L]
                for l in range(nl):
                    pU, kU = PS()
                    for hq in range(8):
                        MM(pU[0:L, hq * 64:(hq + 1) * 64], PTv(hq), U[cur][0:L, hq * 64:(hq + 1) * 64], r=[PTk, (kp, "U", cur)], w=[kU])
                    TT("dve", U[1 - cur][0:L, :], U[cur][0:L, :], pU[0:L, :], ALU.add, r=[(kp, "U", cur), kU], w=[(kp, "U", 1 - cur)])
                    cur = 1 - cur
                    if l < nl - 1:
                        need_p = (l < nl - 2)
                        pT, kT = PS()
                        for hq in range(8):
                            MM(pT[0:L, hq * L:(hq + 1) * L], Pt[0:L, hq, 0:L], PTv(hq), r=[PTk, Pk], w=[kT])
                        nP, nPT = PN[l % 2], PTN[l % 2]
                        if need_p:
                            pP, kP = PS()
                            for hq in range(8):
                                MM(pP[0:L, hq * L:(hq + 1) * L], PTv(hq), Pt[0:L, hq, 0:L], r=[PTk, Pk], w=[kP])
                            CP("act", nP[0:L, :, 0:L], pP[0:L, 0:8 * L].rearrange("p (a b) -> p a b", a=8), r=[kP], w=[(kp, "PN", l % 2)])
                        CP("dve", nPT[0:L, :, 0:L], pT[0:L, 0:8 * L].rearrange("p (a b) -> p a b", a=8), r=[kT], w=[(kp, "PTN", l % 2)])
                        Pt, Pk = nP, (kp, "PN", l % 2)
                        PTt, PTk = nPT, (kp, "PTN", l % 2)
                    yield
                yield
                pY2 = [PS(), PS()]
                for hq in range(8):
                    j, jj, hp, rows, tc = hd(hq)
                    o_ = pY2[hp][0][rows, jj * L:(jj + 1) * L]
                    MM(o_, Hbt[rows, j, :], RTb(j)[rows, cs], start=(jj == 0), stop=False, r=[hbk, ("RTb", j)], w=[pY2[hp][1]])
                for hq in range(8):
                    j, jj, hp, rows, tc = hd(hq)
                    o_ = pY2[hp][0][rows, jj * L:(jj + 1) * L]
                    MM(o_, VTK[0:L, tc], S1[0:L, hq, 1, 0:L], start=False, stop=False, r=[(kp, "VTK"), (kp, "S1")], w=[pY2[hp][1]], strict=(hp == 1 and jj == 0))
                    MM(o_, U[cur][0:L, hq * 64:(hq + 1) * 64], S2[0:L, hq, 1, 0:L], start=False, stop=(jj == 3), r=[(kp, "U", cur), (kp, "S2")], w=[pY2[hp][1]])
                pH, kH = PS()
                for hq in range(8):
                    j, jj, hp, rows, tc = hd(hq)
                    o_ = pH[rows, jj * 64:(jj + 1) * 64]
                    MM(o_, KHK[0:L, tc], VTK[0:L, tc], start=True, stop=False, r=[(kp, "KHK"), (kp, "VTK")], w=[kH])
                    MM(o_, BHK[0:L, tc], U[cur][0:L, hq * 64:(hq + 1) * 64], start=False, stop=True, r=[(kp, "BHK"), (kp, "U", cur)], w=[kH])
                yield
                for hp in range(2):
                    rows = slice(64 * hp, 64 * hp + 64)
                    CP("act", slots(RB + 4 * jh, 4)[rows, :, cs], pY2[hp][0][rows, 0:4 * L].rearrange("p (a b) -> p a b", a=4), r=[pY2[hp][1]],
                       w=[AK(RB + 4 * jh + q) for q in range(4)])
                TT("dve", TMPH[:], Ht[:, 4 * jh:4 * jh + 4, :], pH[:, 0:256].rearrange("p (a b) -> p a b", a=4), ALU.add, r=[hk, kH], w=[(kp, "TMPH")])
                TT("pool", Ht[:, 4 * jh:4 * jh + 4, :], TMPH[:], GL[:, 4 * jh:4 * jh + 4, ci:ci + 1].broadcast_to([128, 4, 64]), ALU.mult,
                   r=[(kp, "TMPH")] + [("GL", 4 * jh + q) for q in range(4)], w=[hk])
                CP("act", Hbt[:, 4 * jh:4 * jh + 4, :], Ht[:, 4 * jh:4 * jh + 4, :], r=[hk], w=[hbk])

            setA = (VTK, KHK, BHK, S1, S2, PA, PN, PTN, U, TMPH, "A")
            if smp:
                gens = [half_steps(0, setA), half_steps(1, setB)]
                while gens:
                    for g_ in list(gens):
                        try:
                            next(g_)
                        except StopIteration:
                            gens.remove(g_)
            else:
                for jh in range(2):
                    for _ in half_steps(jh, setA):
                        pass
            if smp:
                DMA("pool", oH_s[ci], Ht[:], r=[("H", hb, 0), ("H", hb, 1)])
            if DBGSTEP and ci >= DBGCHUNK:
                break
        if (not smp) and tok0 + NT == 2064:
            DMA("pool", oH_p, H[0][:], r=[("H", 0, 0), ("H", 0, 1)])

        if stop == "B3":
            P.barrier()
            continue
        TN3 = ["LW", "AA", "KKS", "SQ", "NRM", "KKN", "T1", "BB", "CUM", "EGP", "EGN", "CX"]
        for g0 in (0, 4):
            js = list(range(g0, g0 + 4))
            tq = {j: (TN3[3 * (j - g0)], TN3[3 * (j - g0) + 1], TN3[3 * (j - g0) + 2]) for j in js}
            pk_ = {}
            for j in js:
                pm, km = PS()
                pk_[j] = (pm, km)
                MM(pm[:, 0:NT], BLK[:, :], slot(RB + j)[:, 0:NT], r=["BLK", AK(RB + j)], w=[km])
            for j in js:
                pm, km = pk_[j]
                Yj, yk = slot(RB + j)[:, 0:NT], AK(RB + j)
                STT(Yj, pm[:, 0:NT], -1.0 / 64, Yj, ALU.mult, ALU.add, r=[km, yk], w=[yk])
            for j in js:
                Yj, yk = slot(RB + j)[:, 0:NT], AK(RB + j)
                TT("pool", T[tq[j][0]][:, 0:NT], Yj, Yj, ALU.mult, r=[yk], w=[tq[j][0]])
            for j in js:
                pv2, kv2 = PS()
                pk_[j] = (pv2, kv2)
                MM(pv2[:, 0:NT], BLK[:, :], T[tq[j][0]][:, 0:NT], r=["BLK", tq[j][0]], w=[kv2])
            for j in js:
                pv2, kv2 = pk_[j]
                ACT(T[tq[j][1]][:, 0:NT], pv2[:, 0:NT], AF.Ln, r=[kv2], w=[tq[j][1]], bias=GN_EPS, scale=1.0 / 64)
            for j in js:
                ACT(T[tq[j][1]][:, 0:NT], T[tq[j][1]][:, 0:NT], AF.Exp, r=[tq[j][1]], w=[tq[j][1]], scale=-0.5)
            for j in js:
                pg, kg = PS()
                pk_[j] = (pg, kg)
                MM(pg[:, 0:NT], G2[:, j * 128:(j + 1) * 128], XG[:, 0:NT], r=["G2", "XG"], w=[kg])
            for j in js:
                Yj, yk = slot(RB + j)[:, 0:NT], AK(RB + j)
                t1 = T[tq[j][2]][:, 0:NT]
                STT(t1, Yj, PCc(PC_LXG + j), T[tq[j][1]][:, 0:NT], ALU.mult, ALU.mult, r=[yk, tq[j][1], "PC"], w=[tq[j][2]])
                STT(t1, t1, PCc(PC_LXB + j), slot(BON + j)[:, 0:NT], ALU.add, ALU.add, r=[tq[j][2], AK(BON + j), "PC"], w=[tq[j][2]])
            for j in js:
                pg, kg = pk_[j]
                t1 = T[tq[j][2]][:, 0:NT]
                TT("dve", t1, t1, pg[:, 0:NT], ALU.mult, r=[tq[j][2], kg], w=[tq[j][2]])
            for j in js:
                t1 = T[tq[j][2]][:, 0:NT]
                TT("pool", t1, t1, slot(GB + j)[:, 0:NT], ALU.mult, r=[tq[j][2], AK(GB + j)], w=[tq[j][2]])
                TT("pool", slot(MERGED + j)[:, 0:NT], slot(MERGED + j)[:, 0:NT], t1, ALU.add, r=[tq[j][2], AK(MERGED + j)], w=[AK(MERGED + j)])
        P.barrier()
        if stop == "B":
            continue

        def big_out(wv, nrowch, lhs_of, resid):
            for cp_ in range((nrowch + 3) // 4):
                wb, wk = nextWB()
                wbv2 = wb[:, :].rearrange("p (c d) -> p c d", c=4)
                ncc = min(4, nrowch - 4 * cp_)
                DMA("sp", wbv2[:, 0:ncc, :], wv[:, 4 * cp_:4 * cp_ + ncc, :], w=[wk])
                for ci_ in range(ncc):
                    c = 4 * cp_ + ci_
                    for ti, (col0, n, _) in enumerate(tiles):
                        for half in range(2):
                            MM(ps[2 * ti + half][0:n, 0:512], lhs_of(c)[:, col0:col0 + n], wbv2[:, ci_, half * 512:(half + 1) * 512],
                               start=(c == 0), stop=False, r=[wk] + resid[1], w=[("ps", 2 * ti + half)])
            for ti, (col0, n, _) in enumerate(tiles):
                for c in range(8):
                    o_ = ps[2 * ti + c // 4][0:n, (c % 4) * 128:(c % 4 + 1) * 128]
                    MM(o_, XTb[:, c, col0:col0 + n], IDb[:, :], start=False, stop=False, r=["XTb", "ID"], w=[("ps", 2 * ti + c // 4)])
                    MM(o_, XTlo[:, c, col0:col0 + n], IDb[:, :], start=False, stop=(c % 4 == 3), r=["XTlo", "ID"], w=[("ps", 2 * ti + c // 4)])

        MRGb = ARENA[:, 52 * W:56 * W].bitcast(BF16).rearrange("p (c w) -> p c w", c=8)
        TS("pool", MRGb[:, :, 0:NT], slots(MERGED, 8)[:, :, 0:NT], 1.0 / ALPHA, ALU.mult, r=[AK(MERGED + c) for c in range(8)], w=["MRGb"])
        big_out(wout_v, 8, lambda c: MRGb[:, c, :], (None, ["MRGb"]))
        for ti, (col0, n, _) in enumerate(tiles):
            Tt, tkey = TOK[ti % 2], ("TOK", ti % 2)
            CP("act", Tt[0:n, 0:512], ps[2 * ti][0:n, :], r=[("ps", 2 * ti)], w=[tkey])
            CP("dve", Tt[0:n, 512:1024], ps[2 * ti + 1][0:n, :], r=[("ps", 2 * ti + 1)], w=[tkey])
            ln_stats(Tt, n, tkey, eps=EPS / (ALPHA * ALPHA))
            to_feature_major(Tt, n, tkey, col0, PC_L1G, PC_L1B, banks=(6, 7))
        CP("pool", XTb[:, :, 0:NT], XT[:, :, 0:NT], r=["XT"], w=["XTb"])
        TT("dve", XTlo[:, :, 0:NT], XT[:, :, 0:NT], XTb[:, :, 0:NT], ALU.subtract, r=["XT", "XTb"], w=["XTlo"])

        def evAG(i, pt_, pk_):
            if smp:
                d = slot(AG + i)[:, 0:160].rearrange("p (s t) -> p s t", t=10)[:, :, 2:10]
                CP("act", d, pt_[:, 0:128].rearrange("p (s t) -> p s t", t=8), r=[pk_], w=[AK(AG + i)])
            else:
                CP("act", slot(AG + i)[:, 2:2 + NT], pt_[:, 0:NT], r=[pk_], w=[AK(AG + i)])

        def evAV(i, pt_, pk_):
            CP("dve", slot(AV + i)[:, 0:NT], pt_[:, 0:NT], r=[pk_], w=[AK(AV + i)])

        proj_chunks(wup_v, 0, 22, NT, evAG)
        proj_chunks(wup_v, DF, 22, NT, evAV)
        _tn = ["LW", "AA", "KKS", "SQ", "NRM", "KKN", "T1", "BB", "CUM", "EGP", "EGN", "CX"]
        for g0 in range(0, 22, 6):
            idx = list(range(g0, min(g0 + 6, 22)))
            tk = {i: (_tn[2 * (i - g0)], _tn[2 * (i - g0) + 1]) for i in idx}
            for i in idx:
                ag, agk = slot(AG + i), AK(AG + i)
                if smp:
                    a3 = ag[:, 0:160].rearrange("p (s t) -> p s t", t=10)
                    CP("pool", a3[:, :, 0:2], CV0[:, i, :, :], r=[("CV0", i)], w=[agk])
                    CP("pool", CV0[:, i, :, :], a3[:, :, 8:10], r=[agk], w=[("CV0", i)])
                else:
                    CP("pool", ag[:, 0:2], AGC[:, i, :], r=[("AGC", i)], w=[agk])
                    CP("pool", AGC[:, i, :], ag[:, NT:NT + 2], r=[agk], w=[("AGC", i)])
            for i in idx:
                ag, agk = slot(AG + i), AK(AG + i)
                cvk, g1k = tk[i]
                cv = T[cvk]
                if smp:
                    a3 = ag[:, 0:160].rearrange("p (s t) -> p s t", t=10)
                    cv3 = cv[:, 0:128].rearrange("p (s t) -> p s t", t=8)
                    TS("dve", cv3, a3[:, :, 0:8], PCc(PC_CW0 + i), ALU.mult, r=[agk, "PC"], w=[cvk], s2=PCc(PC_CB + i), op1=ALU.add)
                    STT(cv3, a3[:, :, 1:9], PCc(PC_CW1 + i), cv3, ALU.mult, ALU.add, r=[agk, cvk, "PC"], w=[cvk])
                    STT(cv3, a3[:, :, 2:10], PCc(PC_CW2 + i), cv3, ALU.mult, ALU.add, r=[agk, cvk, "PC"], w=[cvk])
                else:
                    ACT(cv[:, 0:NT], ag[:, 0:NT], AF.Identity, r=[agk, "PC"], w=[cvk], bias=PCc(PC_CB + i), scale=PCc(PC_CW0 + i))
                    STT(cv[:, 0:NT], ag[:, 1:NT + 1], PCc(PC_CW1 + i), cv[:, 0:NT], ALU.mult, ALU.add, r=[agk, cvk, "PC"], w=[cvk])
                    STT(cv[:, 0:NT], ag[:, 2:NT + 2], PCc(PC_CW2 + i), cv[:, 0:NT], ALU.mult, ALU.add, r=[agk, cvk, "PC"], w=[cvk])
            for i in idx:
                cvk, g1k = tk[i]
                ACT(T[g1k][:, 0:NT], T[cvk][:, 0:NT], AF.Square, r=[cvk], w=[g1k])
            for i in idx:
                cvk, g1k = tk[i]
                TS("dve", T[g1k][:, 0:NT], T[g1k][:, 0:NT], 0.044715, ALU.mult, r=[g1k], w=[g1k], s2=1.0, op1=ALU.add)
            for i in idx:
                cvk, g1k = tk[i]
                TT("pool", T[g1k][:, 0:NT], T[g1k][:, 0:NT], T[cvk][:, 0:NT], ALU.mult, r=[g1k, cvk], w=[g1k])
            for i in idx:
                cvk, g1k = tk[i]
                ACT(T[g1k][:, 0:NT], T[g1k][:, 0:NT], AF.Sigmoid, r=[g1k], w=[g1k], scale=GELU_K)
            for i in idx:
                cvk, g1k = tk[i]
                TT("pool", T[g1k][:, 0:NT], T[g1k][:, 0:NT], T[cvk][:, 0:NT], ALU.mult, r=[g1k, cvk], w=[g1k])
            for i in idx:
                cvk, g1k = tk[i]
                STT(slot(AG + i).bitcast(BF16)[:, 0:NT], T[g1k][:, 0:NT], 1.0 / ALPHA, slot(AV + i)[:, 0:NT], ALU.mult, ALU.mult,
                    r=[g1k, AK(AV + i), cvk], w=[AK(AG + i)])
        if smp:
            DMA("pool", ocv_s, CV0[:], r=[("CV0", i) for i in range(22)])
        elif tok0 + NT == 2064:
            DMA("pool", ocv_p, AGC[:], r=[("AGC", i) for i in range(22)])

        big_out(wdn_v, 22, lambda c: slot(AG + c).bitcast(BF16), (None, [AK(AG + c) for c in range(22)]))
        for ti, (col0, n, _) in enumerate(tiles):
            Tt, tkey = TOK[ti % 2], ("TOK", ti % 2)
            CP("act", Tt[0:n, 0:512], ps[2 * ti][0:n, :], r=[("ps", 2 * ti)], w=[tkey])
            CP("dve", Tt[0:n, 512:1024], ps[2 * ti + 1][0:n, :], r=[("ps", 2 * ti + 1)], w=[tkey])
            ln_stats(Tt, n, tkey, eps=EPS / (ALPHA * ALPHA))
            TT("pool", Tt[0:n, :], Tt[0:n, :], LNG[0:n, :], ALU.mult, r=[tkey, "LNG"], w=[tkey])
            TT("dve", Tt[0:n, :], Tt[0:n, :], LNB[0:n, :], ALU.add, r=[tkey, "LNB"], w=[tkey])
            if smp:
                DMA("pool", oy_s, Tt[0:128, :], r=[tkey])
            else:
                t_lo = tok0 + col0
                if t_lo < 16:
                    DMA("pool", oy_p[0:n - (16 - t_lo), :], Tt[16 - t_lo:n, :], r=[tkey])
                else:
                    DMA("pool", oy_p[t_lo - 16:t_lo - 16 + n, :], Tt[0:n, :], r=[tkey])

    P.emit(nc)
    st.close()
    return nc


_NC_CACHE = {}


def _host_inputs(inp, b):
    f = lambda a: np.ascontiguousarray(a, dtype=np.float32)
    s0, s1 = 16 * b, 16 * b + 16
    prm0 = np.concatenate([inp["rwkv_mu"][0], inp["rwkv_w0"][0], inp["rwkv_a0"][0], inp["rwkv_kk_scale"][0], inp["rwkv_ka_scale"][0],
                           inp["rwkv_rk"][0], inp["rwkv_lnx_g"][0], inp["rwkv_lnx_b"][0], inp["mlstm_norm_g"][0],
                           inp["ln_in_g"], inp["ln_in_b"], inp["ln1_g"][0], inp["ln1_b"][0]]).reshape(122, 128)
    cw = inp["ffn_conv_w"][0]
    prm1 = np.concatenate([cw[0], cw[1], cw[2], inp["ffn_conv_b"][0]]).reshape(88, 128)
    sS = inp["state_rwkv_S"][0, s0:s1]
    sH = sS.reshape(16, 8, 2, 64, 64).transpose(0, 2, 4, 1, 3).reshape(16, 128, 8, 64)
    ssh = inp["state_rwkv_shift"][0, s0:s1].reshape(16, 8, 128).transpose(2, 1, 0)
    scv = inp["state_ffn_conv"][0, s0:s1].reshape(16, 2, 22, 128).transpose(3, 2, 0, 1)
    return {
        "xp": f(inp["x_prompt"][b]), "xs": f(inp["x_sample"][s0:s1].reshape(128, D)), "meta": f(inp["meta_tokens"]),
        "sC": f(inp["state_mlstm_C"][0, s0:s1]), "sn": f(inp["state_mlstm_n"][0, s0:s1].transpose(0, 2, 1)),
        "sm": f(inp["state_mlstm_m"][0, s0:s1].T), "sH": f(sH), "ssh": f(ssh), "scv": f(scv),
        "prm0": f(prm0), "prm1": f(prm1), "bif": f(inp["b_if"][0].reshape(8, 1)),
        "ln2g": f(inp["ln2_g"][0]), "ln2b": f(inp["ln2_b"][0]),
        "w_in": f(inp["w_in"][0]), "w2a2": f(np.concatenate([inp["rwkv_w2"][0], inp["rwkv_a2"][0]], 0)), "g2": f(inp["rwkv_g2"][0]),
        "w_out": f(inp["w_out"][0]), "w_up": f(inp["ffn_w_up"][0]), "w_down": f(inp["ffn_w_down"][0]),
    }


def kernel(**inputs):
    inp = {k: np.asarray(v) for k, v in inputs.items()}
    if "nc" not in _NC_CACHE:
        _NC_CACHE["nc"] = build()
    nc = _NC_CACHE["nc"]
    in_maps = [_host_inputs(inp, b) for b in range(8)]
    res = run_bass_kernel_spmd(nc, in_maps, core_ids=list(range(8))).results
    g = lambda k: [np.asarray(r[k], dtype=np.float32) for r in res]
    y_p = np.stack(g("oy_p"), 0)
    y_s = np.concatenate(g("oy_s"), 0).reshape(128, 8, D)
    cn_p = np.stack(g("oCN_p"), 0)
    pC = cn_p[..., 0:256].transpose(0, 2, 1, 3)[None]
    pn = cn_p[..., 256].transpose(0, 2, 1)[None]
    pm = np.stack(g("om_p"), 0)[:, :, 0][None]
    Hp = np.stack(g("oH_p"), 0)
    pS = Hp.reshape(8, 2, 64, 8, 64).transpose(0, 3, 1, 4, 2).reshape(8, 16, 64, 64)[None]
    psh = np.stack(g("osh_p"), 0)[..., 0].transpose(0, 2, 1).reshape(8, D)[None]
    pcv = np.stack(g("ocv_p"), 0).transpose(0, 3, 2, 1).reshape(8, 2, DF)[None]
    cn_s = np.concatenate(g("oCN_s"), 0)
    sC = cn_s[..., 0:256].transpose(0, 2, 1, 3)[None]
    sn = cn_s[..., 256].transpose(0, 2, 1)[None]
    sm = np.concatenate([a.T for a in g("om_s")], 0)[None]
    Hs = np.concatenate(g("oH_s"), 0)
    sS = Hs.reshape(128, 2, 64, 8, 64).transpose(0, 3, 1, 4, 2).reshape(128, 16, 64, 64)[None]
    ssh = np.concatenate([a.transpose(2, 1, 0).reshape(16, D) for a in g("osh_s")], 0)[None]
    scv = np.concatenate([a.transpose(2, 3, 1, 0).reshape(16, 2, DF) for a in g("ocv_s")], 0)[None]
    c = lambda a: np.ascontiguousarray(a, dtype=np.float32)
    return (c(y_p), c(y_s), c(pC), c(pn), c(pm), c(pS), c(psh), c(pcv), c(sC), c(sn), c(sm), c(sS), c(ssh), c(scv))
```

```python
import contextlib
import os
import numpy as np
DBGSTEP = int(os.environ.get("DBGSTEP", "0"))
DBGSUB = int(os.environ.get("DBGSUB", "0"))
DBGCHUNK = int(os.environ.get("DBGCHUNK", "0"))
import concourse.bass as bass
import concourse.mybir as mybir
from concourse.bass_utils import run_bass_kernel_spmd

F32 = mybir.dt.float32
BF16 = mybir.dt.bfloat16
AF = mybir.ActivationFunctionType
ALU = mybir.AluOpType

NS_DMA = 6
ENGS = ("pe", "act", "dve", "pool", "sp")


class Op:
    __slots__ = ("eng", "fn", "deps", "dma", "signal", "val", "slot", "strict")


class Prog:
    def __init__(self):
        self.ops = {e: [] for e in ENGS}
        self.lastw = {}
        self.readers = {}
        self.pend = {e: [] for e in ENGS}
        self.dmaq = {e: [] for e in ENGS}

    def op(self, eng, fn, r=(), w=(), dma=False, strict=False):
        o = Op()
        o.strict = strict
        o.eng, o.fn, o.dma, o.signal, o.val, o.slot = eng, fn, dma, False, 0, 0
        deps = set(self.pend[eng])
        self.pend[eng] = []
        for k in r:
            lw = self.lastw.get(k)
            if lw is not None:
                deps.add(lw)
        for k in w:
            lw = self.lastw.get(k)
            if lw is not None:
                deps.add(lw)
            deps.update(self.readers.get(k, ()))
        if dma:
            q = self.dmaq[eng]
            n = len(q)
            o.slot = n % NS_DMA
            o.val = 16 * (n // NS_DMA + 1)
            if n >= NS_DMA:
                deps.add(q[n - NS_DMA])
            q.append(o)
        o.deps = deps
        for k in r:
            lst = self.readers.setdefault(k, [])
            if not dma:
                lst[:] = [x for x in lst if x.dma or x.eng != eng]
            lst.append(o)
        for k in w:
            self.lastw[k] = o
            self.readers[k] = []
        self.ops[eng].append(o)
        return o

    def barrier(self):
        lasts = []
        for e in ENGS:
            comp = [x for x in self.ops[e] if not x.dma]
            if comp:
                lasts.append(comp[-1])
            lasts.extend(self.dmaq[e][-NS_DMA:])
        for e in ENGS:
            self.pend[e].extend(lasts)
        self.lastw.clear()
        self.readers.clear()

    def emit(self, nc):
        for e in ENGS:
            for o in self.ops[e]:
                for d in o.deps:
                    if not d.dma and not (d.eng == "pe" and e == "pe" and not o.strict):
                        d.signal = True
        for e in ENGS:
            c = 0
            for o in self.ops[e]:
                if not o.dma and o.signal:
                    c += 1
                    o.val = c
        with contextlib.ExitStack() as st:
            csem = {e: st.enter_context(nc.semaphore("c_" + e)) for e in ENGS}
            dsem = {e: [st.enter_context(nc.semaphore("d_%s%d" % (e, i))) for i in range(NS_DMA)]
                    for e in ENGS if self.dmaq[e]}
            block = st.enter_context(nc.Block())

            def semof(o):
                return dsem[o.eng][o.slot] if o.dma else csem[o.eng]

            def run(e, h):
                waited = {}
                for o in self.ops[e]:
                    need = {}
                    for d in o.deps:
                        if d.eng == "pe" and e == "pe" and not d.dma and not o.strict:
                            continue
                        sm = semof(d)
                        if waited.get(sm, 0) < d.val and need.get(sm, 0) < d.val:
                            need[sm] = d.val
                    for sm, v in need.items():
                        h.wait_ge(sm, v)
                        waited[sm] = v
                    ins = o.fn(h)
                    if o.dma:
                        ins.then_inc(semof(o), 16)
                    elif o.signal:
                        ins.then_inc(csem[e], 1)
                for o in self.dmaq[e][-NS_DMA:]:
                    if waited.get(semof(o), 0) < o.val:
                        h.wait_ge(semof(o), o.val)
                        waited[semof(o)] = o.val

            @block.tensor
            def _(h):
                run("pe", h)

            @block.scalar
            def _(h):
                run("act", h)

            @block.vector
            def _(h):
                run("dve", h)

            @block.gpsimd
            def _(h):
                run("pool", h)

            @block.sync
            def _(h):
                run("sp", h)


D = 1024
DF = 2816
W = 274
EPS = 1e-5
GN_EPS = 64e-5
ALPHA = float(2.0 ** 0.25)
C0 = float(np.exp(-0.5))
NEG = -1.0e30
GELU_K = float(2.0 * np.sqrt(2.0 / np.pi))

MERGED = 0
QA, KA, VA, OA, GA = 8, 12, 16, 24, 32
GB, KKT, RB, KB, VB, BHT, BON = 8, 16, 24, 32, 40, 48, 56
AG, AV = 8, 30
NSLOT = 64

PC_MU, PC_W0, PC_A0, PC_KKS, PC_KAS, PC_RK, PC_LXG, PC_LXB, PC_MNG = 0, 26, 34, 42, 50, 58, 66, 74, 82
PC_LIG, PC_LIB, PC_L1G, PC_L1B = 90, 98, 106, 114
PC_CW0, PC_CW1, PC_CW2, PC_CB = 128, 150, 172, 194

BLOCKS = [("p", 0, 272, [(0, 16), (16, 64), (80, 64), (144, 64), (208, 64)], 0)]
for _i in range(1, 8):
    BLOCKS.append(("p", 272 + 256 * (_i - 1), 256, [(64 * c, 64) for c in range(4)], 1))
BLOCKS.append(("s", 0, 128, [(8 * c, 8) for c in range(16)], 2))


def build(blocks=BLOCKS, stop=None):
    nc = bass.Bass("TRN2", target_bir_lowering=False)

    def din(name, shape):
        return nc.dram_tensor(name, list(shape), F32, kind="ExternalInput").ap()

    def dout(name, shape):
        return nc.dram_tensor(name, list(shape), F32, kind="ExternalOutput").ap()

    xp = din("xp", [2048, D]); xs = din("xs", [128, D]); meta = din("meta", [16, D])
    sC = din("sC", [16, 4, 128, 256]); sn = din("sn", [16, 128, 4]); sm = din("sm", [4, 16])
    sH = din("sH", [16, 128, 8, 64]); ssh = din("ssh", [128, 8, 16]); scv = din("scv", [128, 22, 16, 2])
    prm0 = din("prm0", [122, 128]); prm1 = din("prm1", [88, 128])
    bif = din("bif", [8, 1]); ln2g = din("ln2g", [D]); ln2b = din("ln2b", [D])
    w_in = din("w_in", [D, 8456]); w2a2 = din("w2a2", [128, D]); g2 = din("g2", [128, D])
    w_out = din("w_out", [D, D]); w_up = din("w_up", [D, 2 * DF]); w_down = din("w_down", [DF, D])

    oy_p = dout("oy_p", [2048, D]); oy_s = dout("oy_s", [128, D])
    oCN_p = dout("oCN_p", [128, 4, 257]); om_p = dout("om_p", [4, 1]); oH_p = dout("oH_p", [128, 8, 64])
    osh_p = dout("osh_p", [128, 8, 1]); ocv_p = dout("ocv_p", [128, 22, 2])
    oCN_s = dout("oCN_s", [16, 128, 4, 257]); om_s = dout("om_s", [4, 16]); oH_s = dout("oH_s", [16, 128, 8, 64])
    osh_s = dout("osh_s", [128, 8, 16]); ocv_s = dout("ocv_s", [128, 22, 16, 2])

    wb_in = nc.dram_tensor("wb_in", [D, 8456], BF16).ap(); wb_out = nc.dram_tensor("wb_out", [D, D], BF16).ap()
    wb_up = nc.dram_tensor("wb_up", [D, 2 * DF], BF16).ap(); wb_down = nc.dram_tensor("wb_down", [DF, D], BF16).ap()
    win_v = wb_in.rearrange("(kc p) c -> p kc c", p=128)
    wup_v = wb_up.rearrange("(kc p) c -> p kc c", p=128)
    wout_v = wb_out.rearrange("(c p) d -> p c d", p=128)
    wdn_v = wb_down.rearrange("(c p) d -> p c d", p=128)

    P = Prog()
    st = contextlib.ExitStack()

    def sb(name, shape):
        return st.enter_context(nc.sbuf_tensor(name, list(shape), F32))

    ARENA = sb("ARENA", [128, NSLOT * W])
    XT = sb("XT", [128, 8, 272])
    TOK = [sb("TOK0", [128, D]), sb("TOK1", [128, D])]
    LNG = sb("LNG", [128, D]); LNB = sb("LNB", [128, D])
    WB = [st.enter_context(nc.sbuf_tensor("WB%d" % i, [128, 4096], BF16)) for i in range(2)]
    XTb = st.enter_context(nc.sbuf_tensor("XTb", [128, 8, 272], BF16))
    XTlo = st.enter_context(nc.sbuf_tensor("XTlo", [128, 8, 272], BF16))
    PC = sb("PC", [128, 216]); PCN = sb("PCN", [128, 16])
    W2A2 = sb("W2A2", [128, D]); G2 = sb("G2", [128, D])
    ID = sb("ID", [128, 128]); ONES = sb("ONES", [128, 128]); BLK = sb("BLK", [128, 128]); AID = sb("AID", [128, 128])
    MK1 = sb("MK1", [64, 2, 64]); MK1N = sb("MK1N", [64, 2, 64]); MLN = sb("MLN", [64, 64]); MNEG = sb("MNEG", [64, 64])
    SEL = sb("SEL", [4, 4, 128])
    RST01_ = sb("RST01", [128, W])
    RST01 = [RST01_, RST01_, RST01_]
    RSTN = sb("RSTN", [4, W])
    CN = [sb("CN0", [128, 4, 257]), sb("CN1", [128, 4, 257])]
    NBC = st.enter_context(nc.sbuf_tensor("NBC", [128, 4, 128], BF16))
    CNb = st.enter_context(nc.sbuf_tensor("CNb", [128, 4, 257], BF16))
    Hb = [st.enter_context(nc.sbuf_tensor("Hb0", [128, 8, 64], BF16)), st.enter_context(nc.sbuf_tensor("Hb1", [128, 8, 64], BF16))]
    IDb = st.enter_context(nc.sbuf_tensor("IDb", [128, 128], BF16)); ONESb = st.enter_context(nc.sbuf_tensor("ONESb", [128, 128], BF16))
    H = [sb("H0", [128, 8, 64]), sb("H1", [128, 8, 64])]
    CARRY = sb("CARRY", [128, 26]); AGC = sb("AGC", [128, 22, 2])
    CV0 = sb("CV0", [128, 22, 16, 2]);
    XWA = sb("XWA", [128, W]); XG = sb("XG", [128, W])
    GL = sb("GL", [128, 8, 16])
    BI = sb("BI", [4, 1]); NBF = sb("NBF", [4, 1]); MCOL = sb("MCOL", [4, 2]); M0ROW = sb("M0ROW", [4, 16]); MNEW = sb("MNEW", [4, 16])
    ST6 = sb("ST6", [128, 12]); MV = sb("MV", [128, 2]); SD = sb("SD", [128, 1]); RS = sb("RS", [128, 1])
    TN = ["LW", "AA", "KKS", "SQ", "NRM", "KKN", "T1", "BB", "CUM", "EGP", "EGN", "CX", "EGX"]
    T = {n: sb("T_" + n, [128, W]) for n in TN}
    PR0 = T["KKS"][:, 0:128]; PR1 = T["SQ"][:, 0:128]
    SHS = T["EGX"][:, 0:128].rearrange("p (a b) -> p a b", a=8)
    LI = sb("LI", [4, W]); LP = sb("LP", [4, W]); CUMP = sb("CUMP", [4, W]); DD = sb("DD", [4, W]); CM = sb("CM", [4, W])
    MX = sb("MX", [4, 64]); ROWS = sb("ROWS", [4, 3, 64]); TR4 = sb("TR4", [4, 64]); WKR = sb("WKR", [4, 64])
    COLS = sb("COLS", [64, 8])
    WST = [sb("WST%d" % i, [64, 64]) for i in range(4)]; SST = [st.enter_context(nc.sbuf_tensor("SST%d" % i, [64, 64], BF16)) for i in range(4)]
    QW = [st.enter_context(nc.sbuf_tensor("QW%d" % i, [128, 64], BF16)) for i in range(4)]; ADEN = [sb("ADEN%d" % i, [128, 64]) for i in range(4)]
    DDT = [sb("DDT%d" % i, [128, 64]) for i in range(4)]; KW = [st.enter_context(nc.sbuf_tensor("KW%d" % i, [64, 128], BF16)) for i in range(4)]
    DCOL = sb("DCOL", [128, 4])
    VTK = st.enter_context(nc.sbuf_tensor("VTK", [64, 512], BF16)); KHK = st.enter_context(nc.sbuf_tensor("KHK", [64, 512], BF16)); BHK = st.enter_context(nc.sbuf_tensor("BHK", [64, 512], BF16))
    S1raw = st.enter_context(nc.sbuf_tensor("S1raw", [64, 1028], BF16)); S2 = st.enter_context(nc.sbuf_tensor("S2", [64, 8, 2, 64], BF16)); PA = st.enter_context(nc.sbuf_tensor("PA", [64, 8, 64], BF16))
    S1 = S1raw[:, 0:1024].rearrange("p (a b c) -> p a b c", a=8, b=2)
    VTK1 = S1raw[:, 0:1028].rearrange("p (h c) -> p h c", h=4)
    KTK = KHK
    PN = [st.enter_context(nc.sbuf_tensor("PN%d" % i, [64, 8, 64], BF16)) for i in range(2)]; PTN = [st.enter_context(nc.sbuf_tensor("PTN%d" % i, [64, 8, 64], BF16)) for i in range(2)]
    U = [st.enter_context(nc.sbuf_tensor("U0", [64, 512], BF16)), st.enter_context(nc.sbuf_tensor("U1", [64, 512], BF16))]
    TMPH = sb("TMPH", [128, 4, 64])

    def sbs(name, shape):
        return st.enter_context(nc.sbuf_tensor(name, shape, BF16))
    setB = (sbs("VTKs", [16, 512]), sbs("KHKs", [16, 512]), sbs("BHKs", [16, 512]), sbs("S1s", [16, 8, 2, 16]), sbs("S2s", [16, 8, 2, 16]),
            sbs("PAs", [16, 8, 16]), [sbs("PNs%d" % i, [16, 8, 16]) for i in range(2)], [sbs("PTNs%d" % i, [16, 8, 16]) for i in range(2)],
            [sbs("Us%d" % i, [16, 512]) for i in range(2)], sb("TMPHs", [128, 4, 64]), "B")
    ps = [st.enter_context(nc.psum_tensor("ps%d" % i, [128, 512], F32)) for i in range(8)]
    psi = [0]

    def PS():
        i = psi[0]
        psi[0] = (i + 1) % 8
        return ps[i], ("ps", i)

    def slot(i):
        return ARENA[:, i * W:(i + 1) * W]

    def slots(i, n):
        return ARENA[:, i * W:(i + n) * W].rearrange("p (c w) -> p c w", c=n)

    def AK(i):
        return ("A", i)

    BFA = ARENA[:, KKT * W:(KKT + 8) * W].bitcast(BF16)
    BFB = ARENA[:, BHT * W:(BHT + 8) * W].bitcast(BF16)

    def KKTb(j):
        return BFA[:, j * W:(j + 1) * W]

    def RTb(j):
        return BFA[:, (8 + j) * W:(9 + j) * W]

    def KHb(j):
        return BFB[:, j * W:(j + 1) * W]

    def BHb(j):
        return BFB[:, (8 + j) * W:(9 + j) * W]

    def Qb(h):
        return slot(QA + h).bitcast(BF16)

    def Kb(h):
        return slot(KA + h).bitcast(BF16)

    def MM(out, lhsT, rhs, start=True, stop=True, r=(), w=(), strict=False):
        P.op("pe", lambda h: h.matmul(out, lhsT=lhsT, rhs=rhs, start=start, stop=stop), r, w, strict=strict)

    def TR(out, in_, n, r=(), w=(), bf=False):
        idn = IDb[0:n, 0:n] if bf else ID[0:n, 0:n]
        P.op("pe", lambda h: h.transpose(out=out, in_=in_, identity=idn), list(r) + ["ID"], w)

    def ACT(out, in_, func, r=(), w=(), bias=0.0, scale=1.0):
        P.op("act", lambda h: h.activation(out=out, in_=in_, func=func, bias=bias, scale=scale), r, w)

    def TT(eng, out, in0, in1, op, r=(), w=()):
        P.op(eng, lambda h: h.tensor_tensor(out=out, in0=in0, in1=in1, op=op), r, w)

    def TS(eng, out, in0, s1, op0, r=(), w=(), s2=None, op1=None):
        if op1 is None and eng == "pool" and op0 == ALU.mult:
            s2, op1 = 1.0, ALU.mult
        if op1 is None:
            P.op(eng, lambda h: h.tensor_scalar(out=out, in0=in0, scalar1=s1, scalar2=None, op0=op0), r, w)
        else:
            P.op(eng, lambda h: h.tensor_scalar(out=out, in0=in0, scalar1=s1, scalar2=s2, op0=op0, op1=op1), r, w)

    def STT(out, in0, sc, in1, op0, op1, r=(), w=()):
        P.op("dve", lambda h: h.scalar_tensor_tensor(out=out, in0=in0, scalar=sc, in1=in1, op0=op0, op1=op1), r, w)

    def CP(eng, out, in_, r=(), w=()):
        if eng == "act":
            P.op("act", lambda h: h.copy(out=out, in_=in_), r, w)
        else:
            P.op(eng, lambda h: h.tensor_copy(out=out, in_=in_), r, w)

    def RECIP(out, in_, r=(), w=()):
        P.op("dve", lambda h: h.reciprocal(out=out, in_=in_), r, w)

    def MSET(eng, ap, v, w=()):
        P.op(eng, lambda h: h.memset(ap, v), (), w)

    def DMA(q, out, in_, r=(), w=(), slow=False):
        P.op(q, lambda h: h.dma_start(out=out, in_=in_, allow_slow_non_contiguous=slow), r, w, dma=True)

    def PCc(i):
        return PC[:, i:i + 1]

    DMA("sp", PR0[0:122, :], prm0, w=["PR0"])
    DMA("sp", PR1[0:88, :], prm1, w=["PR1"])
    DMA("sp", W2A2[:], w2a2, w=["W2A2"])
    DMA("sp", G2[:], g2, w=["G2"])
    DMA("sp", LNG[:], ln2g.partition_broadcast(128), w=["LNG"])
    DMA("sp", LNB[:], ln2b.partition_broadcast(128), w=["LNB"])
    DMA("sp", BI[:], bif[0:4, :], w=["BI"])
    DMA("sp", NBF[:], bif[4:8, :], w=["NBF"])
    DMA("sp", M0ROW[:], sm, w=["M0ROW"])
    DMA("sp", CV0[:], scv, w=[("CV0", i) for i in range(22)])
    MSET("pool", ONES[:], 1.0, w=["ONES"])
    ZER = T["LW"][:, 0:128]; NEG1 = T["AA"][:, 0:128]
    MSET("pool", ZER, 0.0, w=["ZER"])
    MSET("pool", NEG1, -1.0, w=["NEG1"])
    P.op("pool", lambda h: h.affine_select(out=ID[:], in_=ONES[:], pattern=[[1, 128]], compare_op=ALU.is_equal,
                                           fill=0.0, base=0, channel_multiplier=-1), ["ONES"], ["ID"])
    TS("pool", AID[:], ID[:], ALPHA, ALU.mult, r=["ID"], w=["AID"])
    CP("pool", IDb[:], ID[:], r=["ID"], w=["ID"])
    MSET("pool", ONESb[:], 1.0, w=["ONES"])
    MSET("pool", BLK[:], 0.0, w=["BLK"])
    MSET("pool", BLK[0:64, 0:64], 1.0, w=["BLK"])
    MSET("pool", BLK[64:128, 64:128], 1.0, w=["BLK"])
    P.op("pool", lambda h: h.affine_select(out=MK1[:, 0, :], in_=ONES[0:64, 0:64], pattern=[[1, 64]], compare_op=ALU.is_gt,
                                           fill=0.0, base=0, channel_multiplier=-1), ["ONES"], ["MK1"])
    P.op("pool", lambda h: h.affine_select(out=MK1[:, 1, :], in_=ONES[0:64, 0:64], pattern=[[1, 64]], compare_op=ALU.is_ge,
                                           fill=0.0, base=0, channel_multiplier=-1), ["ONES"], ["MK1"])
    TS("pool", MK1N[:], MK1[:], -1.0, ALU.mult, r=["MK1"], w=["MK1N"])
    P.op("pool", lambda h: h.affine_select(out=MLN[:], in_=NEG1[0:64, 0:64], pattern=[[-1, 64]], compare_op=ALU.is_gt,
                                           fill=0.0, base=0, channel_multiplier=1), ["NEG1"], ["MLN"])
    P.op("pool", lambda h: h.affine_select(out=MNEG[:], in_=ZER[0:64, 0:64], pattern=[[1, 64]], compare_op=ALU.is_ge,
                                           fill=NEG, base=0, channel_multiplier=-1), ["ZER"], ["MNEG"])
    CP("dve", SEL[:], ID[0:4, 0:4].unsqueeze(2).broadcast_to([4, 4, 128]), r=["ID"], w=["SEL"])
    TS("pool", NBF[:], NBF[:], -1.0, ALU.mult, r=["NBF"], w=["NBF"])
    pt, pk = PS()
    TR(pt[:, 0:122], PR0[0:122, :], 122, r=["PR0"], w=[pk])
    TR(pt[:, 128:216], PR1[0:88, :], 88, r=["PR1"], w=[pk])
    CP("dve", PC[:, 0:122], pt[:, 0:122], r=[pk], w=["PC"])
    CP("dve", PC[:, 128:216], pt[:, 128:216], r=[pk], w=["PC"])
    TS("dve", PCN[:], PC[:, PC_W0:PC_W0 + 16], -1.0, ALU.mult, r=["PC"], w=["PC"])
    MSET("pool", CN[0][:], 0.0, w=[("CN", 0, h) for h in range(4)])
    MSET("pool", NBC[:], 0.0, w=[("NBC", h) for h in range(4)])
    MSET("pool", H[0][:], 0.0, w=[("H", 0, 0), ("H", 0, 1)])
    MSET("pool", Hb[0][:], 0.0, w=[("Hb", 0, 0), ("Hb", 0, 1)])
    MSET("pool", CNb[:], 0.0, w=[("CNb", h) for h in range(4)])
    MSET("pool", MCOL[:], 0.0, w=["MC0", "MC1"])
    MSET("pool", CARRY[:], 0.0, w=[("CARRY", c) for c in range(26)])
    MSET("pool", AGC[:], 0.0, w=[("AGC", i) for i in range(22)])
    pcs = []
    for (wsrc, wdst, R, C) in ((w_in, wb_in, D, 8456), (w_out, wb_out, D, D), (w_up, wb_up, D, 2 * DF), (w_down, wb_down, DF, D)):
        for r0 in range(0, R, 128):
            for c0 in range(0, C, 4228):
                pcs.append((wsrc, wdst, r0, c0, min(4228, C - c0)))
    for pi_, (wsrc, wdst, r0, c0, n_) in enumerate(pcs):
        sl = pi_ % 2
        stg = ARENA[:, sl * 4228:sl * 4228 + n_]
        ob = ARENA[:, 8456 + sl * 2114:8456 + (sl + 1) * 2114].bitcast(BF16)[:, 0:n_]
        DMA("sp", stg, wsrc[r0:r0 + 128, c0:c0 + n_], w=[("STG", sl)])
        CP(("act", "dve", "pool")[pi_ % 3], ob, stg, r=[("STG", sl)], w=[("OB", sl)])
        DMA("pool", wdst[r0:r0 + 128, c0:c0 + n_], ob, r=[("OB", sl)])
    P.barrier()

    wbi = [0]

    def nextWB():
        i = wbi[0]
        wbi[0] = 1 - i
        return WB[i], ("WB", i)

    def ln_stats(Tt, n, tkey, eps=EPS):
        P.op("dve", lambda h: h.bn_stats(out=ST6[0:n, 0:6], in_=Tt[0:n, 0:512]), [tkey], ["ST6"])
        P.op("dve", lambda h: h.bn_stats(out=ST6[0:n, 6:12], in_=Tt[0:n, 512:1024]), [tkey], ["ST6"])
        P.op("dve", lambda h: h.bn_aggr(out=MV[0:n, :], in_=ST6[0:n, :]), ["ST6"], ["MV"])
        ACT(SD[0:n, :], MV[0:n, 1:2], AF.Ln, r=["MV"], w=["SD"], bias=eps)
        ACT(RS[0:n, :], SD[0:n, :], AF.Exp, r=["SD"], w=["RS"], scale=-0.5)
        TS("dve", Tt[0:n, :], Tt[0:n, :], MV[0:n, 0:1], ALU.subtract, r=[tkey, "MV", "RS"], w=[tkey], s2=RS[0:n, 0:1], op1=ALU.mult)

    def to_feature_major(Tt, n, tkey, col0, gcol, bcol, banks=None):
        for half in range(2):
            if banks is None:
                pt_, pk_ = PS()
            else:
                pt_, pk_ = ps[banks[half]], ("ps", banks[half])
            for i in range(4):
                kc = half * 4 + i
                TR(pt_[:, i * 128:i * 128 + n], Tt[0:n, kc * 128:(kc + 1) * 128], n, r=[tkey], w=[pk_])
            for i in range(4):
                kc = half * 4 + i
                if i % 2 == 0:
                    ACT(XT[:, kc, col0:col0 + n], pt_[:, i * 128:i * 128 + n], AF.Identity, r=[pk_, "PC"], w=["XT"],
                        bias=PCc(bcol + kc), scale=PCc(gcol + kc))
                else:
                    TS("dve", XT[:, kc, col0:col0 + n], pt_[:, i * 128:i * 128 + n], PCc(gcol + kc), ALU.mult,
                       r=[pk_, "PC"], w=["XT"], s2=PCc(bcol + kc), op1=ALU.add)

    prefq = []

    def wload_proj(wv, c_lo, nb, prefetch=False):
        tag = ("p", id(wv), c_lo, nb)
        if not prefetch and prefq and prefq[0][0] == tag:
            return prefq.pop(0)[1:]
        wb, wk = nextWB()
        wbv = wb[:, :].rearrange("p (k c) -> p k c", k=8)
        DMA("sp", wbv[:, :, 0:nb * 128], wv[:, :, c_lo:c_lo + nb * 128], w=[wk])
        if prefetch:
            prefq.append((tag, wb, wk))
        return wb, wk

    def wload_rows(wv, cp_, ncc, prefetch=False):
        tag = ("b", id(wv), cp_, ncc)
        if not prefetch and prefq and prefq[0][0] == tag:
            return prefq.pop(0)[1:]
        wb, wk = nextWB()
        wbv2 = wb[:, :].rearrange("p (c d) -> p c d", c=4)
        DMA("sp", wbv2[:, 0:ncc, :], wv[:, 4 * cp_:4 * cp_ + ncc, :], w=[wk])
        if prefetch:
            prefq.append((tag, wb, wk))
        return wb, wk

    def proj_chunks(wv, col0, nch, ncols, evac):
        i = 0
        while i < nch:
            nb = min(4, nch - i)
            wb, wk = wload_proj(wv, col0 + i * 128, nb)
            wbv = wb[:, :].rearrange("p (k c) -> p k c", k=8)
            for b in range(nb):
                pt_, pk_ = PS()
                for kc in range(8):
                    MM(pt_[:, 0:ncols], wbv[:, kc, b * 128:(b + 1) * 128], XTb[:, kc, 0:ncols], start=(kc == 0), stop=(kc == 7),
                       r=[wk, "XTb"], w=[pk_])
                evac(i + b, pt_, pk_)
            i += nb

    cur_ty = [-1]
    gchunk = [0]

    for (kind, tok0, NT, chunks, ty) in blocks:
        smp = (kind == "s")
        NTB = NT + (16 if smp else 0)
        if ty != cur_ty[0]:
            cur_ty[0] = ty
            MSET("pool", RST01_[:], 1.0, w=["RST"])
            if ty == 0:
                MSET("pool", RST01_[:, 0:1], 0.0, w=["RST"])
                MSET("pool", RST01_[:, 16:272:64], 0.0, w=["RST"])
            elif ty == 1:
                MSET("pool", RST01_[:, 0:256:64], 0.0, w=["RST"])
            else:
                MSET("pool", RST01_[:, 0:128:8], 0.0, w=["RST"])
        tiles = []
        if smp:
            tiles.append((0, 128, [(0, 128, xs)]))
        else:
            c = 0
            while c < NT:
                n = min(128, NT - c)
                srcs = []
                t_lo, t_hi = tok0 + c, tok0 + c + n
                if t_lo < 16:
                    srcs.append((0, 16 - t_lo, meta[t_lo:16, :]))
                    srcs.append((16 - t_lo, n - (16 - t_lo), xp[0:t_hi - 16, :]))
                else:
                    srcs.append((0, n, xp[t_lo - 16:t_hi - 16, :]))
                tiles.append((c, n, srcs))
                c += n

        for ti, (col0, n, srcs) in enumerate(tiles):
            Tt, tkey = TOK[ti % 2], ("TOK", ti % 2)
            for (r0, nr, ap) in srcs:
                DMA("sp", Tt[r0:r0 + nr, :], ap, w=[tkey])
            ln_stats(Tt, n, tkey)
            to_feature_major(Tt, n, tkey, col0, PC_LIG, PC_LIB)
        if smp:
            DMA("sp", XT[:, :, 128:144], ssh, w=["XT"])
        CP("pool", XTb[:, :, 0:NTB], XT[:, :, 0:NTB], r=["XT"], w=["XTb"])
        TT("dve", XTlo[:, :, 0:NT], XT[:, :, 0:NT], XTb[:, :, 0:NT], ALU.subtract, r=["XT", "XTb"], w=["XTlo"])
        if smp:
            pass
            CP("pool", SHS[:], XT[:, :, 7:128:8], r=["XT"], w=["EGX"])
            DMA("pool", osh_s, SHS[:], r=["EGX"])
        elif tok0 + NT == 2064:
            CP("pool", SHS[:, :, 0:1], XT[:, :, NT - 1:NT], r=["XT"], w=["EGX"])
            DMA("pool", osh_p, SHS[:, :, 0:1], r=["EGX"], slow=True)

        if stop == "0":
            P.barrier()
            continue
        def evA(dst0, kindA):
            def f(i, pt_, pk_):
                d = slot(dst0 + i)[:, 0:NT]
                if dst0 in (QA, KA):
                    d = slot(dst0 + i).bitcast(BF16)[:, 0:NT]
                if kindA == "copy":
                    if i % 2 == 0:
                        CP("act", d, pt_[:, 0:NT], r=[pk_], w=[AK(dst0 + i)])
                    else:
                        CP("dve", d, pt_[:, 0:NT], r=[pk_], w=[AK(dst0 + i)])
                elif kindA == "kscale":
                    ACT(d, pt_[:, 0:NT], AF.Identity, r=[pk_], w=[AK(dst0 + i)], scale=float(128 ** -0.5))
                else:
                    ACT(d, pt_[:, 0:NT], AF.Sigmoid, r=[pk_], w=[AK(dst0 + i)])
            return f

        proj_chunks(win_v, 2048, 4, NT, evA(QA, "copy"))
        proj_chunks(win_v, 2560, 4, NT, evA(KA, "kscale"))
        proj_chunks(win_v, 3072, 8, NT, evA(VA, "copy"))
        proj_chunks(win_v, 4096, 8, NT, evA(OA, "sig"))
        wb, wk = nextWB()
        wbv = wb[:, :].rearrange("p (k c) -> p k c", k=8)
        DMA("sp", wbv[:, :, 0:8], win_v[:, :, 5120:5128], w=[wk])
        pi_, ki_ = PS()
        pf_, kf_ = PS()
        for kc in range(8):
            MM(pi_[0:4, 0:NT], wbv[:, kc, 0:4], XTb[:, kc, 0:NT], start=(kc == 0), stop=(kc == 7), r=[wk, "XTb"], w=[ki_])
        for kc in range(8):
            MM(pf_[0:4, 0:NT], wbv[:, kc, 4:8], XTb[:, kc, 0:NT], start=(kc == 0), stop=(kc == 7), r=[wk, "XTb"], w=[kf_])
        ACT(LI[0:4, 0:NT], pi_[0:4, 0:NT], AF.Identity, r=[ki_, "BI"], w=["LI"], bias=BI[0:4, 0:1])
        ACT(LP[0:4, 0:NT], pf_[0:4, 0:NT], AF.Exp, r=[kf_, "NBF"], w=["LP"], bias=NBF[0:4, 0:1], scale=-1.0)
        ACT(LP[0:4, 0:NT], LP[0:4, 0:NT], AF.Ln, r=["LP"], w=["LP"], bias=1.0)
        P.op("dve", lambda h, NT=NT, ty=ty: h.tensor_tensor_scan(out=CUMP[0:4, 0:NT], data0=RST01[ty][0:4, 0:NT], data1=LP[0:4, 0:NT],
                                                  initial=0.0, op0=ALU.mult, op1=ALU.add), ["LP", "RST"], ["CUMP"])
        TT("dve", DD[0:4, 0:NT], LI[0:4, 0:NT], CUMP[0:4, 0:NT], ALU.add, r=["LI", "CUMP"], w=["DD"])
        TS("pool", RSTN[0:4, 0:NT], RST01[ty][0:4, 0:NT], 1.0, ALU.subtract, r=["RST"], w=["RSN"], s2=1.0e30, op1=ALU.mult)
        P.op("dve", lambda h, NT=NT, ty=ty: h.tensor_tensor_scan(out=CM[0:4, 0:NT], data0=RSTN[0:4, 0:NT], data1=DD[0:4, 0:NT],
                                                  initial=0.0, op0=ALU.add, op1=ALU.max), ["DD", "RSN"], ["CM"])
        proj_chunks(win_v, 0, 8, NT, evA(GA, "sig"))
        if stop is None:
            wload_proj(win_v, 1024, 4, prefetch=True)
            wload_proj(win_v, 1536, 4, prefetch=True)
        for c in range(8):
            TT("pool", slot(OA + c)[:, 0:NT], slot(OA + c)[:, 0:NT], slot(GA + c)[:, 0:NT], ALU.mult, r=[AK(OA + c), AK(GA + c)], w=[AK(OA + c)])

        MSET("pool", VTK1[:, :, 256:257], 1.0, w=["S1"])
        for ci, (c0, L) in enumerate(chunks):
            cs = slice(c0, c0 + L)
            if smp:
                cnb = ci % 2
                m_in, m_in_k = M0ROW[0:4, ci:ci + 1], "M0ROW"
                m_out, m_out_k = MNEW[0:4, ci:ci + 1], "MNEW"
                cnk = [("CN", cnb, h) for h in range(4)]
                DMA("sp", CN[cnb][:, :, 0:256], sC[ci].rearrange("h k v -> k h v"), w=cnk)
                DMA("sp", CN[cnb][:, :, 256:257], sn[ci].unsqueeze(2), w=cnk, slow=True)
                CP("pool", NBC[:], CN[cnb][:, :, 256:257].broadcast_to([128, 4, 128]), r=cnk, w=[("NBC", h) for h in range(4)])
                CP("act", CNb[:], CN[cnb][:], r=cnk, w=[("CNb", h) for h in range(4)])
            else:
                cnb = 0
                g = gchunk[0]
                gchunk[0] += 1
                m_in, m_in_k = MCOL[0:4, g % 2:g % 2 + 1], "MC%d" % (g % 2)
                m_out, m_out_k = MCOL[0:4, (g + 1) % 2:(g + 1) % 2 + 1], "MC%d" % ((g + 1) % 2)
            CNt = CN[cnb]
            TS("dve", MX[0:4, 0:L], CM[0:4, cs], m_in, ALU.max, r=["CM", m_in_k], w=["MX"])
            TS("dve", ROWS[0:4, 0, 0:L], MX[0:4, 0:L], -1.0, ALU.mult, r=["MX"], w=["ROWS"])
            ACT(ROWS[0:4, 1, 0:L], MX[0:4, 0:L], AF.Exp, r=["MX", m_in_k], w=["ROWS"], bias=m_in, scale=-1.0)
            TT("dve", TR4[0:4, 0:L], CUMP[0:4, cs], MX[0:4, 0:L], ALU.subtract, r=["CUMP", "MX"], w=["TR4"])
            ACT(ROWS[0:4, 2, 0:L], TR4[0:4, 0:L], AF.Exp, r=["TR4"], w=["ROWS"])
            ACT(WKR[0:4, 0:L], DD[0:4, cs], AF.Exp, r=["DD", "ROWS"], w=["WKR"], bias=ROWS[0:4, 0, L - 1:L])
            TT("dve", m_out, MX[0:4, L - 1:L], CUMP[0:4, c0 + L - 1:c0 + L], ALU.subtract, r=["MX", "CUMP"], w=[m_out_k])
            pt_, pk_ = PS()
            TR(pt_[0:L, 0:4], DD[0:4, cs], 4, r=["DD"], w=[pk_])
            TR(pt_[0:L, 4:8], WKR[0:4, 0:L], 4, r=["WKR"], w=[pk_])
            CP("act", COLS[0:L, 0:8], pt_[0:L, 0:8], r=[pk_], w=["COLS"])
            pk1, kk1 = PS()
            pk1b = pk1[:, :].bitcast(BF16)
            for h in range(4):
                TR(pk1b[0:L, h * 128:(h + 1) * 128], Kb(h)[:, cs], 128, r=[AK(KA + h)], w=[kk1], bf=True)
            CP("dve", KTK[0:L, :], pk1b[0:L, 0:512], r=[kk1], w=["KHK"])
            for half in range(2):
                pv, kv = PS()
                for i in range(4):
                    c = half * 4 + i
                    TR(pv[0:L, i * 128:(i + 1) * 128], slot(VA + c)[:, cs], 128, r=[AK(VA + c)], w=[kv])
                CP("act", VTK1[0:L, half * 2:half * 2 + 2, 0:256], pv[0:L, :].rearrange("p (a b) -> p a b", a=2), r=[kv], w=["S1"])
            for h in range(4):
                TS("pool", KW[h][0:L, :], KTK[0:L, h * 128:(h + 1) * 128], COLS[0:L, 4 + h:5 + h], ALU.mult, r=["KHK", "COLS"], w=[("KW", h)])
            PB = [(ps[2 * h], ("ps", 2 * h)) for h in range(4)]
            PN_ = [(ps[2 * h + 1], ("ps", 2 * h + 1)) for h in range(4)]
            for h in range(4):
                pb, kb = PB[h]
                MM(pb[0:L, 0:L], SEL[0:4, h, 0:L], ROWS[0:4, 0, 0:L], start=True, stop=False, r=["SEL", "ROWS"], w=[kb])
                MM(pb[0:L, 0:L], ID[0:L, 0:L], MNEG[0:L, 0:L], start=False, stop=True, r=["ID", "MNEG"], w=[kb])
                MM(pb[:, L:3 * L], SEL[0:4, h, :], ROWS[0:4, 1:3, 0:L], r=["SEL", "ROWS"], w=[kb])
                MM(pb[0:L, 256:256 + L], Kb(h)[:, cs], Qb(h)[:, cs], r=[AK(KA + h), AK(QA + h)], w=[kb])
            for h in range(4):
                pb, kb = PB[h]
                ACT(WST[h][0:L, 0:L], pb[0:L, 0:L], AF.Exp, r=[kb, "COLS"], w=[("WST", h)], bias=COLS[0:L, h:h + 1])
            for h in range(4):
                pb, kb = PB[h]
                TT("dve", SST[h][0:L, 0:L], WST[h][0:L, 0:L], pb[0:L, 256:256 + L], ALU.mult, r=[("WST", h), kb], w=[("SST", h)])
                TT("dve", QW[h][:, 0:L], Qb(h)[:, cs], pb[:, L:2 * L], ALU.mult, r=[AK(QA + h), kb], w=[("QW", h)])
            for h in range(4):
                pn_, kn = PN_[h]
                for vc in range(2):
                    MM(pn_[:, vc * L:(vc + 1) * L], VTK1[0:L, h, vc * 128:(vc + 1) * 128], SST[h][0:L, 0:L], start=True, stop=False,
                       r=["S1", ("SST", h)], w=[kn])
                    MM(pn_[:, vc * L:(vc + 1) * L], CNb[:, h, vc * 128:(vc + 1) * 128], QW[h][:, 0:L], start=False, stop=True,
                       r=[("CNb", h), ("QW", h)], w=[kn])
                MM(pn_[:, 2 * L:3 * L], ONESb[0:L, :], SST[h][0:L, 0:L], start=True, stop=False, r=["ONES", ("SST", h)], w=[kn])
                MM(pn_[:, 2 * L:3 * L], NBC[:, h, :], QW[h][:, 0:L], start=False, stop=True, r=[("NBC", h), ("QW", h)], w=[kn])
                MM(pn_[:, 192:449], KW[h][0:L, :], VTK1[0:L, h, :], r=[("KW", h), "S1"], w=[kn])
            for h in range(4):
                pb, kb = PB[h]
                pn_, kn = PN_[h]
                ACT(ADEN[h][:, 0:L], pn_[:, 2 * L:3 * L], AF.Abs, r=[kn], w=[("ADEN", h)])
                CP("act", DCOL[:, h:h + 1], pb[:, 2 * L - 1:2 * L], r=[kb], w=[("DCOL", h)])
            for h in range(4):
                pb, kb = PB[h]
                pn_, kn = PN_[h]
                TT("dve", DDT[h][:, 0:L], ADEN[h][:, 0:L], pb[:, 2 * L:3 * L], ALU.max, r=[("ADEN", h), kb], w=[("DDT", h)])
                RECIP(DDT[h][:, 0:L], DDT[h][:, 0:L], r=[("DDT", h)], w=[("DDT", h)])
                TT("dve", slots(VA + 2 * h, 2)[:, :, cs], pn_[:, 0:2 * L].rearrange("p (a b) -> p a b", a=2),
                   DDT[h][:, 0:L].unsqueeze(1).broadcast_to([128, 2, L]), ALU.mult,
                   r=[kn, ("DDT", h)], w=[AK(VA + 2 * h), AK(VA + 2 * h + 1)])
                STT(CNt[:, h, :], CNt[:, h, :], DCOL[:, h:h + 1], pn_[:, 192:449], ALU.mult, ALU.add,
                    r=[("CN", cnb, h), kn, ("DCOL", h)], w=[("CN", cnb, h)])
            for h in range(4):
                CP("pool", NBC[:, h, :], CNt[:, h, 256:257].broadcast_to([128, 128]), r=[("CN", cnb, h)], w=[("NBC", h)])
                CP("act", CNb[:, h, :], CNt[:, h, :], r=[("CN", cnb, h)], w=[("CNb", h)])
            if smp:
                DMA("pool", oCN_s[ci], CNt[:], r=[("CN", cnb, h) for h in range(4)])
        if smp:
            DMA("pool", om_s, MNEW[:], r=["MNEW"])
        elif tok0 + NT == 2064:
            DMA("pool", oCN_p, CN[0][:], r=[("CN", 0, h) for h in range(4)])
            gl = gchunk[0] % 2
            DMA("pool", om_p, MCOL[0:4, gl:gl + 1], r=["MC%d" % gl])

        TN3 = ["LW", "AA", "KKS", "SQ", "NRM", "KKN", "T1", "BB", "CUM", "EGP", "EGN", "CX"]
        hs = list(range(4))
        tq = {h: (TN3[3 * h], TN3[3 * h + 1], TN3[3 * h + 2]) for h in hs}
        pmk = {}
        for h in hs:
            c0_, c1_ = VA + 2 * h, VA + 2 * h + 1
            pm, km = PS()
            pmk[h] = (pm, km)
            MM(pm[:, 0:NT], ONES[:, :], slot(c0_)[:, 0:NT], start=True, stop=False, r=["ONES", AK(c0_)], w=[km])
            MM(pm[:, 0:NT], ONES[:, :], slot(c1_)[:, 0:NT], start=False, stop=True, r=["ONES", AK(c1_)], w=[km])
        for h in hs:
            pm, km = pmk[h]
            for c_ in (VA + 2 * h, VA + 2 * h + 1):
                STT(slot(c_)[:, 0:NT], pm[:, 0:NT], -1.0 / 256, slot(c_)[:, 0:NT], ALU.mult, ALU.add, r=[km, AK(c_)], w=[AK(c_)])
        for h in hs:
            c0_, c1_ = VA + 2 * h, VA + 2 * h + 1
            TT("pool", T[tq[h][0]][:, 0:NT], slot(c0_)[:, 0:NT], slot(c0_)[:, 0:NT], ALU.mult, r=[AK(c0_)], w=[tq[h][0]])
            TT("pool", T[tq[h][1]][:, 0:NT], slot(c1_)[:, 0:NT], slot(c1_)[:, 0:NT], ALU.mult, r=[AK(c1_)], w=[tq[h][1]])
        for h in hs:
            pv2, kv2 = PS()
            pmk[h] = (pv2, kv2)
            MM(pv2[:, 0:NT], ONES[:, :], T[tq[h][0]][:, 0:NT], start=True, stop=False, r=["ONES", tq[h][0]], w=[kv2])
            MM(pv2[:, 0:NT], ONES[:, :], T[tq[h][1]][:, 0:NT], start=False, stop=True, r=["ONES", tq[h][1]], w=[kv2])
        for h in hs:
            pv2, kv2 = pmk[h]
            ACT(T[tq[h][2]][:, 0:NT], pv2[:, 0:NT], AF.Ln, r=[kv2], w=[tq[h][2]], bias=EPS, scale=1.0 / 256)
        for h in hs:
            ACT(T[tq[h][2]][:, 0:NT], T[tq[h][2]][:, 0:NT], AF.Exp, r=[tq[h][2]], w=[tq[h][2]], scale=-0.5)
        for h in hs:
            for vc, c_ in enumerate((VA + 2 * h, VA + 2 * h + 1)):
                cc = 2 * h + vc
                STT(slot(c_)[:, 0:NT], slot(c_)[:, 0:NT], PCc(PC_MNG + cc), T[tq[h][2]][:, 0:NT], ALU.mult, ALU.mult, r=[AK(c_), tq[h][2], "PC"], w=[AK(c_)])
        for h in hs:
            for vc, c_ in enumerate((VA + 2 * h, VA + 2 * h + 1)):
                cc = 2 * h + vc
                TT("pool" if vc else "dve", slot(MERGED + cc)[:, 0:NT], slot(c_)[:, 0:NT], slot(OA + cc)[:, 0:NT], ALU.mult, r=[AK(c_), AK(OA + cc)], w=[AK(MERGED + cc)])
        P.barrier()
        if stop == "A":
            continue

        def evGB(i, pt_, pk_):
            ACT(slot(GB + i)[:, 0:NT], pt_[:, 0:NT], AF.Sigmoid, r=[pk_], w=[AK(GB + i)])

        proj_chunks(win_v, 1024, 8, NT, evGB)

        def evPB(cc, pt_, pk_):
            if cc < 8:
                dst, dk = slot(RB + cc), AK(RB + cc)
            elif cc < 16:
                dst, dk = slot(KB + cc - 8), AK(KB + cc - 8)
            elif cc < 24:
                dst, dk = slot(VB + cc - 16), AK(VB + cc - 16)
            elif cc == 24:
                dst, dk = XWA, "XWA"
            else:
                dst, dk = XG, "XG"
            if cc % 2 == 0:
                CP("act", dst[:, 0:NTB], pt_[:, 0:NTB], r=[pk_], w=[dk])
            else:
                CP("dve", dst[:, 0:NTB], pt_[:, 0:NTB], r=[pk_], w=[dk])
            ds, dsk = (T["CX"], "CX") if cc % 2 == 0 else (T["EGX"], "EGX")
            if smp:
                d3 = dst[:, 0:128].rearrange("p (s t) -> p s t", t=8)
                s3 = ds[:, 0:128].rearrange("p (s t) -> p s t", t=8)
                TT("pool", s3[:, :, 1:8], d3[:, :, 0:7], d3[:, :, 1:8], ALU.subtract, r=[dk], w=[dsk])
                TT("pool", s3[:, :, 0:1], dst[:, 128:144].unsqueeze(2), d3[:, :, 0:1], ALU.subtract, r=[dk], w=[dsk])
            else:
                TT("pool", ds[:, 1:NT], dst[:, 0:NT - 1], dst[:, 1:NT], ALU.subtract, r=[dk], w=[dsk])
                TT("pool", ds[:, 0:1], CARRY[:, cc:cc + 1], dst[:, 0:1], ALU.subtract, r=[dk, ("CARRY", cc)], w=[dsk])
                CP("pool", CARRY[:, cc:cc + 1], dst[:, NT - 1:NT], r=[dk], w=[("CARRY", cc)])
            STT(dst[:, 0:NT], ds[:, 0:NT], PCc(PC_MU + cc), dst[:, 0:NT], ALU.mult, ALU.add, r=[dsk, dk, "PC"], w=[dk])

        proj_chunks(win_v, 5128, 26, NTB, evPB)
        if stop is None:
            wload_rows(wout_v, 0, 4, prefetch=True)
            wload_rows(wout_v, 1, 4, prefetch=True)
        if stop == "B1":
            P.barrier()
            continue
        ACT(XWA[0:64, 0:NT], XWA[0:64, 0:NT], AF.Tanh, r=["XWA"], w=["XWA"])
        ACT(XG[:, 0:NT], XG[:, 0:NT], AF.Sigmoid, r=["XG"], w=["XG"])

        nchunks = len(chunks)
        last0, Lc = chunks[-1][0] + chunks[-1][1], chunks[-1][1]
        first_last = chunks[0][0] + chunks[0][1] - 1
        for j in range(8):
            Rj, Kj, Vj = slot(RB + j)[:, 0:NT], slot(KB + j)[:, 0:NT], slot(VB + j)[:, 0:NT]
            rk_, kk_, vk_ = AK(RB + j), AK(KB + j), AK(VB + j)
            cols = slice(j * 128, (j + 1) * 128)

            def t(n):
                return T[n][:, 0:NT]
            pw, kw = PS()
            MM(pw[:, 0:NT], W2A2[0:64, cols], XWA[0:64, 0:NT], r=["W2A2", "XWA"], w=[kw])
            pa, ka = PS()
            MM(pa[:, 0:NT], W2A2[64:128, cols], XWA[64:128, 0:NT], r=["W2A2", "XWA"], w=[ka])
            ACT(t("LW"), pw[:, 0:NT], AF.Exp, r=[kw, "PC"], w=["LW"], bias=PCN[:, j:j + 1], scale=-1.0)
            ACT(t("AA"), pa[:, 0:NT], AF.Exp, r=[ka, "PC"], w=["AA"], bias=PCN[:, 8 + j:9 + j], scale=-1.0)
            ACT(t("LW"), t("LW"), AF.Ln, r=["LW"], w=["LW"], bias=1.0)
            ACT(t("AA"), t("AA"), AF.Ln, r=["AA"], w=["AA"], bias=1.0)
            ACT(t("LW"), t("LW"), AF.Exp, r=["LW"], w=["LW"], scale=-1.0)
            ACT(t("AA"), t("AA"), AF.Exp, r=["AA"], w=["AA"], scale=-1.0)
            TS("pool", t("KKS"), Kj, PCc(PC_KKS + j), ALU.mult, r=[kk_, "PC"], w=["KKS"])
            TT("pool", t("SQ"), t("KKS"), t("KKS"), ALU.mult, r=["KKS"], w=["SQ"])
            pn2, kn2 = PS()
            MM(pn2[:, 0:NT], BLK[:, :], t("SQ"), r=["BLK", "SQ"], w=[kn2])
            P.op("dve", lambda h, NT=NT, ty=ty: h.tensor_tensor_scan(out=T["CUM"][:, 0:NT], data0=RST01[ty][:, 0:NT], data1=T["LW"][:, 0:NT],
                                                         initial=0.0, op0=ALU.mult, op1=ALU.add), ["LW", "RST"], ["CUM"])
            TS("dve", t("NRM"), pn2[:, 0:NT], 1e-24, ALU.max, r=[kn2], w=["NRM"])
            ACT(t("NRM"), t("NRM"), AF.Ln, r=["NRM"], w=["NRM"])
            ACT(t("NRM"), t("NRM"), AF.Exp, r=["NRM"], w=["NRM"], scale=-0.5)
            ACT(t("EGP"), t("CUM"), AF.Exp, r=["CUM"], w=["EGP"], scale=-C0)
            ACT(t("EGN"), t("CUM"), AF.Exp, r=["CUM"], w=["EGN"], scale=C0)
            TT("pool", t("CX"), t("CUM"), t("LW"), ALU.subtract, r=["CUM", "LW"], w=["CX"])
            ACT(t("EGX"), t("CX"), AF.Exp, r=["CX"], w=["EGX"], scale=-C0)
            TT("dve", t("KKN"), t("KKS"), t("NRM"), ALU.mult, r=["KKS", "NRM"], w=["KKN"])
            TS("dve", t("T1"), t("AA"), 1.0, ALU.subtract, r=["AA", "PC"], w=["T1"], s2=PCc(PC_KAS + j), op1=ALU.mult)
            STT(Kj, t("T1"), 1.0, Kj, ALU.add, ALU.mult, r=["T1", kk_], w=[kk_])
            TT("pool", t("BB"), t("KKN"), t("AA"), ALU.mult, r=["KKN", "AA"], w=["BB"])
            CP("pool", GL[:, j, 0:nchunks], T["EGP"][:, first_last:last0:Lc] if nchunks > 1 else T["EGP"][:, first_last:first_last + 1],
               r=["EGP"], w=[("GL", j)])
            bon = slot(BON + j)[:, 0:NT]
            STT(bon, Rj, PCc(PC_RK + j), Kj, ALU.mult, ALU.mult, r=[rk_, kk_, "PC"], w=[AK(BON + j)])
            pb2, kb2 = PS()
            MM(pb2[:, 0:NT], BLK[:, :], bon, r=["BLK", AK(BON + j)], w=[kb2])
            TT("dve", RTb(j)[:, 0:NT], Rj, t("EGP"), ALU.mult, r=[rk_, "EGP"], w=[("RTb", j)])
            TT("dve", KHb(j)[:, 0:NT], Kj, t("EGN"), ALU.mult, r=[kk_, "EGN"], w=[("KHb", j)])
            TT("pool", KKTb(j)[:, 0:NT], t("KKN"), t("EGX"), ALU.mult, r=["KKN", "EGX"], w=[AK(KKT + j)])
            TT("pool", BHb(j)[:, 0:NT], t("BB"), t("EGN"), ALU.mult, r=["BB", "EGN"], w=[AK(BHT + j)])
            TT("dve", bon, pb2[:, 0:NT], Vj, ALU.mult, r=[kb2, vk_], w=[AK(BON + j)])
        if stop == "B2":
            P.barrier()
            continue

        def QQ(j, rows, cs):
            return ARENA[:, KKT * W:(KKT + 16) * W].bitcast(BF16)[:, j * W:j * W + 16 * W].rearrange("p (two d) -> p two d", two=2)[rows, :, cs]

        for ci, (c0, L) in enumerate(chunks):
            cs = slice(c0, c0 + L)
            nl = {8: 3, 16: 4, 64: 6}[L]
            if DBGSTEP and ci < DBGCHUNK:
                continue
            hb = ci % 2 if smp else 0
            Ht = H[hb]
            if smp:
                DMA("sp", Ht[:], sH[ci], w=[("H", hb, 0), ("H", hb, 1)])
                CP("act", Hb[hb][:], Ht[:], r=[("H", hb, 0), ("H", hb, 1)], w=[("Hb", hb, 0), ("Hb", hb, 1)])
            def half_steps(jh, TSet):
                VTK, KHK, BHK, S1, S2, PA, PN, PTN, U, TMPH, kp = TSet
                hk = ("H", hb, jh)
                hbk = ("Hb", hb, jh)
                Hbt = Hb[hb]
                pA, kA = PS(); pB, kB = PS(); pC, kC = PS()
                pBb = pB[:, :].bitcast(BF16); pCb = pC[:, :].bitcast(BF16)
                for jj in range(4):
                    j = 4 * jh + jj
                    TR(pA[0:L, jj * 128:(jj + 1) * 128], slot(VB + j)[:, cs], 128, r=[AK(VB + j)], w=[kA])
                    TR(pBb[0:L, jj * 128:(jj + 1) * 128], KHb(j)[:, cs], 128, r=[("KHb", j)], w=[kB], bf=True)
                    TR(pCb[0:L, jj * 128:(jj + 1) * 128], BHb(j)[:, cs], 128, r=[AK(BHT + j)], w=[kC], bf=True)
                CP("act", VTK[0:L, :], pA[0:L, :], r=[kA], w=[(kp, "VTK")])
                CP("dve", KHK[0:L, :], pBb[0:L, 0:512], r=[kB], w=[(kp, "KHK")])
                ACT(BHK[0:L, :], pCb[0:L, 0:512], AF.Identity, r=[kC], w=[(kp, "BHK")], scale=-1.0)

                yield
                def hd(hq):
                    hp, jj = divmod(hq, 4)
                    return 4 * jh + jj, jj, hp, slice(64 * hp, 64 * hp + 64), slice(jj * 128 + hp * 64, jj * 128 + hp * 64 + 64)
                b1 = [PS(), PS()]
                for hq in range(8):
                    j, jj, hp, rows, tc = hd(hq)
                    MM(b1[hp][0][0:L, jj * 2 * L:(jj + 1) * 2 * L], KHb(j)[rows, cs], QQ(j, rows, cs),
                       r=[("KHb", j), AK(KKT + j), ("RTb", j)], w=[b1[hp][1]])
                yield
                for hp in range(2):
                    mk = MK1[0:L, :, 0:L].unsqueeze(1).broadcast_to([L, 4, 2, L])
                    TT("dve", S1[0:L, 4 * hp:4 * hp + 4, :, 0:L],
                       b1[hp][0][0:L, 0:8 * L].rearrange("p (a b c) -> p a b c", a=4, b=2), mk, ALU.mult,
                       r=[b1[hp][1], "MK1"], w=[(kp, "S1")])
                yield
                b2 = [PS(), PS()]
                for hq in range(8):
                    j, jj, hp, rows, tc = hd(hq)
                    MM(b2[hp][0][0:L, jj * 2 * L:(jj + 1) * 2 * L], BHb(j)[rows, cs], QQ(j, rows, cs),
                       r=[AK(BHT + j), AK(KKT + j), ("RTb", j)], w=[b2[hp][1]])
                for hp in range(2):
                    mkn = MK1N[0:L, :, 0:L].unsqueeze(1).broadcast_to([L, 4, 2, L])
                    TT("dve", S2[0:L, 4 * hp:4 * hp + 4, :, 0:L],
                       b2[hp][0][0:L, 0:8 * L].rearrange("p (a b c) -> p a b c", a=4, b=2), mkn, ALU.mult,
                       r=[b2[hp][1], "MK1N"], w=[(kp, "S2")])
                yield
                p3 = [PS(), PS()]
                for hq in range(8):
                    j, jj, hp, rows, tc = hd(hq)
                    MM(p3[hp][0][0:L, jj * L:(jj + 1) * L], KKTb(j)[rows, cs], BHb(j)[rows, cs],
                       r=[AK(KKT + j), AK(BHT + j)], w=[p3[hp][1]])
                for hp in range(2):
                    TT("dve", PA[0:L, 4 * hp:4 * hp + 4, 0:L], p3[hp][0][0:L, 0:4 * L].rearrange("p (a b) -> p a b", a=4),
                       MLN[0:L, 0:L].unsqueeze(1).broadcast_to([L, 4, L]), ALU.mult, r=[p3[hp][1], "MLN"], w=[(kp, "PA")])
                yield
                pU2 = [PS(), PS()]
                for hq in range(8):
                    j, jj, hp, rows, tc = hd(hq)
                    MM(pU2[hp][0][0:L, jj * 64:(jj + 1) * 64], KKTb(j)[rows, cs], Hbt[rows, j, :], start=(jj == 0), stop=False,
                       r=[AK(KKT + j), hbk], w=[pU2[hp][1]])
                for hq in range(8):
                    j, jj, hp, rows, tc = hd(hq)
                    MM(pU2[hp][0][0:L, jj * 64:(jj + 1) * 64], S1[0:L, hq, 0, 0:L], VTK[0:L, tc], start=False, stop=(jj == 3),
                       r=[(kp, "S1"), (kp, "VTK")], w=[pU2[hp][1]], strict=(hp == 1 and jj == 0))
                CP("act", U[0][0:L, 0:256], pU2[0][0][0:L, 0:256], r=[pU2[0][1]], w=[(kp, "U", 0)])
                CP("act", U[0][0:L, 256:512], pU2[1][0][0:L, 0:256], r=[pU2[1][1]], w=[(kp, "U", 0)])
                yield
                cur = 0
                Pt, Pk = PA, (kp, "PA")
                PTt, PTk = S2, (kp, "S2")

                def PTv(hq):
                    return PTt[0:L, hq, 0, 0:L] if PTk == (kp, "S2") else PTt[0:L, hq, 0:L]
                for l in range(nl):
                    pU, kU = PS()
                    for hq in range(8):
                        MM(pU[0:L, hq * 64:(hq + 1) * 64], PTv(hq), U[cur][0:L, hq * 64:(hq + 1) * 64], r=[PTk, (kp, "U", cur)], w=[kU])
                    TT("dve", U[1 - cur][0:L, :], U[cur][0:L, :], pU[0:L, :], ALU.add, r=[(kp, "U", cur), kU], w=[(kp, "U", 1 - cur)])
                    cur = 1 - cur
                    if l < nl - 1:
                        need_p = (l < nl - 2)
                        pT, kT = PS()
                        for hq in range(8):
                            MM(pT[0:L, hq * L:(hq + 1) * L], Pt[0:L, hq, 0:L], PTv(hq), r=[PTk, Pk], w=[kT])
                        nP, nPT = PN[l % 2], PTN[l % 2]
                        if need_p:
                            pP, kP = PS()
                            for hq in range(8):
                                MM(pP[0:L, hq * L:(hq + 1) * L], PTv(hq), Pt[0:L, hq, 0:L], r=[PTk, Pk], w=[kP])
                            CP("act", nP[0:L, :, 0:L], pP[0:L, 0:8 * L].rearrange("p (a b) -> p a b", a=8), r=[kP], w=[(kp, "PN", l % 2)])
                        CP("dve", nPT[0:L, :, 0:L], pT[0:L, 0:8 * L].rearrange("p (a b) -> p a b", a=8), r=[kT], w=[(kp, "PTN", l % 2)])
                        Pt, Pk = nP, (kp, "PN", l % 2)
                        PTt, PTk = nPT, (kp, "PTN", l % 2)
                    yield
                yield
                pY2 = [PS(), PS()]
                for hq in range(8):
                    j, jj, hp, rows, tc = hd(hq)
                    o_ = pY2[hp][0][rows, jj * L:(jj + 1) * L]
                    MM(o_, Hbt[rows, j, :], RTb(j)[rows, cs], start=(jj == 0), stop=False, r=[hbk, ("RTb", j)], w=[pY2[hp][1]])
                for hq in range(8):
                    j, jj, hp, rows, tc = hd(hq)
                    o_ = pY2[hp][0][rows, jj * L:(jj + 1) * L]
                    MM(o_, VTK[0:L, tc], S1[0:L, hq, 1, 0:L], start=False, stop=False, r=[(kp, "VTK"), (kp, "S1")], w=[pY2[hp][1]], strict=(hp == 1 and jj == 0))
                    MM(o_, U[cur][0:L, hq * 64:(hq + 1) * 64], S2[0:L, hq, 1, 0:L], start=False, stop=(jj == 3), r=[(kp, "U", cur), (kp, "S2")], w=[pY2[hp][1]])
                pH, kH = PS()
                for hq in range(8):
                    j, jj, hp, rows, tc = hd(hq)
                    o_ = pH[rows, jj * 64:(jj + 1) * 64]
                    MM(o_, KHK[0:L, tc], VTK[0:L, tc], start=True, stop=False, r=[(kp, "KHK"), (kp, "VTK")], w=[kH])
                    MM(o_, BHK[0:L, tc], U[cur][0:L, hq * 64:(hq + 1) * 64], start=False, stop=True, r=[(kp, "BHK"), (kp, "U", cur)], w=[kH])
                yield
                for hp in range(2):
                    rows = slice(64 * hp, 64 * hp + 64)
                    CP("act", slots(RB + 4 * jh, 4)[rows, :, cs], pY2[hp][0][rows, 0:4 * L].rearrange("p (a b) -> p a b", a=4), r=[pY2[hp][1]],
                       w=[AK(RB + 4 * jh + q) for q in range(4)])
                TT("dve", TMPH[:], Ht[:, 4 * jh:4 * jh + 4, :], pH[:, 0:256].rearrange("p (a b) -> p a b", a=4), ALU.add, r=[hk, kH], w=[(kp, "TMPH")])
                TT("pool", Ht[:, 4 * jh:4 * jh + 4, :], TMPH[:], GL[:, 4 * jh:4 * jh + 4, ci:ci + 1].broadcast_to([128, 4, 64]), ALU.mult,
                   r=[(kp, "TMPH")] + [("GL", 4 * jh + q) for q in range(4)], w=[hk])
                CP("act", Hbt[:, 4 * jh:4 * jh + 4, :], Ht[:, 4 * jh:4 * jh + 4, :], r=[hk], w=[hbk])

            setA = (VTK, KHK, BHK, S1, S2, PA, PN, PTN, U, TMPH, "A")
            if smp:
                gens = [half_steps(0, setA), half_steps(1, setB)]
                while gens:
                    for g_ in list(gens):
                        try:
                            next(g_)
                        except StopIteration:
                            gens.remove(g_)
            else:
                for jh in range(2):
                    for _ in half_steps(jh, setA):
                        pass
            if smp:
                DMA("pool", oH_s[ci], Ht[:], r=[("H", hb, 0), ("H", hb, 1)])
            if DBGSTEP and ci >= DBGCHUNK:
                break
        if (not smp) and tok0 + NT == 2064:
            DMA("pool", oH_p, H[0][:], r=[("H", 0, 0), ("H", 0, 1)])

        if stop == "B3":
            P.barrier()
            continue
        TN3 = ["LW", "AA", "KKS", "SQ", "NRM", "KKN", "T1", "BB", "CUM", "EGP", "EGN", "CX"]
        for g0 in (0, 4):
            js = list(range(g0, g0 + 4))
            tq = {j: (TN3[3 * (j - g0)], TN3[3 * (j - g0) + 1], TN3[3 * (j - g0) + 2]) for j in js}
            pk_ = {}
            for j in js:
                pm, km = PS()
                pk_[j] = (pm, km)
                MM(pm[:, 0:NT], BLK[:, :], slot(RB + j)[:, 0:NT], r=["BLK", AK(RB + j)], w=[km])
            for j in js:
                pm, km = pk_[j]
                Yj, yk = slot(RB + j)[:, 0:NT], AK(RB + j)
                STT(Yj, pm[:, 0:NT], -1.0 / 64, Yj, ALU.mult, ALU.add, r=[km, yk], w=[yk])
            for j in js:
                Yj, yk = slot(RB + j)[:, 0:NT], AK(RB + j)
                TT("pool", T[tq[j][0]][:, 0:NT], Yj, Yj, ALU.mult, r=[yk], w=[tq[j][0]])
            for j in js:
                pv2, kv2 = PS()
                pk_[j] = (pv2, kv2)
                MM(pv2[:, 0:NT], BLK[:, :], T[tq[j][0]][:, 0:NT], r=["BLK", tq[j][0]], w=[kv2])
            for j in js:
                pv2, kv2 = pk_[j]
                ACT(T[tq[j][1]][:, 0:NT], pv2[:, 0:NT], AF.Ln, r=[kv2], w=[tq[j][1]], bias=GN_EPS, scale=1.0 / 64)
            for j in js:
                ACT(T[tq[j][1]][:, 0:NT], T[tq[j][1]][:, 0:NT], AF.Exp, r=[tq[j][1]], w=[tq[j][1]], scale=-0.5)
            for j in js:
                pg, kg = PS()
                pk_[j] = (pg, kg)
                MM(pg[:, 0:NT], G2[:, j * 128:(j + 1) * 128], XG[:, 0:NT], r=["G2", "XG"], w=[kg])
            for j in js:
                Yj, yk = slot(RB + j)[:, 0:NT], AK(RB + j)
                t1 = T[tq[j][2]][:, 0:NT]
                STT(t1, Yj, PCc(PC_LXG + j), T[tq[j][1]][:, 0:NT], ALU.mult, ALU.mult, r=[yk, tq[j][1], "PC"], w=[tq[j][2]])
                STT(t1, t1, PCc(PC_LXB + j), slot(BON + j)[:, 0:NT], ALU.add, ALU.add, r=[tq[j][2], AK(BON + j), "PC"], w=[tq[j][2]])
            for j in js:
                pg, kg = pk_[j]
                t1 = T[tq[j][2]][:, 0:NT]
                TT("dve", t1, t1, pg[:, 0:NT], ALU.mult, r=[tq[j][2], kg], w=[tq[j][2]])
            for j in js:
                t1 = T[tq[j][2]][:, 0:NT]
                TT("pool", t1, t1, slot(GB + j)[:, 0:NT], ALU.mult, r=[tq[j][2], AK(GB + j)], w=[tq[j][2]])
                TT("pool", slot(MERGED + j)[:, 0:NT], slot(MERGED + j)[:, 0:NT], t1, ALU.add, r=[tq[j][2], AK(MERGED + j)], w=[AK(MERGED + j)])
        P.barrier()
        if stop == "B":
            continue

        def big_out(wv, nrowch, lhs_of, resid):
            for cp_ in range((nrowch + 3) // 4):
                ncc = min(4, nrowch - 4 * cp_)
                wb, wk = wload_rows(wv, cp_, ncc)
                wbv2 = wb[:, :].rearrange("p (c d) -> p c d", c=4)
                for ci_ in range(ncc):
                    c = 4 * cp_ + ci_
                    for ti, (col0, n, _) in enumerate(tiles):
                        for half in range(2):
                            MM(ps[2 * ti + half][0:n, 0:512], lhs_of(c)[:, col0:col0 + n], wbv2[:, ci_, half * 512:(half + 1) * 512],
                               start=(c == 0), stop=False, r=[wk] + resid[1], w=[("ps", 2 * ti + half)])
            for ti, (col0, n, _) in enumerate(tiles):
                for c in range(8):
                    o_ = ps[2 * ti + c // 4][0:n, (c % 4) * 128:(c % 4 + 1) * 128]
                    MM(o_, XTb[:, c, col0:col0 + n], IDb[:, :], start=False, stop=False, r=["XTb", "ID"], w=[("ps", 2 * ti + c // 4)])
                    MM(o_, XTlo[:, c, col0:col0 + n], IDb[:, :], start=False, stop=(c % 4 == 3), r=["XTlo", "ID"], w=[("ps", 2 * ti + c // 4)])

        MRGb = ARENA[:, 52 * W:56 * W].bitcast(BF16).rearrange("p (c w) -> p c w", c=8)
        TS("pool", MRGb[:, :, 0:NT], slots(MERGED, 8)[:, :, 0:NT], 1.0 / ALPHA, ALU.mult, r=[AK(MERGED + c) for c in range(8)], w=["MRGb"])
        big_out(wout_v, 8, lambda c: MRGb[:, c, :], (None, ["MRGb"]))
        for ti, (col0, n, _) in enumerate(tiles):
            Tt, tkey = TOK[ti % 2], ("TOK", ti % 2)
            CP("act", Tt[0:n, 0:512], ps[2 * ti][0:n, :], r=[("ps", 2 * ti)], w=[tkey])
            CP("dve", Tt[0:n, 512:1024], ps[2 * ti + 1][0:n, :], r=[("ps", 2 * ti + 1)], w=[tkey])
            ln_stats(Tt, n, tkey, eps=EPS / (ALPHA * ALPHA))
            to_feature_major(Tt, n, tkey, col0, PC_L1G, PC_L1B, banks=(6, 7))
        CP("pool", XTb[:, :, 0:NT], XT[:, :, 0:NT], r=["XT"], w=["XTb"])
        TT("dve", XTlo[:, :, 0:NT], XT[:, :, 0:NT], XTb[:, :, 0:NT], ALU.subtract, r=["XT", "XTb"], w=["XTlo"])

        def evAG(i, pt_, pk_):
            if smp:
                d = slot(AG + i)[:, 0:160].rearrange("p (s t) -> p s t", t=10)[:, :, 2:10]
                CP("act", d, pt_[:, 0:128].rearrange("p (s t) -> p s t", t=8), r=[pk_], w=[AK(AG + i)])
            else:
                CP("act", slot(AG + i)[:, 2:2 + NT], pt_[:, 0:NT], r=[pk_], w=[AK(AG + i)])

        def evAV(i, pt_, pk_):
            CP("dve", slot(AV + i)[:, 0:NT], pt_[:, 0:NT], r=[pk_], w=[AK(AV + i)])

        proj_chunks(wup_v, 0, 22, NT, evAG)
        proj_chunks(wup_v, DF, 22, NT, evAV)
        _tn = ["LW", "AA", "KKS", "SQ", "NRM", "KKN", "T1", "BB", "CUM", "EGP", "EGN", "CX"]
        for g0 in range(0, 22, 6):
            idx = list(range(g0, min(g0 + 6, 22)))
            tk = {i: (_tn[2 * (i - g0)], _tn[2 * (i - g0) + 1]) for i in idx}
            for i in idx:
                ag, agk = slot(AG + i), AK(AG + i)
                if smp:
                    a3 = ag[:, 0:160].rearrange("p (s t) -> p s t", t=10)
                    CP("pool", a3[:, :, 0:2], CV0[:, i, :, :], r=[("CV0", i)], w=[agk])
                    CP("pool", CV0[:, i, :, :], a3[:, :, 8:10], r=[agk], w=[("CV0", i)])
                else:
                    CP("pool", ag[:, 0:2], AGC[:, i, :], r=[("AGC", i)], w=[agk])
                    CP("pool", AGC[:, i, :], ag[:, NT:NT + 2], r=[agk], w=[("AGC", i)])
            for i in idx:
                ag, agk = slot(AG + i), AK(AG + i)
                cvk, g1k = tk[i]
                cv = T[cvk]
                if smp:
                    a3 = ag[:, 0:160].rearrange("p (s t) -> p s t", t=10)
                    cv3 = cv[:, 0:128].rearrange("p (s t) -> p s t", t=8)
                    TS("dve", cv3, a3[:, :, 0:8], PCc(PC_CW0 + i), ALU.mult, r=[agk, "PC"], w=[cvk], s2=PCc(PC_CB + i), op1=ALU.add)
                    STT(cv3, a3[:, :, 1:9], PCc(PC_CW1 + i), cv3, ALU.mult, ALU.add, r=[agk, cvk, "PC"], w=[cvk])
                    STT(cv3, a3[:, :, 2:10], PCc(PC_CW2 + i), cv3, ALU.mult, ALU.add, r=[agk, cvk, "PC"], w=[cvk])
                else:
                    ACT(cv[:, 0:NT], ag[:, 0:NT], AF.Identity, r=[agk, "PC"], w=[cvk], bias=PCc(PC_CB + i), scale=PCc(PC_CW0 + i))
                    STT(cv[:, 0:NT], ag[:, 1:NT + 1], PCc(PC_CW1 + i), cv[:, 0:NT], ALU.mult, ALU.add, r=[agk, cvk, "PC"], w=[cvk])
                    STT(cv[:, 0:NT], ag[:, 2:NT + 2], PCc(PC_CW2 + i), cv[:, 0:NT], ALU.mult, ALU.add, r=[agk, cvk, "PC"], w=[cvk])
            for i in idx:
                cvk, g1k = tk[i]
                ACT(T[g1k][:, 0:NT], T[cvk][:, 0:NT], AF.Square, r=[cvk], w=[g1k])
            for i in idx:
                cvk, g1k = tk[i]
                TS("dve", T[g1k][:, 0:NT], T[g1k][:, 0:NT], 0.044715, ALU.mult, r=[g1k], w=[g1k], s2=1.0, op1=ALU.add)
            for i in idx:
                cvk, g1k = tk[i]
                TT("pool", T[g1k][:, 0:NT], T[g1k][:, 0:NT], T[cvk][:, 0:NT], ALU.mult, r=[g1k, cvk], w=[g1k])
            for i in idx:
                cvk, g1k = tk[i]
                ACT(T[g1k][:, 0:NT], T[g1k][:, 0:NT], AF.Sigmoid, r=[g1k], w=[g1k], scale=GELU_K)
            for i in idx:
                cvk, g1k = tk[i]
                TT("pool", T[g1k][:, 0:NT], T[g1k][:, 0:NT], T[cvk][:, 0:NT], ALU.mult, r=[g1k, cvk], w=[g1k])
            for i in idx:
                cvk, g1k = tk[i]
                STT(slot(AG + i).bitcast(BF16)[:, 0:NT], T[g1k][:, 0:NT], 1.0 / ALPHA, slot(AV + i)[:, 0:NT], ALU.mult, ALU.mult,
                    r=[g1k, AK(AV + i), cvk], w=[AK(AG + i)])
        if smp:
            DMA("pool", ocv_s, CV0[:], r=[("CV0", i) for i in range(22)])
        elif tok0 + NT == 2064:
            DMA("pool", ocv_p, AGC[:], r=[("AGC", i) for i in range(22)])

        big_out(wdn_v, 22, lambda c: slot(AG + c).bitcast(BF16), (None, [AK(AG + c) for c in range(22)]))
        for ti, (col0, n, _) in enumerate(tiles):
            Tt, tkey = TOK[ti % 2], ("TOK", ti % 2)
            CP("act", Tt[0:n, 0:512], ps[2 * ti][0:n, :], r=[("ps", 2 * ti)], w=[tkey])
            CP("dve", Tt[0:n, 512:1024], ps[2 * ti + 1][0:n, :], r=[("ps", 2 * ti + 1)], w=[tkey])
            ln_stats(Tt, n, tkey, eps=EPS / (ALPHA * ALPHA))
            TT("pool", Tt[0:n, :], Tt[0:n, :], LNG[0:n, :], ALU.mult, r=[tkey, "LNG"], w=[tkey])
            TT("dve", Tt[0:n, :], Tt[0:n, :], LNB[0:n, :], ALU.add, r=[tkey, "LNB"], w=[tkey])
            if smp:
                DMA("pool", oy_s, Tt[0:128, :], r=[tkey])
            else:
                t_lo = tok0 + col0
                if t_lo < 16:
                    DMA("pool", oy_p[0:n - (16 - t_lo), :], Tt[16 - t_lo:n, :], r=[tkey])
                else:
                    DMA("pool", oy_p[t_lo - 16:t_lo - 16 + n, :], Tt[0:n, :], r=[tkey])

    P.emit(nc)
    st.close()
    return nc


_NC_CACHE = {}


def _host_inputs(inp, b):
    f = lambda a: np.ascontiguousarray(a, dtype=np.float32)
    s0, s1 = 16 * b, 16 * b + 16
    prm0 = np.concatenate([inp["rwkv_mu"][0], inp["rwkv_w0"][0], inp["rwkv_a0"][0], inp["rwkv_kk_scale"][0], inp["rwkv_ka_scale"][0],
                           inp["rwkv_rk"][0], inp["rwkv_lnx_g"][0], inp["rwkv_lnx_b"][0], inp["mlstm_norm_g"][0],
                           inp["ln_in_g"], inp["ln_in_b"], inp["ln1_g"][0], inp["ln1_b"][0]]).reshape(122, 128)
    cw = inp["ffn_conv_w"][0]
    prm1 = np.concatenate([cw[0], cw[1], cw[2], inp["ffn_conv_b"][0]]).reshape(88, 128)
    sS = inp["state_rwkv_S"][0, s0:s1]
    sH = sS.reshape(16, 8, 2, 64, 64).transpose(0, 2, 4, 1, 3).reshape(16, 128, 8, 64)
    ssh = inp["state_rwkv_shift"][0, s0:s1].reshape(16, 8, 128).transpose(2, 1, 0)
    scv = inp["state_ffn_conv"][0, s0:s1].reshape(16, 2, 22, 128).transpose(3, 2, 0, 1)
    return {
        "xp": f(inp["x_prompt"][b]), "xs": f(inp["x_sample"][s0:s1].reshape(128, D)), "meta": f(inp["meta_tokens"]),
        "sC": f(inp["state_mlstm_C"][0, s0:s1]), "sn": f(inp["state_mlstm_n"][0, s0:s1].transpose(0, 2, 1)),
        "sm": f(inp["state_mlstm_m"][0, s0:s1].T), "sH": f(sH), "ssh": f(ssh), "scv": f(scv),
        "prm0": f(prm0), "prm1": f(prm1), "bif": f(inp["b_if"][0].reshape(8, 1)),
        "ln2g": f(inp["ln2_g"][0]), "ln2b": f(inp["ln2_b"][0]),
        "w_in": f(inp["w_in"][0]), "w2a2": f(np.concatenate([inp["rwkv_w2"][0], inp["rwkv_a2"][0]], 0)), "g2": f(inp["rwkv_g2"][0]),
        "w_out": f(inp["w_out"][0]), "w_up": f(inp["ffn_w_up"][0]), "w_down": f(inp["ffn_w_down"][0]),
    }


def kernel(**inputs):
    inp = {k: np.asarray(v) for k, v in inputs.items()}
    if "nc" not in _NC_CACHE:
        _NC_CACHE["nc"] = build()
    nc = _NC_CACHE["nc"]
    in_maps = [_host_inputs(inp, b) for b in range(8)]
    res = run_bass_kernel_spmd(nc, in_maps, core_ids=list(range(8))).results
    g = lambda k: [np.asarray(r[k], dtype=np.float32) for r in res]
    y_p = np.stack(g("oy_p"), 0)
    y_s = np.concatenate(g("oy_s"), 0).reshape(128, 8, D)
    cn_p = np.stack(g("oCN_p"), 0)
    pC = cn_p[..., 0:256].transpose(0, 2, 1, 3)[None]
    pn = cn_p[..., 256].transpose(0, 2, 1)[None]
    pm = np.stack(g("om_p"), 0)[:, :, 0][None]
    Hp = np.stack(g("oH_p"), 0)
    pS = Hp.reshape(8, 2, 64, 8, 64).transpose(0, 3, 1, 4, 2).reshape(8, 16, 64, 64)[None]
    psh = np.stack(g("osh_p"), 0)[..., 0].transpose(0, 2, 1).reshape(8, D)[None]
    pcv = np.stack(g("ocv_p"), 0).transpose(0, 3, 2, 1).reshape(8, 2, DF)[None]
    cn_s = np.concatenate(g("oCN_s"), 0)
    sC = cn_s[..., 0:256].transpose(0, 2, 1, 3)[None]
    sn = cn_s[..., 256].transpose(0, 2, 1)[None]
    sm = np.concatenate([a.T for a in g("om_s")], 0)[None]
    Hs = np.concatenate(g("oH_s"), 0)
    sS = Hs.reshape(128, 2, 64, 8, 64).transpose(0, 3, 1, 4, 2).reshape(128, 16, 64, 64)[None]
    ssh = np.concatenate([a.transpose(2, 1, 0).reshape(16, D) for a in g("osh_s")], 0)[None]
    scv = np.concatenate([a.transpose(2, 3, 1, 0).reshape(16, 2, DF) for a in g("ocv_s")], 0)[None]
    c = lambda a: np.ascontiguousarray(a, dtype=np.float32)
    return (c(y_p), c(y_s), c(pC), c(pn), c(pm), c(pS), c(psh), c(pcv), c(sC), c(sn), c(sm), c(sS), c(ssh), c(scv))
```

```python
import contextlib
import os
import numpy as np
DBGSTEP = int(os.environ.get("DBGSTEP", "0"))
DBGSUB = int(os.environ.get("DBGSUB", "0"))
DBGCHUNK = int(os.environ.get("DBGCHUNK", "0"))
import concourse.bass as bass
import concourse.mybir as mybir
from concourse.bass_utils import run_bass_kernel_spmd

F32 = mybir.dt.float32
BF16 = mybir.dt.bfloat16
AF = mybir.ActivationFunctionType
ALU = mybir.AluOpType

NS_DMA = 6
ENGS = ("pe", "act", "dve", "pool", "sp")


class Op:
    __slots__ = ("eng", "fn", "deps", "dma", "signal", "val", "slot", "strict")


class Prog:
    def __init__(self):
        self.ops = {e: [] for e in ENGS}
        self.lastw = {}
        self.readers = {}
        self.pend = {e: [] for e in ENGS}
        self.dmaq = {e: [] for e in ENGS}

    def op(self, eng, fn, r=(), w=(), dma=False, strict=False):
        o = Op()
        o.strict = strict
        o.eng, o.fn, o.dma, o.signal, o.val, o.slot = eng, fn, dma, False, 0, 0
        deps = set(self.pend[eng])
        self.pend[eng] = []
        for k in r:
            lw = self.lastw.get(k)
            if lw is not None:
                deps.add(lw)
        for k in w:
            lw = self.lastw.get(k)
            if lw is not None:
                deps.add(lw)
            deps.update(self.readers.get(k, ()))
        if dma:
            q = self.dmaq[eng]
            n = len(q)
            o.slot = n % NS_DMA
            o.val = 16 * (n // NS_DMA + 1)
            if n >= NS_DMA:
                deps.add(q[n - NS_DMA])
            q.append(o)
        o.deps = deps
        for k in r:
            lst = self.readers.setdefault(k, [])
            if not dma:
                lst[:] = [x for x in lst if x.dma or x.eng != eng]
            lst.append(o)
        for k in w:
            self.lastw[k] = o
            self.readers[k] = []
        self.ops[eng].append(o)
        return o

    def barrier(self):
        lasts = []
        for e in ENGS:
            comp = [x for x in self.ops[e] if not x.dma]
            if comp:
                lasts.append(comp[-1])
            lasts.extend(self.dmaq[e][-NS_DMA:])
        for e in ENGS:
            self.pend[e].extend(lasts)
        self.lastw.clear()
        self.readers.clear()

    def emit(self, nc):
        for e in ENGS:
            for o in self.ops[e]:
                for d in o.deps:
                    if not d.dma and not (d.eng == "pe" and e == "pe" and not o.strict):
                        d.signal = True
        for e in ENGS:
            c = 0
            for o in self.ops[e]:
                if not o.dma and o.signal:
                    c += 1
                    o.val = c
        with contextlib.ExitStack() as st:
            csem = {e: st.enter_context(nc.semaphore("c_" + e)) for e in ENGS}
            dsem = {e: [st.enter_context(nc.semaphore("d_%s%d" % (e, i))) for i in range(NS_DMA)]
                    for e in ENGS if self.dmaq[e]}
            block = st.enter_context(nc.Block())

            def semof(o):
                return dsem[o.eng][o.slot] if o.dma else csem[o.eng]

            def run(e, h):
                waited = {}
                for o in self.ops[e]:
                    need = {}
                    for d in o.deps:
                        if d.eng == "pe" and e == "pe" and not d.dma and not o.strict:
                            continue
                        sm = semof(d)
                        if waited.get(sm, 0) < d.val and need.get(sm, 0) < d.val:
                            need[sm] = d.val
                    for sm, v in need.items():
                        h.wait_ge(sm, v)
                        waited[sm] = v
                    ins = o.fn(h)
                    if o.dma:
                        ins.then_inc(semof(o), 16)
                    elif o.signal:
                        ins.then_inc(csem[e], 1)
                for o in self.dmaq[e][-NS_DMA:]:
                    if waited.get(semof(o), 0) < o.val:
                        h.wait_ge(semof(o), o.val)
                        waited[semof(o)] = o.val

            @block.tensor
            def _(h):
                run("pe", h)

            @block.scalar
            def _(h):
                run("act", h)

            @block.vector
            def _(h):
                run("dve", h)

            @block.gpsimd
            def _(h):
                run("pool", h)

            @block.sync
            def _(h):
                run("sp", h)


D = 1024
DF = 2816
W = 274
EPS = 1e-5
GN_EPS = 64e-5
ALPHA = float(2.0 ** 0.25)
C0 = float(np.exp(-0.5))
NEG = -1.0e30
GELU_K = float(2.0 * np.sqrt(2.0 / np.pi))

MERGED = 0
QA, KA, VA, OA, GA = 8, 12, 16, 24, 32
GB, KKT, RB, KB, VB, BHT, BON = 8, 16, 24, 32, 40, 48, 56
AG, AV = 8, 30
NSLOT = 64

PC_MU, PC_W0, PC_A0, PC_KKS, PC_KAS, PC_RK, PC_LXG, PC_LXB, PC_MNG = 0, 26, 34, 42, 50, 58, 66, 74, 82
PC_LIG, PC_LIB, PC_L1G, PC_L1B = 90, 98, 106, 114
PC_CW0, PC_CW1, PC_CW2, PC_CB = 128, 150, 172, 194

BLOCKS = [("p", 0, 272, [(0, 16), (16, 64), (80, 64), (144, 64), (208, 64)], 0)]
for _i in range(1, 8):
    BLOCKS.append(("p", 272 + 256 * (_i - 1), 256, [(64 * c, 64) for c in range(4)], 1))
BLOCKS.append(("s", 0, 128, [(8 * c, 8) for c in range(16)], 2))


def build(blocks=BLOCKS, stop=None):
    nc = bass.Bass("TRN2", target_bir_lowering=False)

    def din(name, shape):
        return nc.dram_tensor(name, list(shape), F32, kind="ExternalInput").ap()

    def dout(name, shape):
        return nc.dram_tensor(name, list(shape), F32, kind="ExternalOutput").ap()

    xp = din("xp", [2048, D]); xs = din("xs", [128, D]); meta = din("meta", [16, D])
    sC = din("sC", [16, 4, 128, 256]); sn = din("sn", [16, 128, 4]); sm = din("sm", [4, 16])
    sH = din("sH", [16, 128, 8, 64]); ssh = din("ssh", [128, 8, 16]); scv = din("scv", [128, 22, 16, 2])
    prm0 = din("prm0", [122, 128]); prm1 = din("prm1", [88, 128])
    bif = din("bif", [8, 1]); ln2g = din("ln2g", [D]); ln2b = din("ln2b", [D])
    w_in = din("w_in", [D, 8456]); w2a2 = din("w2a2", [128, D]); g2 = din("g2", [128, D])
    w_out = din("w_out", [D, D]); w_up = din("w_up", [D, 2 * DF]); w_down = din("w_down", [DF, D])

    oy_p = dout("oy_p", [2048, D]); oy_s = dout("oy_s", [128, D])
    oCN_p = dout("oCN_p", [128, 4, 257]); om_p = dout("om_p", [4, 1]); oH_p = dout("oH_p", [128, 8, 64])
    osh_p = dout("osh_p", [128, 8, 1]); ocv_p = dout("ocv_p", [128, 22, 2])
    oCN_s = dout("oCN_s", [16, 128, 4, 257]); om_s = dout("om_s", [4, 16]); oH_s = dout("oH_s", [16, 128, 8, 64])
    osh_s = dout("osh_s", [128, 8, 16]); ocv_s = dout("ocv_s", [128, 22, 16, 2])

    wb_in = nc.dram_tensor("wb_in", [D, 8456], BF16).ap(); wb_out = nc.dram_tensor("wb_out", [D, D], BF16).ap()
    wb_up = nc.dram_tensor("wb_up", [D, 2 * DF], BF16).ap(); wb_down = nc.dram_tensor("wb_down", [DF, D], BF16).ap()
    win_v = wb_in.rearrange("(kc p) c -> p kc c", p=128)
    wup_v = wb_up.rearrange("(kc p) c -> p kc c", p=128)
    wout_v = wb_out.rearrange("(c p) d -> p c d", p=128)
    wdn_v = wb_down.rearrange("(c p) d -> p c d", p=128)

    P = Prog()
    st = contextlib.ExitStack()

    def sb(name, shape):
        return st.enter_context(nc.sbuf_tensor(name, list(shape), F32))

    ARENA = sb("ARENA", [128, NSLOT * W])
    XT = sb("XT", [128, 8, 272])
    TOK = [sb("TOK0", [128, D]), sb("TOK1", [128, D])]
    LNG = sb("LNG", [128, D]); LNB = sb("LNB", [128, D])
    WB = [st.enter_context(nc.sbuf_tensor("WB%d" % i, [128, 4096], BF16)) for i in range(2)]
    XTb = st.enter_context(nc.sbuf_tensor("XTb", [128, 8, 272], BF16))
    XTlo = st.enter_context(nc.sbuf_tensor("XTlo", [128, 8, 272], BF16))
    PC = sb("PC", [128, 216]); PCN = sb("PCN", [128, 16])
    W2A2 = sb("W2A2", [128, D]); G2 = sb("G2", [128, D])
    ID = sb("ID", [128, 128]); ONES = sb("ONES", [128, 128]); BLK = sb("BLK", [128, 128]); AID = sb("AID", [128, 128])
    MK1 = sb("MK1", [64, 2, 64]); MK1N = sb("MK1N", [64, 2, 64]); MLN = sb("MLN", [64, 64]); MNEG = sb("MNEG", [64, 64])
    SEL = sb("SEL", [4, 4, 128])
    RST01_ = sb("RST01", [128, W])
    RST01 = [RST01_, RST01_, RST01_]
    RSTN = sb("RSTN", [4, W])
    CN = [sb("CN0", [128, 4, 257]), sb("CN1", [128, 4, 257])]
    NBC = st.enter_context(nc.sbuf_tensor("NBC", [128, 4, 128], BF16))
    CNb = st.enter_context(nc.sbuf_tensor("CNb", [128, 4, 257], BF16))
    Hb = [st.enter_context(nc.sbuf_tensor("Hb0", [128, 8, 64], BF16)), st.enter_context(nc.sbuf_tensor("Hb1", [128, 8, 64], BF16))]
    IDb = st.enter_context(nc.sbuf_tensor("IDb", [128, 128], BF16)); ONESb = st.enter_context(nc.sbuf_tensor("ONESb", [128, 128], BF16))
    H = [sb("H0", [128, 8, 64]), sb("H1", [128, 8, 64])]
    CARRY = sb("CARRY", [128, 26]); AGC = sb("AGC", [128, 22, 2])
    CV0 = sb("CV0", [128, 22, 16, 2]);
    XWA = sb("XWA", [128, W]); XG = sb("XG", [128, W])
    GL = sb("GL", [128, 8, 16])
    BI = sb("BI", [4, 1]); NBF = sb("NBF", [4, 1]); MCOL = sb("MCOL", [4, 2]); M0ROW = sb("M0ROW", [4, 16]); MNEW = sb("MNEW", [4, 16])
    ST6 = sb("ST6", [128, 12]); MV = sb("MV", [128, 2]); SD = sb("SD", [128, 1]); RS = sb("RS", [128, 1])
    TN = ["LW", "AA", "KKS", "SQ", "NRM", "KKN", "T1", "BB", "CUM", "EGP", "EGN", "CX", "EGX"]
    T = {n: sb("T_" + n, [128, W]) for n in TN}
    PR0 = T["KKS"][:, 0:128]; PR1 = T["SQ"][:, 0:128]
    SHS = T["EGX"][:, 0:128].rearrange("p (a b) -> p a b", a=8)
    LI = sb("LI", [4, W]); LP = sb("LP", [4, W]); CUMP = sb("CUMP", [4, W]); DD = sb("DD", [4, W]); CM = sb("CM", [4, W])
    MX = sb("MX", [4, 64]); ROWS = sb("ROWS", [4, 3, 64]); TR4 = sb("TR4", [4, 64]); WKR = sb("WKR", [4, 64])
    COLS = sb("COLS", [64, 8])
    WST = [sb("WST%d" % i, [64, 64]) for i in range(4)]; SST = [st.enter_context(nc.sbuf_tensor("SST%d" % i, [64, 64], BF16)) for i in range(4)]
    QW = [st.enter_context(nc.sbuf_tensor("QW%d" % i, [128, 64], BF16)) for i in range(4)]; ADEN = [sb("ADEN%d" % i, [128, 64]) for i in range(4)]
    DDT = [sb("DDT%d" % i, [128, 64]) for i in range(4)]; KW = [st.enter_context(nc.sbuf_tensor("KW%d" % i, [64, 128], BF16)) for i in range(4)]
    DCOL = sb("DCOL", [128, 4])
    VTK = st.enter_context(nc.sbuf_tensor("VTK", [64, 512], BF16)); KHK = st.enter_context(nc.sbuf_tensor("KHK", [64, 512], BF16)); BHK = st.enter_context(nc.sbuf_tensor("BHK", [64, 512], BF16))
    S1raw = st.enter_context(nc.sbuf_tensor("S1raw", [64, 1028], BF16)); S2 = st.enter_context(nc.sbuf_tensor("S2", [64, 8, 2, 64], BF16)); PA = st.enter_context(nc.sbuf_tensor("PA", [64, 8, 64], BF16))
    S1 = S1raw[:, 0:1024].rearrange("p (a b c) -> p a b c", a=8, b=2)
    VTK1 = S1raw[:, 0:1028].rearrange("p (h c) -> p h c", h=4)
    KTK = KHK
    PN = [st.enter_context(nc.sbuf_tensor("PN%d" % i, [64, 8, 64], BF16)) for i in range(2)]; PTN = [st.enter_context(nc.sbuf_tensor("PTN%d" % i, [64, 8, 64], BF16)) for i in range(2)]
    U = [st.enter_context(nc.sbuf_tensor("U0", [64, 512], BF16)), st.enter_context(nc.sbuf_tensor("U1", [64, 512], BF16))]
    TMPH = sb("TMPH", [128, 4, 64])

    def sbs(name, shape):
        return st.enter_context(nc.sbuf_tensor(name, shape, BF16))
    setB = (sbs("VTKs", [16, 512]), sbs("KHKs", [16, 512]), sbs("BHKs", [16, 512]), sbs("S1s", [16, 8, 2, 16]), sbs("S2s", [16, 8, 2, 16]),
            sbs("PAs", [16, 8, 16]), [sbs("PNs%d" % i, [16, 8, 16]) for i in range(2)], [sbs("PTNs%d" % i, [16, 8, 16]) for i in range(2)],
            [sbs("Us%d" % i, [16, 512]) for i in range(2)], sb("TMPHs", [128, 4, 64]), "B")
    ps = [st.enter_context(nc.psum_tensor("ps%d" % i, [128, 512], F32)) for i in range(8)]
    psi = [0]

    def PS():
        i = psi[0]
        psi[0] = (i + 1) % 8
        return ps[i], ("ps", i)

    def slot(i):
        return ARENA[:, i * W:(i + 1) * W]

    def slots(i, n):
        return ARENA[:, i * W:(i + n) * W].rearrange("p (c w) -> p c w", c=n)

    def AK(i):
        return ("A", i)

    BFA = ARENA[:, KKT * W:(KKT + 8) * W].bitcast(BF16)
    BFB = ARENA[:, BHT * W:(BHT + 8) * W].bitcast(BF16)

    def KKTb(j):
        return BFA[:, j * W:(j + 1) * W]

    def RTb(j):
        return BFA[:, (8 + j) * W:(9 + j) * W]

    def KHb(j):
        return BFB[:, j * W:(j + 1) * W]

    def BHb(j):
        return BFB[:, (8 + j) * W:(9 + j) * W]

    def Qb(h):
        return slot(QA + h).bitcast(BF16)

    def Kb(h):
        return slot(KA + h).bitcast(BF16)

    def MM(out, lhsT, rhs, start=True, stop=True, r=(), w=(), strict=False):
        P.op("pe", lambda h: h.matmul(out, lhsT=lhsT, rhs=rhs, start=start, stop=stop), r, w, strict=strict)

    def TR(out, in_, n, r=(), w=(), bf=False):
        idn = IDb[0:n, 0:n] if bf else ID[0:n, 0:n]
        P.op("pe", lambda h: h.transpose(out=out, in_=in_, identity=idn), list(r) + ["ID"], w)

    def ACT(out, in_, func, r=(), w=(), bias=0.0, scale=1.0):
        P.op("act", lambda h: h.activation(out=out, in_=in_, func=func, bias=bias, scale=scale), r, w)

    def TT(eng, out, in0, in1, op, r=(), w=()):
        P.op(eng, lambda h: h.tensor_tensor(out=out, in0=in0, in1=in1, op=op), r, w)

    def TS(eng, out, in0, s1, op0, r=(), w=(), s2=None, op1=None):
        if op1 is None and eng == "pool" and op0 == ALU.mult:
            s2, op1 = 1.0, ALU.mult
        if op1 is None:
            P.op(eng, lambda h: h.tensor_scalar(out=out, in0=in0, scalar1=s1, scalar2=None, op0=op0), r, w)
        else:
            P.op(eng, lambda h: h.tensor_scalar(out=out, in0=in0, scalar1=s1, scalar2=s2, op0=op0, op1=op1), r, w)

    def STT(out, in0, sc, in1, op0, op1, r=(), w=()):
        P.op("dve", lambda h: h.scalar_tensor_tensor(out=out, in0=in0, scalar=sc, in1=in1, op0=op0, op1=op1), r, w)

    def CP(eng, out, in_, r=(), w=()):
        if eng == "act":
            P.op("act", lambda h: h.copy(out=out, in_=in_), r, w)
        else:
            P.op(eng, lambda h: h.tensor_copy(out=out, in_=in_), r, w)

    def RECIP(out, in_, r=(), w=()):
        P.op("dve", lambda h: h.reciprocal(out=out, in_=in_), r, w)

    def MSET(eng, ap, v, w=()):
        P.op(eng, lambda h: h.memset(ap, v), (), w)

    def DMA(q, out, in_, r=(), w=(), slow=False):
        P.op(q, lambda h: h.dma_start(out=out, in_=in_, allow_slow_non_contiguous=slow), r, w, dma=True)

    def PCc(i):
        return PC[:, i:i + 1]

    DMA("sp", PR0[0:122, :], prm0, w=["PR0"])
    DMA("sp", PR1[0:88, :], prm1, w=["PR1"])
    DMA("sp", W2A2[:], w2a2, w=["W2A2"])
    DMA("sp", G2[:], g2, w=["G2"])
    DMA("sp", LNG[:], ln2g.partition_broadcast(128), w=["LNG"])
    DMA("sp", LNB[:], ln2b.partition_broadcast(128), w=["LNB"])
    DMA("sp", BI[:], bif[0:4, :], w=["BI"])
    DMA("sp", NBF[:], bif[4:8, :], w=["NBF"])
    DMA("sp", M0ROW[:], sm, w=["M0ROW"])
    DMA("sp", CV0[:], scv, w=[("CV0", i) for i in range(22)])
    MSET("pool", ONES[:], 1.0, w=["ONES"])
    ZER = T["LW"][:, 0:128]; NEG1 = T["AA"][:, 0:128]
    MSET("pool", ZER, 0.0, w=["ZER"])
    MSET("pool", NEG1, -1.0, w=["NEG1"])
    P.op("pool", lambda h: h.affine_select(out=ID[:], in_=ONES[:], pattern=[[1, 128]], compare_op=ALU.is_equal,
                                           fill=0.0, base=0, channel_multiplier=-1), ["ONES"], ["ID"])
    TS("pool", AID[:], ID[:], ALPHA, ALU.mult, r=["ID"], w=["AID"])
    CP("pool", IDb[:], ID[:], r=["ID"], w=["ID"])
    MSET("pool", ONESb[:], 1.0, w=["ONES"])
    MSET("pool", BLK[:], 0.0, w=["BLK"])
    MSET("pool", BLK[0:64, 0:64], 1.0, w=["BLK"])
    MSET("pool", BLK[64:128, 64:128], 1.0, w=["BLK"])
    P.op("pool", lambda h: h.affine_select(out=MK1[:, 0, :], in_=ONES[0:64, 0:64], pattern=[[1, 64]], compare_op=ALU.is_gt,
                                           fill=0.0, base=0, channel_multiplier=-1), ["ONES"], ["MK1"])
    P.op("pool", lambda h: h.affine_select(out=MK1[:, 1, :], in_=ONES[0:64, 0:64], pattern=[[1, 64]], compare_op=ALU.is_ge,
                                           fill=0.0, base=0, channel_multiplier=-1), ["ONES"], ["MK1"])
    TS("pool", MK1N[:], MK1[:], -1.0, ALU.mult, r=["MK1"], w=["MK1N"])
    P.op("pool", lambda h: h.affine_select(out=MLN[:], in_=NEG1[0:64, 0:64], pattern=[[-1, 64]], compare_op=ALU.is_gt,
                                           fill=0.0, base=0, channel_multiplier=1), ["NEG1"], ["MLN"])
    P.op("pool", lambda h: h.affine_select(out=MNEG[:], in_=ZER[0:64, 0:64], pattern=[[1, 64]], compare_op=ALU.is_ge,
                                           fill=NEG, base=0, channel_multiplier=-1), ["ZER"], ["MNEG"])
    CP("dve", SEL[:], ID[0:4, 0:4].unsqueeze(2).broadcast_to([4, 4, 128]), r=["ID"], w=["SEL"])
    TS("pool", NBF[:], NBF[:], -1.0, ALU.mult, r=["NBF"], w=["NBF"])
    pt, pk = PS()
    TR(pt[:, 0:122], PR0[0:122, :], 122, r=["PR0"], w=[pk])
    TR(pt[:, 128:216], PR1[0:88, :], 88, r=["PR1"], w=[pk])
    CP("dve", PC[:, 0:122], pt[:, 0:122], r=[pk], w=["PC"])
    CP("dve", PC[:, 128:216], pt[:, 128:216], r=[pk], w=["PC"])
    TS("dve", PCN[:], PC[:, PC_W0:PC_W0 + 16], -1.0, ALU.mult, r=["PC"], w=["PC"])
    MSET("pool", CN[0][:], 0.0, w=[("CN", 0, h) for h in range(4)])
    MSET("pool", NBC[:], 0.0, w=[("NBC", h) for h in range(4)])
    MSET("pool", H[0][:], 0.0, w=[("H", 0, 0), ("H", 0, 1)])
    MSET("pool", Hb[0][:], 0.0, w=[("Hb", 0, 0), ("Hb", 0, 1)])
    MSET("pool", CNb[:], 0.0, w=[("CNb", h) for h in range(4)])
    MSET("pool", MCOL[:], 0.0, w=["MC0", "MC1"])
    MSET("pool", CARRY[:], 0.0, w=[("CARRY", c) for c in range(26)])
    MSET("pool", AGC[:], 0.0, w=[("AGC", i) for i in range(22)])
    pcs = []
    for (wsrc, wdst, R, C) in ((w_in, wb_in, D, 8456), (w_out, wb_out, D, D), (w_up, wb_up, D, 2 * DF), (w_down, wb_down, DF, D)):
        for r0 in range(0, R, 128):
            for c0 in range(0, C, 4228):
                pcs.append((wsrc, wdst, r0, c0, min(4228, C - c0)))
    for pi_, (wsrc, wdst, r0, c0, n_) in enumerate(pcs):
        sl = pi_ % 2
        stg = ARENA[:, sl * 4228:sl * 4228 + n_]
        ob = ARENA[:, 8456 + sl * 2114:8456 + (sl + 1) * 2114].bitcast(BF16)[:, 0:n_]
        DMA("sp", stg, wsrc[r0:r0 + 128, c0:c0 + n_], w=[("STG", sl)])
        CP(("act", "dve", "pool")[pi_ % 3], ob, stg, r=[("STG", sl)], w=[("OB", sl)])
        DMA("pool", wdst[r0:r0 + 128, c0:c0 + n_], ob, r=[("OB", sl)])
    P.barrier()

    wbi = [0]

    def nextWB():
        i = wbi[0]
        wbi[0] = 1 - i
        return WB[i], ("WB", i)

    def ln_stats(Tt, n, tkey, eps=EPS):
        P.op("dve", lambda h: h.bn_stats(out=ST6[0:n, 0:6], in_=Tt[0:n, 0:512]), [tkey], ["ST6"])
        P.op("dve", lambda h: h.bn_stats(out=ST6[0:n, 6:12], in_=Tt[0:n, 512:1024]), [tkey], ["ST6"])
        P.op("dve", lambda h: h.bn_aggr(out=MV[0:n, :], in_=ST6[0:n, :]), ["ST6"], ["MV"])
        ACT(SD[0:n, :], MV[0:n, 1:2], AF.Ln, r=["MV"], w=["SD"], bias=eps)
        ACT(RS[0:n, :], SD[0:n, :], AF.Exp, r=["SD"], w=["RS"], scale=-0.5)
        TS("dve", Tt[0:n, :], Tt[0:n, :], MV[0:n, 0:1], ALU.subtract, r=[tkey, "MV", "RS"], w=[tkey], s2=RS[0:n, 0:1], op1=ALU.mult)

    def to_feature_major(Tt, n, tkey, col0, gcol, bcol, banks=None):
        for half in range(2):
            if banks is None:
                pt_, pk_ = PS()
            else:
                pt_, pk_ = ps[banks[half]], ("ps", banks[half])
            for i in range(4):
                kc = half * 4 + i
                TR(pt_[:, i * 128:i * 128 + n], Tt[0:n, kc * 128:(kc + 1) * 128], n, r=[tkey], w=[pk_])
            for i in range(4):
                kc = half * 4 + i
                if i % 2 == 0:
                    ACT(XT[:, kc, col0:col0 + n], pt_[:, i * 128:i * 128 + n], AF.Identity, r=[pk_, "PC"], w=["XT"],
                        bias=PCc(bcol + kc), scale=PCc(gcol + kc))
                else:
                    TS("dve", XT[:, kc, col0:col0 + n], pt_[:, i * 128:i * 128 + n], PCc(gcol + kc), ALU.mult,
                       r=[pk_, "PC"], w=["XT"], s2=PCc(bcol + kc), op1=ALU.add)

    prefq = []

    def wload_proj(wv, c_lo, nb, prefetch=False):
        tag = ("p", id(wv), c_lo, nb)
        if not prefetch and prefq and prefq[0][0] == tag:
            return prefq.pop(0)[1:]
        wb, wk = nextWB()
        wbv = wb[:, :].rearrange("p (k c) -> p k c", k=8)
        DMA("sp", wbv[:, :, 0:nb * 128], wv[:, :, c_lo:c_lo + nb * 128], w=[wk])
        if prefetch:
            prefq.append((tag, wb, wk))
        return wb, wk

    def wload_rows(wv, cp_, ncc, prefetch=False):
        tag = ("b", id(wv), cp_, ncc)
        if not prefetch and prefq and prefq[0][0] == tag:
            return prefq.pop(0)[1:]
        wb, wk = nextWB()
        wbv2 = wb[:, :].rearrange("p (c d) -> p c d", c=4)
        DMA("sp", wbv2[:, 0:ncc, :], wv[:, 4 * cp_:4 * cp_ + ncc, :], w=[wk])
        if prefetch:
            prefq.append((tag, wb, wk))
        return wb, wk

    def proj_chunks(wv, col0, nch, ncols, evac):
        i = 0
        while i < nch:
            nb = min(4, nch - i)
            wb, wk = wload_proj(wv, col0 + i * 128, nb)
            wbv = wb[:, :].rearrange("p (k c) -> p k c", k=8)
            for b in range(nb):
                pt_, pk_ = PS()
                for kc in range(8):
                    MM(pt_[:, 0:ncols], wbv[:, kc, b * 128:(b + 1) * 128], XTb[:, kc, 0:ncols], start=(kc == 0), stop=(kc == 7),
                       r=[wk, "XTb"], w=[pk_])
                evac(i + b, pt_, pk_)
            i += nb

    cur_ty = [-1]
    gchunk = [0]

    for (kind, tok0, NT, chunks, ty) in blocks:
        smp = (kind == "s")
        NTB = NT + (16 if smp else 0)
        if ty != cur_ty[0]:
            cur_ty[0] = ty
            MSET("pool", RST01_[:], 1.0, w=["RST"])
            if ty == 0:
                MSET("pool", RST01_[:, 0:1], 0.0, w=["RST"])
                MSET("pool", RST01_[:, 16:272:64], 0.0, w=["RST"])
            elif ty == 1:
                MSET("pool", RST01_[:, 0:256:64], 0.0, w=["RST"])
            else:
                MSET("pool", RST01_[:, 0:128:8], 0.0, w=["RST"])
        tiles = []
        if smp:
            tiles.append((0, 128, [(0, 128, xs)]))
        else:
            c = 0
            while c < NT:
                n = min(128, NT - c)
                srcs = []
                t_lo, t_hi = tok0 + c, tok0 + c + n
                if t_lo < 16:
                    srcs.append((0, 16 - t_lo, meta[t_lo:16, :]))
                    srcs.append((16 - t_lo, n - (16 - t_lo), xp[0:t_hi - 16, :]))
                else:
                    srcs.append((0, n, xp[t_lo - 16:t_hi - 16, :]))
                tiles.append((c, n, srcs))
                c += n

        for ti, (col0, n, srcs) in enumerate(tiles):
            Tt, tkey = TOK[ti % 2], ("TOK", ti % 2)
            for (r0, nr, ap) in srcs:
                DMA("sp", Tt[r0:r0 + nr, :], ap, w=[tkey])
            ln_stats(Tt, n, tkey)
            to_feature_major(Tt, n, tkey, col0, PC_LIG, PC_LIB)
        if smp:
            DMA("sp", XT[:, :, 128:144], ssh, w=["XT"])
        CP("pool", XTb[:, :, 0:NTB], XT[:, :, 0:NTB], r=["XT"], w=["XTb"])
        TT("dve", XTlo[:, :, 0:NT], XT[:, :, 0:NT], XTb[:, :, 0:NT], ALU.subtract, r=["XT", "XTb"], w=["XTlo"])
        if smp:
            pass
            CP("pool", SHS[:], XT[:, :, 7:128:8], r=["XT"], w=["EGX"])
            DMA("pool", osh_s, SHS[:], r=["EGX"])
        elif tok0 + NT == 2064:
            CP("pool", SHS[:, :, 0:1], XT[:, :, NT - 1:NT], r=["XT"], w=["EGX"])
            DMA("pool", osh_p, SHS[:, :, 0:1], r=["EGX"], slow=True)

        if stop == "0":
            P.barrier()
            continue
        def evA(dst0, kindA):
            def f(i, pt_, pk_):
                d = slot(dst0 + i)[:, 0:NT]
                if dst0 in (QA, KA):
                    d = slot(dst0 + i).bitcast(BF16)[:, 0:NT]
                if kindA == "copy":
                    if i % 2 == 0:
                        CP("act", d, pt_[:, 0:NT], r=[pk_], w=[AK(dst0 + i)])
                    else:
                        CP("dve", d, pt_[:, 0:NT], r=[pk_], w=[AK(dst0 + i)])
                elif kindA == "kscale":
                    ACT(d, pt_[:, 0:NT], AF.Identity, r=[pk_], w=[AK(dst0 + i)], scale=float(128 ** -0.5))
                else:
                    ACT(d, pt_[:, 0:NT], AF.Sigmoid, r=[pk_], w=[AK(dst0 + i)])
            return f

        proj_chunks(win_v, 2048, 4, NT, evA(QA, "copy"))
        proj_chunks(win_v, 2560, 4, NT, evA(KA, "kscale"))
        proj_chunks(win_v, 3072, 8, NT, evA(VA, "copy"))
        proj_chunks(win_v, 4096, 8, NT, evA(OA, "sig"))
        wb, wk = nextWB()
        wbv = wb[:, :].rearrange("p (k c) -> p k c", k=8)
        DMA("sp", wbv[:, :, 0:8], win_v[:, :, 5120:5128], w=[wk])
        pi_, ki_ = PS()
        pf_, kf_ = PS()
        for kc in range(8):
            MM(pi_[0:4, 0:NT], wbv[:, kc, 0:4], XTb[:, kc, 0:NT], start=(kc == 0), stop=(kc == 7), r=[wk, "XTb"], w=[ki_])
        for kc in range(8):
            MM(pf_[0:4, 0:NT], wbv[:, kc, 4:8], XTb[:, kc, 0:NT], start=(kc == 0), stop=(kc == 7), r=[wk, "XTb"], w=[kf_])
        ACT(LI[0:4, 0:NT], pi_[0:4, 0:NT], AF.Identity, r=[ki_, "BI"], w=["LI"], bias=BI[0:4, 0:1])
        ACT(LP[0:4, 0:NT], pf_[0:4, 0:NT], AF.Exp, r=[kf_, "NBF"], w=["LP"], bias=NBF[0:4, 0:1], scale=-1.0)
        ACT(LP[0:4, 0:NT], LP[0:4, 0:NT], AF.Ln, r=["LP"], w=["LP"], bias=1.0)
        P.op("dve", lambda h, NT=NT, ty=ty: h.tensor_tensor_scan(out=CUMP[0:4, 0:NT], data0=RST01[ty][0:4, 0:NT], data1=LP[0:4, 0:NT],
                                                  initial=0.0, op0=ALU.mult, op1=ALU.add), ["LP", "RST"], ["CUMP"])
        TT("dve", DD[0:4, 0:NT], LI[0:4, 0:NT], CUMP[0:4, 0:NT], ALU.add, r=["LI", "CUMP"], w=["DD"])
        TS("pool", RSTN[0:4, 0:NT], RST01[ty][0:4, 0:NT], 1.0, ALU.subtract, r=["RST"], w=["RSN"], s2=1.0e30, op1=ALU.mult)
        P.op("dve", lambda h, NT=NT, ty=ty: h.tensor_tensor_scan(out=CM[0:4, 0:NT], data0=RSTN[0:4, 0:NT], data1=DD[0:4, 0:NT],
                                                  initial=0.0, op0=ALU.add, op1=ALU.max), ["DD", "RSN"], ["CM"])
        proj_chunks(win_v, 0, 8, NT, evA(GA, "sig"))
        if stop is None:
            wload_proj(win_v, 1024, 4, prefetch=True)
            wload_proj(win_v, 1536, 4, prefetch=True)
        for c in range(8):
            TT("pool", slot(OA + c)[:, 0:NT], slot(OA + c)[:, 0:NT], slot(GA + c)[:, 0:NT], ALU.mult, r=[AK(OA + c), AK(GA + c)], w=[AK(OA + c)])

        MSET("pool", VTK1[:, :, 256:257], 1.0, w=[("A", "S1")])
        for ci, (c0, L) in enumerate(chunks):
            cs = slice(c0, c0 + L)
            if smp:
                cnb = ci % 2
                m_in, m_in_k = M0ROW[0:4, ci:ci + 1], "M0ROW"
                m_out, m_out_k = MNEW[0:4, ci:ci + 1], "MNEW"
                cnk = [("CN", cnb, h) for h in range(4)]
                DMA("sp", CN[cnb][:, :, 0:256], sC[ci].rearrange("h k v -> k h v"), w=cnk)
                DMA("sp", CN[cnb][:, :, 256:257], sn[ci].unsqueeze(2), w=cnk, slow=True)
                CP("pool", NBC[:], CN[cnb][:, :, 256:257].broadcast_to([128, 4, 128]), r=cnk, w=[("NBC", h) for h in range(4)])
                CP("act", CNb[:], CN[cnb][:], r=cnk, w=[("CNb", h) for h in range(4)])
            else:
                cnb = 0
                g = gchunk[0]
                gchunk[0] += 1
                m_in, m_in_k = MCOL[0:4, g % 2:g % 2 + 1], "MC%d" % (g % 2)
                m_out, m_out_k = MCOL[0:4, (g + 1) % 2:(g + 1) % 2 + 1], "MC%d" % ((g + 1) % 2)
            CNt = CN[cnb]
            TS("dve", MX[0:4, 0:L], CM[0:4, cs], m_in, ALU.max, r=["CM", m_in_k], w=["MX"])
            TS("dve", ROWS[0:4, 0, 0:L], MX[0:4, 0:L], -1.0, ALU.mult, r=["MX"], w=["ROWS"])
            ACT(ROWS[0:4, 1, 0:L], MX[0:4, 0:L], AF.Exp, r=["MX", m_in_k], w=["ROWS"], bias=m_in, scale=-1.0)
            TT("dve", TR4[0:4, 0:L], CUMP[0:4, cs], MX[0:4, 0:L], ALU.subtract, r=["CUMP", "MX"], w=["TR4"])
            ACT(ROWS[0:4, 2, 0:L], TR4[0:4, 0:L], AF.Exp, r=["TR4"], w=["ROWS"])
            ACT(WKR[0:4, 0:L], DD[0:4, cs], AF.Exp, r=["DD", "ROWS"], w=["WKR"], bias=ROWS[0:4, 0, L - 1:L])
            TT("dve", m_out, MX[0:4, L - 1:L], CUMP[0:4, c0 + L - 1:c0 + L], ALU.subtract, r=["MX", "CUMP"], w=[m_out_k])
            pt_, pk_ = PS()
            TR(pt_[0:L, 0:4], DD[0:4, cs], 4, r=["DD"], w=[pk_])
            TR(pt_[0:L, 4:8], WKR[0:4, 0:L], 4, r=["WKR"], w=[pk_])
            CP("act", COLS[0:L, 0:8], pt_[0:L, 0:8], r=[pk_], w=["COLS"])
            pk1, kk1 = PS()
            pk1b = pk1[:, :].bitcast(BF16)
            for h in range(4):
                TR(pk1b[0:L, h * 128:(h + 1) * 128], Kb(h)[:, cs], 128, r=[AK(KA + h)], w=[kk1], bf=True)
            CP("dve", KTK[0:L, :], pk1b[0:L, 0:512], r=[kk1], w=[("A", "KHK")])
            for half in range(2):
                pv, kv = PS()
                for i in range(4):
                    c = half * 4 + i
                    TR(pv[0:L, i * 128:(i + 1) * 128], slot(VA + c)[:, cs], 128, r=[AK(VA + c)], w=[kv])
                CP("act", VTK1[0:L, half * 2:half * 2 + 2, 0:256], pv[0:L, :].rearrange("p (a b) -> p a b", a=2), r=[kv], w=[("A", "S1")])
            for h in range(4):
                TS("pool", KW[h][0:L, :], KTK[0:L, h * 128:(h + 1) * 128], COLS[0:L, 4 + h:5 + h], ALU.mult, r=[("A", "KHK"), "COLS"], w=[("KW", h)])
            PB = [(ps[2 * h], ("ps", 2 * h)) for h in range(4)]
            PN_ = [(ps[2 * h + 1], ("ps", 2 * h + 1)) for h in range(4)]
            for h in range(4):
                pb, kb = PB[h]
                MM(pb[0:L, 0:L], SEL[0:4, h, 0:L], ROWS[0:4, 0, 0:L], start=True, stop=False, r=["SEL", "ROWS"], w=[kb])
                MM(pb[0:L, 0:L], ID[0:L, 0:L], MNEG[0:L, 0:L], start=False, stop=True, r=["ID", "MNEG"], w=[kb])
                MM(pb[:, L:3 * L], SEL[0:4, h, :], ROWS[0:4, 1:3, 0:L], r=["SEL", "ROWS"], w=[kb])
                MM(pb[0:L, 256:256 + L], Kb(h)[:, cs], Qb(h)[:, cs], r=[AK(KA + h), AK(QA + h)], w=[kb])
            for h in range(4):
                pb, kb = PB[h]
                ACT(WST[h][0:L, 0:L], pb[0:L, 0:L], AF.Exp, r=[kb, "COLS"], w=[("WST", h)], bias=COLS[0:L, h:h + 1])
            for h in range(4):
                pb, kb = PB[h]
                TT("dve", SST[h][0:L, 0:L], WST[h][0:L, 0:L], pb[0:L, 256:256 + L], ALU.mult, r=[("WST", h), kb], w=[("SST", h)])
                TT("dve", QW[h][:, 0:L], Qb(h)[:, cs], pb[:, L:2 * L], ALU.mult, r=[AK(QA + h), kb], w=[("QW", h)])
            for h in range(4):
                pn_, kn = PN_[h]
                for vc in range(2):
                    MM(pn_[:, vc * L:(vc + 1) * L], VTK1[0:L, h, vc * 128:(vc + 1) * 128], SST[h][0:L, 0:L], start=True, stop=False,
                       r=[("A", "S1"), ("SST", h)], w=[kn])
                    MM(pn_[:, vc * L:(vc + 1) * L], CNb[:, h, vc * 128:(vc + 1) * 128], QW[h][:, 0:L], start=False, stop=True,
                       r=[("CNb", h), ("QW", h)], w=[kn])
                MM(pn_[:, 2 * L:3 * L], ONESb[0:L, :], SST[h][0:L, 0:L], start=True, stop=False, r=["ONES", ("SST", h)], w=[kn])
                MM(pn_[:, 2 * L:3 * L], NBC[:, h, :], QW[h][:, 0:L], start=False, stop=True, r=[("NBC", h), ("QW", h)], w=[kn])
                MM(pn_[:, 192:449], KW[h][0:L, :], VTK1[0:L, h, :], r=[("KW", h), ("A", "S1")], w=[kn])
            for h in range(4):
                pb, kb = PB[h]
                pn_, kn = PN_[h]
                ACT(ADEN[h][:, 0:L], pn_[:, 2 * L:3 * L], AF.Abs, r=[kn], w=[("ADEN", h)])
                CP("act", DCOL[:, h:h + 1], pb[:, 2 * L - 1:2 * L], r=[kb], w=[("DCOL", h)])
            for h in range(4):
                pb, kb = PB[h]
                pn_, kn = PN_[h]
                TT("dve", DDT[h][:, 0:L], ADEN[h][:, 0:L], pb[:, 2 * L:3 * L], ALU.max, r=[("ADEN", h), kb], w=[("DDT", h)])
                RECIP(DDT[h][:, 0:L], DDT[h][:, 0:L], r=[("DDT", h)], w=[("DDT", h)])
                TT("dve", slots(VA + 2 * h, 2)[:, :, cs], pn_[:, 0:2 * L].rearrange("p (a b) -> p a b", a=2),
                   DDT[h][:, 0:L].unsqueeze(1).broadcast_to([128, 2, L]), ALU.mult,
                   r=[kn, ("DDT", h)], w=[AK(VA + 2 * h), AK(VA + 2 * h + 1)])
                STT(CNt[:, h, :], CNt[:, h, :], DCOL[:, h:h + 1], pn_[:, 192:449], ALU.mult, ALU.add,
                    r=[("CN", cnb, h), kn, ("DCOL", h)], w=[("CN", cnb, h)])
            for h in range(4):
                CP("pool", NBC[:, h, :], CNt[:, h, 256:257].broadcast_to([128, 128]), r=[("CN", cnb, h)], w=[("NBC", h)])
                CP("act", CNb[:, h, :], CNt[:, h, :], r=[("CN", cnb, h)], w=[("CNb", h)])
            if smp:
                DMA("pool", oCN_s[ci], CNt[:], r=[("CN", cnb, h) for h in range(4)])
        if smp:
            DMA("pool", om_s, MNEW[:], r=["MNEW"])
        elif tok0 + NT == 2064:
            DMA("pool", oCN_p, CN[0][:], r=[("CN", 0, h) for h in range(4)])
            gl = gchunk[0] % 2
            DMA("pool", om_p, MCOL[0:4, gl:gl + 1], r=["MC%d" % gl])

        TN3 = ["LW", "AA", "KKS", "SQ", "NRM", "KKN", "T1", "BB", "CUM", "EGP", "EGN", "CX"]
        hs = list(range(4))
        tq = {h: (TN3[3 * h], TN3[3 * h + 1], TN3[3 * h + 2]) for h in hs}
        pmk = {}
        for h in hs:
            c0_, c1_ = VA + 2 * h, VA + 2 * h + 1
            pm, km = PS()
            pmk[h] = (pm, km)
            MM(pm[:, 0:NT], ONES[:, :], slot(c0_)[:, 0:NT], start=True, stop=False, r=["ONES", AK(c0_)], w=[km])
            MM(pm[:, 0:NT], ONES[:, :], slot(c1_)[:, 0:NT], start=False, stop=True, r=["ONES", AK(c1_)], w=[km])
        for h in hs:
            pm, km = pmk[h]
            for c_ in (VA + 2 * h, VA + 2 * h + 1):
                STT(slot(c_)[:, 0:NT], pm[:, 0:NT], -1.0 / 256, slot(c_)[:, 0:NT], ALU.mult, ALU.add, r=[km, AK(c_)], w=[AK(c_)])
        for h in hs:
            c0_, c1_ = VA + 2 * h, VA + 2 * h + 1
            TT("pool", T[tq[h][0]][:, 0:NT], slot(c0_)[:, 0:NT], slot(c0_)[:, 0:NT], ALU.mult, r=[AK(c0_)], w=[tq[h][0]])
            TT("pool", T[tq[h][1]][:, 0:NT], slot(c1_)[:, 0:NT], slot(c1_)[:, 0:NT], ALU.mult, r=[AK(c1_)], w=[tq[h][1]])
        for h in hs:
            pv2, kv2 = PS()
            pmk[h] = (pv2, kv2)
            MM(pv2[:, 0:NT], ONES[:, :], T[tq[h][0]][:, 0:NT], start=True, stop=False, r=["ONES", tq[h][0]], w=[kv2])
            MM(pv2[:, 0:NT], ONES[:, :], T[tq[h][1]][:, 0:NT], start=False, stop=True, r=["ONES", tq[h][1]], w=[kv2])
        for h in hs:
            pv2, kv2 = pmk[h]
            ACT(T[tq[h][2]][:, 0:NT], pv2[:, 0:NT], AF.Ln, r=[kv2], w=[tq[h][2]], bias=EPS, scale=1.0 / 256)
        for h in hs:
            ACT(T[tq[h][2]][:, 0:NT], T[tq[h][2]][:, 0:NT], AF.Exp, r=[tq[h][2]], w=[tq[h][2]], scale=-0.5)
        for h in hs:
            for vc, c_ in enumerate((VA + 2 * h, VA + 2 * h + 1)):
                cc = 2 * h + vc
                STT(slot(c_)[:, 0:NT], slot(c_)[:, 0:NT], PCc(PC_MNG + cc), T[tq[h][2]][:, 0:NT], ALU.mult, ALU.mult, r=[AK(c_), tq[h][2], "PC"], w=[AK(c_)])
        for h in hs:
            for vc, c_ in enumerate((VA + 2 * h, VA + 2 * h + 1)):
                cc = 2 * h + vc
                TT("pool" if vc else "dve", slot(MERGED + cc)[:, 0:NT], slot(c_)[:, 0:NT], slot(OA + cc)[:, 0:NT], ALU.mult, r=[AK(c_), AK(OA + cc)], w=[AK(MERGED + cc)])
        if stop == "A":
            P.barrier()
            continue

        def evGB(i, pt_, pk_):
            ACT(slot(GB + i)[:, 0:NT], pt_[:, 0:NT], AF.Sigmoid, r=[pk_], w=[AK(GB + i)])

        proj_chunks(win_v, 1024, 8, NT, evGB)

        def evPB(cc, pt_, pk_):
            if cc < 8:
                dst, dk = slot(RB + cc), AK(RB + cc)
            elif cc < 16:
                dst, dk = slot(KB + cc - 8), AK(KB + cc - 8)
            elif cc < 24:
                dst, dk = slot(VB + cc - 16), AK(VB + cc - 16)
            elif cc == 24:
                dst, dk = XWA, "XWA"
            else:
                dst, dk = XG, "XG"
            if cc % 2 == 0:
                CP("act", dst[:, 0:NTB], pt_[:, 0:NTB], r=[pk_], w=[dk])
            else:
                CP("dve", dst[:, 0:NTB], pt_[:, 0:NTB], r=[pk_], w=[dk])
            ds, dsk = (T["CX"], "CX") if cc % 2 == 0 else (T["EGX"], "EGX")
            if smp:
                d3 = dst[:, 0:128].rearrange("p (s t) -> p s t", t=8)
                s3 = ds[:, 0:128].rearrange("p (s t) -> p s t", t=8)
                TT("pool", s3[:, :, 1:8], d3[:, :, 0:7], d3[:, :, 1:8], ALU.subtract, r=[dk], w=[dsk])
                TT("pool", s3[:, :, 0:1], dst[:, 128:144].unsqueeze(2), d3[:, :, 0:1], ALU.subtract, r=[dk], w=[dsk])
            else:
                TT("pool", ds[:, 1:NT], dst[:, 0:NT - 1], dst[:, 1:NT], ALU.subtract, r=[dk], w=[dsk])
                TT("pool", ds[:, 0:1], CARRY[:, cc:cc + 1], dst[:, 0:1], ALU.subtract, r=[dk, ("CARRY", cc)], w=[dsk])
                CP("pool", CARRY[:, cc:cc + 1], dst[:, NT - 1:NT], r=[dk], w=[("CARRY", cc)])
            STT(dst[:, 0:NT], ds[:, 0:NT], PCc(PC_MU + cc), dst[:, 0:NT], ALU.mult, ALU.add, r=[dsk, dk, "PC"], w=[dk])

        proj_chunks(win_v, 5128, 26, NTB, evPB)
        if stop is None:
            wload_rows(wout_v, 0, 4, prefetch=True)
            wload_rows(wout_v, 1, 4, prefetch=True)
        if stop == "B1":
            P.barrier()
            continue
        ACT(XWA[0:64, 0:NT], XWA[0:64, 0:NT], AF.Tanh, r=["XWA"], w=["XWA"])
        ACT(XG[:, 0:NT], XG[:, 0:NT], AF.Sigmoid, r=["XG"], w=["XG"])

        nchunks = len(chunks)
        last0, Lc = chunks[-1][0] + chunks[-1][1], chunks[-1][1]
        first_last = chunks[0][0] + chunks[0][1] - 1
        for j in range(8):
            Rj, Kj, Vj = slot(RB + j)[:, 0:NT], slot(KB + j)[:, 0:NT], slot(VB + j)[:, 0:NT]
            rk_, kk_, vk_ = AK(RB + j), AK(KB + j), AK(VB + j)
            cols = slice(j * 128, (j + 1) * 128)

            def t(n):
                return T[n][:, 0:NT]
            pw, kw = PS()
            MM(pw[:, 0:NT], W2A2[0:64, cols], XWA[0:64, 0:NT], r=["W2A2", "XWA"], w=[kw])
            pa, ka = PS()
            MM(pa[:, 0:NT], W2A2[64:128, cols], XWA[64:128, 0:NT], r=["W2A2", "XWA"], w=[ka])
            ACT(t("LW"), pw[:, 0:NT], AF.Exp, r=[kw, "PC"], w=["LW"], bias=PCN[:, j:j + 1], scale=-1.0)
            ACT(t("AA"), pa[:, 0:NT], AF.Exp, r=[ka, "PC"], w=["AA"], bias=PCN[:, 8 + j:9 + j], scale=-1.0)
            ACT(t("LW"), t("LW"), AF.Ln, r=["LW"], w=["LW"], bias=1.0)
            ACT(t("AA"), t("AA"), AF.Ln, r=["AA"], w=["AA"], bias=1.0)
            ACT(t("LW"), t("LW"), AF.Exp, r=["LW"], w=["LW"], scale=-1.0)
            ACT(t("AA"), t("AA"), AF.Exp, r=["AA"], w=["AA"], scale=-1.0)
            TS("pool", t("KKS"), Kj, PCc(PC_KKS + j), ALU.mult, r=[kk_, "PC"], w=["KKS"])
            TT("pool", t("SQ"), t("KKS"), t("KKS"), ALU.mult, r=["KKS"], w=["SQ"])
            pn2, kn2 = PS()
            MM(pn2[:, 0:NT], BLK[:, :], t("SQ"), r=["BLK", "SQ"], w=[kn2])
            P.op("dve", lambda h, NT=NT, ty=ty: h.tensor_tensor_scan(out=T["CUM"][:, 0:NT], data0=RST01[ty][:, 0:NT], data1=T["LW"][:, 0:NT],
                                                         initial=0.0, op0=ALU.mult, op1=ALU.add), ["LW", "RST"], ["CUM"])
            TS("dve", t("NRM"), pn2[:, 0:NT], 1e-24, ALU.max, r=[kn2], w=["NRM"])
            ACT(t("NRM"), t("NRM"), AF.Ln, r=["NRM"], w=["NRM"])
            ACT(t("NRM"), t("NRM"), AF.Exp, r=["NRM"], w=["NRM"], scale=-0.5)
            ACT(t("EGP"), t("CUM"), AF.Exp, r=["CUM"], w=["EGP"], scale=-C0)
            ACT(t("EGN"), t("CUM"), AF.Exp, r=["CUM"], w=["EGN"], scale=C0)
            TT("pool", t("CX"), t("CUM"), t("LW"), ALU.subtract, r=["CUM", "LW"], w=["CX"])
            ACT(t("EGX"), t("CX"), AF.Exp, r=["CX"], w=["EGX"], scale=-C0)
            TT("dve", t("KKN"), t("KKS"), t("NRM"), ALU.mult, r=["KKS", "NRM"], w=["KKN"])
            TS("dve", t("T1"), t("AA"), 1.0, ALU.subtract, r=["AA", "PC"], w=["T1"], s2=PCc(PC_KAS + j), op1=ALU.mult)
            STT(Kj, t("T1"), 1.0, Kj, ALU.add, ALU.mult, r=["T1", kk_], w=[kk_])
            TT("pool", t("BB"), t("KKN"), t("AA"), ALU.mult, r=["KKN", "AA"], w=["BB"])
            CP("pool", GL[:, j, 0:nchunks], T["EGP"][:, first_last:last0:Lc] if nchunks > 1 else T["EGP"][:, first_last:first_last + 1],
               r=["EGP"], w=[("GL", j)])
            bon = slot(BON + j)[:, 0:NT]
            STT(bon, Rj, PCc(PC_RK + j), Kj, ALU.mult, ALU.mult, r=[rk_, kk_, "PC"], w=[AK(BON + j)])
            pb2, kb2 = PS()
            MM(pb2[:, 0:NT], BLK[:, :], bon, r=["BLK", AK(BON + j)], w=[kb2])
            TT("dve", RTb(j)[:, 0:NT], Rj, t("EGP"), ALU.mult, r=[rk_, "EGP"], w=[("RTb", j), AK(KKT + 4 + j // 2)])
            TT("dve", KHb(j)[:, 0:NT], Kj, t("EGN"), ALU.mult, r=[kk_, "EGN"], w=[("KHb", j)])
            TT("pool", KKTb(j)[:, 0:NT], t("KKN"), t("EGX"), ALU.mult, r=["KKN", "EGX"], w=[AK(KKT + j), AK(KKT + j // 2)])
            TT("pool", BHb(j)[:, 0:NT], t("BB"), t("EGN"), ALU.mult, r=["BB", "EGN"], w=[AK(BHT + j)])
            TT("dve", bon, pb2[:, 0:NT], Vj, ALU.mult, r=[kb2, vk_], w=[AK(BON + j)])
        if stop == "B2":
            P.barrier()
            continue

        def QQ(j, rows, cs):
            return ARENA[:, KKT * W:(KKT + 16) * W].bitcast(BF16)[:, j * W:j * W + 16 * W].rearrange("p (two d) -> p two d", two=2)[rows, :, cs]

        for ci, (c0, L) in enumerate(chunks):
            cs = slice(c0, c0 + L)
            nl = {8: 3, 16: 4, 64: 6}[L]
            if DBGSTEP and ci < DBGCHUNK:
                continue
            hb = ci % 2 if smp else 0
            Ht = H[hb]
            if smp:
                DMA("sp", Ht[:], sH[ci], w=[("H", hb, 0), ("H", hb, 1)])
                CP("act", Hb[hb][:], Ht[:], r=[("H", hb, 0), ("H", hb, 1)], w=[("Hb", hb, 0), ("Hb", hb, 1)])
            def half_steps(jh, TSet):
                VTK, KHK, BHK, S1, S2, PA, PN, PTN, U, TMPH, kp = TSet
                hk = ("H", hb, jh)
                hbk = ("Hb", hb, jh)
                Hbt = Hb[hb]
                pA, kA = PS(); pB, kB = PS(); pC, kC = PS()
                pBb = pB[:, :].bitcast(BF16); pCb = pC[:, :].bitcast(BF16)
                for jj in range(4):
                    j = 4 * jh + jj
                    TR(pA[0:L, jj * 128:(jj + 1) * 128], slot(VB + j)[:, cs], 128, r=[AK(VB + j)], w=[kA])
                    TR(pBb[0:L, jj * 128:(jj + 1) * 128], KHb(j)[:, cs], 128, r=[("KHb", j)], w=[kB], bf=True)
                    TR(pCb[0:L, jj * 128:(jj + 1) * 128], BHb(j)[:, cs], 128, r=[AK(BHT + j)], w=[kC], bf=True)
                CP("act", VTK[0:L, :], pA[0:L, :], r=[kA], w=[(kp, "VTK")])
                CP("dve", KHK[0:L, :], pBb[0:L, 0:512], r=[kB], w=[(kp, "KHK")])
                ACT(BHK[0:L, :], pCb[0:L, 0:512], AF.Identity, r=[kC], w=[(kp, "BHK")], scale=-1.0)

                yield
                def hd(hq):
                    hp, jj = divmod(hq, 4)
                    return 4 * jh + jj, jj, hp, slice(64 * hp, 64 * hp + 64), slice(jj * 128 + hp * 64, jj * 128 + hp * 64 + 64)
                b1 = [PS(), PS()]
                for hq in range(8):
                    j, jj, hp, rows, tc = hd(hq)
                    MM(b1[hp][0][0:L, jj * 2 * L:(jj + 1) * 2 * L], KHb(j)[rows, cs], QQ(j, rows, cs),
                       r=[("KHb", j), AK(KKT + j), ("RTb", j)], w=[b1[hp][1]])
                yield
                for hp in range(2):
                    mk = MK1[0:L, :, 0:L].unsqueeze(1).broadcast_to([L, 4, 2, L])
                    TT("dve", S1[0:L, 4 * hp:4 * hp + 4, :, 0:L],
                       b1[hp][0][0:L, 0:8 * L].rearrange("p (a b c) -> p a b c", a=4, b=2), mk, ALU.mult,
                       r=[b1[hp][1], "MK1"], w=[(kp, "S1")])
                yield
                b2 = [PS(), PS()]
                for hq in range(8):
                    j, jj, hp, rows, tc = hd(hq)
                    MM(b2[hp][0][0:L, jj * 2 * L:(jj + 1) * 2 * L], BHb(j)[rows, cs], QQ(j, rows, cs),
                       r=[AK(BHT + j), AK(KKT + j), ("RTb", j)], w=[b2[hp][1]])
                for hp in range(2):
                    mkn = MK1N[0:L, :, 0:L].unsqueeze(1).broadcast_to([L, 4, 2, L])
                    TT("dve", S2[0:L, 4 * hp:4 * hp + 4, :, 0:L],
                       b2[hp][0][0:L, 0:8 * L].rearrange("p (a b c) -> p a b c", a=4, b=2), mkn, ALU.mult,
                       r=[b2[hp][1], "MK1N"], w=[(kp, "S2")])
                yield
                p3 = [PS(), PS()]
                for hq in range(8):
                    j, jj, hp, rows, tc = hd(hq)
                    MM(p3[hp][0][0:L, jj * L:(jj + 1) * L], KKTb(j)[rows, cs], BHb(j)[rows, cs],
                       r=[AK(KKT + j), AK(BHT + j)], w=[p3[hp][1]])
                for hp in range(2):
                    TT("dve", PA[0:L, 4 * hp:4 * hp + 4, 0:L], p3[hp][0][0:L, 0:4 * L].rearrange("p (a b) -> p a b", a=4),
                       MLN[0:L, 0:L].unsqueeze(1).broadcast_to([L, 4, L]), ALU.mult, r=[p3[hp][1], "MLN"], w=[(kp, "PA")])
                yield
                pU2 = [PS(), PS()]
                for hq in range(8):
                    j, jj, hp, rows, tc = hd(hq)
                    MM(pU2[hp][0][0:L, jj * 64:(jj + 1) * 64], KKTb(j)[rows, cs], Hbt[rows, j, :], start=(jj == 0), stop=False,
                       r=[AK(KKT + j), hbk], w=[pU2[hp][1]])
                for hq in range(8):
                    j, jj, hp, rows, tc = hd(hq)
                    MM(pU2[hp][0][0:L, jj * 64:(jj + 1) * 64], S1[0:L, hq, 0, 0:L], VTK[0:L, tc], start=False, stop=(jj == 3),
                       r=[(kp, "S1"), (kp, "VTK")], w=[pU2[hp][1]], strict=(hp == 1 and jj == 0))
                CP("act", U[0][0:L, 0:256], pU2[0][0][0:L, 0:256], r=[pU2[0][1]], w=[(kp, "U", 0)])
                CP("act", U[0][0:L, 256:512], pU2[1][0][0:L, 0:256], r=[pU2[1][1]], w=[(kp, "U", 0)])
                yield
                cur = 0
                Pt, Pk = PA, (kp, "PA")
                PTt, PTk = S2, (kp, "S2")

                def PTv(hq):
                    return PTt[0:L, hq, 0, 0:L] if PTk == (kp, "S2") else PTt[0:L, hq, 0:L]
                for l in range(nl):
                    pU, kU = PS()
                    for hq in range(8):
                        MM(pU[0:L, hq * 64:(hq + 1) * 64], PTv(hq), U[cur][0:L, hq * 64:(hq + 1) * 64], r=[PTk, (kp, "U", cur)], w=[kU])
                    TT("dve", U[1 - cur][0:L, :], U[cur][0:L, :], pU[0:L, :], ALU.add, r=[(kp, "U", cur), kU], w=[(kp, "U", 1 - cur)])
                    cur = 1 - cur
                    if l < nl - 1:
                        need_p = (l < nl - 2)
                        pT, kT = PS()
                        for hq in range(8):
                            MM(pT[0:L, hq * L:(hq + 1) * L], Pt[0:L, hq, 0:L], PTv(hq), r=[PTk, Pk], w=[kT])
                        nP, nPT = PN[l % 2], PTN[l % 2]
                        if need_p:
                            pP, kP = PS()
                            for hq in range(8):
                                MM(pP[0:L, hq * L:(hq + 1) * L], PTv(hq), Pt[0:L, hq, 0:L], r=[PTk, Pk], w=[kP])
                            CP("act", nP[0:L, :, 0:L], pP[0:L, 0:8 * L].rearrange("p (a b) -> p a b", a=8), r=[kP], w=[(kp, "PN", l % 2)])
                        CP("dve", nPT[0:L, :, 0:L], pT[0:L, 0:8 * L].rearrange("p (a b) -> p a b", a=8), r=[kT], w=[(kp, "PTN", l % 2)])
                        Pt, Pk = nP, (kp, "PN", l % 2)
                        PTt, PTk = nPT, (kp, "PTN", l % 2)
                    yield
                yield
                pY2 = [PS(), PS()]
                for hq in range(8):
                    j, jj, hp, rows, tc = hd(hq)
                    o_ = pY2[hp][0][rows, jj * L:(jj + 1) * L]
                    MM(o_, Hbt[rows, j, :], RTb(j)[rows, cs], start=(jj == 0), stop=False, r=[hbk, ("RTb", j)], w=[pY2[hp][1]])
                for hq in range(8):
                    j, jj, hp, rows, tc = hd(hq)
                    o_ = pY2[hp][0][rows, jj * L:(jj + 1) * L]
                    MM(o_, VTK[0:L, tc], S1[0:L, hq, 1, 0:L], start=False, stop=False, r=[(kp, "VTK"), (kp, "S1")], w=[pY2[hp][1]], strict=(hp == 1 and jj == 0))
                    MM(o_, U[cur][0:L, hq * 64:(hq + 1) * 64], S2[0:L, hq, 1, 0:L], start=False, stop=(jj == 3), r=[(kp, "U", cur), (kp, "S2")], w=[pY2[hp][1]])
                pH, kH = PS()
                for hq in range(8):
                    j, jj, hp, rows, tc = hd(hq)
                    o_ = pH[rows, jj * 64:(jj + 1) * 64]
                    MM(o_, KHK[0:L, tc], VTK[0:L, tc], start=True, stop=False, r=[(kp, "KHK"), (kp, "VTK")], w=[kH])
                    MM(o_, BHK[0:L, tc], U[cur][0:L, hq * 64:(hq + 1) * 64], start=False, stop=True, r=[(kp, "BHK"), (kp, "U", cur)], w=[kH])
                yield
                for hp in range(2):
                    rows = slice(64 * hp, 64 * hp + 64)
                    CP("act", slots(RB + 4 * jh, 4)[rows, :, cs], pY2[hp][0][rows, 0:4 * L].rearrange("p (a b) -> p a b", a=4), r=[pY2[hp][1]],
                       w=[AK(RB + 4 * jh + q) for q in range(4)])
                TT("dve", TMPH[:], Ht[:, 4 * jh:4 * jh + 4, :], pH[:, 0:256].rearrange("p (a b) -> p a b", a=4), ALU.add, r=[hk, kH], w=[(kp, "TMPH")])
                TT("pool", Ht[:, 4 * jh:4 * jh + 4, :], TMPH[:], GL[:, 4 * jh:4 * jh + 4, ci:ci + 1].broadcast_to([128, 4, 64]), ALU.mult,
                   r=[(kp, "TMPH")] + [("GL", 4 * jh + q) for q in range(4)], w=[hk])
                CP("act", Hbt[:, 4 * jh:4 * jh + 4, :], Ht[:, 4 * jh:4 * jh + 4, :], r=[hk], w=[hbk])

            setA = (VTK, KHK, BHK, S1, S2, PA, PN, PTN, U, TMPH, "A")
            if smp:
                gens = [half_steps(0, setA), half_steps(1, setB)]
                while gens:
                    for g_ in list(gens):
                        try:
                            next(g_)
                        except StopIteration:
                            gens.remove(g_)
            else:
                for jh in range(2):
                    for _ in half_steps(jh, setA):
                        pass
            if smp:
                DMA("pool", oH_s[ci], Ht[:], r=[("H", hb, 0), ("H", hb, 1)])
            if DBGSTEP and ci >= DBGCHUNK:
                break
        if (not smp) and tok0 + NT == 2064:
            DMA("pool", oH_p, H[0][:], r=[("H", 0, 0), ("H", 0, 1)])

        if stop == "B3":
            P.barrier()
            continue
        TN3 = ["LW", "AA", "KKS", "SQ", "NRM", "KKN", "T1", "BB", "CUM", "EGP", "EGN", "CX"]
        for g0 in (0, 4):
            js = list(range(g0, g0 + 4))
            tq = {j: (TN3[3 * (j - g0)], TN3[3 * (j - g0) + 1], TN3[3 * (j - g0) + 2]) for j in js}
            pk_ = {}
            for j in js:
                pm, km = PS()
                pk_[j] = (pm, km)
                MM(pm[:, 0:NT], BLK[:, :], slot(RB + j)[:, 0:NT], r=["BLK", AK(RB + j)], w=[km])
            for j in js:
                pm, km = pk_[j]
                Yj, yk = slot(RB + j)[:, 0:NT], AK(RB + j)
                STT(Yj, pm[:, 0:NT], -1.0 / 64, Yj, ALU.mult, ALU.add, r=[km, yk], w=[yk])
            for j in js:
                Yj, yk = slot(RB + j)[:, 0:NT], AK(RB + j)
                TT("pool", T[tq[j][0]][:, 0:NT], Yj, Yj, ALU.mult, r=[yk], w=[tq[j][0]])
            for j in js:
                pv2, kv2 = PS()
                pk_[j] = (pv2, kv2)
                MM(pv2[:, 0:NT], BLK[:, :], T[tq[j][0]][:, 0:NT], r=["BLK", tq[j][0]], w=[kv2])
            for j in js:
                pv2, kv2 = pk_[j]
                ACT(T[tq[j][1]][:, 0:NT], pv2[:, 0:NT], AF.Ln, r=[kv2], w=[tq[j][1]], bias=GN_EPS, scale=1.0 / 64)
            for j in js:
                ACT(T[tq[j][1]][:, 0:NT], T[tq[j][1]][:, 0:NT], AF.Exp, r=[tq[j][1]], w=[tq[j][1]], scale=-0.5)
            for j in js:
                pg, kg = PS()
                pk_[j] = (pg, kg)
                MM(pg[:, 0:NT], G2[:, j * 128:(j + 1) * 128], XG[:, 0:NT], r=["G2", "XG"], w=[kg])
            for j in js:
                Yj, yk = slot(RB + j)[:, 0:NT], AK(RB + j)
                t1 = T[tq[j][2]][:, 0:NT]
                STT(t1, Yj, PCc(PC_LXG + j), T[tq[j][1]][:, 0:NT], ALU.mult, ALU.mult, r=[yk, tq[j][1], "PC"], w=[tq[j][2]])
                STT(t1, t1, PCc(PC_LXB + j), slot(BON + j)[:, 0:NT], ALU.add, ALU.add, r=[tq[j][2], AK(BON + j), "PC"], w=[tq[j][2]])
            for j in js:
                pg, kg = pk_[j]
                t1 = T[tq[j][2]][:, 0:NT]
                TT("dve", t1, t1, pg[:, 0:NT], ALU.mult, r=[tq[j][2], kg], w=[tq[j][2]])
            for j in js:
                t1 = T[tq[j][2]][:, 0:NT]
                TT("pool", t1, t1, slot(GB + j)[:, 0:NT], ALU.mult, r=[tq[j][2], AK(GB + j)], w=[tq[j][2]])
                TT("pool", slot(MERGED + j)[:, 0:NT], slot(MERGED + j)[:, 0:NT], t1, ALU.add, r=[tq[j][2], AK(MERGED + j)], w=[AK(MERGED + j)])
        P.barrier()
        if stop == "B":
            continue

        def big_out(wv, nrowch, lhs_of, resid):
            for cp_ in range((nrowch + 3) // 4):
                ncc = min(4, nrowch - 4 * cp_)
                wb, wk = wload_rows(wv, cp_, ncc)
                wbv2 = wb[:, :].rearrange("p (c d) -> p c d", c=4)
                for ci_ in range(ncc):
                    c = 4 * cp_ + ci_
                    for ti, (col0, n, _) in enumerate(tiles):
                        for half in range(2):
                            MM(ps[2 * ti + half][0:n, 0:512], lhs_of(c)[:, col0:col0 + n], wbv2[:, ci_, half * 512:(half + 1) * 512],
                               start=(c == 0), stop=False, r=[wk] + resid[1], w=[("ps", 2 * ti + half)])
            for ti, (col0, n, _) in enumerate(tiles):
                for c in range(8):
                    o_ = ps[2 * ti + c // 4][0:n, (c % 4) * 128:(c % 4 + 1) * 128]
                    MM(o_, XTb[:, c, col0:col0 + n], IDb[:, :], start=False, stop=False, r=["XTb", "ID"], w=[("ps", 2 * ti + c // 4)])
                    MM(o_, XTlo[:, c, col0:col0 + n], IDb[:, :], start=False, stop=(c % 4 == 3), r=["XTlo", "ID"], w=[("ps", 2 * ti + c // 4)])

        MRGb = ARENA[:, 52 * W:56 * W].bitcast(BF16).rearrange("p (c w) -> p c w", c=8)
        TS("pool", MRGb[:, :, 0:NT], slots(MERGED, 8)[:, :, 0:NT], 1.0 / ALPHA, ALU.mult, r=[AK(MERGED + c) for c in range(8)], w=["MRGb"])
        big_out(wout_v, 8, lambda c: MRGb[:, c, :], (None, ["MRGb"]))
        for ti, (col0, n, _) in enumerate(tiles):
            Tt, tkey = TOK[ti % 2], ("TOK", ti % 2)
            CP("act", Tt[0:n, 0:512], ps[2 * ti][0:n, :], r=[("ps", 2 * ti)], w=[tkey])
            CP("dve", Tt[0:n, 512:1024], ps[2 * ti + 1][0:n, :], r=[("ps", 2 * ti + 1)], w=[tkey])
            ln_stats(Tt, n, tkey, eps=EPS / (ALPHA * ALPHA))
            to_feature_major(Tt, n, tkey, col0, PC_L1G, PC_L1B, banks=(6, 7))
        CP("pool", XTb[:, :, 0:NT], XT[:, :, 0:NT], r=["XT"], w=["XTb"])
        TT("dve", XTlo[:, :, 0:NT], XT[:, :, 0:NT], XTb[:, :, 0:NT], ALU.subtract, r=["XT", "XTb"], w=["XTlo"])

        def evAG(i, pt_, pk_):
            if smp:
                d = slot(AG + i)[:, 0:160].rearrange("p (s t) -> p s t", t=10)[:, :, 2:10]
                CP("act", d, pt_[:, 0:128].rearrange("p (s t) -> p s t", t=8), r=[pk_], w=[AK(AG + i)])
            else:
                CP("act", slot(AG + i)[:, 2:2 + NT], pt_[:, 0:NT], r=[pk_], w=[AK(AG + i)])

        def evAV(i, pt_, pk_):
            CP("dve", slot(AV + i)[:, 0:NT], pt_[:, 0:NT], r=[pk_], w=[AK(AV + i)])

        proj_chunks(wup_v, 0, 22, NT, evAG)
        proj_chunks(wup_v, DF, 22, NT, evAV)
        _tn = ["LW", "AA", "KKS", "SQ", "NRM", "KKN", "T1", "BB", "CUM", "EGP", "EGN", "CX"]
        for g0 in range(0, 22, 6):
            idx = list(range(g0, min(g0 + 6, 22)))
            tk = {i: (_tn[2 * (i - g0)], _tn[2 * (i - g0) + 1]) for i in idx}
            for i in idx:
                ag, agk = slot(AG + i), AK(AG + i)
                if smp:
                    a3 = ag[:, 0:160].rearrange("p (s t) -> p s t", t=10)
                    CP("pool", a3[:, :, 0:2], CV0[:, i, :, :], r=[("CV0", i)], w=[agk])
                    CP("pool", CV0[:, i, :, :], a3[:, :, 8:10], r=[agk], w=[("CV0", i)])
                else:
                    CP("pool", ag[:, 0:2], AGC[:, i, :], r=[("AGC", i)], w=[agk])
                    CP("pool", AGC[:, i, :], ag[:, NT:NT + 2], r=[agk], w=[("AGC", i)])
            for i in idx:
                ag, agk = slot(AG + i), AK(AG + i)
                cvk, g1k = tk[i]
                cv = T[cvk]
                if smp:
                    a3 = ag[:, 0:160].rearrange("p (s t) -> p s t", t=10)
                    cv3 = cv[:, 0:128].rearrange("p (s t) -> p s t", t=8)
                    TS("dve", cv3, a3[:, :, 0:8], PCc(PC_CW0 + i), ALU.mult, r=[agk, "PC"], w=[cvk], s2=PCc(PC_CB + i), op1=ALU.add)
                    STT(cv3, a3[:, :, 1:9], PCc(PC_CW1 + i), cv3, ALU.mult, ALU.add, r=[agk, cvk, "PC"], w=[cvk])
                    STT(cv3, a3[:, :, 2:10], PCc(PC_CW2 + i), cv3, ALU.mult, ALU.add, r=[agk, cvk, "PC"], w=[cvk])
                else:
                    ACT(cv[:, 0:NT], ag[:, 0:NT], AF.Identity, r=[agk, "PC"], w=[cvk], bias=PCc(PC_CB + i), scale=PCc(PC_CW0 + i))
                    STT(cv[:, 0:NT], ag[:, 1:NT + 1], PCc(PC_CW1 + i), cv[:, 0:NT], ALU.mult, ALU.add, r=[agk, cvk, "PC"], w=[cvk])
                    STT(cv[:, 0:NT], ag[:, 2:NT + 2], PCc(PC_CW2 + i), cv[:, 0:NT], ALU.mult, ALU.add, r=[agk, cvk, "PC"], w=[cvk])
            for i in idx:
                cvk, g1k = tk[i]
                ACT(T[g1k][:, 0:NT], T[cvk][:, 0:NT], AF.Square, r=[cvk], w=[g1k])
            for i in idx:
                cvk, g1k = tk[i]
                TS("dve", T[g1k][:, 0:NT], T[g1k][:, 0:NT], 0.044715, ALU.mult, r=[g1k], w=[g1k], s2=1.0, op1=ALU.add)
            for i in idx:
                cvk, g1k = tk[i]
                TT("pool", T[g1k][:, 0:NT], T[g1k][:, 0:NT], T[cvk][:, 0:NT], ALU.mult, r=[g1k, cvk], w=[g1k])
            for i in idx:
                cvk, g1k = tk[i]
                ACT(T[g1k][:, 0:NT], T[g1k][:, 0:NT], AF.Sigmoid, r=[g1k], w=[g1k], scale=GELU_K)
            for i in idx:
                cvk, g1k = tk[i]
                TT("pool", T[g1k][:, 0:NT], T[g1k][:, 0:NT], T[cvk][:, 0:NT], ALU.mult, r=[g1k, cvk], w=[g1k])
            for i in idx:
                cvk, g1k = tk[i]
                STT(slot(AG + i).bitcast(BF16)[:, 0:NT], T[g1k][:, 0:NT], 1.0 / ALPHA, slot(AV + i)[:, 0:NT], ALU.mult, ALU.mult,
                    r=[g1k, AK(AV + i), cvk], w=[AK(AG + i)])
        if smp:
            DMA("pool", ocv_s, CV0[:], r=[("CV0", i) for i in range(22)])
        elif tok0 + NT == 2064:
            DMA("pool", ocv_p, AGC[:], r=[("AGC", i) for i in range(22)])

        big_out(wdn_v, 22, lambda c: slot(AG + c).bitcast(BF16), (None, [AK(AG + c) for c in range(22)]))
        for ti, (col0, n, _) in enumerate(tiles):
            Tt, tkey = TOK[ti % 2], ("TOK", ti % 2)
            CP("act", Tt[0:n, 0:512], ps[2 * ti][0:n, :], r=[("ps", 2 * ti)], w=[tkey])
            CP("dve", Tt[0:n, 512:1024], ps[2 * ti + 1][0:n, :], r=[("ps", 2 * ti + 1)], w=[tkey])
            ln_stats(Tt, n, tkey, eps=EPS / (ALPHA * ALPHA))
            TT("pool", Tt[0:n, :], Tt[0:n, :], LNG[0:n, :], ALU.mult, r=[tkey, "LNG"], w=[tkey])
            TT("dve", Tt[0:n, :], Tt[0:n, :], LNB[0:n, :], ALU.add, r=[tkey, "LNB"], w=[tkey])
            if smp:
                DMA("pool", oy_s, Tt[0:128, :], r=[tkey])
            else:
                t_lo = tok0 + col0
                if t_lo < 16:
                    DMA("pool", oy_p[0:n - (16 - t_lo), :], Tt[16 - t_lo:n, :], r=[tkey])
                else:
                    DMA("pool", oy_p[t_lo - 16:t_lo - 16 + n, :], Tt[0:n, :], r=[tkey])

    P.emit(nc)
    st.close()
    return nc


_NC_CACHE = {}


def _host_inputs(inp, b):
    f = lambda a: np.ascontiguousarray(a, dtype=np.float32)
    s0, s1 = 16 * b, 16 * b + 16
    prm0 = np.concatenate([inp["rwkv_mu"][0], inp["rwkv_w0"][0], inp["rwkv_a0"][0], inp["rwkv_kk_scale"][0], inp["rwkv_ka_scale"][0],
                           inp["rwkv_rk"][0], inp["rwkv_lnx_g"][0], inp["rwkv_lnx_b"][0], inp["mlstm_norm_g"][0],
                           inp["ln_in_g"], inp["ln_in_b"], inp["ln1_g"][0], inp["ln1_b"][0]]).reshape(122, 128)
    cw = inp["ffn_conv_w"][0]
    prm1 = np.concatenate([cw[0], cw[1], cw[2], inp["ffn_conv_b"][0]]).reshape(88, 128)
    sS = inp["state_rwkv_S"][0, s0:s1]
    sH = sS.reshape(16, 8, 2, 64, 64).transpose(0, 2, 4, 1, 3).reshape(16, 128, 8, 64)
    ssh = inp["state_rwkv_shift"][0, s0:s1].reshape(16, 8, 128).transpose(2, 1, 0)
    scv = inp["state_ffn_conv"][0, s0:s1].reshape(16, 2, 22, 128).transpose(3, 2, 0, 1)
    return {
        "xp": f(inp["x_prompt"][b]), "xs": f(inp["x_sample"][s0:s1].reshape(128, D)), "meta": f(inp["meta_tokens"]),
        "sC": f(inp["state_mlstm_C"][0, s0:s1]), "sn": f(inp["state_mlstm_n"][0, s0:s1].transpose(0, 2, 1)),
        "sm": f(inp["state_mlstm_m"][0, s0:s1].T), "sH": f(sH), "ssh": f(ssh), "scv": f(scv),
        "prm0": f(prm0), "prm1": f(prm1), "bif": f(inp["b_if"][0].reshape(8, 1)),
        "ln2g": f(inp["ln2_g"][0]), "ln2b": f(inp["ln2_b"][0]),
        "w_in": f(inp["w_in"][0]), "w2a2": f(np.concatenate([inp["rwkv_w2"][0], inp["rwkv_a2"][0]], 0)), "g2": f(inp["rwkv_g2"][0]),
        "w_out": f(inp["w_out"][0]), "w_up": f(inp["ffn_w_up"][0]), "w_down": f(inp["ffn_w_down"][0]),
    }


def kernel(**inputs):
    inp = {k: np.asarray(v) for k, v in inputs.items()}
    if "nc" not in _NC_CACHE:
        _NC_CACHE["nc"] = build()
    nc = _NC_CACHE["nc"]
    in_maps = [_host_inputs(inp, b) for b in range(8)]
    res = run_bass_kernel_spmd(nc, in_maps, core_ids=list(range(8))).results
    g = lambda k: [np.asarray(r[k], dtype=np.float32) for r in res]
    y_p = np.stack(g("oy_p"), 0)
    y_s = np.concatenate(g("oy_s"), 0).reshape(128, 8, D)
    cn_p = np.stack(g("oCN_p"), 0)
    pC = cn_p[..., 0:256].transpose(0, 2, 1, 3)[None]
    pn = cn_p[..., 256].transpose(0, 2, 1)[None]
    pm = np.stack(g("om_p"), 0)[:, :, 0][None]
    Hp = np.stack(g("oH_p"), 0)
    pS = Hp.reshape(8, 2, 64, 8, 64).transpose(0, 3, 1, 4, 2).reshape(8, 16, 64, 64)[None]
    psh = np.stack(g("osh_p"), 0)[..., 0].transpose(0, 2, 1).reshape(8, D)[None]
    pcv = np.stack(g("ocv_p"), 0).transpose(0, 3, 2, 1).reshape(8, 2, DF)[None]
    cn_s = np.concatenate(g("oCN_s"), 0)
    sC = cn_s[..., 0:256].transpose(0, 2, 1, 3)[None]
    sn = cn_s[..., 256].transpose(0, 2, 1)[None]
    sm = np.concatenate([a.T for a in g("om_s")], 0)[None]
    Hs = np.concatenate(g("oH_s"), 0)
    sS = Hs.reshape(128, 2, 64, 8, 64).transpose(0, 3, 1, 4, 2).reshape(128, 16, 64, 64)[None]
    ssh = np.concatenate([a.transpose(2, 1, 0).reshape(16, D) for a in g("osh_s")], 0)[None]
    scv = np.concatenate([a.transpose(2, 3, 1, 0).reshape(16, 2, DF) for a in g("ocv_s")], 0)[None]
    c = lambda a: np.ascontiguousarray(a, dtype=np.float32)
    return (c(y_p), c(y_s), c(pC), c(pn), c(pm), c(pS), c(psh), c(pcv), c(sC), c(sn), c(sm), c(sS), c(ssh), c(scv))
```
